# Optimizing a Trainium2 kernel written in Bass

```python
import math
import jax, jax.numpy as jnp
from jax import lax
import numpy as np

D_MODEL = 1024
BATCH = 16
SEQ = 2048
DEPTH = 1
DEC_BATCH = 32
DEC_SEQ = 16
PAST_LEN = 1024

CHUNK = 64
A_HEADS = 16
A_HEAD_DIM = D_MODEL // A_HEADS
A_LEFT_CHUNKS = 8
A_WINDOW = A_LEFT_CHUNKS * CHUNK
A_BAND = A_WINDOW + CHUNK
A_REL_CLIP = 128
B_HEADS = 8
B_HEAD_DIM = D_MODEL // B_HEADS
B_CONV = 4
MEM_TOKENS = 256
C_HEADS = 4
C_HEAD_DIM = D_MODEL // C_HEADS
D_FF = 4 * D_MODEL

A_WIDTH = A_HEADS * A_HEAD_DIM
B_WIDTH = B_HEADS * B_HEAD_DIM
C_WIDTH = C_HEADS * C_HEAD_DIM
N_BRANCH = 3
IN_COLS = 3 * A_WIDTH + 3 * B_WIDTH + C_WIDTH + N_BRANCH * D_MODEL + 2 * B_HEADS
DN_ALPHA = (2.0 * DEPTH) ** 0.25
DN_BETA = (8.0 * DEPTH) ** -0.25
LN_EPS = 1e-5
RMS_EPS = 1e-6
L2_EPS = 1e-6
NEG_INF = -1e30

kernel_name = 'hybrid_stream_chunkband_gdn_mem'


def _layer_norm(x, g, b):
    xf = x.astype(jnp.float32)
    mu = jnp.mean(xf, axis=-1, keepdims=True)
    var = jnp.mean(jnp.square(xf - mu), axis=-1, keepdims=True)
    return ((xf - mu) * lax.rsqrt(var + LN_EPS) * g.astype(jnp.float32) + b.astype(jnp.float32)).astype(x.dtype)


def _l2norm(x):
    return x * lax.rsqrt(jnp.sum(jnp.square(x), axis=-1, keepdims=True) + L2_EPS)


def _split_in(z):
    sizes = (A_WIDTH, A_WIDTH, A_WIDTH, 3 * B_WIDTH, C_WIDTH, N_BRANCH * D_MODEL, B_HEADS, B_HEADS)
    offs = tuple(int(o) for o in np.cumsum(sizes)[:-1])
    return jnp.split(z, offs, axis=-1)


def _band_attention(q, k, v, q_pos, k_pos, k_valid, rel_bias):
    s = jnp.einsum('bqhd,bkhd->bhqk', q, k).astype(jnp.float32) * (A_HEAD_DIM ** -0.5)
    rel = jnp.clip(q_pos[:, None] - k_pos[None, :], -A_REL_CLIP, A_REL_CLIP) + A_REL_CLIP
    s = s + rel_bias[:, rel].astype(jnp.float32)[None]
    s = jnp.where(k_valid[None, None, None, :], s, NEG_INF)
    p = jax.nn.softmax(s, axis=-1).astype(v.dtype)
    return jnp.einsum('bhqk,bkhd->bqhd', p, v)


def _chunk_band_prompt(q, k, v, rel_bias):
    bsz, t = q.shape[0], q.shape[1]
    nc = t // CHUNK
    pad = ((0, 0), (A_WINDOW, 0), (0, 0), (0, 0))
    kp = jnp.pad(k, pad)
    vp = jnp.pad(v, pad)
    qc = jnp.moveaxis(q.reshape(bsz, nc, CHUNK, A_HEADS, A_HEAD_DIM), 1, 0)

    def one_chunk(args):
        c, qb = args
        start = c * CHUNK
        kb = lax.dynamic_slice_in_dim(kp, start, A_BAND, axis=1)
        vb = lax.dynamic_slice_in_dim(vp, start, A_BAND, axis=1)
        q_pos = start + jnp.arange(CHUNK)
        k_pos = start - A_WINDOW + jnp.arange(A_BAND)
        return _band_attention(qb, kb, vb, q_pos, k_pos, k_pos >= 0, rel_bias)

    out = lax.map(one_chunk, (jnp.arange(nc), qc))
    return jnp.moveaxis(out, 0, 1).reshape(bsz, t, A_WIDTH)


def _causal_conv(x, buf, w):
    t = x.shape[1]
    xp = jnp.concatenate([buf.astype(x.dtype), x], axis=1)
    y = xp[:, 0:t] * w[0]
    for i in range(1, B_CONV):
        y = y + xp[:, i:i + t] * w[i]
    return jax.nn.silu(y), xp[:, xp.shape[1] - (B_CONV - 1):]


def _gated_delta(q, k, v, g, beta, state):
    bsz, t, h, dk = k.shape
    dv = v.shape[-1]
    c = min(CHUNK, t)
    nc = t // c

    def blk(a):
        return jnp.swapaxes(a.reshape((bsz, nc, c) + a.shape[2:]), 2, 3)

    qc, kc, vc, gc, bc = blk(q), blk(k), blk(v), blk(g), blk(beta)
    G = jnp.cumsum(gc, axis=-1)
    diff = G[..., :, None] - G[..., None, :]
    tri_strict = jnp.tril(jnp.ones((c, c), dtype=bool), -1)
    tri_incl = jnp.tril(jnp.ones((c, c), dtype=bool))
    dec_strict = jnp.exp(jnp.where(tri_strict, diff, -jnp.inf))
    dec_incl = jnp.exp(jnp.where(tri_incl, diff, -jnp.inf))
    a_kk = bc[..., :, None] * jnp.einsum('bnhtd,bnhsd->bnhts', kc, kc) * dec_strict
    lower = jnp.eye(c, dtype=q.dtype) + a_kk
    rhs = jnp.concatenate([bc[..., None] * vc, (bc * jnp.exp(G))[..., None] * kc], axis=-1)
    sol = lax.linalg.triangular_solve(lower, rhs, left_side=True, lower=True, unit_diagonal=True)
    u_blk, w_blk = sol[..., :dv], sol[..., dv:]
    a_qk = jnp.einsum('bnhtd,bnhsd->bnhts', qc, kc) * dec_incl
    q_dec = qc * jnp.exp(G)[..., None]
    k_dec = kc * jnp.exp(G[..., -1:] - G)[..., None]
    g_end = jnp.exp(G[..., -1])

    def step(s, inp):
        u_n, w_n, qd_n, aqk_n, kd_n, ge_n = inp
        wv = u_n - jnp.einsum('bhtk,bhkv->bhtv', w_n, s)
        o = jnp.einsum('bhtk,bhkv->bhtv', qd_n, s) + jnp.einsum('bhts,bhsv->bhtv', aqk_n, wv)
        s = ge_n[..., None, None] * s + jnp.einsum('bhsk,bhsv->bhkv', kd_n, wv)
        return s, o

    xs = tuple(jnp.moveaxis(a, 1, 0) for a in (u_blk, w_blk, q_dec, a_qk, k_dec, g_end))
    s_final, o = lax.scan(step, state, xs)
    o = jnp.transpose(o, (1, 0, 3, 2, 4)).reshape(bsz, t, h, dv)
    return o, s_final


def _mixer_b(qkv_b, a_b, b_b, conv_buf, state, conv_w, a_log, dt_bias, norm_g):
    bsz, t, _ = qkv_b.shape
    y, new_buf = _causal_conv(qkv_b, conv_buf, conv_w)
    y = y.astype(jnp.float32).reshape(bsz, t, 3, B_HEADS, B_HEAD_DIM)
    q = _l2norm(y[:, :, 0]) * (B_HEAD_DIM ** -0.5)
    k = _l2norm(y[:, :, 1])
    v = y[:, :, 2]
    g = -jnp.exp(a_log.astype(jnp.float32)) * jax.nn.softplus(a_b.astype(jnp.float32) + dt_bias.astype(jnp.float32))
    beta = jax.nn.sigmoid(b_b.astype(jnp.float32))
    o, new_state = _gated_delta(q, k, v, g, beta, state.astype(jnp.float32))
    o = o * lax.rsqrt(jnp.mean(jnp.square(o), axis=-1, keepdims=True) + RMS_EPS) * norm_g.astype(jnp.float32)
    return o.reshape(bsz, t, B_WIDTH).astype(qkv_b.dtype), new_buf, new_state.astype(qkv_b.dtype)


def _mem_kv(mem, w_kv):
    bsz, m, _ = mem.shape
    kv = (mem @ w_kv).reshape(bsz, m, 2, C_HEADS, C_HEAD_DIM)
    return kv[:, :, 0], kv[:, :, 1]


def _mem_attention(q, mk, mv):
    s = jnp.einsum('bqhd,bmhd->bhqm', q, mk).astype(jnp.float32) * (C_HEAD_DIM ** -0.5)
    p = jax.nn.softmax(s, axis=-1).astype(mv.dtype)
    o = jnp.einsum('bhqm,bmhd->bqhd', p, mv)
    return o.reshape(q.shape[0], q.shape[1], C_WIDTH)


def _merge_and_ffn(x, o_a, o_b, o_c, gate_logits, w_out, ln1_g, ln1_b, w_ff1, b_ff1, w_ff2, b_ff2, ln2_g, ln2_b):
    bsz, t, _ = x.shape
    gates = jax.nn.sigmoid(gate_logits.reshape(bsz, t, N_BRANCH, D_MODEL))
    merged = jnp.sum(gates * jnp.stack([o_a, o_b, o_c], axis=2), axis=2)
    h = _layer_norm(DN_ALPHA * x + merged @ w_out, ln1_g, ln1_b)
    f = jnp.square(jax.nn.relu(h @ w_ff1 + b_ff1)) @ w_ff2 + b_ff2
    return _layer_norm(DN_ALPHA * h + f, ln2_g, ln2_b)


def setup_inputs(seed: int = 0) -> dict:
    key = jax.random.key(seed)
    ks = jax.random.split(key, 26)
    f32 = jnp.float32
    a_cache = min(A_WINDOW, PAST_LEN)

    def nrm(k, shape, scale=1.0):
        return jax.random.normal(k, shape, f32) * scale

    dt = jnp.exp(jax.random.uniform(ks[12], (DEPTH, B_HEADS), f32, math.log(1e-3), math.log(1e-1)))
    return {
        'x_prompt': nrm(ks[0], (BATCH, SEQ, D_MODEL)),
        'x_sample': nrm(ks[1], (DEC_BATCH, DEC_SEQ, D_MODEL)),
        'cache_a_k': nrm(ks[2], (DEPTH, DEC_BATCH, a_cache, A_HEADS, A_HEAD_DIM)),
        'cache_a_v': nrm(ks[3], (DEPTH, DEC_BATCH, a_cache, A_HEADS, A_HEAD_DIM)),
        'state_b_conv': nrm(ks[4], (DEPTH, DEC_BATCH, B_CONV - 1, 3 * B_WIDTH)),
        'state_b_ssm': nrm(ks[5], (DEPTH, DEC_BATCH, B_HEADS, B_HEAD_DIM, B_HEAD_DIM), 0.1),
        'cache_mem_k': nrm(ks[6], (DEPTH, DEC_BATCH, MEM_TOKENS, C_HEADS, C_HEAD_DIM)),
        'cache_mem_v': nrm(ks[7], (DEPTH, DEC_BATCH, MEM_TOKENS, C_HEADS, C_HEAD_DIM)),
        'mem_prompt': nrm(ks[8], (BATCH, MEM_TOKENS, D_MODEL)),
        'w_in': nrm(ks[9], (DEPTH, D_MODEL, IN_COLS), D_MODEL ** -0.5),
        'w_b_conv': nrm(ks[10], (DEPTH, B_CONV, 3 * B_WIDTH), B_CONV ** -0.5),
        'b_a_log': jnp.log(jax.random.uniform(ks[11], (DEPTH, B_HEADS), f32, 1.0, 16.0)),
        'b_dt_bias': dt + jnp.log(-jnp.expm1(-dt)),
        'b_norm_g': 1.0 + nrm(ks[13], (DEPTH, B_HEAD_DIM), 0.05),
        'a_rel_bias': nrm(ks[14], (DEPTH, A_HEADS, 2 * A_REL_CLIP + 1), 0.5),
        'w_mem_kv': nrm(ks[15], (DEPTH, D_MODEL, 2 * C_WIDTH), D_MODEL ** -0.5),
        'w_out': nrm(ks[16], (DEPTH, D_MODEL, D_MODEL), D_MODEL ** -0.5 * DN_BETA),
        'ln1_g': 1.0 + nrm(ks[17], (DEPTH, D_MODEL), 0.05),
        'ln1_b': nrm(ks[18], (DEPTH, D_MODEL), 0.05),
        'w_ff1': nrm(ks[19], (DEPTH, D_MODEL, D_FF), D_MODEL ** -0.5),
        'b_ff1': nrm(ks[20], (DEPTH, D_FF), 0.05),
        'w_ff2': nrm(ks[21], (DEPTH, D_FF, D_MODEL), D_FF ** -0.5 * DN_BETA),
        'b_ff2': nrm(ks[22], (DEPTH, D_MODEL), 0.05),
        'ln2_g': 1.0 + nrm(ks[23], (DEPTH, D_MODEL), 0.05),
        'ln2_b': nrm(ks[24], (DEPTH, D_MODEL), 0.05),
    }


def reference(x_prompt, x_sample, cache_a_k, cache_a_v, state_b_conv, state_b_ssm,
              cache_mem_k, cache_mem_v, mem_prompt, w_in, w_b_conv, b_a_log, b_dt_bias,
              b_norm_g, a_rel_bias, w_mem_kv, w_out, ln1_g, ln1_b, w_ff1, b_ff1,
              w_ff2, b_ff2, ln2_g, ln2_b):
    bp, tp, _ = x_prompt.shape
    bs, tn, _ = x_sample.shape
    keep_p = min(A_WINDOW, tp)
    lc = cache_a_k.shape[2]
    xp, xs = x_prompt, x_sample
    a_k_p, a_v_p, conv_p, ssm_p, mk_p, mv_p = [], [], [], [], [], []
    a_k_s, a_v_s, conv_s, ssm_s = [], [], [], []
    for l in range(DEPTH):
        ffn_w = (w_out[l], ln1_g[l], ln1_b[l], w_ff1[l], b_ff1[l], w_ff2[l], b_ff2[l], ln2_g[l], ln2_b[l])
        q_a, k_a, v_a, qkv_b, q_c, g_logit, a_b, b_b = _split_in(xp @ w_in[l])
        q_a, k_a, v_a = (t.reshape(bp, tp, A_HEADS, A_HEAD_DIM) for t in (q_a, k_a, v_a))
        o_a = _chunk_band_prompt(q_a, k_a, v_a, a_rel_bias[l])
        conv0 = jnp.zeros((bp, B_CONV - 1, 3 * B_WIDTH), xp.dtype)
        ssm0 = jnp.zeros((bp, B_HEADS, B_HEAD_DIM, B_HEAD_DIM), jnp.float32)
        o_b, cbuf, ssm = _mixer_b(qkv_b, a_b, b_b, conv0, ssm0, w_b_conv[l], b_a_log[l], b_dt_bias[l], b_norm_g[l])
        mk, mv = _mem_kv(mem_prompt, w_mem_kv[l])
        o_c = _mem_attention(q_c.reshape(bp, tp, C_HEADS, C_HEAD_DIM), mk, mv)
        xp = _merge_and_ffn(xp, o_a, o_b, o_c, g_logit, *ffn_w)
        a_k_p.append(k_a[:, tp - keep_p:])
        a_v_p.append(v_a[:, tp - keep_p:])
        conv_p.append(cbuf)
        ssm_p.append(ssm)
        mk_p.append(mk)
        mv_p.append(mv)
        q_a, k_a, v_a, qkv_b, q_c, g_logit, a_b, b_b = _split_in(xs @ w_in[l])
        q_a, k_a, v_a = (t.reshape(bs, tn, A_HEADS, A_HEAD_DIM) for t in (q_a, k_a, v_a))
        k_all = jnp.concatenate([cache_a_k[l], k_a], axis=1)
        v_all = jnp.concatenate([cache_a_v[l], v_a], axis=1)
        q_pos = PAST_LEN + jnp.arange(tn)
        k_pos = PAST_LEN - lc + jnp.arange(lc + tn)
        o_a = _band_attention(q_a, k_all, v_all, q_pos, k_pos, k_pos >= 0, a_rel_bias[l]).reshape(bs, tn, A_WIDTH)
        o_b, cbuf, ssm = _mixer_b(qkv_b, a_b, b_b, state_b_conv[l], state_b_ssm[l], w_b_conv[l], b_a_log[l], b_dt_bias[l], b_norm_g[l])
        o_c = _mem_attention(q_c.reshape(bs, tn, C_HEADS, C_HEAD_DIM), cache_mem_k[l], cache_mem_v[l])
        xs = _merge_and_ffn(xs, o_a, o_b, o_c, g_logit, *ffn_w)
        a_k_s.append(k_all[:, tn:])
        a_v_s.append(v_all[:, tn:])
        conv_s.append(cbuf)
        ssm_s.append(ssm)
    return (xp, xs,
            jnp.stack(a_k_p), jnp.stack(a_v_p), jnp.stack(conv_p), jnp.stack(ssm_p),
            jnp.stack(mk_p), jnp.stack(mv_p),
            jnp.stack(a_k_s), jnp.stack(a_v_s), jnp.stack(conv_s), jnp.stack(ssm_s))
```

```python
import contextlib
import numpy as np
import concourse.bass as bass
import concourse.mybir as mybir
from concourse.bass_utils import run_bass_kernel_spmd

F32 = mybir.dt.float32
BF16 = mybir.dt.bfloat16
AF = mybir.ActivationFunctionType
ALU = mybir.AluOpType

D = 1024
IN_COLS = 10256
OFF_QA, OFF_KA, OFF_VA = 0, 1024, 2048
OFF_QB, OFF_KB, OFF_VB = 3072, 4096, 5120
OFF_QC = 6144
OFF_GA, OFF_GB, OFF_GC = 7168, 8192, 9216
OFF_AB = 10240
ALPHA = 2.0 ** 0.25
NEG = -30000.0
LN_EPS = 1e-5
RMS_EPS = 1e-6
L2_EPS = 1e-6
PAST_LEN = 1024
LC = 512

C_ID = 0
C_TRI = 128
C_ONESBD = 256
C_SEL0 = 384
C_SEL1 = 512
C_MNEG = 640
C_STRICT = 768
C_ROWM = 896
C_ONES = 900
C_MASKA = 1028
C_MASKN = 1668
NCONST = 1732


def make_consts():
    c = np.zeros((128, NCONST), np.float32)
    r = np.arange(128)[:, None]
    t = np.arange(128)[None, :]
    same = (r // 64) == (t // 64)
    c[:, C_ID:C_ID + 128] = np.eye(128)
    c[:, C_TRI:C_TRI + 128] = (same & (r <= t))
    c[:, C_ONESBD:C_ONESBD + 128] = same
    c[:, C_SEL0:C_SEL0 + 128] = (r < 64) & (t >= 0)
    c[:, C_SEL1:C_SEL1 + 128] = (r >= 64) & (t >= 0)
    c[:, C_MNEG:C_MNEG + 128] = np.where(same & (r <= t), 0.0, -60000.0)
    c[:, C_STRICT:C_STRICT + 128] = (same & (r < t))
    c[:, C_ROWM] = ((np.arange(128) % 64) < 16)
    c[:, C_ONES:C_ONES + 128] = 1.0
    kk = np.arange(128)[:, None, None]
    j = np.arange(5)[None, :, None]
    qq = np.arange(128)[None, None, :]
    cq = qq // 64
    pos = 128 * j + kk
    valid = (pos >= 64 * cq) & (pos < 576 + 64 * cq)
    c[:, C_MASKA:C_MASKA + 640] = np.where(valid, 0.0, NEG).reshape(128, 640)
    mn = np.zeros((128, 64), np.float32)
    mn[(np.arange(128) % 64) >= 16, :] = NEG
    c[:, C_MASKN:C_MASKN + 64] = mn
    return c


def make_bias_tables(rel_bias):
    rb = np.asarray(rel_bias, np.float32)
    kk = np.arange(128)[:, None, None]
    j5 = np.arange(5)[None, :, None]
    qq = np.arange(128)[None, None, :]
    rel = 512 - 128 * j5 + qq - kk
    idx = np.clip(rel, -128, 128) + 128
    biasP = rb[:, idx].reshape(16, 128, 640)
    j4 = np.arange(4)[None, :, None]
    q64 = np.arange(64)[None, None, :]
    rel = 512 + q64 - 128 * j4 - kk
    idx = np.clip(rel, -128, 128) + 128
    biasS = rb[:, idx].reshape(16, 128, 256)
    k64 = np.arange(64)[:, None]
    q64 = np.arange(64)[None, :]
    idx = np.clip(q64 - k64, -128, 128) + 128
    biasN = rb[:, idx].reshape(16, 64, 64)
    return (np.ascontiguousarray(biasP), np.ascontiguousarray(biasS), np.ascontiguousarray(biasN))


class Res:
    __slots__ = ("name", "w", "r", "ds", "ps")

    def __init__(self, name, ps=False):
        self.name = name
        self.w = None
        self.r = []
        self.ds = {}
        self.ps = ps


class DSem:
    def __init__(self, sem):
        self.sem = sem
        self.cnt = 0


class Sync:
    ENG = ["tensor", "vector", "scalar", "gpsimd", "sync"]

    def __init__(self, nc, n_dma_sems=72):
        self.nc = nc
        self.E = {}
        for n in self.ENG:
            self.E[n] = dict(e=getattr(nc, n), sem=nc.alloc_semaphore(name="s_" + n), cnt=0, seen={})
        self.free_ds = {"hw": [DSem(nc.alloc_semaphore(name="d%d" % i)) for i in range(n_dma_sems)],
                        "sw": [DSem(nc.alloc_semaphore(name="q%d" % i)) for i in range(10)]}
        self.all_ds = self.free_ds["hw"] + self.free_ds["sw"]
        self.owned = []
        self.ninst = 0

    def _wait(self, en, deps):
        E = self.E[en]
        need = {}
        for (sem, val) in deps:
            k = sem.num
            if E["seen"].get(k, 0) >= val:
                continue
            if k not in need or need[k][1] < val:
                need[k] = (sem, val)
        for k, (sem, val) in need.items():
            E["e"].wait_ge(sem, val)
            E["seen"][k] = val
            self.ninst += 1

    @staticmethod
    def _deps(reads, writes, acc, own=None):
        deps = []
        for r in reads:
            if r.w is not None:
                deps.append(r.w)
            if r.ps:
                deps.extend(t for t in r.r if t[0].num != own)
        if not acc:
            for w in writes:
                if w.w is not None:
                    deps.append(w.w)
                deps.extend(w.r)
        return deps

    def op(self, en, fn, reads=(), writes=(), acc=False):
        E = self.E[en]
        self._wait(en, self._deps(reads, writes, acc, E["sem"].num))
        inst = fn(E["e"])
        E["cnt"] += 1
        inst.then_inc(E["sem"], 1)
        tok = (E["sem"], E["cnt"])
        for r in reads:
            r.r.append(tok)
        for w in writes:
            w.w = tok
            if not acc:
                w.r = []
        self.ninst += 1
        return inst

    def dma(self, en, out, in_, reads=(), writes=(), owner=None, **kw):
        E = self.E[en]
        self._wait(en, self._deps(reads, writes, False))
        if owner is None:
            owner = (list(writes) + list(reads))[0]
        qk = "sw" if en == "gpsimd" else "hw"
        if qk not in owner.ds:
            owner.ds[qk] = self.free_ds[qk].pop()
            self.owned.append((owner, qk))
        ds = owner.ds[qk]
        ds.cnt += 16
        inst = E["e"].dma_start(out=out, in_=in_, **kw)
        inst.then_inc(ds.sem, 16)
        tok = (ds.sem, ds.cnt)
        for r in reads:
            r.r.append(tok)
        for w in writes:
            w.w = tok
            w.r = []
        self.ninst += 1
        return inst

    def barrier(self, release=True):
        toks = [(self.E[n]["sem"], self.E[n]["cnt"]) for n in self.ENG if self.E[n]["cnt"] > 0]
        toks += [(d.sem, d.cnt) for d in self.all_ds if d.cnt > 0]
        for n in self.ENG:
            self._wait(n, toks)
        if release:
            for (o, qk) in self.owned:
                self.free_ds[qk].append(o.ds.pop(qk))
            self.owned = []


class Tl:
    def __init__(self, h, F, name):
        self.h = h
        self.F = F
        self.res = Res(name)

    def v(self, off=0, dims=None, p0=0, pn=128):
        if dims is None:
            dims = [[1, self.F - off]]
        return bass.AP(tensor=self.h, offset=p0 * self.F + off, ap=[[self.F, pn]] + [list(d) for d in dims])


def dr(t, off, dims):
    return bass.AP(tensor=t.tensor, offset=off, ap=[list(d) for d in dims])


class Builder:
    def __init__(self, NBP, T, NBS, debug=False):
        assert T % 512 == 0 and NBS % 2 == 0
        self.NBP, self.T, self.NBS = NBP, T, NBS
        self.debug = debug
        nc = bass.Bass("TRN2", target_bir_lowering=False)
        self.nc = nc
        self.S = Sync(nc)
        di = lambda n, s: nc.dram_tensor(n, s, F32, kind="ExternalInput").ap()
        do = lambda n, s: nc.dram_tensor(n, s, F32, kind="ExternalOutput").ap()
        self.xp = di("xp", [NBP * T, D])
        self.xs = di("xs", [NBS * 16, D])
        self.cak = di("cak", [NBS * LC, D])
        self.cav = di("cav", [NBS * LC, D])
        self.sconv = di("sconv", [NBS * 3, 3072])
        self.sssm = di("sssm", [NBS * 8 * 128, 128])
        self.cmk = di("cmk", [NBS * 256, D])
        self.cmv = di("cmv", [NBS * 256, D])
        self.memp = di("memp", [NBP * 256, D])
        self.w_in = di("w_in", [D, IN_COLS])
        self.wconv = di("wconv", [4, 3072])
        self.alog = di("alog", [1, 8])
        self.dtb = di("dtb", [1, 8])
        self.normg = di("normg", [1, 128])
        self.biasP = di("biasP", [16 * 128, 640])
        self.biasS = di("biasS", [16 * 128, 256])
        self.biasN = di("biasN", [16 * 64, 64])
        self.wkv = di("wkv", [D, 2048])
        self.wout = di("wout", [D, D])
        self.ln1g = di("ln1g", [1, D])
        self.ln1b = di("ln1b", [1, D])
        self.wff1 = di("wff1", [D, 4096])
        self.bff1 = di("bff1", [32, 128])
        self.wff2 = di("wff2", [4096, D])
        self.bff2 = di("bff2", [1, D])
        self.ln2g = di("ln2g", [1, D])
        self.ln2b = di("ln2b", [1, D])
        self.consts = di("consts", [128, NCONST])
        self.yp = do("yp", [NBP * T, D])
        self.ys = do("ys", [NBS * 16, D])
        self.akp = do("akp", [NBP * 512, D])
        self.avp = do("avp", [NBP * 512, D])
        self.bcp = do("bcp", [NBP * 3, 3072])
        self.bsp = do("bsp", [NBP * 8 * 128, 128])
        self.mkp = do("mkp", [NBP * 256, D])
        self.mvp = do("mvp", [NBP * 256, D])
        self.aks = do("aks", [NBS * LC, D])
        self.avs = do("avs", [NBS * LC, D])
        self.bcs = do("bcs", [NBS * 3, 3072])
        self.bss = do("bss", [NBS * 8 * 128, 128])
        if debug:
            self.dbg_mp = do("dbg_mp", [NBP * T, D])
            self.dbg_ms = do("dbg_ms", [NBS * 64, D])
        self.flip = 0

    def sb(self, es, name, F, dt=F32):
        self.uid = getattr(self, "uid", 0) + 1
        name = "%s_%d" % (name, self.uid)
        h = es.enter_context(self.nc.sbuf_tensor(name, [128, F], dt))
        return Tl(h, F, name)

    def evac_eng(self):
        self.flip ^= 1
        return "vector" if self.flip else "scalar"

    def copy(self, en, out, in_, reads, writes, scale=None):
        S = self.S
        if en == "scalar":
            if scale is None:
                S.op("scalar", lambda e: e.activation(out, in_, AF.Copy), reads=reads, writes=writes)
            elif isinstance(scale, float):
                S.op("scalar", lambda e: e.activation(out, in_, AF.Copy, scale=scale), reads=reads, writes=writes)
            else:
                S.op("scalar", lambda e: e.activation(out, in_, AF.Copy, scale=scale), reads=reads, writes=writes)
        else:
            if scale is None:
                S.op(en, lambda e: e.tensor_copy(out, in_), reads=reads, writes=writes)
            else:
                S.op(en, lambda e: e.tensor_scalar(out, in_, scale, None, ALU.mult), reads=reads, writes=writes)

    def load_slab(self, slab, src, col0, ncols=512, rows0=0, kc=8, dst_off=0):
        W = src.tensor.shape[1]
        self.S.dma("gpsimd", slab.v(dst_off, [[512, kc], [1, ncols]]),
                   dr(src, rows0 * W + col0, [[W, 128], [128 * W, kc], [1, ncols]]),
                   writes=[slab.res])

    def build(self):
        nc, S = self.nc, self.S
        with contextlib.ExitStack() as es:
            ph = es.enter_context(nc.psum_tensor("ps", [128, 4096], F32))
            self.PS = [None] * 8
            self.psh = ph
            self.psres = [Res("psb%d" % i, ps=True) for i in range(8)]
            self.cst = self.sb(es, "cst", NCONST)
            S.dma("sync", self.cst.v(), dr(self.consts, 0, [[NCONST, 128], [1, NCONST]]), writes=[self.cst.res])
            self.idb = self.sb(es, "idb", 128, BF16)
            S.op("vector", lambda e: e.tensor_copy(self.idb.v(), self.cst.v(C_ID, [[1, 128]])),
                 reads=[self.cst.res], writes=[self.idb.res])
            self.slabs = [self.sb(es, "slab%d" % i, 4096, BF16) for i in range(3)]
            self.slab_i = 0
            self.setup_small(es)
            ok = getattr(self, "only_kind", "PS")
            if "P" in ok:
                for b in range(self.NBP):
                    self.run_pass(es, "P", b)
            if "S" in ok:
                self.run_pass(es, "S", 0)
            S.barrier(release=False)
        return nc

    def next_slab(self):
        s = self.slabs[self.slab_i % 3]
        self.slab_i += 1
        return s

    def ps(self, bank, off=0, dims=None, p0=0, pn=128):
        if dims is None:
            dims = [[1, 512 - off]]
        return bass.AP(tensor=self.psh, offset=p0 * 4096 + bank * 512 + off, ap=[[4096, pn]] + [list(d) for d in dims])

    def setup_small(self, es):
        nc, S = self.nc, self.S
        cst = self.cst
        self.dtb_bc = self.sb(es, "dtb_bc", 8)
        self.nealog = self.sb(es, "nealog", 8)
        self.normg_bc = self.sb(es, "normg_bc", 128)
        S.dma("sync", self.dtb_bc.v(), dr(self.dtb, 0, [[0, 128], [1, 8]]), writes=[self.dtb_bc.res])
        S.dma("sync", self.nealog.v(), dr(self.alog, 0, [[0, 128], [1, 8]]), writes=[self.nealog.res])
        S.dma("sync", self.normg_bc.v(), dr(self.normg, 0, [[0, 128], [1, 128]]), writes=[self.normg_bc.res])
        S.op("scalar", lambda e: e.activation(self.nealog.v(), self.nealog.v(), AF.Exp),
             reads=[self.nealog.res], writes=[self.nealog.res])
        S.op("vector", lambda e: e.tensor_scalar(self.nealog.v(), self.nealog.v(), -1.0, None, ALU.mult),
             reads=[self.nealog.res], writes=[self.nealog.res])
        self.wc = self.sb(es, "wc", 96)
        self.b1T = self.sb(es, "b1T", 32)
        with contextlib.ExitStack() as es2:
            wtok = self.sb(es2, "wtok", 3072)
            S.dma("sync", wtok.v(0, [[1, 3072]], 0, 4), dr(self.wconv, 0, [[3072, 4], [1, 3072]]), writes=[wtok.res])
            pr = self.psres[0]
            for blk in range(24):
                S.op("tensor", lambda e, blk=blk: e.matmul(self.ps(0, blk * 4, [[1, 4]]),
                                                           wtok.v(blk * 128, [[1, 128]], 0, 4),
                                                           cst.v(C_ID, [[1, 4]], 0, 4), start=True, stop=True),
                     reads=[wtok.res, cst.res], writes=[pr], acc=(blk > 0))
            S.op("vector", lambda e: e.tensor_copy(self.wc.v(), self.ps(0, 0, [[1, 96]])), reads=[pr], writes=[self.wc.res])
            btok = self.sb(es2, "btok", 128)
            S.dma("sync", btok.v(0, [[1, 128]], 0, 32), dr(self.bff1, 0, [[128, 32], [1, 128]]), writes=[btok.res])
            pr1 = self.psres[1]
            S.op("tensor", lambda e: e.matmul(self.ps(1, 0, [[1, 32]]), btok.v(0, [[1, 128]], 0, 32),
                                              cst.v(C_ID, [[1, 32]], 0, 32), start=True, stop=True),
                 reads=[btok.res, cst.res], writes=[pr1])
            S.op("vector", lambda e: e.tensor_copy(self.b1T.v(), self.ps(1, 0, [[1, 32]])), reads=[pr1], writes=[self.b1T.res])
            S.barrier()
        self.g1a = self.sb(es, "g1a", D)
        self.c1 = self.sb(es, "c1", D)
        self.g2 = self.sb(es, "g2", D)
        self.b2 = self.sb(es, "b2", D)
        self.g1c = self.sb(es, "g1c", 8)
        self.b1c = self.sb(es, "b1c", 8)
        with contextlib.ExitStack() as es2:
            tmp = self.sb(es2, "tmpbc", D)
            S.dma("sync", self.g1a.v(), dr(self.ln1g, 0, [[0, 128], [1, D]]), writes=[self.g1a.res])
            S.dma("sync", self.c1.v(), dr(self.ln1b, 0, [[0, 128], [1, D]]), writes=[self.c1.res])
            S.dma("sync", tmp.v(), dr(self.bff2, 0, [[0, 128], [1, D]]), writes=[tmp.res])
            S.dma("sync", self.g2.v(), dr(self.ln2g, 0, [[0, 128], [1, D]]), writes=[self.g2.res])
            S.dma("sync", self.b2.v(), dr(self.ln2b, 0, [[0, 128], [1, D]]), writes=[self.b2.res])
            gtok = self.sb(es2, "gtok", 256)
            S.dma("sync", gtok.v(0, [[1, 128]], 0, 8), dr(self.ln1g, 0, [[128, 8], [1, 128]]), writes=[gtok.res])
            S.dma("sync", gtok.v(128, [[1, 128]], 0, 8), dr(self.ln1b, 0, [[128, 8], [1, 128]]), writes=[gtok.res])
            pr = self.psres[2]
            S.op("tensor", lambda e: e.matmul(self.ps(2, 0, [[1, 8]]), gtok.v(0, [[1, 128]], 0, 8),
                                              cst.v(C_ID, [[1, 8]], 0, 8), start=True, stop=True),
                 reads=[gtok.res, cst.res], writes=[pr])
            S.op("tensor", lambda e: e.matmul(self.ps(2, 8, [[1, 8]]), gtok.v(128, [[1, 128]], 0, 8),
                                              cst.v(C_ID, [[1, 8]], 0, 8), start=True, stop=True),
                 reads=[gtok.res, cst.res], writes=[pr], acc=True)
            S.op("vector", lambda e: e.tensor_copy(self.g1c.v(), self.ps(2, 0, [[1, 8]])), reads=[pr], writes=[self.g1c.res])
            S.op("vector", lambda e: e.tensor_copy(self.b1c.v(), self.ps(2, 8, [[1, 8]])), reads=[pr], writes=[self.b1c.res])
            S.op("vector", lambda e: e.tensor_scalar(self.g1a.v(), self.g1a.v(), ALPHA, None, ALU.mult),
                 reads=[self.g1a.res], writes=[self.g1a.res])
            S.op("vector", lambda e: e.scalar_tensor_tensor(self.c1.v(), self.c1.v(), ALPHA, tmp.v(), ALU.mult, ALU.add),
                 reads=[self.c1.res, tmp.res], writes=[self.c1.res])
            S.barrier()

    def run_pass(self, es_outer, kind, b):
        nc, S = self.nc, self.S
        T = self.T
        NT = (T // 128) if kind == "P" else (self.NBS // 2)
        NCOL = NT * 128
        blocks = [(c, min(512, NCOL - c)) for c in range(0, NCOL, 512)]
        self.kind, self.b, self.NT, self.NCOL, self.blocks = kind, b, NT, NCOL, blocks
        with contextlib.ExitStack() as es:
            self.merged = self.sb(es, "merged", NT * D, BF16)
            self.mres = [Res("mrg%d" % t) for t in range(NT)]
            stop = getattr(self, "stop_at", 99)
            with contextlib.ExitStack() as es1:
                self.xT = self.sb(es1, "xT", 8 * NCOL, BF16)
                if stop >= 1:
                    self.phase0(es1)
                if stop >= 2:
                    self.phaseA(es1)
                if stop >= 3:
                    self.phaseC(es1)
                if stop >= 4:
                    self.phaseB(es1)
                S.barrier()
            if self.debug and stop >= 4:
                self.dump_merged()
            if stop >= 5:
                self.phaseD(es)
            S.barrier()

    def xT_v(self, kc, c0, n):
        return self.xT.v(kc * self.NCOL + c0, [[1, n]])

    def x_tile_src(self, xt, ti, en="sync"):
        S = self.S
        if self.kind == "P":
            S.dma(en, xt.v(), dr(self.xp, (self.b * self.T + ti * 128) * D, [[D, 128], [1, D]]), writes=[xt.res])
        else:
            S.op("vector", lambda e: e.memset(xt.v(), 0.0), writes=[xt.res])
            for i in range(2):
                sq = 2 * ti + i
                S.dma(en, xt.v(0, [[1, D]], 64 * i, 16), dr(self.xs, sq * 16 * D, [[D, 16], [1, D]]), writes=[xt.res])

    def phase0(self, es):
        S = self.S
        with contextlib.ExitStack() as es2:
            xts = [self.sb(es2, "xt%d" % i, D) for i in range(2)]
            for ti in range(self.NT):
                xt = xts[ti % 2]
                self.x_tile_src(xt, ti)
                for half in range(2):
                    bank = (2 * ti + half) % 4
                    pr = self.psres[bank]
                    for q in range(4):
                        kc = half * 4 + q
                        S.op("tensor", lambda e, kc=kc, q=q, bank=bank: e.transpose(
                            self.ps(bank, q * 128, [[1, 128]]), xt.v(kc * 128, [[1, 128]]), self.cst.v(C_ID, [[1, 128]])),
                            reads=[xt.res, self.cst.res], writes=[pr], acc=(q > 0))
                    en = self.evac_eng()
                    self.copy(en, self.xT.v((half * 4) * self.NCOL + ti * 128, [[self.NCOL, 4], [1, 128]]),
                              self.ps(bank, 0, [[128, 4], [1, 128]]), reads=[pr], writes=[self.xT.res])
            S.barrier()

    def phaseA(self, es_pass):
        S = self.S
        NT, NCOL, kind = self.NT, self.NCOL, self.kind
        cst = self.cst
        with contextlib.ExitStack() as es:
            qT = self.sb(es, "qT", NCOL, BF16)
            kT = self.sb(es, "kT", NCOL, BF16)
            vaug = self.sb(es, "vaug", NT * 130, BF16)
            sg = self.sb(es, "sgA", NT * 128, BF16)
            tbf = self.sb(es, "tbf", 2 * 640)
            tb = self.sb(es, "tb", 2 * 640, BF16)
            PT = self.sb(es, "PT", 640, BF16)
            kvo = [self.sb(es, "kvo%d" % i, 256) for i in range(2)]
            rden = self.sb(es, "rdenA", 1)
            if kind == "S":
                ckf = self.sb(es, "ckf", 512)
                ckT = self.sb(es, "ckT", 512, BF16)
                cvf = self.sb(es, "cvf", 512)
                cvaug = self.sb(es, "cvaug", 4 * 130, BF16)
                tnf = self.sb(es, "tnf", 2 * 64)
                tn = self.sb(es, "tn", 2 * 64, BF16)
                S.op("vector", lambda e: e.memset(cvaug.v(), 1.0), writes=[cvaug.res])
            S.op("vector", lambda e: e.memset(vaug.v(), 1.0), writes=[vaug.res])
            kvo_i = 0
            for hp in range(8):
                slab = self.next_slab()
                for i, off in enumerate([OFF_QA, OFF_KA, OFF_VA, OFF_GA]):
                    self.load_slab(slab, self.w_in, off + hp * 128, 128, dst_off=i * 128)
                if kind == "P":
                    S.dma("sync", tbf.v(0, [[640, 2], [1, 640]]),
                          dr(self.biasP, (2 * hp) * 128 * 640, [[640, 128], [128 * 640, 2], [1, 640]]), writes=[tbf.res])
                    S.op("vector", lambda e: e.tensor_tensor(tb.v(0, [[640, 2], [1, 640]]), tbf.v(0, [[640, 2], [1, 640]]),
                                                             cst.v(C_MASKA, [[0, 2], [1, 640]]), ALU.add),
                         reads=[tbf.res, cst.res], writes=[tb.res])
                else:
                    S.dma("sync", tbf.v(0, [[640, 2], [1, 256]]),
                          dr(self.biasS, (2 * hp) * 128 * 256, [[256, 128], [128 * 256, 2], [1, 256]]), writes=[tbf.res])
                    S.op("vector", lambda e: e.tensor_copy(tb.v(0, [[640, 2], [1, 256]]), tbf.v(0, [[640, 2], [1, 256]])),
                         reads=[tbf.res], writes=[tb.res])
                    S.dma("sync", tnf.v(0, [[1, 64]]),
                          dr(self.biasN, (2 * hp) * 64 * 64, [[64, 128], [1, 64]]), writes=[tnf.res])
                    S.op("vector", lambda e: e.tensor_tensor(tn.v(0, [[1, 64]]), tnf.v(0, [[1, 64]]), cst.v(C_MASKN, [[1, 64]]), ALU.add),
                         reads=[tnf.res, cst.res], writes=[tn.res])
                for bi, (c0, n) in enumerate(self.blocks):
                    for j, dst in enumerate([qT, kT]):
                        bank = (2 * bi + j) % 4
                        pr = self.psres[bank]
                        for kc in range(8):
                            S.op("tensor", lambda e, kc=kc, j=j, bank=bank: e.matmul(
                                self.ps(bank, 0, [[1, n]]), slab.v(kc * 512 + j * 128, [[1, 128]]), self.xT_v(kc, c0, n),
                                start=(kc == 0), stop=(kc == 7)),
                                reads=[slab.res, self.xT.res], writes=[pr], acc=(kc > 0))
                        if j == 0:
                            self.copy("scalar", dst.v(c0, [[1, n]]), self.ps(bank, 0, [[1, n]]), [pr], [dst.res], scale=0.125)
                        else:
                            self.copy("vector", dst.v(c0, [[1, n]]), self.ps(bank, 0, [[1, n]]), [pr], [dst.res])
                a_stop = getattr(self, "a_stop", 99)
                if a_stop < 1:
                    continue
                for ti in range(NT):
                    out_tile = (kind == "S") or (ti >= NT - 4)
                    bank = 4 + (ti % 2)
                    pr = self.psres[bank]
                    ncol = 384 if out_tile else 256
                    for kc in range(8):
                        S.op("tensor", lambda e, kc=kc, bank=bank: e.matmul(
                            self.ps(bank, 0, [[1, 256]]), self.xT_v(kc, ti * 128, 128), slab.v(kc * 512 + 256, [[1, 256]]),
                            start=(kc == 0), stop=(kc == 7)),
                            reads=[slab.res, self.xT.res], writes=[pr], acc=(kc > 0))
                    if out_tile:
                        for kc in range(8):
                            S.op("tensor", lambda e, kc=kc, bank=bank: e.matmul(
                                self.ps(bank, 256, [[1, 128]]), self.xT_v(kc, ti * 128, 128), slab.v(kc * 512 + 128, [[1, 128]]),
                                start=(kc == 0), stop=(kc == 7)),
                                reads=[slab.res, self.xT.res], writes=[pr], acc=True)
                    S.op("vector", lambda e, bank=bank: e.tensor_copy(vaug.v(ti * 130, [[65, 2], [1, 64]]),
                                                                      self.ps(bank, 0, [[64, 2], [1, 64]])),
                         reads=[pr], writes=[vaug.res])
                    S.op("scalar", lambda e, bank=bank: e.activation(sg.v(ti * 128, [[1, 128]]), self.ps(bank, 128, [[1, 128]]), AF.Sigmoid),
                         reads=[pr], writes=[sg.res])
                    if out_tile:
                        ko = kvo[kvo_i % 2]
                        kvo_i += 1
                        S.op("vector", lambda e, bank=bank: e.tensor_copy(ko.v(0, [[1, 128]]), self.ps(bank, 256, [[1, 128]])),
                             reads=[pr], writes=[ko.res])
                        S.op("scalar", lambda e, bank=bank: e.activation(ko.v(128, [[1, 128]]), self.ps(bank, 0, [[1, 128]]), AF.Copy),
                             reads=[pr], writes=[ko.res])
                        if kind == "P":
                            r0 = self.b * 512 + (ti - (NT - 4)) * 128
                            S.dma("sync", dr(self.akp, r0 * D + hp * 128, [[D, 128], [1, 128]]), ko.v(0, [[1, 128]]), reads=[ko.res])
                            S.dma("sync", dr(self.avp, r0 * D + hp * 128, [[D, 128], [1, 128]]), ko.v(128, [[1, 128]]), reads=[ko.res])
                        else:
                            for i in range(2):
                                sq = 2 * ti + i
                                r0 = sq * LC + (LC - 16)
                                S.dma("sync", dr(self.aks, r0 * D + hp * 128, [[D, 16], [1, 128]]),
                                      ko.v(0, [[1, 128]], 64 * i, 16), reads=[ko.res])
                                S.dma("sync", dr(self.avs, r0 * D + hp * 128, [[D, 16], [1, 128]]),
                                      ko.v(128, [[1, 128]], 64 * i, 16), reads=[ko.res])
                if a_stop < 2:
                    continue
                for ti in range(NT):
                    mr = self.mres[ti]
                    if kind == "P":
                        jlist = [j for j in range(5) if ti * 128 - 512 + 128 * j >= 0]
                        for h2 in range(2):
                            pb = 64 * h2
                            prs = [self.psres[0], self.psres[1]]
                            first = True
                            for j in jlist:
                                kc0 = ti * 128 - 512 + 128 * j
                                bank = 0 if j < 4 else 1
                                S.op("tensor", lambda e, j=j, kc0=kc0, bank=bank, pb=pb: e.matmul(
                                    self.ps(bank, (j % 4) * 128, [[1, 128]]), kT.v(kc0, [[1, 128]], pb, 64),
                                    qT.v(ti * 128, [[1, 128]], pb, 64), start=True, stop=False),
                                    reads=[kT.res, qT.res], writes=prs, acc=(not first))
                                first = False
                                S.op("tensor", lambda e, j=j, bank=bank, h2=h2: e.matmul(
                                    self.ps(bank, (j % 4) * 128, [[1, 128]]), self.idb.v(),
                                    tb.v(h2 * 640 + j * 128, [[1, 128]]), start=False, stop=True),
                                    reads=[self.idb.res, tb.res], writes=prs, acc=True)
                            j0 = jlist[0]
                            nj = len(jlist)
                            S.op("scalar", lambda e, j0=j0, nj=nj: e.activation(
                                PT.v(j0 * 128, [[1, nj * 128]]), self.ps(0, j0 * 128, [[1, nj * 128]]), AF.Exp),
                                reads=prs, writes=[PT.res])
                            po = self.psres[2 + h2]
                            for idx, j in enumerate(jlist):
                                kt = ti - 4 + j
                                S.op("tensor", lambda e, j=j, kt=kt, h2=h2, idx=idx, nj=nj: e.matmul(
                                    self.ps(2 + h2, 0, [[1, 65]]), PT.v(j * 128, [[1, 128]]),
                                    vaug.v(kt * 130 + h2 * 65, [[1, 65]]), start=(idx == 0), stop=(idx == nj - 1)),
                                    reads=[PT.res, vaug.res], writes=[po], acc=(idx > 0))
                            S.op("vector", lambda e, h2=h2: e.reciprocal(rden.v(), self.ps(2 + h2, 64, [[1, 1]])),
                                 reads=[po], writes=[rden.res])
                            S.op("vector", lambda e, h2=h2: e.scalar_tensor_tensor(
                                self.merged.v(ti * D + (2 * hp + h2) * 64, [[1, 64]]), self.ps(2 + h2, 0, [[1, 64]]),
                                rden.v(), sg.v(ti * 128 + h2 * 64, [[1, 64]]), ALU.mult, ALU.mult),
                                reads=[po, rden.res, sg.res], writes=[mr])
                    else:
                        for i in range(2):
                            sq = 2 * ti + i
                            c0 = 64 * i
                            S.dma("sync", ckf.v(0, [[128, 4], [1, 128]]),
                                  dr(self.cak, sq * LC * D + hp * 128, [[D, 128], [128 * D, 4], [1, 128]]), writes=[ckf.res])
                            S.dma("sync", cvf.v(0, [[128, 4], [1, 128]]),
                                  dr(self.cav, sq * LC * D + hp * 128, [[D, 128], [128 * D, 4], [1, 128]]), writes=[cvf.res])
                            prt = self.psres[4]
                            for j in range(4):
                                S.op("tensor", lambda e, j=j: e.transpose(self.ps(4, j * 128, [[1, 128]]), ckf.v(j * 128, [[1, 128]]),
                                                                          cst.v(C_ID, [[1, 128]])),
                                     reads=[ckf.res, cst.res], writes=[prt], acc=(j > 0))
                            S.op("vector", lambda e: e.tensor_copy(ckT.v(), self.ps(4, 0, [[1, 512]])), reads=[prt], writes=[ckT.res])
                            S.op("vector", lambda e: e.tensor_copy(cvaug.v(0, [[130, 4], [65, 2], [1, 64]]),
                                                                   cvf.v(0, [[128, 4], [64, 2], [1, 64]])),
                                 reads=[cvf.res], writes=[cvaug.res])
                            for h2 in range(2):
                                pb = 64 * h2
                                prs = [self.psres[0], self.psres[1]]
                                for j in range(4):
                                    S.op("tensor", lambda e, j=j, pb=pb: e.matmul(
                                        self.ps(0, j * 64, [[1, 64]]), ckT.v(j * 128, [[1, 128]], pb, 64),
                                        qT.v(ti * 128 + c0, [[1, 64]], pb, 64), start=True, stop=False),
                                        reads=[ckT.res, qT.res], writes=prs, acc=(j > 0))
                                    S.op("tensor", lambda e, j=j, h2=h2: e.matmul(
                                        self.ps(0, j * 64, [[1, 64]]), self.idb.v(),
                                        tb.v(h2 * 640 + j * 64, [[1, 64]]), start=False, stop=True),
                                        reads=[self.idb.res, tb.res], writes=prs, acc=True)
                                S.op("tensor", lambda e, pb=pb: e.matmul(
                                    self.ps(1, 0, [[1, 64]], c0, 64), kT.v(ti * 128 + c0, [[1, 64]], pb, 64),
                                    qT.v(ti * 128 + c0, [[1, 64]], pb, 64), start=True, stop=False),
                                    reads=[kT.res, qT.res], writes=prs, acc=True)
                                S.op("tensor", lambda e, pb=pb: e.matmul(
                                    self.ps(1, 0, [[1, 64]], c0, 64), self.idb.v(pb, [[1, 64]], pb, 64),
                                    tn.v(0, [[1, 64]], pb, 64), start=False, stop=True),
                                    reads=[self.idb.res, tn.res], writes=prs, acc=True)
                                S.op("scalar", lambda e: e.activation(PT.v(0, [[1, 256]]), self.ps(0, 0, [[1, 256]]), AF.Exp),
                                     reads=prs, writes=[PT.res])
                                S.op("scalar", lambda e: e.activation(PT.v(256, [[1, 64]], c0, 64), self.ps(1, 0, [[1, 64]], c0, 64), AF.Exp),
                                     reads=prs, writes=[PT.res])
                                po = self.psres[2 + h2]
                                for j in range(4):
                                    S.op("tensor", lambda e, j=j, h2=h2: e.matmul(
                                        self.ps(2 + h2, 0, [[1, 65]], c0, 64), PT.v(j * 64, [[1, 64]]),
                                        cvaug.v(j * 130 + h2 * 65, [[1, 65]]), start=(j == 0), stop=False),
                                        reads=[PT.res, cvaug.res], writes=[po], acc=(j > 0))
                                S.op("tensor", lambda e, h2=h2: e.matmul(
                                    self.ps(2 + h2, 0, [[1, 65]], c0, 64), PT.v(256, [[1, 64]], c0, 64),
                                    vaug.v(ti * 130 + h2 * 65, [[1, 65]], c0, 64), start=False, stop=True),
                                    reads=[PT.res, vaug.res], writes=[po], acc=True)
                                S.op("vector", lambda e, h2=h2: e.reciprocal(rden.v(0, [[1, 1]], c0, 64), self.ps(2 + h2, 64, [[1, 1]], c0, 64)),
                                     reads=[po], writes=[rden.res])
                                S.op("vector", lambda e, h2=h2: e.scalar_tensor_tensor(
                                    self.merged.v(ti * D + (2 * hp + h2) * 64, [[1, 64]], c0, 64), self.ps(2 + h2, 0, [[1, 64]], c0, 64),
                                    rden.v(0, [[1, 1]], c0, 64), sg.v(ti * 128 + h2 * 64, [[1, 64]], c0, 64), ALU.mult, ALU.mult),
                                    reads=[po, rden.res, sg.res], writes=[mr])
            if kind == "S" and getattr(self, "a_stop", 99) >= 3:
                for sq in range(self.NBS):
                    for src, dst in ((self.cak, self.aks), (self.cav, self.avs)):
                        rr = Res("cpy")
                        for part in range(4):
                            S.dma("sync", dr(dst, (sq * LC + part * 124) * D, [[D, 124], [1, D]]),
                                  dr(src, (sq * LC + 16 + part * 124) * D, [[D, 124], [1, D]]), writes=[rr])
            S.barrier()

    def phaseC(self, es_pass):
        S = self.S
        NT, NCOL, kind = self.NT, self.NCOL, self.kind
        cst = self.cst
        with contextlib.ExitStack() as es:
            mkT = self.sb(es, "mkT", 4 * 2 * 256, BF16)
            mvaug = self.sb(es, "mvaug", 2 * 4 * 257, BF16)
            qcT = self.sb(es, "qcT", 2 * NCOL, BF16)
            sgc = self.sb(es, "sgc", NT * 256, BF16)
            PT = self.sb(es, "PTc", 256, BF16)
            rden = self.sb(es, "rdenC", 1)
            otmp = self.sb(es, "otmpC", 256, BF16)
            S.op("vector", lambda e: e.memset(mvaug.v(), 1.0), writes=[mvaug.res])
            if kind == "P":
                with contextlib.ExitStack() as es2:
                    memf = [self.sb(es2, "memf%d" % i, D) for i in range(2)]
                    memT = self.sb(es2, "memT", 8 * 256, BF16)
                    kvst = [self.sb(es2, "kvst%d" % i, 512) for i in range(2)]
                    for mb in range(2):
                        S.dma("sync", memf[mb].v(), dr(self.memp, (self.b * 256 + mb * 128) * D, [[D, 128], [1, D]]), writes=[memf[mb].res])
                        for half in range(2):
                            bank = 2 * mb + half
                            pr = self.psres[bank]
                            for q in range(4):
                                kc = half * 4 + q
                                S.op("tensor", lambda e, kc=kc, q=q, bank=bank, mb=mb: e.transpose(
                                    self.ps(bank, q * 128, [[1, 128]]), memf[mb].v(kc * 128, [[1, 128]]), cst.v(C_ID, [[1, 128]])),
                                    reads=[memf[mb].res, cst.res], writes=[pr], acc=(q > 0))
                            self.copy(self.evac_eng(), memT.v((half * 4) * 256 + mb * 128, [[256, 4], [1, 128]]),
                                      self.ps(bank, 0, [[128, 4], [1, 128]]), [pr], [memT.res])
                    si = 0
                    for s in range(4):
                        slab = self.next_slab()
                        self.load_slab(slab, self.wkv, s * 512, 512)
                        isv = s // 2
                        h0 = (s % 2) * 2
                        for mb in range(2):
                            bank = 4 + (si % 2)
                            pr = self.psres[bank]
                            for kc in range(8):
                                S.op("tensor", lambda e, kc=kc, bank=bank, mb=mb: e.matmul(
                                    self.ps(bank, 0, [[1, 512]]), memT.v(kc * 256 + mb * 128, [[1, 128]]), slab.v(kc * 512, [[1, 512]]),
                                    start=(kc == 0), stop=(kc == 7)),
                                    reads=[memT.res, slab.res], writes=[pr], acc=(kc > 0))
                            st = kvst[si % 2]
                            si += 1
                            self.copy(self.evac_eng(), st.v(), self.ps(bank, 0, [[1, 512]]), [pr], [st.res])
                            dst = self.mvp if isv else self.mkp
                            S.dma("sync", dr(dst, (self.b * 256 + mb * 128) * D + s % 2 * 512, [[D, 128], [1, 512]]), st.v(), reads=[st.res])
                            if isv:
                                S.op("vector", lambda e, bank=bank, mb=mb, h0=h0: e.tensor_copy(
                                    mvaug.v(mb * 4 * 257 + h0 * 257, [[257, 2], [1, 256]]), self.ps(bank, 0, [[256, 2], [1, 256]])),
                                    reads=[pr], writes=[mvaug.res])
                        if not isv:
                            for cb in range(4):
                                bank = 6 + (cb % 2)
                                pr = self.psres[bank]
                                for kc in range(8):
                                    S.op("tensor", lambda e, kc=kc, bank=bank, cb=cb: e.matmul(
                                        self.ps(bank, 0, [[1, 256]]), slab.v(kc * 512 + cb * 128, [[1, 128]]), memT.v(kc * 256, [[1, 256]]),
                                        start=(kc == 0), stop=(kc == 7)),
                                        reads=[memT.res, slab.res], writes=[pr], acc=(kc > 0))
                                h = h0 + cb // 2
                                dc = cb % 2
                                self.copy(self.evac_eng(), mkT.v((h * 2 + dc) * 256, [[1, 256]]), self.ps(bank, 0, [[1, 256]]), [pr], [mkT.res])
                    S.barrier()
            with contextlib.ExitStack() as es2:
                if kind == "S":
                    cmf = self.sb(es2, "cmf", 2 * D)
                    mkTs = [self.sb(es2, "mkTs%d" % i, 4 * 2 * 256, BF16) for i in range(2)]
                    mvs = [self.sb(es2, "mvs%d" % i, 2 * 4 * 257, BF16) for i in range(2)]
                    for i in range(2):
                        S.op("vector", lambda e, i=i: e.memset(mvs[i].v(), 1.0), writes=[mvs[i].res])

                def load_seq_cache(sq, mk_, mv_):
                    S.dma("sync", cmf.v(0, [[D, 2], [1, D]]),
                          dr(self.cmk, sq * 256 * D, [[D, 128], [128 * D, 2], [1, D]]), writes=[cmf.res])
                    for mb in range(2):
                        for half in range(2):
                            bank = 6 + half
                            prt = self.psres[bank]
                            for q in range(4):
                                cb = half * 4 + q
                                S.op("tensor", lambda e, cb=cb, q=q, bank=bank, mb=mb: e.transpose(
                                    self.ps(bank, q * 128, [[1, 128]]), cmf.v(mb * D + cb * 128, [[1, 128]]), cst.v(C_ID, [[1, 128]])),
                                    reads=[cmf.res, cst.res], writes=[prt], acc=(q > 0))
                            self.copy(self.evac_eng(), mk_.v((half * 4) * 256 + mb * 128, [[256, 4], [1, 128]]),
                                      self.ps(bank, 0, [[128, 4], [1, 128]]), [prt], [mk_.res])
                    S.dma("sync", cmf.v(0, [[D, 2], [1, D]]),
                          dr(self.cmv, sq * 256 * D, [[D, 128], [128 * D, 2], [1, D]]), writes=[cmf.res])
                    S.op("vector", lambda e: e.tensor_copy(mv_.v(0, [[4 * 257, 2], [257, 4], [1, 256]]),
                                                           cmf.v(0, [[D, 2], [256, 4], [1, 256]])),
                         reads=[cmf.res], writes=[mv_.res])

                def headC(hc, tiles):
                    slab = self.next_slab()
                    self.load_slab(slab, self.w_in, OFF_QC + hc * 256, 256, dst_off=0)
                    self.load_slab(slab, self.w_in, OFF_GC + hc * 256, 256, dst_off=256)
                    for bi, (c0, n) in enumerate(self.blocks):
                        for dc in range(2):
                            bank = (2 * bi + dc) % 4
                            pr = self.psres[bank]
                            for kc in range(8):
                                S.op("tensor", lambda e, kc=kc, dc=dc, bank=bank, c0=c0, n=n: e.matmul(
                                    self.ps(bank, 0, [[1, n]]), slab.v(kc * 512 + dc * 128, [[1, 128]]), self.xT_v(kc, c0, n),
                                    start=(kc == 0), stop=(kc == 7)),
                                    reads=[slab.res, self.xT.res], writes=[pr], acc=(kc > 0))
                            self.copy(self.evac_eng(), qcT.v(dc * NCOL + c0, [[1, n]]), self.ps(bank, 0, [[1, n]]), [pr], [qcT.res], scale=0.0625)
                    for ti in tiles:
                        bank = 4 + (ti % 2)
                        pr = self.psres[bank]
                        for kc in range(8):
                            S.op("tensor", lambda e, kc=kc, bank=bank, ti=ti: e.matmul(
                                self.ps(bank, 0, [[1, 256]]), self.xT_v(kc, ti * 128, 128), slab.v(kc * 512 + 256, [[1, 256]]),
                                start=(kc == 0), stop=(kc == 7)),
                                reads=[slab.res, self.xT.res], writes=[pr], acc=(kc > 0))
                        S.op("scalar", lambda e, bank=bank, ti=ti: e.activation(sgc.v(ti * 256, [[1, 256]]), self.ps(bank, 0, [[1, 256]]), AF.Sigmoid),
                             reads=[pr], writes=[sgc.res])
                    for ti in tiles:
                        mr = self.mres[ti]
                        segs = [(0, 128, mkT, mvaug)] if kind == "P" else [(0, 64, mkTs[0], mvs[0]), (64, 64, mkTs[1], mvs[1])]
                        for (c0, nq, mk_, mv_) in segs:
                            prs = self.psres[0]
                            for mb in range(2):
                                for dc in range(2):
                                    S.op("tensor", lambda e, mb=mb, dc=dc, mk_=mk_, c0=c0, nq=nq, ti=ti: e.matmul(
                                        self.ps(0, mb * nq, [[1, nq]]), mk_.v((hc * 2 + dc) * 256 + mb * 128, [[1, 128]]),
                                        qcT.v(dc * NCOL + ti * 128 + c0, [[1, nq]]), start=(dc == 0), stop=(dc == 1)),
                                        reads=[mk_.res, qcT.res], writes=[prs], acc=not (mb == 0 and dc == 0))
                            S.op("scalar", lambda e, nq=nq: e.activation(PT.v(0, [[1, 2 * nq]]), self.ps(0, 0, [[1, 2 * nq]]), AF.Exp),
                                 reads=[prs], writes=[PT.res])
                            po = self.psres[1]
                            for mb in range(2):
                                S.op("tensor", lambda e, mb=mb, mv_=mv_, c0=c0, nq=nq: e.matmul(
                                    self.ps(1, 0, [[1, 257]], c0, nq), PT.v(mb * nq, [[1, nq]]),
                                    mv_.v(mb * 4 * 257 + hc * 257, [[1, 257]]), start=(mb == 0), stop=(mb == 1)),
                                    reads=[PT.res, mv_.res], writes=[po], acc=(mb > 0))
                            S.op("vector", lambda e, c0=c0, nq=nq: e.reciprocal(rden.v(0, [[1, 1]], c0, nq), self.ps(1, 256, [[1, 1]], c0, nq)),
                                 reads=[po], writes=[rden.res])
                            S.op("vector", lambda e, c0=c0, nq=nq, ti=ti: e.scalar_tensor_tensor(
                                otmp.v(0, [[1, 256]], c0, nq), self.ps(1, 0, [[1, 256]], c0, nq),
                                rden.v(0, [[1, 1]], c0, nq), sgc.v(ti * 256, [[1, 256]], c0, nq), ALU.mult, ALU.mult),
                                reads=[po, rden.res, sgc.res], writes=[otmp.res])
                            S.op("vector", lambda e, c0=c0, nq=nq, ti=ti: e.tensor_tensor(
                                self.merged.v(ti * D + hc * 256, [[1, 256]], c0, nq), self.merged.v(ti * D + hc * 256, [[1, 256]], c0, nq),
                                otmp.v(0, [[1, 256]], c0, nq), ALU.add),
                                reads=[otmp.res, mr], writes=[mr])

                if kind == "P":
                    for hc in range(4):
                        headC(hc, list(range(NT)))
                else:
                    for ti in range(NT):
                        for i in range(2):
                            load_seq_cache(2 * ti + i, mkTs[i], mvs[i])
                        for hc in range(4):
                            headC(hc, [ti])
                S.barrier()

    def phaseB(self, es_pass):
        S = self.S
        NT, NCOL, kind = self.NT, self.NCOL, self.kind
        cst = self.cst
        CI = lambda c: cst.v(c, [[1, 128]])
        with contextlib.ExitStack() as es:
            gt = self.sb(es, "gt", NT * 8)
            bt = self.sb(es, "bt", NT * 16)
            Gc = self.sb(es, "Gc", NT * 8)
            GD = self.sb(es, "GD", NT * 24)
            gend = self.sb(es, "gend", NT * 16)
            with contextlib.ExitStack() as es2:
                wab = self.sb(es2, "wab", 8 * 512, BF16)
                t1 = self.sb(es2, "t1", NT * 8)
                t2 = self.sb(es2, "t2", NT * 8)
                self.load_slab(wab, self.w_in, OFF_AB, 16)
                pr = self.psres[0]
                for ti in range(NT):
                    for kc in range(8):
                        S.op("tensor", lambda e, kc=kc, ti=ti: e.matmul(
                            self.ps(0, ti * 16, [[1, 16]]), self.xT_v(kc, ti * 128, 128), wab.v(kc * 512, [[1, 16]]),
                            start=(kc == 0), stop=(kc == 7)),
                            reads=[wab.res, self.xT.res], writes=[pr], acc=not (ti == 0 and kc == 0))
                S.op("vector", lambda e: e.tensor_tensor(t1.v(0, [[8, NT], [1, 8]]), self.ps(0, 0, [[16, NT], [1, 8]]),
                                                         self.dtb_bc.v(0, [[0, NT], [1, 8]]), ALU.add),
                     reads=[pr, self.dtb_bc.res], writes=[t1.res])
                S.op("scalar", lambda e: e.activation(t2.v(), t1.v(), AF.Abs), reads=[t1.res], writes=[t2.res])
                S.op("scalar", lambda e: e.activation(t2.v(), t2.v(), AF.Exp, scale=-1.0), reads=[t2.res], writes=[t2.res])
                S.op("scalar", lambda e: e.activation(t2.v(), t2.v(), AF.Ln, bias=1.0), reads=[t2.res], writes=[t2.res])
                S.op("vector", lambda e: e.scalar_tensor_tensor(t1.v(), t1.v(), 0.0, t2.v(), ALU.max, ALU.add),
                     reads=[t1.res, t2.res], writes=[t1.res])
                S.op("vector", lambda e: e.tensor_tensor(gt.v(0, [[8, NT], [1, 8]]), t1.v(0, [[8, NT], [1, 8]]),
                                                         self.nealog.v(0, [[0, NT], [1, 8]]), ALU.mult),
                     reads=[t1.res, self.nealog.res], writes=[gt.res])
                S.op("scalar", lambda e: e.activation(bt.v(0, [[16, NT], [2, 8]]), self.ps(0, 8, [[16, NT], [1, 8]]), AF.Sigmoid),
                     reads=[pr], writes=[bt.res])
                if kind == "S":
                    S.op("vector", lambda e: e.tensor_scalar(gt.v(), gt.v(), cst.v(C_ROWM, [[1, 1]]), None, ALU.mult),
                         reads=[gt.res, cst.res], writes=[gt.res])
                    S.op("vector", lambda e: e.tensor_scalar(bt.v(0, [[16, NT], [2, 8]]), bt.v(0, [[16, NT], [2, 8]]),
                                                             cst.v(C_ROWM, [[1, 1]]), None, ALU.mult),
                         reads=[bt.res, cst.res], writes=[bt.res])
                S.op("vector", lambda e: e.tensor_scalar(bt.v(1, [[16, NT], [2, 8]]), bt.v(0, [[16, NT], [2, 8]]), -1.0, None, ALU.mult),
                     reads=[bt.res], writes=[bt.res])
                pr1 = self.psres[1]
                for ti in range(NT):
                    for q, cc in enumerate([C_TRI, C_ONESBD, C_SEL0, C_SEL1]):
                        S.op("tensor", lambda e, ti=ti, q=q, cc=cc: e.matmul(
                            self.ps(1, ti * 32 + q * 8, [[1, 8]]), CI(cc), gt.v(ti * 8, [[1, 8]]), start=True, stop=True),
                            reads=[cst.res, gt.res], writes=[pr1], acc=not (ti == 0 and q == 0))
                S.op("vector", lambda e: e.tensor_copy(Gc.v(0, [[8, NT], [1, 8]]), self.ps(1, 0, [[32, NT], [1, 8]])),
                     reads=[pr1], writes=[Gc.res])
                S.op("vector", lambda e: e.memset(GD.v(), 1.0), writes=[GD.res])
                S.op("scalar", lambda e: e.activation(GD.v(1, [[24, NT], [3, 8]]), self.ps(1, 0, [[32, NT], [1, 8]]), AF.Exp),
                     reads=[pr1], writes=[GD.res])
                S.op("vector", lambda e: e.tensor_tensor(t1.v(0, [[8, NT], [1, 8]]), self.ps(1, 8, [[32, NT], [1, 8]]),
                                                         Gc.v(0, [[8, NT], [1, 8]]), ALU.subtract),
                     reads=[pr1, Gc.res], writes=[t1.res])
                S.op("scalar", lambda e: e.activation(GD.v(2, [[24, NT], [3, 8]]), t1.v(0, [[8, NT], [1, 8]]), AF.Exp),
                     reads=[t1.res], writes=[GD.res])
                S.op("scalar", lambda e: e.activation(gend.v(0, [[16, NT], [1, 16]]), self.ps(1, 16, [[32, NT], [1, 16]]), AF.Exp),
                     reads=[pr1], writes=[gend.res])
                S.barrier()
            zT = self.sb(es, "zT", 3 * (NCOL + 3), BF16)
            yT = self.sb(es, "yT", 3 * NCOL)
            ZW = NCOL + 3
            sgn = self.sb(es, "sgn", NT * 128)
            Dw = self.sb(es, "Dw", 12 * 128, BF16)
            Sf = [self.sb(es, "Sf%d" % i, 128) for i in range(2)]
            Sb = [self.sb(es, "Sb%d" % i, 128, BF16) for i in range(2)]
            ss = self.sb(es, "ssB", 2)
            rn = self.sb(es, "rnB", 2)
            cols = self.sb(es, "colsB", 4)
            junk = self.sb(es, "junkB", 128)
            tok5 = self.sb(es, "tok5", 5 * 128, BF16)
            Dg = self.sb(es, "Dg", 128, BF16)
            feat3 = self.sb(es, "feat3", 3 * 128, BF16)
            gbc = self.sb(es, "gbc", 128)
            Dm = self.sb(es, "Dm", 128)
            Lam = self.sb(es, "Lam", 128)
            Nn = self.sb(es, "Nn", 128)
            NTt = self.sb(es, "NTt", 128)
            PP = [self.sb(es, "PP%d" % i, 256) for i in range(2)]
            X = [self.sb(es, "X%d" % i, 128) for i in range(2)]
            Xb = self.sb(es, "Xb", 128, BF16)
            aqk = self.sb(es, "aqk", 128, BF16)
            nwT = self.sb(es, "nwT", 128, BF16)
            wv = self.sb(es, "wv", 128, BF16)
            sso = self.sb(es, "sso", 2)
            otmp = self.sb(es, "otmpB", 128, BF16)
            bco = self.sb(es, "bco", 384)
            if kind == "S":
                stc = self.sb(es, "stc", 3072)
            for hb in range(8):
                slab = self.next_slab()
                for i, off in enumerate([OFF_QB, OFF_KB, OFF_VB, OFF_GB]):
                    self.load_slab(slab, self.w_in, off + hb * 128, 128, dst_off=i * 128)
                for j in range(3):
                    for tap in range(4):
                        S.op("vector", lambda e, j=j, tap=tap: e.tensor_scalar(
                            Dw.v((j * 4 + tap) * 128, [[1, 128]]), CI(C_ID), self.wc.v((j * 8 + hb) * 4 + tap, [[1, 1]]), None, ALU.mult),
                            reads=[cst.res, self.wc.res], writes=[Dw.res])
                if kind == "P":
                    S.op("vector", lambda e: e.memset(zT.v(0, [[ZW, 3], [1, 3]]), 0.0), writes=[zT.res])
                for bi, (c0, n) in enumerate(self.blocks):
                    for j in range(3):
                        bank = (bi * 3 + j) % 2
                        pr = self.psres[bank]
                        for kc in range(8):
                            S.op("tensor", lambda e, kc=kc, j=j, bank=bank: e.matmul(
                                self.ps(bank, 0, [[1, n]]), slab.v(kc * 512 + j * 128, [[1, 128]]), self.xT_v(kc, c0, n),
                                start=(kc == 0), stop=(kc == 7)),
                                reads=[slab.res, self.xT.res], writes=[pr], acc=(kc > 0))
                        self.copy(self.evac_eng(), zT.v(j * ZW + 3 + c0, [[1, n]]), self.ps(bank, 0, [[1, n]]), [pr], [zT.res])
                if kind == "S":
                    for ti in range(NT):
                        S.dma("sync", stc.v(0, [[1, 3072]], 0, 6),
                              dr(self.sconv, (2 * ti) * 3 * 3072, [[3072, 6], [1, 3072]]), writes=[stc.res])
                        pr = self.psres[0]
                        for j in range(3):
                            S.op("tensor", lambda e, j=j: e.matmul(
                                self.ps(0, j * 6, [[1, 6]]), stc.v(j * 1024 + hb * 128, [[1, 128]], 0, 6),
                                cst.v(C_ID, [[1, 6]], 0, 6), start=True, stop=True),
                                reads=[stc.res, cst.res], writes=[pr], acc=(j > 0))
                        for i in range(2):
                            S.op("vector", lambda e, i=i, ti=ti: e.tensor_copy(
                                zT.v(ti * 128 + 64 * i, [[ZW, 3], [1, 3]]), self.ps(0, 3 * i, [[6, 3], [1, 3]])),
                                reads=[pr], writes=[zT.res])
                segs = [(NCOL - 3, self.bcp, self.b)] if kind == "P" else \
                    [(ti * 128 + 64 * i + 13, self.bcs, 2 * ti + i) for ti in range(NT) for i in range(2)]
                for (cc, dst, sq) in segs:
                    pr = self.psres[1]
                    for kc in range(8):
                        S.op("tensor", lambda e, kc=kc, cc=cc: e.matmul(
                            self.ps(1, 0, [[1, 384]], 0, 3), self.xT_v(kc, cc, 3), slab.v(kc * 512, [[1, 384]]),
                            start=(kc == 0), stop=(kc == 7)),
                            reads=[slab.res, self.xT.res], writes=[pr], acc=(kc > 0))
                    S.op("vector", lambda e: e.tensor_copy(bco.v(0, [[1, 384]], 0, 3), self.ps(1, 0, [[1, 384]], 0, 3)),
                         reads=[pr], writes=[bco.res])
                    S.dma("sync", dr(dst, sq * 3 * 3072 + hb * 128, [[3072, 3], [1024, 3], [1, 128]]),
                          bco.v(0, [[128, 3], [1, 128]], 0, 3), reads=[bco.res])
                for bi, (c0, n) in enumerate(self.blocks):
                    for j in range(3):
                        bank = (bi * 3 + j) % 2
                        pr = self.psres[bank]
                        for tap in range(4):
                            S.op("tensor", lambda e, j=j, tap=tap, bank=bank: e.matmul(
                                self.ps(bank, 0, [[1, n]]), Dw.v((j * 4 + tap) * 128, [[1, 128]]), zT.v(j * ZW + c0 + tap, [[1, n]]),
                                start=(tap == 0), stop=(tap == 3)),
                                reads=[Dw.res, zT.res], writes=[pr], acc=(tap > 0))
                        S.op("scalar", lambda e, j=j, bank=bank: e.activation(yT.v(j * NCOL + c0, [[1, n]]), self.ps(bank, 0, [[1, n]]), AF.Silu),
                             reads=[pr], writes=[yT.res])
                for ti in range(NT):
                    bank = ti % 2
                    pr = self.psres[bank]
                    for kc in range(8):
                        S.op("tensor", lambda e, kc=kc, bank=bank: e.matmul(
                            self.ps(bank, 0, [[1, 128]]), self.xT_v(kc, ti * 128, 128), slab.v(kc * 512 + 384, [[1, 128]]),
                            start=(kc == 0), stop=(kc == 7)),
                            reads=[slab.res, self.xT.res], writes=[pr], acc=(kc > 0))
                    S.op("scalar", lambda e, bank=bank: e.activation(sgn.v(ti * 128, [[1, 128]]), self.ps(bank, 0, [[1, 128]]), AF.Sigmoid),
                         reads=[pr], writes=[sgn.res])
                S.op("vector", lambda e: e.tensor_tensor(sgn.v(0, [[128, NT], [1, 128]]), sgn.v(0, [[128, NT], [1, 128]]),
                                                         self.normg_bc.v(0, [[0, NT], [1, 128]]), ALU.mult),
                     reads=[sgn.res, self.normg_bc.res], writes=[sgn.res])
                if kind == "P":
                    S.op("vector", lambda e: e.memset(Sf[0].v(), 0.0), writes=[Sf[0].res])
                    S.op("vector", lambda e: e.memset(Sb[0].v(), 0.0), writes=[Sb[0].res])
                for ti in range(NT):
                    self.gdn_tile(hb, ti, locals())
                if kind == "P":
                    S.dma("sync", dr(self.bsp, ((self.b * 8 + hb) * 128) * 128, [[128, 128], [1, 128]]), Sf[0].v(), reads=[Sf[0].res])
            S.barrier()

    def gdn_tile(self, hb, ti, L):
        S = self.S
        cst = self.cst
        NT, NCOL, kind = self.NT, self.NCOL, self.kind
        CI = lambda c: cst.v(c, [[1, 128]])
        (yT, Sf, Sb, ss, rn, cols, junk, tok5, Dg, feat3, gbc, Dm, Lam, Nn, NTt, PP, X, Xb, aqk, nwT, wv, sso, otmp,
         gt, bt, Gc, GD, gend, sgn) = [L[k] for k in
                                       "yT Sf Sb ss rn cols junk tok5 Dg feat3 gbc Dm Lam Nn NTt PP X Xb aqk nwT wv sso otmp gt bt Gc GD gend sgn".split()]
        P = self.psres
        ps = self.ps
        for j in range(3):
            S.op("tensor", lambda e, j=j: e.transpose(ps(0, j * 128, [[1, 128]]), yT.v(j * NCOL + ti * 128, [[1, 128]]), CI(C_ID)),
                 reads=[yT.res, cst.res], writes=[P[0]], acc=(j > 0))
        for j in range(2):
            S.op("scalar", lambda e, j=j: e.activation(junk.v(), ps(0, j * 128, [[1, 128]]), AF.Square, accum_out=ss.v(j, [[1, 1]])),
                 reads=[P[0]], writes=[junk.res, ss.res])
        S.op("vector", lambda e: e.tensor_scalar(rn.v(), ss.v(), L2_EPS, None, ALU.add), reads=[ss.res], writes=[rn.res])
        S.op("scalar", lambda e: e.activation(rn.v(), rn.v(), AF.Sqrt), reads=[rn.res], writes=[rn.res])
        S.op("vector", lambda e: e.reciprocal(rn.v(), rn.v()), reads=[rn.res], writes=[rn.res])
        S.op("vector", lambda e: e.tensor_scalar(cols.v(0, [[1, 1]]), rn.v(0, [[1, 1]]), 128.0 ** -0.5, None, ALU.mult),
             reads=[rn.res], writes=[cols.res])
        S.op("vector", lambda e: e.tensor_scalar(cols.v(1, [[1, 3]]), GD.v((ti * 8 + hb) * 3, [[1, 3]]), rn.v(1, [[1, 1]]), None, ALU.mult),
             reads=[rn.res, GD.res], writes=[cols.res])
        srcs = [(0, 0), (1, 1), (2, None), (1, 2), (1, 3)]
        for q, (j, cidx) in enumerate(srcs):
            en = "vector" if q % 2 == 0 else "scalar"
            sc = None if cidx is None else cols.v(cidx, [[1, 1]])
            self.copy(en, tok5.v(q * 128, [[1, 128]]), ps(0, j * 128, [[1, 128]]), [P[0], cols.res], [tok5.res], scale=sc)
        S.op("vector", lambda e: e.tensor_scalar(Dg.v(), self.idb.v(), GD.v((ti * 8 + hb) * 3 + 1, [[1, 1]]), None, ALU.mult),
             reads=[self.idb.res, GD.res], writes=[Dg.res])
        S.op("tensor", lambda e: e.matmul(ps(2, 0, [[1, 128]]), tok5.v(128, [[1, 128]]), self.idb.v(), start=True, stop=True),
             reads=[tok5.res, self.idb.res], writes=[P[2]])
        S.op("tensor", lambda e: e.matmul(ps(2, 128, [[1, 128]]), tok5.v(0, [[1, 128]]), self.idb.v(), start=True, stop=True),
             reads=[tok5.res, self.idb.res], writes=[P[2]], acc=True)
        S.op("tensor", lambda e: e.matmul(ps(2, 256, [[1, 128]]), tok5.v(0, [[1, 128]]), Dg.v(), start=True, stop=True),
             reads=[tok5.res, Dg.res], writes=[P[2]], acc=True)
        self.copy("vector", feat3.v(), ps(2, 0, [[1, 384]]), [P[2]], [feat3.res])
        S.op("tensor", lambda e: e.matmul(ps(3, 0, [[1, 128]]), feat3.v(0, [[1, 128]]), feat3.v(0, [[1, 128]]), start=True, stop=True),
             reads=[feat3.res], writes=[P[3]])
        S.op("tensor", lambda e: e.matmul(ps(3, 128, [[1, 128]]), feat3.v(0, [[1, 128]]), feat3.v(128, [[1, 128]]), start=True, stop=True),
             reads=[feat3.res], writes=[P[3]], acc=True)
        S.op("vector", lambda e: e.tensor_scalar(gbc.v(), CI(C_ONES), gt.v(ti * 8 + hb, [[1, 1]]), None, ALU.mult),
             reads=[cst.res, gt.res], writes=[gbc.res])
        S.op("tensor", lambda e: e.matmul(ps(3, 256, [[1, 128]]), gbc.v(), CI(C_TRI), start=True, stop=True),
             reads=[gbc.res, cst.res], writes=[P[3]], acc=True)
        S.op("vector", lambda e: e.scalar_tensor_tensor(Dm.v(), ps(3, 256, [[1, 128]]), Gc.v(ti * 8 + hb, [[1, 1]]), CI(C_MNEG),
                                                        ALU.subtract, ALU.add),
             reads=[P[3], Gc.res, cst.res], writes=[Dm.res])
        S.op("scalar", lambda e: e.activation(Lam.v(), Dm.v(), AF.Exp), reads=[Dm.res], writes=[Lam.res])
        S.op("vector", lambda e: e.tensor_tensor(Dm.v(), ps(3, 0, [[1, 128]]), Lam.v(), ALU.mult),
             reads=[P[3], Lam.res], writes=[Dm.res])
        S.op("vector", lambda e: e.scalar_tensor_tensor(Nn.v(), Dm.v(), bt.v((ti * 8 + hb) * 2 + 1, [[1, 1]]), CI(C_STRICT),
                                                        ALU.mult, ALU.mult),
             reads=[Dm.res, bt.res, cst.res], writes=[Nn.res])
        S.op("vector", lambda e: e.tensor_tensor(aqk.v(), ps(3, 128, [[1, 128]]), Lam.v(), ALU.mult),
             reads=[P[3], Lam.res], writes=[aqk.res])
        S.op("vector", lambda e: e.tensor_tensor(X[0].v(), Nn.v(), CI(C_ID), ALU.add), reads=[Nn.res, cst.res], writes=[X[0].res])
        S.op("tensor", lambda e: e.transpose(ps(4, 0, [[1, 128]]), Nn.v(), CI(C_ID)), reads=[Nn.res, cst.res], writes=[P[4]])
        self.copy("scalar", NTt.v(), ps(4, 0, [[1, 128]]), [P[4]], [NTt.res])
        Pm, PmT = Nn.v(), NTt.v()
        Pres, PTres = Nn.res, NTt.res
        xi = 0
        for lvl in range(1, 6):
            pp = PP[lvl % 2]
            last = (lvl == 5)
            S.op("tensor", lambda e, Pm=Pm, PmT=PmT: e.matmul(ps(4, 128, [[1, 128]]), Pm, PmT, start=True, stop=True),
                 reads=[Pres, PTres], writes=[P[4]])
            if not last:
                S.op("tensor", lambda e, Pm=Pm, PmT=PmT: e.matmul(ps(4, 0, [[1, 128]]), PmT, Pm, start=True, stop=True),
                     reads=[Pres, PTres], writes=[P[4]], acc=True)
                self.copy("scalar", pp.v(), ps(4, 0, [[1, 256]]), [P[4]], [pp.res])
            else:
                self.copy("scalar", pp.v(128, [[1, 128]]), ps(4, 128, [[1, 128]]), [P[4]], [pp.res])
            Pm, PmT = pp.v(0, [[1, 128]]), pp.v(128, [[1, 128]])
            Pres = PTres = pp.res
            xo, xn = X[xi], X[1 - xi]
            S.op("tensor", lambda e, PmT=PmT, xo=xo: e.matmul(ps(5, 0, [[1, 128]]), PmT, xo.v(), start=True, stop=True),
                 reads=[pp.res, xo.res], writes=[P[5]])
            if not last:
                S.op("vector", lambda e, xo=xo, xn=xn: e.tensor_tensor(xn.v(), ps(5, 0, [[1, 128]]), xo.v(), ALU.add),
                     reads=[P[5], xo.res], writes=[xn.res])
            else:
                S.op("vector", lambda e, xo=xo: e.tensor_tensor(Xb.v(), ps(5, 0, [[1, 128]]), xo.v(), ALU.add),
                     reads=[P[5], xo.res], writes=[Xb.res])
            xi = 1 - xi
        S.op("tensor", lambda e: e.matmul(ps(5, 128, [[1, 128]]), tok5.v(3 * 128, [[1, 128]]), Xb.v(), start=True, stop=True),
             reads=[tok5.res, Xb.res], writes=[P[5]])
        self.copy("scalar", nwT.v(), ps(5, 128, [[1, 128]]), [P[5]], [nwT.res], scale=-1.0)
        for i in range(2):
            c0 = 64 * i
            si = i if kind == "S" else 0
            if kind == "S":
                sq = 2 * ti + i
                S.dma("sync", Sf[si].v(), dr(self.sssm, ((sq * 8 + hb) * 128) * 128, [[128, 128], [1, 128]]), writes=[Sf[si].res])
                self.copy("scalar", Sb[si].v(), Sf[si].v(), [Sf[si].res], [Sb[si].res])
            sfl, sbl = Sf[si], Sb[si]
            S.op("tensor", lambda e, c0=c0: e.matmul(ps(6, 0, [[1, 128]], c0, 64), Xb.v(c0, [[1, 64]]), tok5.v(2 * 128, [[1, 128]]),
                                                     start=True, stop=False),
                 reads=[Xb.res, tok5.res], writes=[P[6]])
            S.op("tensor", lambda e, c0=c0, sbl=sbl: e.matmul(ps(6, 0, [[1, 128]], c0, 64), nwT.v(c0, [[1, 64]]), sbl.v(),
                                                              start=False, stop=True),
                 reads=[nwT.res, sbl.res], writes=[P[6]], acc=True)
            S.op("vector", lambda e, c0=c0: e.tensor_scalar(wv.v(0, [[1, 128]], c0, 64), ps(6, 0, [[1, 128]], c0, 64),
                                                            bt.v((ti * 8 + hb) * 2, [[1, 1]], c0, 64), None, ALU.mult),
                 reads=[P[6], bt.res], writes=[wv.res])
            S.op("tensor", lambda e, c0=c0, sbl=sbl: e.matmul(ps(2, 0, [[1, 128]], c0, 64), feat3.v(256 + c0, [[1, 64]]), sbl.v(),
                                                              start=True, stop=False),
                 reads=[feat3.res, sbl.res], writes=[P[2]], acc=(i > 0))
            S.op("tensor", lambda e, c0=c0: e.matmul(ps(2, 0, [[1, 128]], c0, 64), aqk.v(c0, [[1, 64]], c0, 64), wv.v(0, [[1, 128]], c0, 64),
                                                     start=False, stop=True),
                 reads=[aqk.res, wv.res], writes=[P[2]], acc=True)
            S.op("tensor", lambda e, c0=c0: e.matmul(ps(7, 0, [[1, 128]]), tok5.v(4 * 128, [[1, 128]], c0, 64), wv.v(0, [[1, 128]], c0, 64),
                                                     start=True, stop=True),
                 reads=[tok5.res, wv.res], writes=[P[7]])
            S.op("vector", lambda e, sfl=sfl, i=i: e.scalar_tensor_tensor(sfl.v(), sfl.v(), gend.v(ti * 16 + i * 8 + hb, [[1, 1]]),
                                                                          ps(7, 0, [[1, 128]]), ALU.mult, ALU.add),
                 reads=[sfl.res, gend.res, P[7]], writes=[sfl.res])
            self.copy("scalar", sbl.v(), sfl.v(), [sfl.res], [sbl.res])
            if kind == "S":
                S.dma("sync", dr(self.bss, ((sq * 8 + hb) * 128) * 128, [[128, 128], [1, 128]]), sfl.v(), reads=[sfl.res])
        S.op("scalar", lambda e: e.activation(junk.v(), ps(2, 0, [[1, 128]]), AF.Square, accum_out=sso.v(0, [[1, 1]])),
             reads=[P[2]], writes=[junk.res, sso.res])
        S.op("vector", lambda e: e.tensor_scalar(sso.v(1, [[1, 1]]), sso.v(0, [[1, 1]]), 1.0 / 128.0, RMS_EPS, ALU.mult, ALU.add),
             reads=[sso.res], writes=[sso.res])
        S.op("scalar", lambda e: e.activation(sso.v(1, [[1, 1]]), sso.v(1, [[1, 1]]), AF.Sqrt), reads=[sso.res], writes=[sso.res])
        S.op("vector", lambda e: e.reciprocal(sso.v(1, [[1, 1]]), sso.v(1, [[1, 1]])), reads=[sso.res], writes=[sso.res])
        S.op("vector", lambda e: e.scalar_tensor_tensor(otmp.v(), ps(2, 0, [[1, 128]]), sso.v(1, [[1, 1]]), sgn.v(ti * 128, [[1, 128]]),
                                                        ALU.mult, ALU.mult),
             reads=[P[2], sso.res, sgn.res], writes=[otmp.res])
        mr = self.mres[ti]
        S.op("vector", lambda e: e.tensor_tensor(self.merged.v(ti * D + hb * 128, [[1, 128]]), self.merged.v(ti * D + hb * 128, [[1, 128]]),
                                                 otmp.v(), ALU.add),
             reads=[otmp.res, mr], writes=[mr])

    def dump_merged(self):
        S = self.S
        with contextlib.ExitStack() as es:
            st = self.sb(es, "dbgst", D)
            for ti in range(self.NT):
                S.op("vector", lambda e, ti=ti: e.tensor_copy(st.v(), self.merged.v(ti * D, [[1, D]])),
                     reads=[self.mres[ti]], writes=[st.res])
                if self.kind == "P":
                    S.dma("sync", dr(self.dbg_mp, (self.b * self.T + ti * 128) * D, [[D, 128], [1, D]]), st.v(), reads=[st.res])
                else:
                    for i in range(2):
                        S.dma("sync", dr(self.dbg_ms, ((2 * ti + i) * 64) * D, [[D, 64], [1, D]]), st.v(0, [[1, D]], 64 * i, 64), reads=[st.res])
            S.barrier()

    def phaseD(self, es_pass):
        S = self.S
        NT, NCOL, kind = self.NT, self.NCOL, self.kind
        cst = self.cst
        CI = lambda c: cst.v(c, [[1, 128]])
        ps = self.ps
        P = self.psres
        with contextlib.ExitStack() as es:
            woutb = self.sb(es, "woutb", 8 * D, BF16)
            for q in range(2):
                S.dma("gpsimd", woutb.v(q * 512, [[D, 8], [1, 512]]),
                      dr(self.wout, q * 512, [[D, 128], [128 * D, 8], [1, 512]]), writes=[woutb.res])
            f1T = self.sb(es, "f1T", 32 * 512, BF16)
            hT = self.sb(es, "hT", 8 * 512, BF16)
            hn = self.sb(es, "hn", 4 * D)
            mT = self.sb(es, "mT", 8 * 128, BF16)
            xt = [self.sb(es, "xtD%d" % i, D) for i in range(2)]
            hp_ = self.sb(es, "hpD", D)
            st = self.sb(es, "stD", 8)
            junk = self.sb(es, "junkD", D)
            rl = self.sb(es, "rlD", 512)
            yo = [self.sb(es, "yoD%d" % i, D) for i in range(2)]
            yi = 0
            for bi, (c0, n) in enumerate(self.blocks):
                nt_b = n // 128
                for tl in range(nt_b):
                    ti = c0 // 128 + tl
                    mr = self.mres[ti]
                    for half in range(2):
                        for q in range(4):
                            kc = half * 4 + q
                            S.op("tensor", lambda e, kc=kc, q=q, half=half: e.matmul(
                                ps(half, q * 128, [[1, 128]]), self.merged.v(ti * D + kc * 128, [[1, 128]]), self.idb.v(),
                                start=True, stop=True),
                                reads=[mr, self.idb.res], writes=[P[half]], acc=(q > 0))
                        self.copy(self.evac_eng(), mT.v(half * 512, [[1, 512]]), ps(half, 0, [[1, 512]]), [P[half]], [mT.res])
                    x_ = xt[ti % 2]
                    self.x_tile_src(x_, ti)
                    for half in range(2):
                        for kc in range(8):
                            S.op("tensor", lambda e, kc=kc, half=half: e.matmul(
                                ps(2 + half, 0, [[1, 512]]), mT.v(kc * 128, [[1, 128]]), woutb.v(kc * D + half * 512, [[1, 512]]),
                                start=(kc == 0), stop=(kc == 7)),
                                reads=[mT.res, woutb.res], writes=[P[2 + half]], acc=(kc > 0))
                        S.op("vector", lambda e, half=half: e.scalar_tensor_tensor(
                            hp_.v(half * 512, [[1, 512]]), x_.v(half * 512, [[1, 512]]), ALPHA, ps(2 + half, 0, [[1, 512]]), ALU.mult, ALU.add),
                            reads=[x_.res, P[2 + half]], writes=[hp_.res])
                    self.layer_norm(hp_, hn.v(tl * D, [[1, D]]), hn.res, st, junk)
                    for half in range(2):
                        for q in range(4):
                            kc = half * 4 + q
                            S.op("tensor", lambda e, kc=kc, q=q, half=half: e.transpose(
                                ps(4 + half, q * 128, [[1, 128]]), hn.v(tl * D + kc * 128, [[1, 128]]), CI(C_ID)),
                                reads=[hn.res, cst.res], writes=[P[4 + half]], acc=(q > 0))
                        for q in range(4):
                            kc = half * 4 + q
                            S.op("vector", lambda e, kc=kc, q=q, half=half: e.tensor_scalar(
                                hT.v(kc * 512 + tl * 128, [[1, 128]]), ps(4 + half, q * 128, [[1, 128]]),
                                self.g1c.v(kc, [[1, 1]]), self.b1c.v(kc, [[1, 1]]), ALU.mult, ALU.add),
                                reads=[P[4 + half], self.g1c.res, self.b1c.res], writes=[hT.res])
                for s in range(8):
                    slab = self.next_slab()
                    self.load_slab(slab, self.wff1, s * 512, 512)
                    for q in range(4):
                        fc = s * 4 + q
                        bank = fc % 2
                        for kc in range(8):
                            S.op("tensor", lambda e, kc=kc, q=q, bank=bank: e.matmul(
                                ps(bank, 0, [[1, n]]), slab.v(kc * 512 + q * 128, [[1, 128]]), hT.v(kc * 512, [[1, n]]),
                                start=(kc == 0), stop=(kc == 7)),
                                reads=[slab.res, hT.res], writes=[P[bank]], acc=(kc > 0))
                        S.op("scalar", lambda e, fc=fc, bank=bank: e.activation(rl.v(0, [[1, n]]), ps(bank, 0, [[1, n]]), AF.Relu,
                                                                               bias=self.b1T.v(fc, [[1, 1]])),
                             reads=[P[bank], self.b1T.res], writes=[rl.res])
                        S.op("vector", lambda e, fc=fc: e.tensor_tensor(f1T.v(fc * 512, [[1, n]]), rl.v(0, [[1, n]]), rl.v(0, [[1, n]]), ALU.mult),
                             reads=[rl.res], writes=[f1T.res])
                for s in range(8):
                    slab = self.next_slab()
                    self.S.dma("gpsimd", slab.v(0, [[D, 4], [1, D]]),
                               dr(self.wff2, s * 512 * D, [[D, 128], [128 * D, 4], [1, D]]), writes=[slab.res])
                    for q in range(4):
                        fc = s * 4 + q
                        for tl in range(nt_b):
                            for half in range(2):
                                bank = tl * 2 + half
                                S.op("tensor", lambda e, fc=fc, q=q, tl=tl, half=half, bank=bank: e.matmul(
                                    ps(bank, 0, [[1, 512]]), f1T.v(fc * 512 + tl * 128, [[1, 128]]), slab.v(q * D + half * 512, [[1, 512]]),
                                    start=(fc == 0), stop=(fc == 31)),
                                    reads=[f1T.res, slab.res], writes=[P[bank]], acc=(fc > 0))
                for tl in range(nt_b):
                    ti = c0 // 128 + tl
                    S.op("vector", lambda e, tl=tl: e.tensor_tensor(hp_.v(), hn.v(tl * D, [[1, D]]), self.g1a.v(), ALU.mult),
                         reads=[hn.res, self.g1a.res], writes=[hp_.res])
                    S.op("vector", lambda e: e.tensor_tensor(hp_.v(), hp_.v(), self.c1.v(), ALU.add),
                         reads=[hp_.res, self.c1.res], writes=[hp_.res])
                    for half in range(2):
                        bank = tl * 2 + half
                        S.op("vector", lambda e, half=half, bank=bank: e.tensor_tensor(
                            hp_.v(half * 512, [[1, 512]]), hp_.v(half * 512, [[1, 512]]), ps(bank, 0, [[1, 512]]), ALU.add),
                            reads=[hp_.res, P[bank]], writes=[hp_.res])
                    y = yo[yi % 2]
                    yi += 1
                    self.layer_norm(hp_, y.v(), y.res, st, junk)
                    S.op("vector", lambda e: e.tensor_tensor(y.v(), y.v(), self.g2.v(), ALU.mult), reads=[y.res, self.g2.res], writes=[y.res])
                    S.op("vector", lambda e: e.tensor_tensor(y.v(), y.v(), self.b2.v(), ALU.add), reads=[y.res, self.b2.res], writes=[y.res])
                    if kind == "P":
                        S.dma("sync", dr(self.yp, (self.b * self.T + ti * 128) * D, [[D, 128], [1, D]]), y.v(), reads=[y.res])
                    else:
                        for i in range(2):
                            S.dma("sync", dr(self.ys, ((2 * ti + i) * 16) * D, [[D, 16], [1, D]]), y.v(0, [[1, D]], 64 * i, 16), reads=[y.res])
            S.barrier()

    def layer_norm(self, src, out_ap, out_res, st, junk):
        S = self.S
        S.op("scalar", lambda e: e.activation(junk.v(), src.v(), AF.Copy, accum_out=st.v(0, [[1, 1]])),
             reads=[src.res], writes=[junk.res, st.res])
        S.op("vector", lambda e: e.tensor_scalar(st.v(1, [[1, 1]]), st.v(0, [[1, 1]]), -1.0 / D, None, ALU.mult),
             reads=[st.res], writes=[st.res])
        S.op("vector", lambda e: e.tensor_scalar(src.v(), src.v(), st.v(1, [[1, 1]]), None, ALU.add),
             reads=[src.res, st.res], writes=[src.res])
        S.op("scalar", lambda e: e.activation(junk.v(), src.v(), AF.Square, accum_out=st.v(2, [[1, 1]])),
             reads=[src.res], writes=[junk.res, st.res])
        S.op("vector", lambda e: e.tensor_scalar(st.v(3, [[1, 1]]), st.v(2, [[1, 1]]), 1.0 / D, LN_EPS, ALU.mult, ALU.add),
             reads=[st.res], writes=[st.res])
        S.op("scalar", lambda e: e.activation(st.v(3, [[1, 1]]), st.v(3, [[1, 1]]), AF.Sqrt), reads=[st.res], writes=[st.res])
        S.op("vector", lambda e: e.reciprocal(st.v(3, [[1, 1]]), st.v(3, [[1, 1]])), reads=[st.res], writes=[st.res])
        S.op("vector", lambda e: e.tensor_scalar(out_ap, src.v(), st.v(3, [[1, 1]]), None, ALU.mult),
             reads=[src.res, st.res], writes=[out_res])


_CACHE = {}


def _get_nc(NBP, T, NBS, debug=False):
    key = (NBP, T, NBS, debug)
    if key not in _CACHE:
        b = Builder(NBP, T, NBS, debug)
        _CACHE[key] = b.build()
    return _CACHE[key]


def make_in_maps(inp, n_cores, NBP, T, NBS):
    f = lambda a: np.ascontiguousarray(np.asarray(a, np.float32))
    consts = make_consts()
    bP, bS, bN = make_bias_tables(np.asarray(inp["a_rel_bias"])[0])
    shared = dict(
        w_in=f(inp["w_in"][0]), wconv=f(inp["w_b_conv"][0]), alog=f(inp["b_a_log"]).reshape(1, 8),
        dtb=f(inp["b_dt_bias"]).reshape(1, 8), normg=f(inp["b_norm_g"]).reshape(1, 128),
        biasP=bP.reshape(16 * 128, 640), biasS=bS.reshape(16 * 128, 256), biasN=bN.reshape(16 * 64, 64),
        wkv=f(inp["w_mem_kv"][0]), wout=f(inp["w_out"][0]), ln1g=f(inp["ln1_g"]).reshape(1, D), ln1b=f(inp["ln1_b"]).reshape(1, D),
        wff1=f(inp["w_ff1"][0]), bff1=f(inp["b_ff1"]).reshape(32, 128), wff2=f(inp["w_ff2"][0]), bff2=f(inp["b_ff2"]).reshape(1, D),
        ln2g=f(inp["ln2_g"]).reshape(1, D), ln2b=f(inp["ln2_b"]).reshape(1, D), consts=consts)
    maps = []
    for c in range(n_cores):
        ps_, ss_ = slice(c * NBP, (c + 1) * NBP), slice(c * NBS, (c + 1) * NBS)
        m = dict(shared)
        m["xp"] = f(inp["x_prompt"][ps_]).reshape(NBP * T, D)
        m["xs"] = f(inp["x_sample"][ss_]).reshape(NBS * 16, D)
        m["cak"] = f(inp["cache_a_k"][0, ss_]).reshape(NBS * LC, D)
        m["cav"] = f(inp["cache_a_v"][0, ss_]).reshape(NBS * LC, D)
        m["sconv"] = f(inp["state_b_conv"][0, ss_]).reshape(NBS * 3, 3072)
        m["sssm"] = f(inp["state_b_ssm"][0, ss_]).reshape(NBS * 8 * 128, 128)
        m["cmk"] = f(inp["cache_mem_k"][0, ss_]).reshape(NBS * 256, D)
        m["cmv"] = f(inp["cache_mem_v"][0, ss_]).reshape(NBS * 256, D)
        m["memp"] = f(inp["mem_prompt"][ps_]).reshape(NBP * 256, D)
        maps.append(m)
    return maps


def assemble(results, n_cores, NBP, T, NBS):
    cat = lambda k: np.concatenate([np.asarray(r[k]) for r in results], axis=0)
    B, BS = n_cores * NBP, n_cores * NBS
    yp = cat("yp").reshape(B, T, D)
    ys = cat("ys").reshape(BS, 16, D)
    akp = cat("akp").reshape(1, B, 512, 16, 64)
    avp = cat("avp").reshape(1, B, 512, 16, 64)
    bcp = cat("bcp").reshape(1, B, 3, 3072)
    bsp = cat("bsp").reshape(1, B, 8, 128, 128)
    mkp = cat("mkp").reshape(1, B, 256, 4, 256)
    mvp = cat("mvp").reshape(1, B, 256, 4, 256)
    aks = cat("aks").reshape(1, BS, 512, 16, 64)
    avs = cat("avs").reshape(1, BS, 512, 16, 64)
    bcs = cat("bcs").reshape(1, BS, 3, 3072)
    bss = cat("bss").reshape(1, BS, 8, 128, 128)
    return (yp, ys, akp, avp, bcp, bsp, mkp, mvp, aks, avs, bcs, bss)


def kernel(**inputs):
    n_cores = 8
    B, T = inputs["x_prompt"].shape[0], inputs["x_prompt"].shape[1]
    BS = inputs["x_sample"].shape[0]
    NBP, NBS = B // n_cores, BS // n_cores
    nc = _get_nc(NBP, T, NBS)
    maps = make_in_maps(inputs, n_cores, NBP, T, NBS)
    res = run_bass_kernel_spmd(nc, maps, core_ids=list(range(n_cores)))
    return assemble(res.results, n_cores, NBP, T, NBS)
```

```python
import contextlib
import numpy as np
import concourse.bass as bass
import concourse.mybir as mybir
from concourse.bass_utils import run_bass_kernel_spmd

F32 = mybir.dt.float32
BF16 = mybir.dt.bfloat16
AF = mybir.ActivationFunctionType
ALU = mybir.AluOpType

D = 1024
IN_COLS = 10256
OFF_QA, OFF_KA, OFF_VA = 0, 1024, 2048
OFF_QB, OFF_KB, OFF_VB = 3072, 4096, 5120
OFF_QC = 6144
OFF_GA, OFF_GB, OFF_GC = 7168, 8192, 9216
OFF_AB = 10240
ALPHA = 2.0 ** 0.25
NEG = -30000.0
LN_EPS = 1e-5
RMS_EPS = 1e-6
L2_EPS = 1e-6
PAST_LEN = 1024
LC = 512

C_ID = 0
C_TRI = 128
C_ONESBD = 256
C_SEL0 = 384
C_SEL1 = 512
C_MNEG = 640
C_STRICT = 768
C_ROWM = 896
C_ONES = 900
C_MASKA = 1028
C_MASKN = 1668
NCONST = 1732


def make_consts():
    c = np.zeros((128, NCONST), np.float32)
    r = np.arange(128)[:, None]
    t = np.arange(128)[None, :]
    same = (r // 64) == (t // 64)
    c[:, C_ID:C_ID + 128] = np.eye(128)
    c[:, C_TRI:C_TRI + 128] = (same & (r <= t))
    c[:, C_ONESBD:C_ONESBD + 128] = same
    c[:, C_SEL0:C_SEL0 + 128] = (r < 64) & (t >= 0)
    c[:, C_SEL1:C_SEL1 + 128] = (r >= 64) & (t >= 0)
    c[:, C_MNEG:C_MNEG + 128] = np.where(same & (r <= t), 0.0, -60000.0)
    c[:, C_STRICT:C_STRICT + 128] = (same & (r < t))
    c[:, C_ROWM] = ((np.arange(128) % 64) < 16)
    c[:, C_ONES:C_ONES + 128] = 1.0
    kk = np.arange(128)[:, None, None]
    j = np.arange(5)[None, :, None]
    qq = np.arange(128)[None, None, :]
    cq = qq // 64
    pos = 128 * j + kk
    valid = (pos >= 64 * cq) & (pos < 576 + 64 * cq)
    c[:, C_MASKA:C_MASKA + 640] = np.where(valid, 0.0, NEG).reshape(128, 640)
    mn = np.zeros((128, 64), np.float32)
    mn[(np.arange(128) % 64) >= 16, :] = NEG
    c[:, C_MASKN:C_MASKN + 64] = mn
    return c


def make_bias_tables(rel_bias):
    rb = np.asarray(rel_bias, np.float32)
    kk = np.arange(128)[:, None, None]
    j5 = np.arange(5)[None, :, None]
    qq = np.arange(128)[None, None, :]
    rel = 512 - 128 * j5 + qq - kk
    idx = np.clip(rel, -128, 128) + 128
    biasP = rb[:, idx].reshape(16, 128, 640)
    j4 = np.arange(4)[None, :, None]
    q64 = np.arange(64)[None, None, :]
    rel = 512 + q64 - 128 * j4 - kk
    idx = np.clip(rel, -128, 128) + 128
    biasS = rb[:, idx].reshape(16, 128, 256)
    k64 = np.arange(64)[:, None]
    q64 = np.arange(64)[None, :]
    idx = np.clip(q64 - k64, -128, 128) + 128
    biasN = rb[:, idx].reshape(16, 64, 64)
    return (np.ascontiguousarray(biasP), np.ascontiguousarray(biasS), np.ascontiguousarray(biasN))


class Res:
    __slots__ = ("name", "w", "r", "ds", "ps")

    def __init__(self, name, ps=False):
        self.name = name
        self.w = None
        self.r = []
        self.ds = {}
        self.ps = ps


class DSem:
    def __init__(self, sem):
        self.sem = sem
        self.cnt = 0


class Sync:
    ENG = ["tensor", "vector", "scalar", "gpsimd", "sync"]

    def __init__(self, nc, n_dma_sems=72):
        self.nc = nc
        self.E = {}
        for n in self.ENG:
            self.E[n] = dict(e=getattr(nc, n), sem=nc.alloc_semaphore(name="s_" + n), cnt=0, seen={})
        self.free_ds = {"hw": [DSem(nc.alloc_semaphore(name="d%d" % i)) for i in range(n_dma_sems)],
                        "sw": [DSem(nc.alloc_semaphore(name="q%d" % i)) for i in range(10)]}
        self.all_ds = self.free_ds["hw"] + self.free_ds["sw"]
        self.owned = []
        self.ninst = 0

    def _wait(self, en, deps):
        E = self.E[en]
        need = {}
        for (sem, val) in deps:
            k = sem.num
            if E["seen"].get(k, 0) >= val:
                continue
            if k not in need or need[k][1] < val:
                need[k] = (sem, val)
        for k, (sem, val) in need.items():
            E["e"].wait_ge(sem, val)
            E["seen"][k] = val
            self.ninst += 1

    @staticmethod
    def _deps(reads, writes, acc, own=None):
        deps = []
        for r in reads:
            if r.w is not None:
                deps.append(r.w)
            if r.ps:
                deps.extend(t for t in r.r if t[0].num != own)
        if not acc:
            for w in writes:
                if w.w is not None:
                    deps.append(w.w)
                deps.extend(w.r)
        return deps

    def op(self, en, fn, reads=(), writes=(), acc=False):
        E = self.E[en]
        self._wait(en, self._deps(reads, writes, acc, E["sem"].num))
        inst = fn(E["e"])
        E["cnt"] += 1
        inst.then_inc(E["sem"], 1)
        tok = (E["sem"], E["cnt"])
        for r in reads:
            r.r.append(tok)
        for w in writes:
            w.w = tok
            if not acc:
                w.r = []
        self.ninst += 1
        return inst

    def dma(self, en, out, in_, reads=(), writes=(), owner=None, **kw):
        E = self.E[en]
        self._wait(en, self._deps(reads, writes, False))
        if owner is None:
            owner = (list(writes) + list(reads))[0]
        qk = "sw" if en == "gpsimd" else "hw"
        if qk not in owner.ds:
            owner.ds[qk] = self.free_ds[qk].pop()
            self.owned.append((owner, qk))
        ds = owner.ds[qk]
        ds.cnt += 16
        inst = E["e"].dma_start(out=out, in_=in_, **kw)
        inst.then_inc(ds.sem, 16)
        tok = (ds.sem, ds.cnt)
        for r in reads:
            r.r.append(tok)
        for w in writes:
            w.w = tok
            w.r = []
        self.ninst += 1
        return inst

    def barrier(self, release=True):
        toks = [(self.E[n]["sem"], self.E[n]["cnt"]) for n in self.ENG if self.E[n]["cnt"] > 0]
        toks += [(d.sem, d.cnt) for d in self.all_ds if d.cnt > 0]
        for n in self.ENG:
            self._wait(n, toks)
        if release:
            for (o, qk) in self.owned:
                self.free_ds[qk].append(o.ds.pop(qk))
            self.owned = []


class Tl:
    def __init__(self, h, F, name):
        self.h = h
        self.F = F
        self.res = Res(name)

    def v(self, off=0, dims=None, p0=0, pn=128):
        if dims is None:
            dims = [[1, self.F - off]]
        return bass.AP(tensor=self.h, offset=p0 * self.F + off, ap=[[self.F, pn]] + [list(d) for d in dims])


def dr(t, off, dims):
    return bass.AP(tensor=t.tensor, offset=off, ap=[list(d) for d in dims])


class Builder:
    def __init__(self, NBP, T, NBS, debug=False):
        assert T % 512 == 0 and NBS % 2 == 0
        self.NBP, self.T, self.NBS = NBP, T, NBS
        self.debug = debug
        nc = bass.Bass("TRN2", target_bir_lowering=False)
        self.nc = nc
        self.S = Sync(nc)
        di = lambda n, s: nc.dram_tensor(n, s, F32, kind="ExternalInput").ap()
        do = lambda n, s: nc.dram_tensor(n, s, F32, kind="ExternalOutput").ap()
        self.xp = di("xp", [NBP * T, D])
        self.xs = di("xs", [NBS * 16, D])
        self.cak = di("cak", [NBS * LC, D])
        self.cav = di("cav", [NBS * LC, D])
        self.sconv = di("sconv", [NBS * 3, 3072])
        self.sssm = di("sssm", [NBS * 8 * 128, 128])
        self.cmk = di("cmk", [NBS * 256, D])
        self.cmv = di("cmv", [NBS * 256, D])
        self.memp = di("memp", [NBP * 256, D])
        self.w_in = di("w_in", [D, IN_COLS])
        self.wconv = di("wconv", [4, 3072])
        self.alog = di("alog", [1, 8])
        self.dtb = di("dtb", [1, 8])
        self.normg = di("normg", [1, 128])
        self.biasP = di("biasP", [16 * 128, 640])
        self.biasS = di("biasS", [16 * 128, 256])
        self.biasN = di("biasN", [16 * 64, 64])
        self.wkv = di("wkv", [D, 2048])
        self.wout = di("wout", [D, D])
        self.ln1g = di("ln1g", [1, D])
        self.ln1b = di("ln1b", [1, D])
        self.wff1 = di("wff1", [D, 4096])
        self.bff1 = di("bff1", [32, 128])
        self.wff2 = di("wff2", [4096, D])
        self.bff2 = di("bff2", [1, D])
        self.ln2g = di("ln2g", [1, D])
        self.ln2b = di("ln2b", [1, D])
        self.consts = di("consts", [128, NCONST])
        self.yp = do("yp", [NBP * T, D])
        self.ys = do("ys", [NBS * 16, D])
        self.akp = do("akp", [NBP * 512, D])
        self.avp = do("avp", [NBP * 512, D])
        self.bcp = do("bcp", [NBP * 3, 3072])
        self.bsp = do("bsp", [NBP * 8 * 128, 128])
        self.mkp = do("mkp", [NBP * 256, D])
        self.mvp = do("mvp", [NBP * 256, D])
        self.aks = do("aks", [NBS * LC, D])
        self.avs = do("avs", [NBS * LC, D])
        self.bcs = do("bcs", [NBS * 3, 3072])
        self.bss = do("bss", [NBS * 8 * 128, 128])
        if debug:
            self.dbg_mp = do("dbg_mp", [NBP * T, D])
            self.dbg_ms = do("dbg_ms", [NBS * 64, D])
        self.flip = 0
        self.GB = 3

    def sb(self, es, name, F, dt=F32):
        self.uid = getattr(self, "uid", 0) + 1
        name = "%s_%d" % (name, self.uid)
        h = es.enter_context(self.nc.sbuf_tensor(name, [128, F], dt))
        return Tl(h, F, name)

    def evac_eng(self):
        self.flip ^= 1
        return "vector" if self.flip else "scalar"

    def copy(self, en, out, in_, reads, writes, scale=None):
        S = self.S
        if en == "scalar":
            if scale is None:
                S.op("scalar", lambda e: e.activation(out, in_, AF.Copy), reads=reads, writes=writes)
            elif isinstance(scale, float):
                S.op("scalar", lambda e: e.activation(out, in_, AF.Copy, scale=scale), reads=reads, writes=writes)
            else:
                S.op("scalar", lambda e: e.activation(out, in_, AF.Copy, scale=scale), reads=reads, writes=writes)
        else:
            if scale is None:
                S.op(en, lambda e: e.tensor_copy(out, in_), reads=reads, writes=writes)
            else:
                S.op(en, lambda e: e.tensor_scalar(out, in_, scale, None, ALU.mult), reads=reads, writes=writes)

    def load_slab(self, slab, src, col0, ncols=512, rows0=0, kc=8, dst_off=0):
        W = src.tensor.shape[1]
        self.S.dma("gpsimd", slab.v(dst_off, [[512, kc], [1, ncols]]),
                   dr(src, rows0 * W + col0, [[W, 128], [128 * W, kc], [1, ncols]]),
                   writes=[slab.res])

    def build(self):
        nc, S = self.nc, self.S
        with contextlib.ExitStack() as es:
            ph = es.enter_context(nc.psum_tensor("ps", [128, 4096], F32))
            self.PS = [None] * 8
            self.psh = ph
            self.psres = [Res("psb%d" % i, ps=True) for i in range(8)]
            self.cst = self.sb(es, "cst", NCONST)
            S.dma("sync", self.cst.v(), dr(self.consts, 0, [[NCONST, 128], [1, NCONST]]), writes=[self.cst.res])
            self.idb = self.sb(es, "idb", 128, BF16)
            S.op("vector", lambda e: e.tensor_copy(self.idb.v(), self.cst.v(C_ID, [[1, 128]])),
                 reads=[self.cst.res], writes=[self.idb.res])
            self.slabs = [self.sb(es, "slab%d" % i, 4096, BF16) for i in range(3)]
            self.slab_i = 0
            self.setup_small(es)
            ok = getattr(self, "only_kind", "PS")
            if "P" in ok:
                for b in range(self.NBP):
                    self.run_pass(es, "P", b)
            if "S" in ok:
                self.run_pass(es, "S", 0)
            S.barrier(release=False)
        return nc

    def next_slab(self):
        s = self.slabs[self.slab_i % 3]
        self.slab_i += 1
        return s

    def ps(self, bank, off=0, dims=None, p0=0, pn=128):
        if dims is None:
            dims = [[1, 512 - off]]
        return bass.AP(tensor=self.psh, offset=p0 * 4096 + bank * 512 + off, ap=[[4096, pn]] + [list(d) for d in dims])

    def setup_small(self, es):
        nc, S = self.nc, self.S
        cst = self.cst
        self.dtb_bc = self.sb(es, "dtb_bc", 8)
        self.nealog = self.sb(es, "nealog", 8)
        self.normg_bc = self.sb(es, "normg_bc", 128)
        S.dma("sync", self.dtb_bc.v(), dr(self.dtb, 0, [[0, 128], [1, 8]]), writes=[self.dtb_bc.res])
        S.dma("sync", self.nealog.v(), dr(self.alog, 0, [[0, 128], [1, 8]]), writes=[self.nealog.res])
        S.dma("sync", self.normg_bc.v(), dr(self.normg, 0, [[0, 128], [1, 128]]), writes=[self.normg_bc.res])
        S.op("scalar", lambda e: e.activation(self.nealog.v(), self.nealog.v(), AF.Exp),
             reads=[self.nealog.res], writes=[self.nealog.res])
        S.op("vector", lambda e: e.tensor_scalar(self.nealog.v(), self.nealog.v(), -1.0, None, ALU.mult),
             reads=[self.nealog.res], writes=[self.nealog.res])
        self.wc = self.sb(es, "wc", 96)
        self.b1T = self.sb(es, "b1T", 32)
        with contextlib.ExitStack() as es2:
            wtok = self.sb(es2, "wtok", 3072)
            S.dma("sync", wtok.v(0, [[1, 3072]], 0, 4), dr(self.wconv, 0, [[3072, 4], [1, 3072]]), writes=[wtok.res])
            pr = self.psres[0]
            for blk in range(24):
                S.op("tensor", lambda e, blk=blk: e.matmul(self.ps(0, blk * 4, [[1, 4]]),
                                                           wtok.v(blk * 128, [[1, 128]], 0, 4),
                                                           cst.v(C_ID, [[1, 4]], 0, 4), start=True, stop=True),
                     reads=[wtok.res, cst.res], writes=[pr], acc=(blk > 0))
            S.op("vector", lambda e: e.tensor_copy(self.wc.v(), self.ps(0, 0, [[1, 96]])), reads=[pr], writes=[self.wc.res])
            btok = self.sb(es2, "btok", 128)
            S.dma("sync", btok.v(0, [[1, 128]], 0, 32), dr(self.bff1, 0, [[128, 32], [1, 128]]), writes=[btok.res])
            pr1 = self.psres[1]
            S.op("tensor", lambda e: e.matmul(self.ps(1, 0, [[1, 32]]), btok.v(0, [[1, 128]], 0, 32),
                                              cst.v(C_ID, [[1, 32]], 0, 32), start=True, stop=True),
                 reads=[btok.res, cst.res], writes=[pr1])
            S.op("vector", lambda e: e.tensor_copy(self.b1T.v(), self.ps(1, 0, [[1, 32]])), reads=[pr1], writes=[self.b1T.res])
            S.barrier()
        self.g1a = self.sb(es, "g1a", D)
        self.c1 = self.sb(es, "c1", D)
        self.g2 = self.sb(es, "g2", D)
        self.b2 = self.sb(es, "b2", D)
        self.g1c = self.sb(es, "g1c", 8)
        self.b1c = self.sb(es, "b1c", 8)
        with contextlib.ExitStack() as es2:
            tmp = self.sb(es2, "tmpbc", D)
            S.dma("sync", self.g1a.v(), dr(self.ln1g, 0, [[0, 128], [1, D]]), writes=[self.g1a.res])
            S.dma("sync", self.c1.v(), dr(self.ln1b, 0, [[0, 128], [1, D]]), writes=[self.c1.res])
            S.dma("sync", tmp.v(), dr(self.bff2, 0, [[0, 128], [1, D]]), writes=[tmp.res])
            S.dma("sync", self.g2.v(), dr(self.ln2g, 0, [[0, 128], [1, D]]), writes=[self.g2.res])
            S.dma("sync", self.b2.v(), dr(self.ln2b, 0, [[0, 128], [1, D]]), writes=[self.b2.res])
            gtok = self.sb(es2, "gtok", 256)
            S.dma("sync", gtok.v(0, [[1, 128]], 0, 8), dr(self.ln1g, 0, [[128, 8], [1, 128]]), writes=[gtok.res])
            S.dma("sync", gtok.v(128, [[1, 128]], 0, 8), dr(self.ln1b, 0, [[128, 8], [1, 128]]), writes=[gtok.res])
            pr = self.psres[2]
            S.op("tensor", lambda e: e.matmul(self.ps(2, 0, [[1, 8]]), gtok.v(0, [[1, 128]], 0, 8),
                                              cst.v(C_ID, [[1, 8]], 0, 8), start=True, stop=True),
                 reads=[gtok.res, cst.res], writes=[pr])
            S.op("tensor", lambda e: e.matmul(self.ps(2, 8, [[1, 8]]), gtok.v(128, [[1, 128]], 0, 8),
                                              cst.v(C_ID, [[1, 8]], 0, 8), start=True, stop=True),
                 reads=[gtok.res, cst.res], writes=[pr], acc=True)
            S.op("vector", lambda e: e.tensor_copy(self.g1c.v(), self.ps(2, 0, [[1, 8]])), reads=[pr], writes=[self.g1c.res])
            S.op("vector", lambda e: e.tensor_copy(self.b1c.v(), self.ps(2, 8, [[1, 8]])), reads=[pr], writes=[self.b1c.res])
            S.op("vector", lambda e: e.tensor_scalar(self.g1a.v(), self.g1a.v(), ALPHA, None, ALU.mult),
                 reads=[self.g1a.res], writes=[self.g1a.res])
            S.op("vector", lambda e: e.scalar_tensor_tensor(self.c1.v(), self.c1.v(), ALPHA, tmp.v(), ALU.mult, ALU.add),
                 reads=[self.c1.res, tmp.res], writes=[self.c1.res])
            S.barrier()

    def run_pass(self, es_outer, kind, b):
        nc, S = self.nc, self.S
        T = self.T
        NT = (T // 128) if kind == "P" else (self.NBS // 2)
        NCOL = NT * 128
        blocks = [(c, min(512, NCOL - c)) for c in range(0, NCOL, 512)]
        self.kind, self.b, self.NT, self.NCOL, self.blocks = kind, b, NT, NCOL, blocks
        with contextlib.ExitStack() as es:
            self.merged = self.sb(es, "merged", NT * D, BF16)
            self.mres = [Res("mrg%d" % t) for t in range(NT)]
            stop = getattr(self, "stop_at", 99)
            with contextlib.ExitStack() as es1:
                self.xT = self.sb(es1, "xT", 8 * NCOL, BF16)
                if stop >= 1:
                    self.phase0(es1)
                if stop >= 2:
                    self.phaseA(es1)
                if stop >= 3:
                    self.phaseC(es1)
                if stop >= 4:
                    self.phaseB(es1)
                S.barrier()
            if self.debug and stop >= 4:
                self.dump_merged()
            if stop >= 5:
                self.phaseD(es)
            S.barrier()

    def xT_v(self, kc, c0, n):
        return self.xT.v(kc * self.NCOL + c0, [[1, n]])

    def x_tile_src(self, xt, ti, en="sync"):
        S = self.S
        if self.kind == "P":
            S.dma(en, xt.v(), dr(self.xp, (self.b * self.T + ti * 128) * D, [[D, 128], [1, D]]), writes=[xt.res])
        else:
            S.op("vector", lambda e: e.memset(xt.v(), 0.0), writes=[xt.res])
            for i in range(2):
                sq = 2 * ti + i
                S.dma(en, xt.v(0, [[1, D]], 64 * i, 16), dr(self.xs, sq * 16 * D, [[D, 16], [1, D]]), writes=[xt.res])

    def phase0(self, es):
        S = self.S
        with contextlib.ExitStack() as es2:
            xts = [self.sb(es2, "xt%d" % i, D) for i in range(2)]
            for ti in range(self.NT):
                xt = xts[ti % 2]
                self.x_tile_src(xt, ti)
                for half in range(2):
                    bank = (2 * ti + half) % 4
                    pr = self.psres[bank]
                    for q in range(4):
                        kc = half * 4 + q
                        S.op("tensor", lambda e, kc=kc, q=q, bank=bank: e.transpose(
                            self.ps(bank, q * 128, [[1, 128]]), xt.v(kc * 128, [[1, 128]]), self.cst.v(C_ID, [[1, 128]])),
                            reads=[xt.res, self.cst.res], writes=[pr], acc=(q > 0))
                    en = self.evac_eng()
                    self.copy(en, self.xT.v((half * 4) * self.NCOL + ti * 128, [[self.NCOL, 4], [1, 128]]),
                              self.ps(bank, 0, [[128, 4], [1, 128]]), reads=[pr], writes=[self.xT.res])
            S.barrier()

    def phaseA(self, es_pass):
        S = self.S
        NT, NCOL, kind = self.NT, self.NCOL, self.kind
        cst = self.cst
        with contextlib.ExitStack() as es:
            qT = self.sb(es, "qT", NCOL, BF16)
            kT = self.sb(es, "kT", NCOL, BF16)
            vaug = self.sb(es, "vaug", NT * 130, BF16)
            sg = self.sb(es, "sgA", NT * 128, BF16)
            tbf = self.sb(es, "tbf", 2 * 640)
            tb = self.sb(es, "tb", 2 * 640, BF16)
            PT = self.sb(es, "PT", 640, BF16)
            kvo = [self.sb(es, "kvo%d" % i, 256) for i in range(2)]
            rden = self.sb(es, "rdenA", 1)
            if kind == "S":
                ckf = self.sb(es, "ckf", 512)
                ckT = self.sb(es, "ckT", 512, BF16)
                cvf = self.sb(es, "cvf", 512)
                cvaug = self.sb(es, "cvaug", 4 * 130, BF16)
                tnf = self.sb(es, "tnf", 2 * 64)
                tn = self.sb(es, "tn", 2 * 64, BF16)
                S.op("vector", lambda e: e.memset(cvaug.v(), 1.0), writes=[cvaug.res])
            S.op("vector", lambda e: e.memset(vaug.v(), 1.0), writes=[vaug.res])
            kvo_i = 0
            for hp in range(8):
                slab = self.next_slab()
                for i, off in enumerate([OFF_QA, OFF_KA, OFF_VA, OFF_GA]):
                    self.load_slab(slab, self.w_in, off + hp * 128, 128, dst_off=i * 128)
                if kind == "P":
                    S.dma("sync", tbf.v(0, [[640, 2], [1, 640]]),
                          dr(self.biasP, (2 * hp) * 128 * 640, [[640, 128], [128 * 640, 2], [1, 640]]), writes=[tbf.res])
                    S.op("vector", lambda e: e.tensor_tensor(tb.v(0, [[640, 2], [1, 640]]), tbf.v(0, [[640, 2], [1, 640]]),
                                                             cst.v(C_MASKA, [[0, 2], [1, 640]]), ALU.add),
                         reads=[tbf.res, cst.res], writes=[tb.res])
                else:
                    S.dma("sync", tbf.v(0, [[640, 2], [1, 256]]),
                          dr(self.biasS, (2 * hp) * 128 * 256, [[256, 128], [128 * 256, 2], [1, 256]]), writes=[tbf.res])
                    S.op("vector", lambda e: e.tensor_copy(tb.v(0, [[640, 2], [1, 256]]), tbf.v(0, [[640, 2], [1, 256]])),
                         reads=[tbf.res], writes=[tb.res])
                    S.dma("sync", tnf.v(0, [[1, 64]]),
                          dr(self.biasN, (2 * hp) * 64 * 64, [[64, 128], [1, 64]]), writes=[tnf.res])
                    S.op("vector", lambda e: e.tensor_tensor(tn.v(0, [[1, 64]]), tnf.v(0, [[1, 64]]), cst.v(C_MASKN, [[1, 64]]), ALU.add),
                         reads=[tnf.res, cst.res], writes=[tn.res])
                for bi, (c0, n) in enumerate(self.blocks):
                    for j, dst in enumerate([qT, kT]):
                        bank = (2 * bi + j) % 4
                        pr = self.psres[bank]
                        for kc in range(8):
                            S.op("tensor", lambda e, kc=kc, j=j, bank=bank: e.matmul(
                                self.ps(bank, 0, [[1, n]]), slab.v(kc * 512 + j * 128, [[1, 128]]), self.xT_v(kc, c0, n),
                                start=(kc == 0), stop=(kc == 7)),
                                reads=[slab.res, self.xT.res], writes=[pr], acc=(kc > 0))
                        if j == 0:
                            self.copy("scalar", dst.v(c0, [[1, n]]), self.ps(bank, 0, [[1, n]]), [pr], [dst.res], scale=0.125)
                        else:
                            self.copy("vector", dst.v(c0, [[1, n]]), self.ps(bank, 0, [[1, n]]), [pr], [dst.res])
                a_stop = getattr(self, "a_stop", 99)
                if a_stop < 1:
                    continue
                for ti in range(NT):
                    out_tile = (kind == "S") or (ti >= NT - 4)
                    bank = 4 + (ti % 2)
                    pr = self.psres[bank]
                    ncol = 384 if out_tile else 256
                    for kc in range(8):
                        S.op("tensor", lambda e, kc=kc, bank=bank: e.matmul(
                            self.ps(bank, 0, [[1, 256]]), self.xT_v(kc, ti * 128, 128), slab.v(kc * 512 + 256, [[1, 256]]),
                            start=(kc == 0), stop=(kc == 7)),
                            reads=[slab.res, self.xT.res], writes=[pr], acc=(kc > 0))
                    if out_tile:
                        for kc in range(8):
                            S.op("tensor", lambda e, kc=kc, bank=bank: e.matmul(
                                self.ps(bank, 256, [[1, 128]]), self.xT_v(kc, ti * 128, 128), slab.v(kc * 512 + 128, [[1, 128]]),
                                start=(kc == 0), stop=(kc == 7)),
                                reads=[slab.res, self.xT.res], writes=[pr], acc=True)
                    S.op("vector", lambda e, bank=bank: e.tensor_copy(vaug.v(ti * 130, [[65, 2], [1, 64]]),
                                                                      self.ps(bank, 0, [[64, 2], [1, 64]])),
                         reads=[pr], writes=[vaug.res])
                    S.op("scalar", lambda e, bank=bank: e.activation(sg.v(ti * 128, [[1, 128]]), self.ps(bank, 128, [[1, 128]]), AF.Sigmoid),
                         reads=[pr], writes=[sg.res])
                    if out_tile:
                        ko = kvo[kvo_i % 2]
                        kvo_i += 1
                        S.op("vector", lambda e, bank=bank: e.tensor_copy(ko.v(0, [[1, 128]]), self.ps(bank, 256, [[1, 128]])),
                             reads=[pr], writes=[ko.res])
                        S.op("scalar", lambda e, bank=bank: e.activation(ko.v(128, [[1, 128]]), self.ps(bank, 0, [[1, 128]]), AF.Copy),
                             reads=[pr], writes=[ko.res])
                        if kind == "P":
                            r0 = self.b * 512 + (ti - (NT - 4)) * 128
                            S.dma("sync", dr(self.akp, r0 * D + hp * 128, [[D, 128], [1, 128]]), ko.v(0, [[1, 128]]), reads=[ko.res])
                            S.dma("sync", dr(self.avp, r0 * D + hp * 128, [[D, 128], [1, 128]]), ko.v(128, [[1, 128]]), reads=[ko.res])
                        else:
                            for i in range(2):
                                sq = 2 * ti + i
                                r0 = sq * LC + (LC - 16)
                                S.dma("sync", dr(self.aks, r0 * D + hp * 128, [[D, 16], [1, 128]]),
                                      ko.v(0, [[1, 128]], 64 * i, 16), reads=[ko.res])
                                S.dma("sync", dr(self.avs, r0 * D + hp * 128, [[D, 16], [1, 128]]),
                                      ko.v(128, [[1, 128]], 64 * i, 16), reads=[ko.res])
                if a_stop < 2:
                    continue
                for ti in range(NT):
                    mr = self.mres[ti]
                    if kind == "P":
                        jlist = [j for j in range(5) if ti * 128 - 512 + 128 * j >= 0]
                        for h2 in range(2):
                            pb = 64 * h2
                            prs = [self.psres[0], self.psres[1]]
                            first = True
                            for j in jlist:
                                kc0 = ti * 128 - 512 + 128 * j
                                bank = 0 if j < 4 else 1
                                S.op("tensor", lambda e, j=j, kc0=kc0, bank=bank, pb=pb: e.matmul(
                                    self.ps(bank, (j % 4) * 128, [[1, 128]]), kT.v(kc0, [[1, 128]], pb, 64),
                                    qT.v(ti * 128, [[1, 128]], pb, 64), start=True, stop=False),
                                    reads=[kT.res, qT.res], writes=prs, acc=(not first))
                                first = False
                                S.op("tensor", lambda e, j=j, bank=bank, h2=h2: e.matmul(
                                    self.ps(bank, (j % 4) * 128, [[1, 128]]), self.idb.v(),
                                    tb.v(h2 * 640 + j * 128, [[1, 128]]), start=False, stop=True),
                                    reads=[self.idb.res, tb.res], writes=prs, acc=True)
                            j0 = jlist[0]
                            nj = len(jlist)
                            S.op("scalar", lambda e, j0=j0, nj=nj: e.activation(
                                PT.v(j0 * 128, [[1, nj * 128]]), self.ps(0, j0 * 128, [[1, nj * 128]]), AF.Exp),
                                reads=prs, writes=[PT.res])
                            po = self.psres[2 + h2]
                            for idx, j in enumerate(jlist):
                                kt = ti - 4 + j
                                S.op("tensor", lambda e, j=j, kt=kt, h2=h2, idx=idx, nj=nj: e.matmul(
                                    self.ps(2 + h2, 0, [[1, 65]]), PT.v(j * 128, [[1, 128]]),
                                    vaug.v(kt * 130 + h2 * 65, [[1, 65]]), start=(idx == 0), stop=(idx == nj - 1)),
                                    reads=[PT.res, vaug.res], writes=[po], acc=(idx > 0))
                            S.op("vector", lambda e, h2=h2: e.reciprocal(rden.v(), self.ps(2 + h2, 64, [[1, 1]])),
                                 reads=[po], writes=[rden.res])
                            S.op("vector", lambda e, h2=h2: e.scalar_tensor_tensor(
                                self.merged.v(ti * D + (2 * hp + h2) * 64, [[1, 64]]), self.ps(2 + h2, 0, [[1, 64]]),
                                rden.v(), sg.v(ti * 128 + h2 * 64, [[1, 64]]), ALU.mult, ALU.mult),
                                reads=[po, rden.res, sg.res], writes=[mr])
                    else:
                        for i in range(2):
                            sq = 2 * ti + i
                            c0 = 64 * i
                            S.dma("sync", ckf.v(0, [[128, 4], [1, 128]]),
                                  dr(self.cak, sq * LC * D + hp * 128, [[D, 128], [128 * D, 4], [1, 128]]), writes=[ckf.res])
                            S.dma("sync", cvf.v(0, [[128, 4], [1, 128]]),
                                  dr(self.cav, sq * LC * D + hp * 128, [[D, 128], [128 * D, 4], [1, 128]]), writes=[cvf.res])
                            prt = self.psres[4]
                            for j in range(4):
                                S.op("tensor", lambda e, j=j: e.transpose(self.ps(4, j * 128, [[1, 128]]), ckf.v(j * 128, [[1, 128]]),
                                                                          cst.v(C_ID, [[1, 128]])),
                                     reads=[ckf.res, cst.res], writes=[prt], acc=(j > 0))
                            S.op("vector", lambda e: e.tensor_copy(ckT.v(), self.ps(4, 0, [[1, 512]])), reads=[prt], writes=[ckT.res])
                            S.op("vector", lambda e: e.tensor_copy(cvaug.v(0, [[130, 4], [65, 2], [1, 64]]),
                                                                   cvf.v(0, [[128, 4], [64, 2], [1, 64]])),
                                 reads=[cvf.res], writes=[cvaug.res])
                            for h2 in range(2):
                                pb = 64 * h2
                                prs = [self.psres[0], self.psres[1]]
                                for j in range(4):
                                    S.op("tensor", lambda e, j=j, pb=pb: e.matmul(
                                        self.ps(0, j * 64, [[1, 64]]), ckT.v(j * 128, [[1, 128]], pb, 64),
                                        qT.v(ti * 128 + c0, [[1, 64]], pb, 64), start=True, stop=False),
                                        reads=[ckT.res, qT.res], writes=prs, acc=(j > 0))
                                    S.op("tensor", lambda e, j=j, h2=h2: e.matmul(
                                        self.ps(0, j * 64, [[1, 64]]), self.idb.v(),
                                        tb.v(h2 * 640 + j * 64, [[1, 64]]), start=False, stop=True),
                                        reads=[self.idb.res, tb.res], writes=prs, acc=True)
                                S.op("tensor", lambda e, pb=pb: e.matmul(
                                    self.ps(1, 0, [[1, 64]], c0, 64), kT.v(ti * 128 + c0, [[1, 64]], pb, 64),
                                    qT.v(ti * 128 + c0, [[1, 64]], pb, 64), start=True, stop=False),
                                    reads=[kT.res, qT.res], writes=prs, acc=True)
                                S.op("tensor", lambda e, pb=pb: e.matmul(
                                    self.ps(1, 0, [[1, 64]], c0, 64), self.idb.v(pb, [[1, 64]], pb, 64),
                                    tn.v(0, [[1, 64]], pb, 64), start=False, stop=True),
                                    reads=[self.idb.res, tn.res], writes=prs, acc=True)
                                S.op("scalar", lambda e: e.activation(PT.v(0, [[1, 256]]), self.ps(0, 0, [[1, 256]]), AF.Exp),
                                     reads=prs, writes=[PT.res])
                                S.op("scalar", lambda e: e.activation(PT.v(256, [[1, 64]], c0, 64), self.ps(1, 0, [[1, 64]], c0, 64), AF.Exp),
                                     reads=prs, writes=[PT.res])
                                po = self.psres[2 + h2]
                                for j in range(4):
                                    S.op("tensor", lambda e, j=j, h2=h2: e.matmul(
                                        self.ps(2 + h2, 0, [[1, 65]], c0, 64), PT.v(j * 64, [[1, 64]]),
                                        cvaug.v(j * 130 + h2 * 65, [[1, 65]]), start=(j == 0), stop=False),
                                        reads=[PT.res, cvaug.res], writes=[po], acc=(j > 0))
                                S.op("tensor", lambda e, h2=h2: e.matmul(
                                    self.ps(2 + h2, 0, [[1, 65]], c0, 64), PT.v(256, [[1, 64]], c0, 64),
                                    vaug.v(ti * 130 + h2 * 65, [[1, 65]], c0, 64), start=False, stop=True),
                                    reads=[PT.res, vaug.res], writes=[po], acc=True)
                                S.op("vector", lambda e, h2=h2: e.reciprocal(rden.v(0, [[1, 1]], c0, 64), self.ps(2 + h2, 64, [[1, 1]], c0, 64)),
                                     reads=[po], writes=[rden.res])
                                S.op("vector", lambda e, h2=h2: e.scalar_tensor_tensor(
                                    self.merged.v(ti * D + (2 * hp + h2) * 64, [[1, 64]], c0, 64), self.ps(2 + h2, 0, [[1, 64]], c0, 64),
                                    rden.v(0, [[1, 1]], c0, 64), sg.v(ti * 128 + h2 * 64, [[1, 64]], c0, 64), ALU.mult, ALU.mult),
                                    reads=[po, rden.res, sg.res], writes=[mr])
            if kind == "S" and getattr(self, "a_stop", 99) >= 3:
                for sq in range(self.NBS):
                    for src, dst in ((self.cak, self.aks), (self.cav, self.avs)):
                        rr = Res("cpy")
                        for part in range(4):
                            S.dma("sync", dr(dst, (sq * LC + part * 124) * D, [[D, 124], [1, D]]),
                                  dr(src, (sq * LC + 16 + part * 124) * D, [[D, 124], [1, D]]), writes=[rr])
            S.barrier()

    def phaseC(self, es_pass):
        S = self.S
        NT, NCOL, kind = self.NT, self.NCOL, self.kind
        cst = self.cst
        with contextlib.ExitStack() as es:
            mkT = self.sb(es, "mkT", 4 * 2 * 256, BF16)
            mvaug = self.sb(es, "mvaug", 2 * 4 * 257, BF16)
            qcT = self.sb(es, "qcT", 2 * NCOL, BF16)
            sgc = self.sb(es, "sgc", NT * 256, BF16)
            PT = self.sb(es, "PTc", 256, BF16)
            rden = self.sb(es, "rdenC", 1)
            otmp = self.sb(es, "otmpC", 256, BF16)
            S.op("vector", lambda e: e.memset(mvaug.v(), 1.0), writes=[mvaug.res])
            if kind == "P":
                with contextlib.ExitStack() as es2:
                    memf = [self.sb(es2, "memf%d" % i, D) for i in range(2)]
                    memT = self.sb(es2, "memT", 8 * 256, BF16)
                    kvst = [self.sb(es2, "kvst%d" % i, 512) for i in range(2)]
                    for mb in range(2):
                        S.dma("sync", memf[mb].v(), dr(self.memp, (self.b * 256 + mb * 128) * D, [[D, 128], [1, D]]), writes=[memf[mb].res])
                        for half in range(2):
                            bank = 2 * mb + half
                            pr = self.psres[bank]
                            for q in range(4):
                                kc = half * 4 + q
                                S.op("tensor", lambda e, kc=kc, q=q, bank=bank, mb=mb: e.transpose(
                                    self.ps(bank, q * 128, [[1, 128]]), memf[mb].v(kc * 128, [[1, 128]]), cst.v(C_ID, [[1, 128]])),
                                    reads=[memf[mb].res, cst.res], writes=[pr], acc=(q > 0))
                            self.copy(self.evac_eng(), memT.v((half * 4) * 256 + mb * 128, [[256, 4], [1, 128]]),
                                      self.ps(bank, 0, [[128, 4], [1, 128]]), [pr], [memT.res])
                    si = 0
                    for s in range(4):
                        slab = self.next_slab()
                        self.load_slab(slab, self.wkv, s * 512, 512)
                        isv = s // 2
                        h0 = (s % 2) * 2
                        for mb in range(2):
                            bank = 4 + (si % 2)
                            pr = self.psres[bank]
                            for kc in range(8):
                                S.op("tensor", lambda e, kc=kc, bank=bank, mb=mb: e.matmul(
                                    self.ps(bank, 0, [[1, 512]]), memT.v(kc * 256 + mb * 128, [[1, 128]]), slab.v(kc * 512, [[1, 512]]),
                                    start=(kc == 0), stop=(kc == 7)),
                                    reads=[memT.res, slab.res], writes=[pr], acc=(kc > 0))
                            st = kvst[si % 2]
                            si += 1
                            self.copy(self.evac_eng(), st.v(), self.ps(bank, 0, [[1, 512]]), [pr], [st.res])
                            dst = self.mvp if isv else self.mkp
                            S.dma("sync", dr(dst, (self.b * 256 + mb * 128) * D + s % 2 * 512, [[D, 128], [1, 512]]), st.v(), reads=[st.res])
                            if isv:
                                S.op("vector", lambda e, bank=bank, mb=mb, h0=h0: e.tensor_copy(
                                    mvaug.v(mb * 4 * 257 + h0 * 257, [[257, 2], [1, 256]]), self.ps(bank, 0, [[256, 2], [1, 256]])),
                                    reads=[pr], writes=[mvaug.res])
                        if not isv:
                            for cb in range(4):
                                bank = 6 + (cb % 2)
                                pr = self.psres[bank]
                                for kc in range(8):
                                    S.op("tensor", lambda e, kc=kc, bank=bank, cb=cb: e.matmul(
                                        self.ps(bank, 0, [[1, 256]]), slab.v(kc * 512 + cb * 128, [[1, 128]]), memT.v(kc * 256, [[1, 256]]),
                                        start=(kc == 0), stop=(kc == 7)),
                                        reads=[memT.res, slab.res], writes=[pr], acc=(kc > 0))
                                h = h0 + cb // 2
                                dc = cb % 2
                                self.copy(self.evac_eng(), mkT.v((h * 2 + dc) * 256, [[1, 256]]), self.ps(bank, 0, [[1, 256]]), [pr], [mkT.res])
                    S.barrier()
            with contextlib.ExitStack() as es2:
                if kind == "S":
                    cmf = self.sb(es2, "cmf", 2 * D)
                    mkTs = [self.sb(es2, "mkTs%d" % i, 4 * 2 * 256, BF16) for i in range(2)]
                    mvs = [self.sb(es2, "mvs%d" % i, 2 * 4 * 257, BF16) for i in range(2)]
                    for i in range(2):
                        S.op("vector", lambda e, i=i: e.memset(mvs[i].v(), 1.0), writes=[mvs[i].res])

                def load_seq_cache(sq, mk_, mv_):
                    S.dma("sync", cmf.v(0, [[D, 2], [1, D]]),
                          dr(self.cmk, sq * 256 * D, [[D, 128], [128 * D, 2], [1, D]]), writes=[cmf.res])
                    for mb in range(2):
                        for half in range(2):
                            bank = 6 + half
                            prt = self.psres[bank]
                            for q in range(4):
                                cb = half * 4 + q
                                S.op("tensor", lambda e, cb=cb, q=q, bank=bank, mb=mb: e.transpose(
                                    self.ps(bank, q * 128, [[1, 128]]), cmf.v(mb * D + cb * 128, [[1, 128]]), cst.v(C_ID, [[1, 128]])),
                                    reads=[cmf.res, cst.res], writes=[prt], acc=(q > 0))
                            self.copy(self.evac_eng(), mk_.v((half * 4) * 256 + mb * 128, [[256, 4], [1, 128]]),
                                      self.ps(bank, 0, [[128, 4], [1, 128]]), [prt], [mk_.res])
                    S.dma("sync", cmf.v(0, [[D, 2], [1, D]]),
                          dr(self.cmv, sq * 256 * D, [[D, 128], [128 * D, 2], [1, D]]), writes=[cmf.res])
                    S.op("vector", lambda e: e.tensor_copy(mv_.v(0, [[4 * 257, 2], [257, 4], [1, 256]]),
                                                           cmf.v(0, [[D, 2], [256, 4], [1, 256]])),
                         reads=[cmf.res], writes=[mv_.res])

                def headC(hc, tiles):
                    slab = self.next_slab()
                    self.load_slab(slab, self.w_in, OFF_QC + hc * 256, 256, dst_off=0)
                    self.load_slab(slab, self.w_in, OFF_GC + hc * 256, 256, dst_off=256)
                    for bi, (c0, n) in enumerate(self.blocks):
                        for dc in range(2):
                            bank = (2 * bi + dc) % 4
                            pr = self.psres[bank]
                            for kc in range(8):
                                S.op("tensor", lambda e, kc=kc, dc=dc, bank=bank, c0=c0, n=n: e.matmul(
                                    self.ps(bank, 0, [[1, n]]), slab.v(kc * 512 + dc * 128, [[1, 128]]), self.xT_v(kc, c0, n),
                                    start=(kc == 0), stop=(kc == 7)),
                                    reads=[slab.res, self.xT.res], writes=[pr], acc=(kc > 0))
                            self.copy(self.evac_eng(), qcT.v(dc * NCOL + c0, [[1, n]]), self.ps(bank, 0, [[1, n]]), [pr], [qcT.res], scale=0.0625)
                    for ti in tiles:
                        bank = 4 + (ti % 2)
                        pr = self.psres[bank]
                        for kc in range(8):
                            S.op("tensor", lambda e, kc=kc, bank=bank, ti=ti: e.matmul(
                                self.ps(bank, 0, [[1, 256]]), self.xT_v(kc, ti * 128, 128), slab.v(kc * 512 + 256, [[1, 256]]),
                                start=(kc == 0), stop=(kc == 7)),
                                reads=[slab.res, self.xT.res], writes=[pr], acc=(kc > 0))
                        S.op("scalar", lambda e, bank=bank, ti=ti: e.activation(sgc.v(ti * 256, [[1, 256]]), self.ps(bank, 0, [[1, 256]]), AF.Sigmoid),
                             reads=[pr], writes=[sgc.res])
                    for ti in tiles:
                        mr = self.mres[ti]
                        segs = [(0, 128, mkT, mvaug)] if kind == "P" else [(0, 64, mkTs[0], mvs[0]), (64, 64, mkTs[1], mvs[1])]
                        for (c0, nq, mk_, mv_) in segs:
                            prs = self.psres[0]
                            for mb in range(2):
                                for dc in range(2):
                                    S.op("tensor", lambda e, mb=mb, dc=dc, mk_=mk_, c0=c0, nq=nq, ti=ti: e.matmul(
                                        self.ps(0, mb * nq, [[1, nq]]), mk_.v((hc * 2 + dc) * 256 + mb * 128, [[1, 128]]),
                                        qcT.v(dc * NCOL + ti * 128 + c0, [[1, nq]]), start=(dc == 0), stop=(dc == 1)),
                                        reads=[mk_.res, qcT.res], writes=[prs], acc=not (mb == 0 and dc == 0))
                            S.op("scalar", lambda e, nq=nq: e.activation(PT.v(0, [[1, 2 * nq]]), self.ps(0, 0, [[1, 2 * nq]]), AF.Exp),
                                 reads=[prs], writes=[PT.res])
                            po = self.psres[1]
                            for mb in range(2):
                                S.op("tensor", lambda e, mb=mb, mv_=mv_, c0=c0, nq=nq: e.matmul(
                                    self.ps(1, 0, [[1, 257]], c0, nq), PT.v(mb * nq, [[1, nq]]),
                                    mv_.v(mb * 4 * 257 + hc * 257, [[1, 257]]), start=(mb == 0), stop=(mb == 1)),
                                    reads=[PT.res, mv_.res], writes=[po], acc=(mb > 0))
                            S.op("vector", lambda e, c0=c0, nq=nq: e.reciprocal(rden.v(0, [[1, 1]], c0, nq), self.ps(1, 256, [[1, 1]], c0, nq)),
                                 reads=[po], writes=[rden.res])
                            S.op("vector", lambda e, c0=c0, nq=nq, ti=ti: e.scalar_tensor_tensor(
                                otmp.v(0, [[1, 256]], c0, nq), self.ps(1, 0, [[1, 256]], c0, nq),
                                rden.v(0, [[1, 1]], c0, nq), sgc.v(ti * 256, [[1, 256]], c0, nq), ALU.mult, ALU.mult),
                                reads=[po, rden.res, sgc.res], writes=[otmp.res])
                            S.op("vector", lambda e, c0=c0, nq=nq, ti=ti: e.tensor_tensor(
                                self.merged.v(ti * D + hc * 256, [[1, 256]], c0, nq), self.merged.v(ti * D + hc * 256, [[1, 256]], c0, nq),
                                otmp.v(0, [[1, 256]], c0, nq), ALU.add),
                                reads=[otmp.res, mr], writes=[mr])

                if kind == "P":
                    for hc in range(4):
                        headC(hc, list(range(NT)))
                else:
                    for ti in range(NT):
                        for i in range(2):
                            load_seq_cache(2 * ti + i, mkTs[i], mvs[i])
                        for hc in range(4):
                            headC(hc, [ti])
                S.barrier()

    def phaseB(self, es_pass):
        S = self.S
        NT, NCOL, kind = self.NT, self.NCOL, self.kind
        cst = self.cst
        CI = lambda c: cst.v(c, [[1, 128]])
        with contextlib.ExitStack() as es:
            gt = self.sb(es, "gt", NT * 8)
            bt = self.sb(es, "bt", NT * 16)
            Gc = self.sb(es, "Gc", NT * 8)
            GD = self.sb(es, "GD", NT * 24)
            gend = self.sb(es, "gend", NT * 16)
            with contextlib.ExitStack() as es2:
                wab = self.sb(es2, "wab", 8 * 512, BF16)
                t1 = self.sb(es2, "t1", NT * 8)
                t2 = self.sb(es2, "t2", NT * 8)
                self.load_slab(wab, self.w_in, OFF_AB, 16)
                pr = self.psres[0]
                for ti in range(NT):
                    for kc in range(8):
                        S.op("tensor", lambda e, kc=kc, ti=ti: e.matmul(
                            self.ps(0, ti * 16, [[1, 16]]), self.xT_v(kc, ti * 128, 128), wab.v(kc * 512, [[1, 16]]),
                            start=(kc == 0), stop=(kc == 7)),
                            reads=[wab.res, self.xT.res], writes=[pr], acc=not (ti == 0 and kc == 0))
                S.op("vector", lambda e: e.tensor_tensor(t1.v(0, [[8, NT], [1, 8]]), self.ps(0, 0, [[16, NT], [1, 8]]),
                                                         self.dtb_bc.v(0, [[0, NT], [1, 8]]), ALU.add),
                     reads=[pr, self.dtb_bc.res], writes=[t1.res])
                S.op("scalar", lambda e: e.activation(t2.v(), t1.v(), AF.Abs), reads=[t1.res], writes=[t2.res])
                S.op("scalar", lambda e: e.activation(t2.v(), t2.v(), AF.Exp, scale=-1.0), reads=[t2.res], writes=[t2.res])
                S.op("scalar", lambda e: e.activation(t2.v(), t2.v(), AF.Ln, bias=1.0), reads=[t2.res], writes=[t2.res])
                S.op("vector", lambda e: e.scalar_tensor_tensor(t1.v(), t1.v(), 0.0, t2.v(), ALU.max, ALU.add),
                     reads=[t1.res, t2.res], writes=[t1.res])
                S.op("vector", lambda e: e.tensor_tensor(gt.v(0, [[8, NT], [1, 8]]), t1.v(0, [[8, NT], [1, 8]]),
                                                         self.nealog.v(0, [[0, NT], [1, 8]]), ALU.mult),
                     reads=[t1.res, self.nealog.res], writes=[gt.res])
                S.op("scalar", lambda e: e.activation(bt.v(0, [[16, NT], [2, 8]]), self.ps(0, 8, [[16, NT], [1, 8]]), AF.Sigmoid),
                     reads=[pr], writes=[bt.res])
                if kind == "S":
                    S.op("vector", lambda e: e.tensor_scalar(gt.v(), gt.v(), cst.v(C_ROWM, [[1, 1]]), None, ALU.mult),
                         reads=[gt.res, cst.res], writes=[gt.res])
                    S.op("vector", lambda e: e.tensor_scalar(bt.v(0, [[16, NT], [2, 8]]), bt.v(0, [[16, NT], [2, 8]]),
                                                             cst.v(C_ROWM, [[1, 1]]), None, ALU.mult),
                         reads=[bt.res, cst.res], writes=[bt.res])
                S.op("vector", lambda e: e.tensor_scalar(bt.v(1, [[16, NT], [2, 8]]), bt.v(0, [[16, NT], [2, 8]]), -1.0, None, ALU.mult),
                     reads=[bt.res], writes=[bt.res])
                pr1 = self.psres[1]
                for ti in range(NT):
                    for q, cc in enumerate([C_TRI, C_ONESBD, C_SEL0, C_SEL1]):
                        S.op("tensor", lambda e, ti=ti, q=q, cc=cc: e.matmul(
                            self.ps(1, ti * 32 + q * 8, [[1, 8]]), CI(cc), gt.v(ti * 8, [[1, 8]]), start=True, stop=True),
                            reads=[cst.res, gt.res], writes=[pr1], acc=not (ti == 0 and q == 0))
                S.op("vector", lambda e: e.tensor_copy(Gc.v(0, [[8, NT], [1, 8]]), self.ps(1, 0, [[32, NT], [1, 8]])),
                     reads=[pr1], writes=[Gc.res])
                S.op("vector", lambda e: e.memset(GD.v(), 1.0), writes=[GD.res])
                S.op("scalar", lambda e: e.activation(GD.v(1, [[24, NT], [3, 8]]), self.ps(1, 0, [[32, NT], [1, 8]]), AF.Exp),
                     reads=[pr1], writes=[GD.res])
                S.op("vector", lambda e: e.tensor_tensor(t1.v(0, [[8, NT], [1, 8]]), self.ps(1, 8, [[32, NT], [1, 8]]),
                                                         Gc.v(0, [[8, NT], [1, 8]]), ALU.subtract),
                     reads=[pr1, Gc.res], writes=[t1.res])
                S.op("scalar", lambda e: e.activation(GD.v(2, [[24, NT], [3, 8]]), t1.v(0, [[8, NT], [1, 8]]), AF.Exp),
                     reads=[t1.res], writes=[GD.res])
                S.op("scalar", lambda e: e.activation(gend.v(0, [[16, NT], [1, 16]]), self.ps(1, 16, [[32, NT], [1, 16]]), AF.Exp),
                     reads=[pr1], writes=[gend.res])
                S.barrier()
            G = self.GB if kind == "P" else 2
            ZW = NCOL + 3
            Dw = [self.sb(es, "Dw%d" % i, 12 * 128, BF16) for i in range(2)]
            bco = self.sb(es, "bco", 384)
            if kind == "S":
                stc = self.sb(es, "stc", 3072)
            HB = []
            for k in range(G):
                d = {}
                d["zT"] = self.sb(es, "zT%d" % k, 3 * ZW, BF16)
                d["sgn"] = self.sb(es, "sgn%d" % k, NT * 128, BF16)
                d["Sf"] = [self.sb(es, "Sf%d_%d" % (k, i), 128) for i in range(2 if kind == "S" else 1)]
                d["Sb"] = [self.sb(es, "Sb%d_%d" % (k, i), 128, BF16) for i in range(2 if kind == "S" else 1)]
                for nm, F, dt in [("ss", 2, F32), ("rn", 2, F32), ("cols", 4, F32), ("junk", 128, F32), ("tok5", 640, BF16),
                                  ("Dg", 128, BF16), ("feat3", 384, BF16), ("gbc", 128, F32), ("Dm", 128, F32), ("Lam", 128, F32),
                                  ("Nn", 128, F32), ("NTt", 128, F32), ("Xb", 128, BF16), ("aqk", 128, BF16), ("nwT", 128, BF16),
                                  ("wv", 128, BF16), ("sso", 2, F32), ("otmp", 128, BF16)]:
                    d[nm] = self.sb(es, "%s%d" % (nm, k), F, dt)
                d["PP"] = [self.sb(es, "PP%d_%d" % (k, i), 256) for i in range(2)]
                d["X"] = [self.sb(es, "X%d_%d" % (k, i), 128) for i in range(2)]
                d["bx"], d["by"] = 2 * k, 2 * k + 1
                d.update(gt=gt, bt=bt, Gc=Gc, GD=GD, gend=gend)
                HB.append(d)
            dwi = 0
            groups = [list(range(g0, min(8, g0 + G))) for g0 in range(0, 8, G)]
            for grp in groups:
                for k, hb in enumerate(grp):
                    d = HB[k]
                    zT, sgn, Sf, Sb = d["zT"], d["sgn"], d["Sf"], d["Sb"]
                    dw = Dw[dwi % 2]
                    dwi += 1
                    slab = self.next_slab()
                    for i, off in enumerate([OFF_QB, OFF_KB, OFF_VB, OFF_GB]):
                        self.load_slab(slab, self.w_in, off + hb * 128, 128, dst_off=i * 128)
                    for j in range(3):
                        for tap in range(4):
                            S.op("vector", lambda e, j=j, tap=tap, dw=dw, hb=hb: e.tensor_scalar(
                                dw.v((j * 4 + tap) * 128, [[1, 128]]), CI(C_ID), self.wc.v((j * 8 + hb) * 4 + tap, [[1, 1]]), None, ALU.mult),
                                reads=[cst.res, self.wc.res], writes=[dw.res])
                    if kind == "P":
                        S.op("vector", lambda e, zT=zT: e.memset(zT.v(0, [[ZW, 3], [1, 3]]), 0.0), writes=[zT.res])
                    for bi, (c0, n) in enumerate(self.blocks):
                        for j in range(3):
                            bank = (bi * 3 + j) % 2 * 2
                            pr = self.psres[bank]
                            for kc in range(8):
                                S.op("tensor", lambda e, kc=kc, j=j, bank=bank, c0=c0, n=n, slab=slab: e.matmul(
                                    self.ps(bank, 0, [[1, n]]), slab.v(kc * 512 + j * 128, [[1, 128]]), self.xT_v(kc, c0, n),
                                    start=(kc == 0), stop=(kc == 7)),
                                    reads=[slab.res, self.xT.res], writes=[pr], acc=(kc > 0))
                            self.copy(self.evac_eng(), zT.v(j * ZW + 3 + c0, [[1, n]]), self.ps(bank, 0, [[1, n]]), [pr], [zT.res])
                    if kind == "S":
                        for ti in range(NT):
                            S.dma("sync", stc.v(0, [[1, 3072]], 0, 6),
                                  dr(self.sconv, (2 * ti) * 3 * 3072, [[3072, 6], [1, 3072]]), writes=[stc.res])
                            pr = self.psres[0]
                            for j in range(3):
                                S.op("tensor", lambda e, j=j, hb=hb: e.matmul(
                                    self.ps(0, j * 6, [[1, 6]]), stc.v(j * 1024 + hb * 128, [[1, 128]], 0, 6),
                                    cst.v(C_ID, [[1, 6]], 0, 6), start=True, stop=True),
                                    reads=[stc.res, cst.res], writes=[pr], acc=(j > 0))
                            for i in range(2):
                                S.op("vector", lambda e, i=i, ti=ti, zT=zT: e.tensor_copy(
                                    zT.v(ti * 128 + 64 * i, [[ZW, 3], [1, 3]]), self.ps(0, 3 * i, [[6, 3], [1, 3]])),
                                    reads=[pr], writes=[zT.res])
                    segs = [(NCOL - 3, self.bcp, self.b)] if kind == "P" else \
                        [(ti * 128 + 64 * i + 13, self.bcs, 2 * ti + i) for ti in range(NT) for i in range(2)]
                    for (cc, dst, sq) in segs:
                        pr = self.psres[3]
                        for kc in range(8):
                            S.op("tensor", lambda e, kc=kc, cc=cc, slab=slab: e.matmul(
                                self.ps(3, 0, [[1, 384]], 0, 3), self.xT_v(kc, cc, 3), slab.v(kc * 512, [[1, 384]]),
                                start=(kc == 0), stop=(kc == 7)),
                                reads=[slab.res, self.xT.res], writes=[pr], acc=(kc > 0))
                        S.op("vector", lambda e: e.tensor_copy(bco.v(0, [[1, 384]], 0, 3), self.ps(3, 0, [[1, 384]], 0, 3)),
                             reads=[pr], writes=[bco.res])
                        S.dma("sync", dr(dst, sq * 3 * 3072 + hb * 128, [[3072, 3], [1024, 3], [1, 128]]),
                              bco.v(0, [[128, 3], [1, 128]], 0, 3), reads=[bco.res])
                    for bi, (c0, n) in enumerate(self.blocks):
                        for j in range(3):
                            bank = (bi * 3 + j) % 2 * 2
                            pr = self.psres[bank]
                            for tap in range(4):
                                S.op("tensor", lambda e, j=j, tap=tap, bank=bank, c0=c0, n=n, dw=dw, zT=zT: e.matmul(
                                    self.ps(bank, 0, [[1, n]]), dw.v((j * 4 + tap) * 128, [[1, 128]]), zT.v(j * ZW + c0 + tap, [[1, n]]),
                                    start=(tap == 0), stop=(tap == 3)),
                                    reads=[dw.res, zT.res], writes=[pr], acc=(tap > 0))
                            S.op("scalar", lambda e, j=j, bank=bank, c0=c0, n=n, zT=zT: e.activation(
                                zT.v(j * ZW + c0, [[1, n]]), self.ps(bank, 0, [[1, n]]), AF.Silu),
                                reads=[pr], writes=[zT.res])
                    for ti in range(NT):
                        bank = (ti % 2) * 2
                        pr = self.psres[bank]
                        for kc in range(8):
                            S.op("tensor", lambda e, kc=kc, bank=bank, ti=ti, slab=slab: e.matmul(
                                self.ps(bank, 0, [[1, 128]]), self.xT_v(kc, ti * 128, 128), slab.v(kc * 512 + 384, [[1, 128]]),
                                start=(kc == 0), stop=(kc == 7)),
                                reads=[slab.res, self.xT.res], writes=[pr], acc=(kc > 0))
                        S.op("scalar", lambda e, bank=bank, ti=ti, sgn=sgn: e.activation(sgn.v(ti * 128, [[1, 128]]), self.ps(bank, 0, [[1, 128]]), AF.Sigmoid),
                             reads=[pr], writes=[sgn.res])
                    S.op("vector", lambda e, sgn=sgn: e.tensor_tensor(sgn.v(0, [[128, NT], [1, 128]]), sgn.v(0, [[128, NT], [1, 128]]),
                                                                      self.normg_bc.v(0, [[0, NT], [1, 128]]), ALU.mult),
                         reads=[sgn.res, self.normg_bc.res], writes=[sgn.res])
                    if kind == "P":
                        S.op("vector", lambda e, Sf=Sf: e.memset(Sf[0].v(), 0.0), writes=[Sf[0].res])
                        S.op("vector", lambda e, Sb=Sb: e.memset(Sb[0].v(), 0.0), writes=[Sb[0].res])
                for ti in range(NT):
                    gens = [self.gdn_tile(hb, ti, HB[k]) for k, hb in enumerate(grp)]
                    while gens:
                        nxt = []
                        for g in gens:
                            try:
                                next(g)
                                nxt.append(g)
                            except StopIteration:
                                pass
                        gens = nxt
                if kind == "P":
                    for k, hb in enumerate(grp):
                        Sf = HB[k]["Sf"]
                        S.dma("sync", dr(self.bsp, ((self.b * 8 + hb) * 128) * 128, [[128, 128], [1, 128]]), Sf[0].v(), reads=[Sf[0].res])
            S.barrier()

    def gdn_tile(self, hb, ti, L):
        S = self.S
        cst = self.cst
        NT, NCOL, kind = self.NT, self.NCOL, self.kind
        ZW = NCOL + 3
        CI = lambda c: cst.v(c, [[1, 128]])
        (zT, Sf, Sb, ss, rn, cols, junk, tok5, Dg, feat3, gbc, Dm, Lam, Nn, NTt, PP, X, Xb, aqk, nwT, wv, sso, otmp,
         gt, bt, Gc, GD, gend, sgn) = [L[k] for k in
                                       "zT Sf Sb ss rn cols junk tok5 Dg feat3 gbc Dm Lam Nn NTt PP X Xb aqk nwT wv sso otmp gt bt Gc GD gend sgn".split()]
        P = self.psres
        ps = self.ps
        bx, by = L["bx"], L["by"]
        cb = by
        for j in range(3):
            S.op("tensor", lambda e, j=j: e.matmul(ps(bx, j * 128, [[1, 128]]), zT.v(j * ZW + ti * 128, [[1, 128]]), self.idb.v(),
                                                   start=True, stop=True),
                 reads=[zT.res, self.idb.res], writes=[P[bx]], acc=(j > 0))
        yield
        for j in range(2):
            S.op("scalar", lambda e, j=j: e.activation(junk.v(), ps(bx, j * 128, [[1, 128]]), AF.Square, accum_out=ss.v(j, [[1, 1]])),
                 reads=[P[bx]], writes=[junk.res, ss.res])
        S.op("vector", lambda e: e.tensor_scalar(gbc.v(), CI(C_ONES), gt.v(ti * 8 + hb, [[1, 1]]), None, ALU.mult),
             reads=[cst.res, gt.res], writes=[gbc.res])
        S.op("vector", lambda e: e.tensor_scalar(Dg.v(), self.idb.v(), GD.v((ti * 8 + hb) * 3 + 1, [[1, 1]]), None, ALU.mult),
             reads=[self.idb.res, GD.res], writes=[Dg.res])
        S.op("tensor", lambda e: e.matmul(ps(by, 0, [[1, 128]]), gbc.v(), CI(C_TRI), start=True, stop=True),
             reads=[gbc.res, cst.res], writes=[P[by]])
        yield
        S.op("vector", lambda e: e.tensor_scalar(rn.v(), ss.v(), L2_EPS, None, ALU.add), reads=[ss.res], writes=[rn.res])
        S.op("scalar", lambda e: e.activation(rn.v(), rn.v(), AF.Sqrt), reads=[rn.res], writes=[rn.res])
        yield
        S.op("vector", lambda e: e.reciprocal(rn.v(), rn.v()), reads=[rn.res], writes=[rn.res])
        S.op("vector", lambda e: e.tensor_scalar(cols.v(0, [[1, 1]]), rn.v(0, [[1, 1]]), 128.0 ** -0.5, None, ALU.mult),
             reads=[rn.res], writes=[cols.res])
        S.op("vector", lambda e: e.tensor_scalar(cols.v(1, [[1, 3]]), GD.v((ti * 8 + hb) * 3, [[1, 3]]), rn.v(1, [[1, 1]]), None, ALU.mult),
             reads=[rn.res, GD.res], writes=[cols.res])
        srcs = [(0, 0), (1, 1), (2, None), (1, 2), (1, 3)]
        for q, (j, cidx) in enumerate(srcs):
            sc = None if cidx is None else cols.v(cidx, [[1, 1]])
            self.copy("vector", tok5.v(q * 128, [[1, 128]]), ps(bx, j * 128, [[1, 128]]), [P[bx], cols.res], [tok5.res], scale=sc)
        yield
        S.op("tensor", lambda e: e.matmul(ps(by, 128, [[1, 128]]), tok5.v(128, [[1, 128]]), self.idb.v(), start=True, stop=True),
             reads=[tok5.res, self.idb.res], writes=[P[by]])
        S.op("tensor", lambda e: e.matmul(ps(by, 256, [[1, 128]]), tok5.v(0, [[1, 128]]), self.idb.v(), start=True, stop=True),
             reads=[tok5.res, self.idb.res], writes=[P[by]], acc=True)
        S.op("tensor", lambda e: e.matmul(ps(by, 384, [[1, 128]]), tok5.v(0, [[1, 128]]), Dg.v(), start=True, stop=True),
             reads=[tok5.res, Dg.res], writes=[P[by]], acc=True)
        yield
        self.copy("scalar", feat3.v(), ps(by, 128, [[1, 384]]), [P[by]], [feat3.res])
        S.op("vector", lambda e: e.scalar_tensor_tensor(Dm.v(), ps(by, 0, [[1, 128]]), Gc.v(ti * 8 + hb, [[1, 1]]), CI(C_MNEG),
                                                        ALU.subtract, ALU.add),
             reads=[P[by], Gc.res, cst.res], writes=[Dm.res])
        yield
        S.op("scalar", lambda e: e.activation(Lam.v(), Dm.v(), AF.Exp), reads=[Dm.res], writes=[Lam.res])
        S.op("tensor", lambda e: e.matmul(ps(bx, 0, [[1, 128]]), feat3.v(0, [[1, 128]]), feat3.v(0, [[1, 128]]), start=True, stop=True),
             reads=[feat3.res], writes=[P[bx]])
        S.op("tensor", lambda e: e.matmul(ps(bx, 128, [[1, 128]]), feat3.v(0, [[1, 128]]), feat3.v(128, [[1, 128]]), start=True, stop=True),
             reads=[feat3.res], writes=[P[bx]], acc=True)
        yield
        S.op("vector", lambda e: e.tensor_tensor(Dm.v(), ps(bx, 0, [[1, 128]]), Lam.v(), ALU.mult),
             reads=[P[bx], Lam.res], writes=[Dm.res])
        S.op("vector", lambda e: e.scalar_tensor_tensor(Nn.v(), Dm.v(), bt.v((ti * 8 + hb) * 2 + 1, [[1, 1]]), CI(C_STRICT),
                                                        ALU.mult, ALU.mult),
             reads=[Dm.res, bt.res, cst.res], writes=[Nn.res])
        S.op("vector", lambda e: e.tensor_tensor(aqk.v(), ps(bx, 128, [[1, 128]]), Lam.v(), ALU.mult),
             reads=[P[bx], Lam.res], writes=[aqk.res])
        S.op("vector", lambda e: e.tensor_tensor(X[0].v(), Nn.v(), CI(C_ID), ALU.add), reads=[Nn.res, cst.res], writes=[X[0].res])
        S.op("tensor", lambda e: e.transpose(ps(cb, 0, [[1, 128]]), Nn.v(), CI(C_ID)), reads=[Nn.res, cst.res], writes=[P[cb]])
        yield
        self.copy("scalar", NTt.v(), ps(cb, 0, [[1, 128]]), [P[cb]], [NTt.res])
        yield
        Pm, PmT = Nn.v(), NTt.v()
        Pres, PTres = Nn.res, NTt.res
        xi = 0
        for lvl in range(1, 6):
            pp = PP[lvl % 2]
            last = (lvl == 5)
            S.op("tensor", lambda e, Pm=Pm, PmT=PmT: e.matmul(ps(cb, 128, [[1, 128]]), Pm, PmT, start=True, stop=True),
                 reads=[Pres, PTres], writes=[P[cb]])
            if not last:
                S.op("tensor", lambda e, Pm=Pm, PmT=PmT: e.matmul(ps(cb, 0, [[1, 128]]), PmT, Pm, start=True, stop=True),
                     reads=[Pres, PTres], writes=[P[cb]], acc=True)
            yield
            if not last:
                self.copy("scalar", pp.v(), ps(cb, 0, [[1, 256]]), [P[cb]], [pp.res])
            else:
                self.copy("scalar", pp.v(128, [[1, 128]]), ps(cb, 128, [[1, 128]]), [P[cb]], [pp.res])
            yield
            Pm, PmT = pp.v(0, [[1, 128]]), pp.v(128, [[1, 128]])
            Pres = PTres = pp.res
            xo, xn = X[xi], X[1 - xi]
            S.op("tensor", lambda e, PmT=PmT, xo=xo: e.matmul(ps(cb, 256, [[1, 128]]), PmT, xo.v(), start=True, stop=True),
                 reads=[pp.res, xo.res], writes=[P[cb]])
            yield
            if not last:
                S.op("vector", lambda e, xo=xo, xn=xn: e.tensor_tensor(xn.v(), ps(cb, 256, [[1, 128]]), xo.v(), ALU.add),
                     reads=[P[cb], xo.res], writes=[xn.res])
            else:
                S.op("vector", lambda e, xo=xo: e.tensor_tensor(Xb.v(), ps(cb, 256, [[1, 128]]), xo.v(), ALU.add),
                     reads=[P[cb], xo.res], writes=[Xb.res])
            yield
            xi = 1 - xi
        S.op("tensor", lambda e: e.matmul(ps(cb, 384, [[1, 128]]), tok5.v(3 * 128, [[1, 128]]), Xb.v(), start=True, stop=True),
             reads=[tok5.res, Xb.res], writes=[P[cb]])
        yield
        self.copy("scalar", nwT.v(), ps(cb, 384, [[1, 128]]), [P[cb]], [nwT.res], scale=-1.0)
        yield
        for i in range(2):
            c0 = 64 * i
            si = i if kind == "S" else 0
            if kind == "S":
                sq = 2 * ti + i
                S.dma("sync", Sf[si].v(), dr(self.sssm, ((sq * 8 + hb) * 128) * 128, [[128, 128], [1, 128]]), writes=[Sf[si].res])
                self.copy("scalar", Sb[si].v(), Sf[si].v(), [Sf[si].res], [Sb[si].res])
            sfl, sbl = Sf[si], Sb[si]
            S.op("tensor", lambda e, c0=c0: e.matmul(ps(bx, 256, [[1, 128]], c0, 64), Xb.v(c0, [[1, 64]]), tok5.v(2 * 128, [[1, 128]]),
                                                     start=True, stop=False),
                 reads=[Xb.res, tok5.res], writes=[P[bx]])
            S.op("tensor", lambda e, c0=c0, sbl=sbl: e.matmul(ps(bx, 256, [[1, 128]], c0, 64), nwT.v(c0, [[1, 64]]), sbl.v(),
                                                              start=False, stop=True),
                 reads=[nwT.res, sbl.res], writes=[P[bx]], acc=True)
            yield
            S.op("vector", lambda e, c0=c0: e.tensor_scalar(wv.v(0, [[1, 128]], c0, 64), ps(bx, 256, [[1, 128]], c0, 64),
                                                            bt.v((ti * 8 + hb) * 2, [[1, 1]], c0, 64), None, ALU.mult),
                 reads=[P[bx], bt.res], writes=[wv.res])
            yield
            S.op("tensor", lambda e, c0=c0, sbl=sbl: e.matmul(ps(bx, 0, [[1, 128]], c0, 64), feat3.v(256 + c0, [[1, 64]]), sbl.v(),
                                                              start=True, stop=False),
                 reads=[feat3.res, sbl.res], writes=[P[bx]])
            S.op("tensor", lambda e, c0=c0: e.matmul(ps(bx, 0, [[1, 128]], c0, 64), aqk.v(c0, [[1, 64]], c0, 64), wv.v(0, [[1, 128]], c0, 64),
                                                     start=False, stop=True),
                 reads=[aqk.res, wv.res], writes=[P[bx]], acc=True)
            S.op("tensor", lambda e, c0=c0: e.matmul(ps(bx, 384, [[1, 128]]), tok5.v(4 * 128, [[1, 128]], c0, 64), wv.v(0, [[1, 128]], c0, 64),
                                                     start=True, stop=True),
                 reads=[tok5.res, wv.res], writes=[P[bx]], acc=True)
            yield
            S.op("vector", lambda e, sfl=sfl, i=i: e.scalar_tensor_tensor(sfl.v(), sfl.v(), gend.v(ti * 16 + i * 8 + hb, [[1, 1]]),
                                                                          ps(bx, 384, [[1, 128]]), ALU.mult, ALU.add),
                 reads=[sfl.res, gend.res, P[bx]], writes=[sfl.res])
            self.copy("scalar", sbl.v(), sfl.v(), [sfl.res], [sbl.res])
            if kind == "S":
                S.dma("sync", dr(self.bss, ((sq * 8 + hb) * 128) * 128, [[128, 128], [1, 128]]), sfl.v(), reads=[sfl.res])
            yield
        S.op("scalar", lambda e: e.activation(junk.v(), ps(bx, 0, [[1, 128]]), AF.Square, accum_out=sso.v(0, [[1, 1]])),
             reads=[P[bx]], writes=[junk.res, sso.res])
        yield
        S.op("vector", lambda e: e.tensor_scalar(sso.v(1, [[1, 1]]), sso.v(0, [[1, 1]]), 1.0 / 128.0, RMS_EPS, ALU.mult, ALU.add),
             reads=[sso.res], writes=[sso.res])
        S.op("scalar", lambda e: e.activation(sso.v(1, [[1, 1]]), sso.v(1, [[1, 1]]), AF.Sqrt), reads=[sso.res], writes=[sso.res])
        yield
        S.op("vector", lambda e: e.reciprocal(sso.v(1, [[1, 1]]), sso.v(1, [[1, 1]])), reads=[sso.res], writes=[sso.res])
        S.op("vector", lambda e: e.scalar_tensor_tensor(otmp.v(), ps(bx, 0, [[1, 128]]), sso.v(1, [[1, 1]]), sgn.v(ti * 128, [[1, 128]]),
                                                        ALU.mult, ALU.mult),
             reads=[P[bx], sso.res, sgn.res], writes=[otmp.res])
        mr = self.mres[ti]
        S.op("vector", lambda e: e.tensor_tensor(self.merged.v(ti * D + hb * 128, [[1, 128]]), self.merged.v(ti * D + hb * 128, [[1, 128]]),
                                                 otmp.v(), ALU.add),
             reads=[otmp.res, mr], writes=[mr])

    def dump_merged(self):
        S = self.S
        with contextlib.ExitStack() as es:
            st = self.sb(es, "dbgst", D)
            for ti in range(self.NT):
                S.op("vector", lambda e, ti=ti: e.tensor_copy(st.v(), self.merged.v(ti * D, [[1, D]])),
                     reads=[self.mres[ti]], writes=[st.res])
                if self.kind == "P":
                    S.dma("sync", dr(self.dbg_mp, (self.b * self.T + ti * 128) * D, [[D, 128], [1, D]]), st.v(), reads=[st.res])
                else:
                    for i in range(2):
                        S.dma("sync", dr(self.dbg_ms, ((2 * ti + i) * 64) * D, [[D, 64], [1, D]]), st.v(0, [[1, D]], 64 * i, 64), reads=[st.res])
            S.barrier()

    def phaseD(self, es_pass):
        S = self.S
        NT, NCOL, kind = self.NT, self.NCOL, self.kind
        cst = self.cst
        CI = lambda c: cst.v(c, [[1, 128]])
        ps = self.ps
        P = self.psres
        with contextlib.ExitStack() as es:
            woutb = self.sb(es, "woutb", 8 * D, BF16)
            for q in range(2):
                S.dma("gpsimd", woutb.v(q * 512, [[D, 8], [1, 512]]),
                      dr(self.wout, q * 512, [[D, 128], [128 * D, 8], [1, 512]]), writes=[woutb.res])
            f1T = self.sb(es, "f1T", 32 * 512, BF16)
            hT = self.sb(es, "hT", 8 * 512, BF16)
            hn = self.sb(es, "hn", 4 * D)
            mT = self.sb(es, "mT", 8 * 128, BF16)
            xt = [self.sb(es, "xtD%d" % i, D) for i in range(2)]
            hp_ = self.sb(es, "hpD", D)
            st = self.sb(es, "stD", 8)
            junk = self.sb(es, "junkD", D)
            rl = self.sb(es, "rlD", 512)
            yo = [self.sb(es, "yoD%d" % i, D) for i in range(2)]
            yi = 0
            for bi, (c0, n) in enumerate(self.blocks):
                nt_b = n // 128
                for tl in range(nt_b):
                    ti = c0 // 128 + tl
                    mr = self.mres[ti]
                    for half in range(2):
                        for q in range(4):
                            kc = half * 4 + q
                            S.op("tensor", lambda e, kc=kc, q=q, half=half: e.matmul(
                                ps(half, q * 128, [[1, 128]]), self.merged.v(ti * D + kc * 128, [[1, 128]]), self.idb.v(),
                                start=True, stop=True),
                                reads=[mr, self.idb.res], writes=[P[half]], acc=(q > 0))
                        self.copy(self.evac_eng(), mT.v(half * 512, [[1, 512]]), ps(half, 0, [[1, 512]]), [P[half]], [mT.res])
                    x_ = xt[ti % 2]
                    self.x_tile_src(x_, ti)
                    for half in range(2):
                        for kc in range(8):
                            S.op("tensor", lambda e, kc=kc, half=half: e.matmul(
                                ps(2 + half, 0, [[1, 512]]), mT.v(kc * 128, [[1, 128]]), woutb.v(kc * D + half * 512, [[1, 512]]),
                                start=(kc == 0), stop=(kc == 7)),
                                reads=[mT.res, woutb.res], writes=[P[2 + half]], acc=(kc > 0))
                        S.op("vector", lambda e, half=half: e.scalar_tensor_tensor(
                            hp_.v(half * 512, [[1, 512]]), x_.v(half * 512, [[1, 512]]), ALPHA, ps(2 + half, 0, [[1, 512]]), ALU.mult, ALU.add),
                            reads=[x_.res, P[2 + half]], writes=[hp_.res])
                    self.layer_norm(hp_, hn.v(tl * D, [[1, D]]), hn.res, st, junk)
                    for half in range(2):
                        for q in range(4):
                            kc = half * 4 + q
                            S.op("tensor", lambda e, kc=kc, q=q, half=half: e.transpose(
                                ps(4 + half, q * 128, [[1, 128]]), hn.v(tl * D + kc * 128, [[1, 128]]), CI(C_ID)),
                                reads=[hn.res, cst.res], writes=[P[4 + half]], acc=(q > 0))
                        for q in range(4):
                            kc = half * 4 + q
                            S.op("vector", lambda e, kc=kc, q=q, half=half: e.tensor_scalar(
                                hT.v(kc * 512 + tl * 128, [[1, 128]]), ps(4 + half, q * 128, [[1, 128]]),
                                self.g1c.v(kc, [[1, 1]]), self.b1c.v(kc, [[1, 1]]), ALU.mult, ALU.add),
                                reads=[P[4 + half], self.g1c.res, self.b1c.res], writes=[hT.res])
                for s in range(8):
                    slab = self.next_slab()
                    self.load_slab(slab, self.wff1, s * 512, 512)
                    for q in range(4):
                        fc = s * 4 + q
                        bank = fc % 2
                        for kc in range(8):
                            S.op("tensor", lambda e, kc=kc, q=q, bank=bank: e.matmul(
                                ps(bank, 0, [[1, n]]), slab.v(kc * 512 + q * 128, [[1, 128]]), hT.v(kc * 512, [[1, n]]),
                                start=(kc == 0), stop=(kc == 7)),
                                reads=[slab.res, hT.res], writes=[P[bank]], acc=(kc > 0))
                        S.op("scalar", lambda e, fc=fc, bank=bank: e.activation(rl.v(0, [[1, n]]), ps(bank, 0, [[1, n]]), AF.Relu,
                                                                               bias=self.b1T.v(fc, [[1, 1]])),
                             reads=[P[bank], self.b1T.res], writes=[rl.res])
                        S.op("vector", lambda e, fc=fc: e.tensor_tensor(f1T.v(fc * 512, [[1, n]]), rl.v(0, [[1, n]]), rl.v(0, [[1, n]]), ALU.mult),
                             reads=[rl.res], writes=[f1T.res])
                for s in range(8):
                    slab = self.next_slab()
                    self.S.dma("gpsimd", slab.v(0, [[D, 4], [1, D]]),
                               dr(self.wff2, s * 512 * D, [[D, 128], [128 * D, 4], [1, D]]), writes=[slab.res])
                    for q in range(4):
                        fc = s * 4 + q
                        for tl in range(nt_b):
                            for half in range(2):
                                bank = tl * 2 + half
                                S.op("tensor", lambda e, fc=fc, q=q, tl=tl, half=half, bank=bank: e.matmul(
                                    ps(bank, 0, [[1, 512]]), f1T.v(fc * 512 + tl * 128, [[1, 128]]), slab.v(q * D + half * 512, [[1, 512]]),
                                    start=(fc == 0), stop=(fc == 31)),
                                    reads=[f1T.res, slab.res], writes=[P[bank]], acc=(fc > 0))
                for tl in range(nt_b):
                    ti = c0 // 128 + tl
                    S.op("vector", lambda e, tl=tl: e.tensor_tensor(hp_.v(), hn.v(tl * D, [[1, D]]), self.g1a.v(), ALU.mult),
                         reads=[hn.res, self.g1a.res], writes=[hp_.res])
                    S.op("vector", lambda e: e.tensor_tensor(hp_.v(), hp_.v(), self.c1.v(), ALU.add),
                         reads=[hp_.res, self.c1.res], writes=[hp_.res])
                    for half in range(2):
                        bank = tl * 2 + half
                        S.op("vector", lambda e, half=half, bank=bank: e.tensor_tensor(
                            hp_.v(half * 512, [[1, 512]]), hp_.v(half * 512, [[1, 512]]), ps(bank, 0, [[1, 512]]), ALU.add),
                            reads=[hp_.res, P[bank]], writes=[hp_.res])
                    y = yo[yi % 2]
                    yi += 1
                    self.layer_norm(hp_, y.v(), y.res, st, junk)
                    S.op("vector", lambda e: e.tensor_tensor(y.v(), y.v(), self.g2.v(), ALU.mult), reads=[y.res, self.g2.res], writes=[y.res])
                    S.op("vector", lambda e: e.tensor_tensor(y.v(), y.v(), self.b2.v(), ALU.add), reads=[y.res, self.b2.res], writes=[y.res])
                    if kind == "P":
                        S.dma("sync", dr(self.yp, (self.b * self.T + ti * 128) * D, [[D, 128], [1, D]]), y.v(), reads=[y.res])
                    else:
                        for i in range(2):
                            S.dma("sync", dr(self.ys, ((2 * ti + i) * 16) * D, [[D, 16], [1, D]]), y.v(0, [[1, D]], 64 * i, 16), reads=[y.res])
            S.barrier()

    def layer_norm(self, src, out_ap, out_res, st, junk):
        S = self.S
        S.op("scalar", lambda e: e.activation(junk.v(), src.v(), AF.Copy, accum_out=st.v(0, [[1, 1]])),
             reads=[src.res], writes=[junk.res, st.res])
        S.op("vector", lambda e: e.tensor_scalar(st.v(1, [[1, 1]]), st.v(0, [[1, 1]]), -1.0 / D, None, ALU.mult),
             reads=[st.res], writes=[st.res])
        S.op("vector", lambda e: e.tensor_scalar(src.v(), src.v(), st.v(1, [[1, 1]]), None, ALU.add),
             reads=[src.res, st.res], writes=[src.res])
        S.op("scalar", lambda e: e.activation(junk.v(), src.v(), AF.Square, accum_out=st.v(2, [[1, 1]])),
             reads=[src.res], writes=[junk.res, st.res])
        S.op("vector", lambda e: e.tensor_scalar(st.v(3, [[1, 1]]), st.v(2, [[1, 1]]), 1.0 / D, LN_EPS, ALU.mult, ALU.add),
             reads=[st.res], writes=[st.res])
        S.op("scalar", lambda e: e.activation(st.v(3, [[1, 1]]), st.v(3, [[1, 1]]), AF.Sqrt), reads=[st.res], writes=[st.res])
        S.op("vector", lambda e: e.reciprocal(st.v(3, [[1, 1]]), st.v(3, [[1, 1]])), reads=[st.res], writes=[st.res])
        S.op("vector", lambda e: e.tensor_scalar(out_ap, src.v(), st.v(3, [[1, 1]]), None, ALU.mult),
             reads=[src.res, st.res], writes=[out_res])


_CACHE = {}


def _get_nc(NBP, T, NBS, debug=False):
    key = (NBP, T, NBS, debug)
    if key not in _CACHE:
        b = Builder(NBP, T, NBS, debug)
        _CACHE[key] = b.build()
    return _CACHE[key]


def make_in_maps(inp, n_cores, NBP, T, NBS):
    f = lambda a: np.ascontiguousarray(np.asarray(a, np.float32))
    consts = make_consts()
    bP, bS, bN = make_bias_tables(np.asarray(inp["a_rel_bias"])[0])
    shared = dict(
        w_in=f(inp["w_in"][0]), wconv=f(inp["w_b_conv"][0]), alog=f(inp["b_a_log"]).reshape(1, 8),
        dtb=f(inp["b_dt_bias"]).reshape(1, 8), normg=f(inp["b_norm_g"]).reshape(1, 128),
        biasP=bP.reshape(16 * 128, 640), biasS=bS.reshape(16 * 128, 256), biasN=bN.reshape(16 * 64, 64),
        wkv=f(inp["w_mem_kv"][0]), wout=f(inp["w_out"][0]), ln1g=f(inp["ln1_g"]).reshape(1, D), ln1b=f(inp["ln1_b"]).reshape(1, D),
        wff1=f(inp["w_ff1"][0]), bff1=f(inp["b_ff1"]).reshape(32, 128), wff2=f(inp["w_ff2"][0]), bff2=f(inp["b_ff2"]).reshape(1, D),
        ln2g=f(inp["ln2_g"]).reshape(1, D), ln2b=f(inp["ln2_b"]).reshape(1, D), consts=consts)
    maps = []
    for c in range(n_cores):
        ps_, ss_ = slice(c * NBP, (c + 1) * NBP), slice(c * NBS, (c + 1) * NBS)
        m = dict(shared)
        m["xp"] = f(inp["x_prompt"][ps_]).reshape(NBP * T, D)
        m["xs"] = f(inp["x_sample"][ss_]).reshape(NBS * 16, D)
        m["cak"] = f(inp["cache_a_k"][0, ss_]).reshape(NBS * LC, D)
        m["cav"] = f(inp["cache_a_v"][0, ss_]).reshape(NBS * LC, D)
        m["sconv"] = f(inp["state_b_conv"][0, ss_]).reshape(NBS * 3, 3072)
        m["sssm"] = f(inp["state_b_ssm"][0, ss_]).reshape(NBS * 8 * 128, 128)
        m["cmk"] = f(inp["cache_mem_k"][0, ss_]).reshape(NBS * 256, D)
        m["cmv"] = f(inp["cache_mem_v"][0, ss_]).reshape(NBS * 256, D)
        m["memp"] = f(inp["mem_prompt"][ps_]).reshape(NBP * 256, D)
        maps.append(m)
    return maps


def assemble(results, n_cores, NBP, T, NBS):
    cat = lambda k: np.concatenate([np.asarray(r[k]) for r in results], axis=0)
    B, BS = n_cores * NBP, n_cores * NBS
    yp = cat("yp").reshape(B, T, D)
    ys = cat("ys").reshape(BS, 16, D)
    akp = cat("akp").reshape(1, B, 512, 16, 64)
    avp = cat("avp").reshape(1, B, 512, 16, 64)
    bcp = cat("bcp").reshape(1, B, 3, 3072)
    bsp = cat("bsp").reshape(1, B, 8, 128, 128)
    mkp = cat("mkp").reshape(1, B, 256, 4, 256)
    mvp = cat("mvp").reshape(1, B, 256, 4, 256)
    aks = cat("aks").reshape(1, BS, 512, 16, 64)
    avs = cat("avs").reshape(1, BS, 512, 16, 64)
    bcs = cat("bcs").reshape(1, BS, 3, 3072)
    bss = cat("bss").reshape(1, BS, 8, 128, 128)
    return (yp, ys, akp, avp, bcp, bsp, mkp, mvp, aks, avs, bcs, bss)


def kernel(**inputs):
    n_cores = 8
    B, T = inputs["x_prompt"].shape[0], inputs["x_prompt"].shape[1]
    BS = inputs["x_sample"].shape[0]
    NBP, NBS = B // n_cores, BS // n_cores
    nc = _get_nc(NBP, T, NBS)
    maps = make_in_maps(inputs, n_cores, NBP, T, NBS)
    res = run_bass_kernel_spmd(nc, maps, core_ids=list(range(n_cores)))
    return assemble(res.results, n_cores, NBP, T, NBS)
```

```python
import contextlib
import numpy as np
import concourse.bass as bass
import concourse.mybir as mybir
from concourse.bass_utils import run_bass_kernel_spmd

F32 = mybir.dt.float32
BF16 = mybir.dt.bfloat16
AF = mybir.ActivationFunctionType
ALU = mybir.AluOpType

D = 1024
IN_COLS = 10256
OFF_QA, OFF_KA, OFF_VA = 0, 1024, 2048
OFF_QB, OFF_KB, OFF_VB = 3072, 4096, 5120
OFF_QC = 6144
OFF_GA, OFF_GB, OFF_GC = 7168, 8192, 9216
OFF_AB = 10240
ALPHA = 2.0 ** 0.25
NEG = -30000.0
LN_EPS = 1e-5
RMS_EPS = 1e-6
L2_EPS = 1e-6
PAST_LEN = 1024
LC = 512

C_ID = 0
C_TRI = 128
C_ONESBD = 256
C_SEL0 = 384
C_SEL1 = 512
C_MNEG = 640
C_STRICT = 768
C_ROWM = 896
C_ONES = 900
C_MASKA = 1028
C_MASKN = 1668
NCONST = 1732


def make_consts():
    c = np.zeros((128, NCONST), np.float32)
    r = np.arange(128)[:, None]
    t = np.arange(128)[None, :]
    same = (r // 64) == (t // 64)
    c[:, C_ID:C_ID + 128] = np.eye(128)
    c[:, C_TRI:C_TRI + 128] = (same & (r <= t))
    c[:, C_ONESBD:C_ONESBD + 128] = same
    c[:, C_SEL0:C_SEL0 + 128] = (r < 64) & (t >= 0)
    c[:, C_SEL1:C_SEL1 + 128] = (r >= 64) & (t >= 0)
    c[:, C_MNEG:C_MNEG + 128] = np.where(same & (r <= t), 0.0, -60000.0)
    c[:, C_STRICT:C_STRICT + 128] = (same & (r < t))
    c[:, C_ROWM] = ((np.arange(128) % 64) < 16)
    c[:, C_ONES:C_ONES + 128] = 1.0
    kk = np.arange(128)[:, None, None]
    j = np.arange(5)[None, :, None]
    qq = np.arange(128)[None, None, :]
    cq = qq // 64
    pos = 128 * j + kk
    valid = (pos >= 64 * cq) & (pos < 576 + 64 * cq)
    c[:, C_MASKA:C_MASKA + 640] = np.where(valid, 0.0, NEG).reshape(128, 640)
    mn = np.zeros((128, 64), np.float32)
    mn[(np.arange(128) % 64) >= 16, :] = NEG
    c[:, C_MASKN:C_MASKN + 64] = mn
    return c


def make_bias_tables(rel_bias):
    rb = np.asarray(rel_bias, np.float32)
    kk = np.arange(128)[:, None, None]
    j5 = np.arange(5)[None, :, None]
    qq = np.arange(128)[None, None, :]
    rel = 512 - 128 * j5 + qq - kk
    idx = np.clip(rel, -128, 128) + 128
    biasP = rb[:, idx].reshape(16, 128, 640)
    j4 = np.arange(4)[None, :, None]
    q64 = np.arange(64)[None, None, :]
    rel = 512 + q64 - 128 * j4 - kk
    idx = np.clip(rel, -128, 128) + 128
    biasS = rb[:, idx].reshape(16, 128, 256)
    k64 = np.arange(64)[:, None]
    q64 = np.arange(64)[None, :]
    idx = np.clip(q64 - k64, -128, 128) + 128
    biasN = rb[:, idx].reshape(16, 64, 64)
    return (np.ascontiguousarray(biasP), np.ascontiguousarray(biasS), np.ascontiguousarray(biasN))


class Res:
    __slots__ = ("name", "w", "r", "ds", "ps")

    def __init__(self, name, ps=False):
        self.name = name
        self.w = None
        self.r = []
        self.ds = {}
        self.ps = ps


class DSem:
    def __init__(self, sem):
        self.sem = sem
        self.cnt = 0


class Sync:
    ENG = ["tensor", "vector", "scalar", "gpsimd", "sync"]

    def __init__(self, nc, n_dma_sems=72):
        self.nc = nc
        self.E = {}
        for n in self.ENG:
            self.E[n] = dict(e=getattr(nc, n), sem=nc.alloc_semaphore(name="s_" + n), cnt=0, seen={})
        self.free_ds = {"hw": [DSem(nc.alloc_semaphore(name="d%d" % i)) for i in range(n_dma_sems)],
                        "sw": [DSem(nc.alloc_semaphore(name="q%d" % i)) for i in range(10)]}
        self.all_ds = self.free_ds["hw"] + self.free_ds["sw"]
        self.owned = []
        self.ninst = 0

    def _wait(self, en, deps):
        E = self.E[en]
        need = {}
        for (sem, val) in deps:
            k = sem.num
            if E["seen"].get(k, 0) >= val:
                continue
            if k not in need or need[k][1] < val:
                need[k] = (sem, val)
        for k, (sem, val) in need.items():
            E["e"].wait_ge(sem, val)
            E["seen"][k] = val
            self.ninst += 1

    @staticmethod
    def _deps(reads, writes, acc, own=None):
        deps = []
        for r in reads:
            if r.w is not None:
                deps.append(r.w)
            if r.ps:
                deps.extend(t for t in r.r if t[0].num != own)
        if not acc:
            for w in writes:
                if w.w is not None:
                    deps.append(w.w)
                deps.extend(w.r)
        return deps

    def op(self, en, fn, reads=(), writes=(), acc=False):
        E = self.E[en]
        self._wait(en, self._deps(reads, writes, acc, E["sem"].num))
        inst = fn(E["e"])
        E["cnt"] += 1
        inst.then_inc(E["sem"], 1)
        tok = (E["sem"], E["cnt"])
        for r in reads:
            r.r.append(tok)
        for w in writes:
            w.w = tok
            if not acc:
                w.r = []
        self.ninst += 1
        return inst

    def dma(self, en, out, in_, reads=(), writes=(), owner=None, **kw):
        E = self.E[en]
        self._wait(en, self._deps(reads, writes, False, None))
        if owner is None:
            owner = (list(writes) + list(reads))[0]
        qk = "sw" if en == "gpsimd" else "hw"
        if qk not in owner.ds:
            owner.ds[qk] = self.free_ds[qk].pop()
            self.owned.append((owner, qk))
        ds = owner.ds[qk]
        ds.cnt += 16
        inst = E["e"].dma_start(out=out, in_=in_, **kw)
        inst.then_inc(ds.sem, 16)
        tok = (ds.sem, ds.cnt)
        for r in reads:
            r.r.append(tok)
        for w in writes:
            w.w = tok
            w.r = []
        self.ninst += 1
        return inst

    def barrier(self, release=True):
        toks = [(self.E[n]["sem"], self.E[n]["cnt"]) for n in self.ENG if self.E[n]["cnt"] > 0]
        toks += [(d.sem, d.cnt) for d in self.all_ds if d.cnt > 0]
        for n in self.ENG:
            self._wait(n, toks)
        if release:
            for (o, qk) in self.owned:
                self.free_ds[qk].append(o.ds.pop(qk))
            self.owned = []


class Tl:
    def __init__(self, h, F, name):
        self.h = h
        self.F = F
        self.res = Res(name)

    def v(self, off=0, dims=None, p0=0, pn=128):
        if dims is None:
            dims = [[1, self.F - off]]
        return bass.AP(tensor=self.h, offset=p0 * self.F + off, ap=[[self.F, pn]] + [list(d) for d in dims])


def dr(t, off, dims):
    return bass.AP(tensor=t.tensor, offset=off, ap=[list(d) for d in dims])


class Builder:
    def __init__(self, NBP, T, NBS, debug=False):
        assert T % 512 == 0 and NBS % 2 == 0
        self.NBP, self.T, self.NBS = NBP, T, NBS
        self.debug = debug
        nc = bass.Bass("TRN2", target_bir_lowering=False)
        self.nc = nc
        self.S = Sync(nc)
        di = lambda n, s: nc.dram_tensor(n, s, F32, kind="ExternalInput").ap()
        do = lambda n, s: nc.dram_tensor(n, s, F32, kind="ExternalOutput").ap()
        self.xp = di("xp", [NBP * T, D])
        self.xs = di("xs", [NBS * 16, D])
        self.cak = di("cak", [NBS * LC, D])
        self.cav = di("cav", [NBS * LC, D])
        self.sconv = di("sconv", [NBS * 3, 3072])
        self.sssm = di("sssm", [NBS * 8 * 128, 128])
        self.cmk = di("cmk", [NBS * 256, D])
        self.cmv = di("cmv", [NBS * 256, D])
        self.memp = di("memp", [NBP * 256, D])
        self.w_in = di("w_in", [D, IN_COLS])
        self.wconv = di("wconv", [4, 3072])
        self.alog = di("alog", [1, 8])
        self.dtb = di("dtb", [1, 8])
        self.normg = di("normg", [1, 128])
        self.biasP = di("biasP", [16 * 128, 640])
        self.biasS = di("biasS", [16 * 128, 256])
        self.biasN = di("biasN", [16 * 64, 64])
        self.wkv = di("wkv", [D, 2048])
        self.wout = di("wout", [D, D])
        self.ln1g = di("ln1g", [1, D])
        self.ln1b = di("ln1b", [1, D])
        self.wff1 = di("wff1", [D, 4096])
        self.bff1 = di("bff1", [32, 128])
        self.wff2 = di("wff2", [4096, D])
        self.bff2 = di("bff2", [1, D])
        self.ln2g = di("ln2g", [1, D])
        self.ln2b = di("ln2b", [1, D])
        self.consts = di("consts", [128, NCONST])
        self.yp = do("yp", [NBP * T, D])
        self.ys = do("ys", [NBS * 16, D])
        self.akp = do("akp", [NBP * 512, D])
        self.avp = do("avp", [NBP * 512, D])
        self.bcp = do("bcp", [NBP * 3, 3072])
        self.bsp = do("bsp", [NBP * 8 * 128, 128])
        self.mkp = do("mkp", [NBP * 256, D])
        self.mvp = do("mvp", [NBP * 256, D])
        self.aks = do("aks", [NBS * LC, D])
        self.avs = do("avs", [NBS * LC, D])
        self.bcs = do("bcs", [NBS * 3, 3072])
        self.bss = do("bss", [NBS * 8 * 128, 128])
        if debug:
            self.dbg_mp = do("dbg_mp", [NBP * T, D])
            self.dbg_ms = do("dbg_ms", [NBS * 64, D])
        self.flip = 0
        self.GB = 3

    def sb(self, es, name, F, dt=F32):
        self.uid = getattr(self, "uid", 0) + 1
        name = "%s_%d" % (name, self.uid)
        h = es.enter_context(self.nc.sbuf_tensor(name, [128, F], dt))
        return Tl(h, F, name)

    def evac_eng(self):
        self.flip ^= 1
        return "vector" if self.flip else "scalar"

    def copy(self, en, out, in_, reads, writes, scale=None):
        S = self.S
        if en == "scalar":
            if scale is None:
                S.op("scalar", lambda e: e.activation(out, in_, AF.Copy), reads=reads, writes=writes)
            elif isinstance(scale, float):
                S.op("scalar", lambda e: e.activation(out, in_, AF.Copy, scale=scale), reads=reads, writes=writes)
            else:
                S.op("scalar", lambda e: e.activation(out, in_, AF.Copy, scale=scale), reads=reads, writes=writes)
        else:
            if scale is None:
                S.op(en, lambda e: e.tensor_copy(out, in_), reads=reads, writes=writes)
            else:
                S.op(en, lambda e: e.tensor_scalar(out, in_, scale, None, ALU.mult), reads=reads, writes=writes)

    def load_slab(self, slab, src, col0, ncols=512, rows0=0, kc=8, dst_off=0):
        W = src.tensor.shape[1]
        self.S.dma("gpsimd", slab.v(dst_off, [[512, kc], [1, ncols]]),
                   dr(src, rows0 * W + col0, [[W, 128], [128 * W, kc], [1, ncols]]),
                   writes=[slab.res])

    def build(self):
        nc, S = self.nc, self.S
        with contextlib.ExitStack() as es:
            ph = es.enter_context(nc.psum_tensor("ps", [128, 4096], F32))
            self.PS = [None] * 8
            self.psh = ph
            self.psres = [Res("psb%d" % i, ps=True) for i in range(8)]
            self.cst = self.sb(es, "cst", NCONST)
            S.dma("sync", self.cst.v(), dr(self.consts, 0, [[NCONST, 128], [1, NCONST]]), writes=[self.cst.res])
            self.idb = self.sb(es, "idb", 128, BF16)
            S.op("vector", lambda e: e.tensor_copy(self.idb.v(), self.cst.v(C_ID, [[1, 128]])),
                 reads=[self.cst.res], writes=[self.idb.res])
            self.eps_l2 = self.sb(es, "eps_l2", 1)
            self.eps_rms = self.sb(es, "eps_rms", 1)
            self.eps_ln = self.sb(es, "eps_ln", 1)
            for t_, v_ in ((self.eps_l2, L2_EPS), (self.eps_rms, RMS_EPS), (self.eps_ln, LN_EPS)):
                S.op("vector", lambda e, t_=t_, v_=v_: e.memset(t_.v(), v_), writes=[t_.res])
            self.slabs = [self.sb(es, "slab%d" % i, 4096, BF16) for i in range(3)]
            self.slab_i = 0
            self.setup_small(es)
            ok = getattr(self, "only_kind", "PS")
            if "P" in ok:
                for b in range(self.NBP):
                    self.run_pass(es, "P", b)
            if "S" in ok:
                self.run_pass(es, "S", 0)
            S.barrier(release=False)
        return nc

    def next_slab(self):
        s = self.slabs[self.slab_i % 3]
        self.slab_i += 1
        return s

    def ps(self, bank, off=0, dims=None, p0=0, pn=128):
        if dims is None:
            dims = [[1, 512 - off]]
        return bass.AP(tensor=self.psh, offset=p0 * 4096 + bank * 512 + off, ap=[[4096, pn]] + [list(d) for d in dims])

    def setup_small(self, es):
        nc, S = self.nc, self.S
        cst = self.cst
        self.dtb_bc = self.sb(es, "dtb_bc", 8)
        self.nealog = self.sb(es, "nealog", 8)
        self.normg_bc = self.sb(es, "normg_bc", 128)
        S.dma("sync", self.dtb_bc.v(), dr(self.dtb, 0, [[0, 128], [1, 8]]), writes=[self.dtb_bc.res])
        S.dma("sync", self.nealog.v(), dr(self.alog, 0, [[0, 128], [1, 8]]), writes=[self.nealog.res])
        S.dma("sync", self.normg_bc.v(), dr(self.normg, 0, [[0, 128], [1, 128]]), writes=[self.normg_bc.res])
        S.op("scalar", lambda e: e.activation(self.nealog.v(), self.nealog.v(), AF.Exp),
             reads=[self.nealog.res], writes=[self.nealog.res])
        S.op("vector", lambda e: e.tensor_scalar(self.nealog.v(), self.nealog.v(), -1.0, None, ALU.mult),
             reads=[self.nealog.res], writes=[self.nealog.res])
        self.wc = self.sb(es, "wc", 96)
        self.b1T = self.sb(es, "b1T", 32)
        with contextlib.ExitStack() as es2:
            wtok = self.sb(es2, "wtok", 3072)
            S.dma("sync", wtok.v(0, [[1, 3072]], 0, 4), dr(self.wconv, 0, [[3072, 4], [1, 3072]]), writes=[wtok.res])
            pr = self.psres[0]
            for blk in range(24):
                S.op("tensor", lambda e, blk=blk: e.matmul(self.ps(0, blk * 4, [[1, 4]]),
                                                           wtok.v(blk * 128, [[1, 128]], 0, 4),
                                                           cst.v(C_ID, [[1, 4]], 0, 4), start=True, stop=True),
                     reads=[wtok.res, cst.res], writes=[pr], acc=(blk > 0))
            S.op("vector", lambda e: e.tensor_copy(self.wc.v(), self.ps(0, 0, [[1, 96]])), reads=[pr], writes=[self.wc.res])
            btok = self.sb(es2, "btok", 128)
            S.dma("sync", btok.v(0, [[1, 128]], 0, 32), dr(self.bff1, 0, [[128, 32], [1, 128]]), writes=[btok.res])
            pr1 = self.psres[1]
            S.op("tensor", lambda e: e.matmul(self.ps(1, 0, [[1, 32]]), btok.v(0, [[1, 128]], 0, 32),
                                              cst.v(C_ID, [[1, 32]], 0, 32), start=True, stop=True),
                 reads=[btok.res, cst.res], writes=[pr1])
            S.op("vector", lambda e: e.tensor_copy(self.b1T.v(), self.ps(1, 0, [[1, 32]])), reads=[pr1], writes=[self.b1T.res])
            S.barrier()
        self.g1a = self.sb(es, "g1a", D)
        self.c1 = self.sb(es, "c1", D)
        self.g2 = self.sb(es, "g2", D)
        self.b2 = self.sb(es, "b2", D)
        self.g1c = self.sb(es, "g1c", 8)
        self.b1c = self.sb(es, "b1c", 8)
        with contextlib.ExitStack() as es2:
            tmp = self.sb(es2, "tmpbc", D)
            S.dma("sync", self.g1a.v(), dr(self.ln1g, 0, [[0, 128], [1, D]]), writes=[self.g1a.res])
            S.dma("sync", self.c1.v(), dr(self.ln1b, 0, [[0, 128], [1, D]]), writes=[self.c1.res])
            S.dma("sync", tmp.v(), dr(self.bff2, 0, [[0, 128], [1, D]]), writes=[tmp.res])
            S.dma("sync", self.g2.v(), dr(self.ln2g, 0, [[0, 128], [1, D]]), writes=[self.g2.res])
            S.dma("sync", self.b2.v(), dr(self.ln2b, 0, [[0, 128], [1, D]]), writes=[self.b2.res])
            gtok = self.sb(es2, "gtok", 256)
            S.dma("sync", gtok.v(0, [[1, 128]], 0, 8), dr(self.ln1g, 0, [[128, 8], [1, 128]]), writes=[gtok.res])
            S.dma("sync", gtok.v(128, [[1, 128]], 0, 8), dr(self.ln1b, 0, [[128, 8], [1, 128]]), writes=[gtok.res])
            pr = self.psres[2]
            S.op("tensor", lambda e: e.matmul(self.ps(2, 0, [[1, 8]]), gtok.v(0, [[1, 128]], 0, 8),
                                              cst.v(C_ID, [[1, 8]], 0, 8), start=True, stop=True),
                 reads=[gtok.res, cst.res], writes=[pr])
            S.op("tensor", lambda e: e.matmul(self.ps(2, 8, [[1, 8]]), gtok.v(128, [[1, 128]], 0, 8),
                                              cst.v(C_ID, [[1, 8]], 0, 8), start=True, stop=True),
                 reads=[gtok.res, cst.res], writes=[pr], acc=True)
            S.op("vector", lambda e: e.tensor_copy(self.g1c.v(), self.ps(2, 0, [[1, 8]])), reads=[pr], writes=[self.g1c.res])
            S.op("vector", lambda e: e.tensor_copy(self.b1c.v(), self.ps(2, 8, [[1, 8]])), reads=[pr], writes=[self.b1c.res])
            S.op("vector", lambda e: e.tensor_scalar(self.g1a.v(), self.g1a.v(), ALPHA, None, ALU.mult),
                 reads=[self.g1a.res], writes=[self.g1a.res])
            S.op("vector", lambda e: e.scalar_tensor_tensor(self.c1.v(), self.c1.v(), ALPHA, tmp.v(), ALU.mult, ALU.add),
                 reads=[self.c1.res, tmp.res], writes=[self.c1.res])
            S.barrier()

    def run_pass(self, es_outer, kind, b):
        nc, S = self.nc, self.S
        T = self.T
        NT = (T // 128) if kind == "P" else (self.NBS // 2)
        NCOL = NT * 128
        blocks = [(c, min(512, NCOL - c)) for c in range(0, NCOL, 512)]
        self.kind, self.b, self.NT, self.NCOL, self.blocks = kind, b, NT, NCOL, blocks
        with contextlib.ExitStack() as es:
            self.merged = self.sb(es, "merged", NT * D, BF16)
            self.mres = [Res("mrg%d" % t) for t in range(NT)]
            stop = getattr(self, "stop_at", 99)
            with contextlib.ExitStack() as es1:
                self.xT = self.sb(es1, "xT", 8 * NCOL, BF16)
                if stop >= 1:
                    self.phase0(es1)
                if stop >= 2:
                    self.phaseA(es1)
                if stop >= 3:
                    self.phaseC(es1)
                if stop >= 4:
                    self.phaseB(es1)
                S.barrier()
            if self.debug and stop >= 4:
                self.dump_merged()
            if stop >= 5:
                self.phaseD(es)
            S.barrier()

    def xT_v(self, kc, c0, n):
        return self.xT.v(kc * self.NCOL + c0, [[1, n]])

    def x_tile_src(self, xt, ti, en="sync"):
        S = self.S
        if self.kind == "P":
            S.dma(en, xt.v(), dr(self.xp, (self.b * self.T + ti * 128) * D, [[D, 128], [1, D]]), writes=[xt.res])
        else:
            S.op("vector", lambda e: e.memset(xt.v(), 0.0), writes=[xt.res])
            for i in range(2):
                sq = 2 * ti + i
                S.dma(en, xt.v(0, [[1, D]], 64 * i, 16), dr(self.xs, sq * 16 * D, [[D, 16], [1, D]]), writes=[xt.res])

    def phase0(self, es):
        S = self.S
        with contextlib.ExitStack() as es2:
            xts = [self.sb(es2, "xt%d" % i, D) for i in range(2)]
            for ti in range(self.NT):
                xt = xts[ti % 2]
                self.x_tile_src(xt, ti)
                for half in range(2):
                    bank = (2 * ti + half) % 4
                    pr = self.psres[bank]
                    for q in range(4):
                        kc = half * 4 + q
                        S.op("tensor", lambda e, kc=kc, q=q, bank=bank: e.transpose(
                            self.ps(bank, q * 128, [[1, 128]]), xt.v(kc * 128, [[1, 128]]), self.cst.v(C_ID, [[1, 128]])),
                            reads=[xt.res, self.cst.res], writes=[pr], acc=(q > 0))
                    en = self.evac_eng()
                    self.copy(en, self.xT.v((half * 4) * self.NCOL + ti * 128, [[self.NCOL, 4], [1, 128]]),
                              self.ps(bank, 0, [[128, 4], [1, 128]]), reads=[pr], writes=[self.xT.res])
            S.barrier()

    def phaseA(self, es_pass):
        S = self.S
        NT, NCOL, kind = self.NT, self.NCOL, self.kind
        cst = self.cst
        with contextlib.ExitStack() as es:
            qT = self.sb(es, "qT", NCOL, BF16)
            kT = self.sb(es, "kT", NCOL, BF16)
            vaug = self.sb(es, "vaug", NT * 130, BF16)
            sg = self.sb(es, "sgA", NT * 128, BF16)
            tbf = self.sb(es, "tbf", 2 * 640)
            tb = self.sb(es, "tb", 2 * 640, BF16)
            PT = self.sb(es, "PT", 640, BF16)
            PTs = [self.sb(es, "PTs%d" % i, 640, BF16) for i in range(2)]
            rdens = [self.sb(es, "rdA%d" % i, 1) for i in range(2)]
            kvo = [self.sb(es, "kvo%d" % i, 256) for i in range(2)]
            rden = self.sb(es, "rdenA", 1)
            if kind == "S":
                ckf = self.sb(es, "ckf", 512)
                ckT = self.sb(es, "ckT", 512, BF16)
                cvf = self.sb(es, "cvf", 512)
                cvaug = self.sb(es, "cvaug", 4 * 130, BF16)
                tnf = self.sb(es, "tnf", 2 * 64)
                tn = self.sb(es, "tn", 2 * 64, BF16)
                S.op("vector", lambda e: e.memset(cvaug.v(), 1.0), writes=[cvaug.res])
            S.op("vector", lambda e: e.memset(vaug.v(), 1.0), writes=[vaug.res])
            kvo_i = 0
            for hp in range(8):
                slab = self.next_slab()
                for i, off in enumerate([OFF_QA, OFF_KA, OFF_VA, OFF_GA]):
                    self.load_slab(slab, self.w_in, off + hp * 128, 128, dst_off=i * 128)
                if kind == "P":
                    S.dma("sync", tbf.v(0, [[640, 2], [1, 640]]),
                          dr(self.biasP, (2 * hp) * 128 * 640, [[640, 128], [128 * 640, 2], [1, 640]]), writes=[tbf.res])
                    S.op("vector", lambda e: e.tensor_tensor(tbf.v(0, [[640, 2], [1, 640]]), tbf.v(0, [[640, 2], [1, 640]]),
                                                             cst.v(C_MASKA, [[0, 2], [1, 640]]), ALU.add),
                         reads=[tbf.res, cst.res], writes=[tbf.res])
                    S.op("scalar", lambda e: e.activation(tb.v(0, [[640, 2], [1, 640]]), tbf.v(0, [[640, 2], [1, 640]]), AF.Exp),
                         reads=[tbf.res], writes=[tb.res])
                else:
                    S.dma("sync", tbf.v(0, [[640, 2], [1, 256]]),
                          dr(self.biasS, (2 * hp) * 128 * 256, [[256, 128], [128 * 256, 2], [1, 256]]), writes=[tbf.res])
                    S.op("vector", lambda e: e.tensor_copy(tb.v(0, [[640, 2], [1, 256]]), tbf.v(0, [[640, 2], [1, 256]])),
                         reads=[tbf.res], writes=[tb.res])
                    S.dma("sync", tnf.v(0, [[1, 64]]),
                          dr(self.biasN, (2 * hp) * 64 * 64, [[64, 128], [1, 64]]), writes=[tnf.res])
                    S.op("vector", lambda e: e.tensor_tensor(tn.v(0, [[1, 64]]), tnf.v(0, [[1, 64]]), cst.v(C_MASKN, [[1, 64]]), ALU.add),
                         reads=[tnf.res, cst.res], writes=[tn.res])
                for bi, (c0, n) in enumerate(self.blocks):
                    for j, dst in enumerate([qT, kT]):
                        bank = (2 * bi + j) % 4
                        pr = self.psres[bank]
                        for kc in range(8):
                            S.op("tensor", lambda e, kc=kc, j=j, bank=bank: e.matmul(
                                self.ps(bank, 0, [[1, n]]), slab.v(kc * 512 + j * 128, [[1, 128]]), self.xT_v(kc, c0, n),
                                start=(kc == 0), stop=(kc == 7)),
                                reads=[slab.res, self.xT.res], writes=[pr], acc=(kc > 0))
                        if j == 0:
                            self.copy("scalar", dst.v(c0, [[1, n]]), self.ps(bank, 0, [[1, n]]), [pr], [dst.res], scale=0.125)
                        else:
                            self.copy("vector", dst.v(c0, [[1, n]]), self.ps(bank, 0, [[1, n]]), [pr], [dst.res])
                a_stop = getattr(self, "a_stop", 99)
                if a_stop < 1:
                    continue
                for ti in range(NT):
                    out_tile = (kind == "S") or (ti >= NT - 4)
                    bank = 4 + (ti % 2)
                    pr = self.psres[bank]
                    ncol = 384 if out_tile else 256
                    for kc in range(8):
                        S.op("tensor", lambda e, kc=kc, bank=bank: e.matmul(
                            self.ps(bank, 0, [[1, 256]]), self.xT_v(kc, ti * 128, 128), slab.v(kc * 512 + 256, [[1, 256]]),
                            start=(kc == 0), stop=(kc == 7)),
                            reads=[slab.res, self.xT.res], writes=[pr], acc=(kc > 0))
                    if out_tile:
                        for kc in range(8):
                            S.op("tensor", lambda e, kc=kc, bank=bank: e.matmul(
                                self.ps(bank, 256, [[1, 128]]), self.xT_v(kc, ti * 128, 128), slab.v(kc * 512 + 128, [[1, 128]]),
                                start=(kc == 0), stop=(kc == 7)),
                                reads=[slab.res, self.xT.res], writes=[pr], acc=True)
                    S.op("vector", lambda e, bank=bank: e.tensor_copy(vaug.v(ti * 130, [[65, 2], [1, 64]]),
                                                                      self.ps(bank, 0, [[64, 2], [1, 64]])),
                         reads=[pr], writes=[vaug.res])
                    S.op("scalar", lambda e, bank=bank: e.activation(sg.v(ti * 128, [[1, 128]]), self.ps(bank, 128, [[1, 128]]), AF.Sigmoid),
                         reads=[pr], writes=[sg.res])
                    if out_tile:
                        ko = kvo[kvo_i % 2]
                        kvo_i += 1
                        S.op("vector", lambda e, bank=bank: e.tensor_copy(ko.v(0, [[1, 128]]), self.ps(bank, 256, [[1, 128]])),
                             reads=[pr], writes=[ko.res])
                        S.op("scalar", lambda e, bank=bank: e.activation(ko.v(128, [[1, 128]]), self.ps(bank, 0, [[1, 128]]), AF.Copy),
                             reads=[pr], writes=[ko.res])
                        if kind == "P":
                            r0 = self.b * 512 + (ti - (NT - 4)) * 128
                            S.dma("sync", dr(self.akp, r0 * D + hp * 128, [[D, 128], [1, 128]]), ko.v(0, [[1, 128]]), reads=[ko.res])
                            S.dma("sync", dr(self.avp, r0 * D + hp * 128, [[D, 128], [1, 128]]), ko.v(128, [[1, 128]]), reads=[ko.res])
                        else:
                            for i in range(2):
                                sq = 2 * ti + i
                                r0 = sq * LC + (LC - 16)
                                S.dma("sync", dr(self.aks, r0 * D + hp * 128, [[D, 16], [1, 128]]),
                                      ko.v(0, [[1, 128]], 64 * i, 16), reads=[ko.res])
                                S.dma("sync", dr(self.avs, r0 * D + hp * 128, [[D, 16], [1, 128]]),
                                      ko.v(128, [[1, 128]], 64 * i, 16), reads=[ko.res])
                if a_stop < 2:
                    continue
                for ti in range(NT):
                    mr = self.mres[ti]
                    if kind == "P":
                        jlist = [j for j in range(5) if ti * 128 - 512 + 128 * j >= 0]

                        def unitA(h2, ti=ti, jlist=jlist, mr=mr):
                            pb = 64 * h2
                            b0 = 3 * h2
                            prs = [self.psres[b0], self.psres[b0 + 1]]
                            PTu = PTs[h2]
                            rd = rdens[h2]
                            first = True
                            for j in jlist:
                                kc0 = ti * 128 - 512 + 128 * j
                                bank = b0 if j < 4 else b0 + 1
                                S.op("tensor", lambda e, j=j, kc0=kc0, bank=bank: e.matmul(
                                    self.ps(bank, (j % 4) * 128, [[1, 128]]), kT.v(kc0, [[1, 128]], pb, 64),
                                    qT.v(ti * 128, [[1, 128]], pb, 64), start=True, stop=True),
                                    reads=[kT.res, qT.res], writes=prs, acc=(not first))
                                first = False
                            yield
                            j0, nj = jlist[0], len(jlist)
                            S.op("scalar", lambda e: e.activation(
                                PTu.v(j0 * 128, [[1, nj * 128]]), self.ps(b0, j0 * 128, [[1, nj * 128]]), AF.Exp),
                                reads=prs, writes=[PTu.res])
                            yield
                            S.op("vector", lambda e: e.tensor_tensor(
                                PTu.v(j0 * 128, [[1, nj * 128]]), PTu.v(j0 * 128, [[1, nj * 128]]),
                                tb.v(h2 * 640 + j0 * 128, [[1, nj * 128]]), ALU.mult),
                                reads=[PTu.res, tb.res], writes=[PTu.res])
                            yield
                            po = self.psres[b0 + 2]
                            for idx, j in enumerate(jlist):
                                kt = ti - 4 + j
                                S.op("tensor", lambda e, j=j, kt=kt, idx=idx: e.matmul(
                                    self.ps(b0 + 2, 0, [[1, 65]]), PTu.v(j * 128, [[1, 128]]),
                                    vaug.v(kt * 130 + h2 * 65, [[1, 65]]), start=(idx == 0), stop=(idx == nj - 1)),
                                    reads=[PTu.res, vaug.res], writes=[po], acc=(idx > 0))
                            yield
                            S.op("vector", lambda e: e.reciprocal(rd.v(), self.ps(b0 + 2, 64, [[1, 1]])),
                                 reads=[po], writes=[rd.res])
                            S.op("vector", lambda e: e.scalar_tensor_tensor(
                                self.merged.v(ti * D + (2 * hp + h2) * 64, [[1, 64]]), self.ps(b0 + 2, 0, [[1, 64]]),
                                rd.v(), sg.v(ti * 128 + h2 * 64, [[1, 64]]), ALU.mult, ALU.mult),
                                reads=[po, rd.res, sg.res], writes=[mr])

                        gens = [unitA(0), unitA(1)]
                        while gens:
                            nxt = []
                            for g_ in gens:
                                try:
                                    next(g_)
                                    nxt.append(g_)
                                except StopIteration:
                                    pass
                            gens = nxt
                    else:
                        for i in range(2):
                            sq = 2 * ti + i
                            c0 = 64 * i
                            S.dma("sync", ckf.v(0, [[128, 4], [1, 128]]),
                                  dr(self.cak, sq * LC * D + hp * 128, [[D, 128], [128 * D, 4], [1, 128]]), writes=[ckf.res])
                            S.dma("sync", cvf.v(0, [[128, 4], [1, 128]]),
                                  dr(self.cav, sq * LC * D + hp * 128, [[D, 128], [128 * D, 4], [1, 128]]), writes=[cvf.res])
                            prt = self.psres[4]
                            for j in range(4):
                                S.op("tensor", lambda e, j=j: e.transpose(self.ps(4, j * 128, [[1, 128]]), ckf.v(j * 128, [[1, 128]]),
                                                                          cst.v(C_ID, [[1, 128]])),
                                     reads=[ckf.res, cst.res], writes=[prt], acc=(j > 0))
                            S.op("vector", lambda e: e.tensor_copy(ckT.v(), self.ps(4, 0, [[1, 512]])), reads=[prt], writes=[ckT.res])
                            S.op("vector", lambda e: e.tensor_copy(cvaug.v(0, [[130, 4], [65, 2], [1, 64]]),
                                                                   cvf.v(0, [[128, 4], [64, 2], [1, 64]])),
                                 reads=[cvf.res], writes=[cvaug.res])
                            for h2 in range(2):
                                pb = 64 * h2
                                prs = [self.psres[0], self.psres[1]]
                                for j in range(4):
                                    S.op("tensor", lambda e, j=j, pb=pb: e.matmul(
                                        self.ps(0, j * 64, [[1, 64]]), ckT.v(j * 128, [[1, 128]], pb, 64),
                                        qT.v(ti * 128 + c0, [[1, 64]], pb, 64), start=True, stop=False),
                                        reads=[ckT.res, qT.res], writes=prs, acc=(j > 0))
                                    S.op("tensor", lambda e, j=j, h2=h2: e.matmul(
                                        self.ps(0, j * 64, [[1, 64]]), self.idb.v(),
                                        tb.v(h2 * 640 + j * 64, [[1, 64]]), start=False, stop=True),
                                        reads=[self.idb.res, tb.res], writes=prs, acc=True)
                                S.op("tensor", lambda e, pb=pb: e.matmul(
                                    self.ps(1, 0, [[1, 64]], c0, 64), kT.v(ti * 128 + c0, [[1, 64]], pb, 64),
                                    qT.v(ti * 128 + c0, [[1, 64]], pb, 64), start=True, stop=False),
                                    reads=[kT.res, qT.res], writes=prs, acc=True)
                                S.op("tensor", lambda e, pb=pb: e.matmul(
                                    self.ps(1, 0, [[1, 64]], c0, 64), self.idb.v(pb, [[1, 64]], pb, 64),
                                    tn.v(0, [[1, 64]], pb, 64), start=False, stop=True),
                                    reads=[self.idb.res, tn.res], writes=prs, acc=True)
                                S.op("scalar", lambda e: e.activation(PT.v(0, [[1, 256]]), self.ps(0, 0, [[1, 256]]), AF.Exp),
                                     reads=prs, writes=[PT.res])
                                S.op("scalar", lambda e: e.activation(PT.v(256, [[1, 64]], c0, 64), self.ps(1, 0, [[1, 64]], c0, 64), AF.Exp),
                                     reads=prs, writes=[PT.res])
                                po = self.psres[2 + h2]
                                for j in range(4):
                                    S.op("tensor", lambda e, j=j, h2=h2: e.matmul(
                                        self.ps(2 + h2, 0, [[1, 65]], c0, 64), PT.v(j * 64, [[1, 64]]),
                                        cvaug.v(j * 130 + h2 * 65, [[1, 65]]), start=(j == 0), stop=False),
                                        reads=[PT.res, cvaug.res], writes=[po], acc=(j > 0))
                                S.op("tensor", lambda e, h2=h2: e.matmul(
                                    self.ps(2 + h2, 0, [[1, 65]], c0, 64), PT.v(256, [[1, 64]], c0, 64),
                                    vaug.v(ti * 130 + h2 * 65, [[1, 65]], c0, 64), start=False, stop=True),
                                    reads=[PT.res, vaug.res], writes=[po], acc=True)
                                S.op("vector", lambda e, h2=h2: e.reciprocal(rden.v(0, [[1, 1]], c0, 64), self.ps(2 + h2, 64, [[1, 1]], c0, 64)),
                                     reads=[po], writes=[rden.res])
                                S.op("vector", lambda e, h2=h2: e.scalar_tensor_tensor(
                                    self.merged.v(ti * D + (2 * hp + h2) * 64, [[1, 64]], c0, 64), self.ps(2 + h2, 0, [[1, 64]], c0, 64),
                                    rden.v(0, [[1, 1]], c0, 64), sg.v(ti * 128 + h2 * 64, [[1, 64]], c0, 64), ALU.mult, ALU.mult),
                                    reads=[po, rden.res, sg.res], writes=[mr])
            if kind == "S" and getattr(self, "a_stop", 99) >= 3:
                for sq in range(self.NBS):
                    for src, dst in ((self.cak, self.aks), (self.cav, self.avs)):
                        rr = Res("cpy")
                        for part in range(4):
                            S.dma("sync", dr(dst, (sq * LC + part * 124) * D, [[D, 124], [1, D]]),
                                  dr(src, (sq * LC + 16 + part * 124) * D, [[D, 124], [1, D]]), writes=[rr])
            S.barrier()

    def phaseC(self, es_pass):
        S = self.S
        NT, NCOL, kind = self.NT, self.NCOL, self.kind
        cst = self.cst
        with contextlib.ExitStack() as es:
            mkT = self.sb(es, "mkT", 4 * 2 * 256, BF16)
            mvaug = self.sb(es, "mvaug", 2 * 4 * 257, BF16)
            qcT = self.sb(es, "qcT", 2 * NCOL, BF16)
            sgc = self.sb(es, "sgc", NT * 256, BF16)
            PT = self.sb(es, "PTc", 256, BF16)
            rden = self.sb(es, "rdenC", 1)
            otmp = self.sb(es, "otmpC", 256, BF16)
            S.op("vector", lambda e: e.memset(mvaug.v(), 1.0), writes=[mvaug.res])
            if kind == "P":
                with contextlib.ExitStack() as es2:
                    memf = [self.sb(es2, "memf%d" % i, D) for i in range(2)]
                    memT = self.sb(es2, "memT", 8 * 256, BF16)
                    kvst = [self.sb(es2, "kvst%d" % i, 512) for i in range(2)]
                    for mb in range(2):
                        S.dma("sync", memf[mb].v(), dr(self.memp, (self.b * 256 + mb * 128) * D, [[D, 128], [1, D]]), writes=[memf[mb].res])
                        for half in range(2):
                            bank = 2 * mb + half
                            pr = self.psres[bank]
                            for q in range(4):
                                kc = half * 4 + q
                                S.op("tensor", lambda e, kc=kc, q=q, bank=bank, mb=mb: e.transpose(
                                    self.ps(bank, q * 128, [[1, 128]]), memf[mb].v(kc * 128, [[1, 128]]), cst.v(C_ID, [[1, 128]])),
                                    reads=[memf[mb].res, cst.res], writes=[pr], acc=(q > 0))
                            self.copy(self.evac_eng(), memT.v((half * 4) * 256 + mb * 128, [[256, 4], [1, 128]]),
                                      self.ps(bank, 0, [[128, 4], [1, 128]]), [pr], [memT.res])
                    si = 0
                    for s in range(4):
                        slab = self.next_slab()
                        self.load_slab(slab, self.wkv, s * 512, 512)
                        isv = s // 2
                        h0 = (s % 2) * 2
                        for mb in range(2):
                            bank = 4 + (si % 2)
                            pr = self.psres[bank]
                            for kc in range(8):
                                S.op("tensor", lambda e, kc=kc, bank=bank, mb=mb: e.matmul(
                                    self.ps(bank, 0, [[1, 512]]), memT.v(kc * 256 + mb * 128, [[1, 128]]), slab.v(kc * 512, [[1, 512]]),
                                    start=(kc == 0), stop=(kc == 7)),
                                    reads=[memT.res, slab.res], writes=[pr], acc=(kc > 0))
                            st = kvst[si % 2]
                            si += 1
                            self.copy(self.evac_eng(), st.v(), self.ps(bank, 0, [[1, 512]]), [pr], [st.res])
                            dst = self.mvp if isv else self.mkp
                            S.dma("sync", dr(dst, (self.b * 256 + mb * 128) * D + s % 2 * 512, [[D, 128], [1, 512]]), st.v(), reads=[st.res])
                            if isv:
                                S.op("vector", lambda e, bank=bank, mb=mb, h0=h0: e.tensor_copy(
                                    mvaug.v(mb * 4 * 257 + h0 * 257, [[257, 2], [1, 256]]), self.ps(bank, 0, [[256, 2], [1, 256]])),
                                    reads=[pr], writes=[mvaug.res])
                        if not isv:
                            for cb in range(4):
                                bank = 6 + (cb % 2)
                                pr = self.psres[bank]
                                for kc in range(8):
                                    S.op("tensor", lambda e, kc=kc, bank=bank, cb=cb: e.matmul(
                                        self.ps(bank, 0, [[1, 256]]), slab.v(kc * 512 + cb * 128, [[1, 128]]), memT.v(kc * 256, [[1, 256]]),
                                        start=(kc == 0), stop=(kc == 7)),
                                        reads=[memT.res, slab.res], writes=[pr], acc=(kc > 0))
                                h = h0 + cb // 2
                                dc = cb % 2
                                self.copy(self.evac_eng(), mkT.v((h * 2 + dc) * 256, [[1, 256]]), self.ps(bank, 0, [[1, 256]]), [pr], [mkT.res])
                    S.barrier()
            with contextlib.ExitStack() as es2:
                if kind == "S":
                    cmf = self.sb(es2, "cmf", 2 * D)
                    mkTs = [self.sb(es2, "mkTs%d" % i, 4 * 2 * 256, BF16) for i in range(2)]
                    mvs = [self.sb(es2, "mvs%d" % i, 2 * 4 * 257, BF16) for i in range(2)]
                    for i in range(2):
                        S.op("vector", lambda e, i=i: e.memset(mvs[i].v(), 1.0), writes=[mvs[i].res])

                def load_seq_cache(sq, mk_, mv_):
                    S.dma("sync", cmf.v(0, [[D, 2], [1, D]]),
                          dr(self.cmk, sq * 256 * D, [[D, 128], [128 * D, 2], [1, D]]), writes=[cmf.res])
                    for mb in range(2):
                        for half in range(2):
                            bank = 6 + half
                            prt = self.psres[bank]
                            for q in range(4):
                                cb = half * 4 + q
                                S.op("tensor", lambda e, cb=cb, q=q, bank=bank, mb=mb: e.transpose(
                                    self.ps(bank, q * 128, [[1, 128]]), cmf.v(mb * D + cb * 128, [[1, 128]]), cst.v(C_ID, [[1, 128]])),
                                    reads=[cmf.res, cst.res], writes=[prt], acc=(q > 0))
                            self.copy(self.evac_eng(), mk_.v((half * 4) * 256 + mb * 128, [[256, 4], [1, 128]]),
                                      self.ps(bank, 0, [[128, 4], [1, 128]]), [prt], [mk_.res])
                    S.dma("sync", cmf.v(0, [[D, 2], [1, D]]),
                          dr(self.cmv, sq * 256 * D, [[D, 128], [128 * D, 2], [1, D]]), writes=[cmf.res])
                    S.op("vector", lambda e: e.tensor_copy(mv_.v(0, [[4 * 257, 2], [257, 4], [1, 256]]),
                                                           cmf.v(0, [[D, 2], [256, 4], [1, 256]])),
                         reads=[cmf.res], writes=[mv_.res])

                def headC(hc, tiles):
                    slab = self.next_slab()
                    self.load_slab(slab, self.w_in, OFF_QC + hc * 256, 256, dst_off=0)
                    self.load_slab(slab, self.w_in, OFF_GC + hc * 256, 256, dst_off=256)
                    for bi, (c0, n) in enumerate(self.blocks):
                        for dc in range(2):
                            bank = (2 * bi + dc) % 4
                            pr = self.psres[bank]
                            for kc in range(8):
                                S.op("tensor", lambda e, kc=kc, dc=dc, bank=bank, c0=c0, n=n: e.matmul(
                                    self.ps(bank, 0, [[1, n]]), slab.v(kc * 512 + dc * 128, [[1, 128]]), self.xT_v(kc, c0, n),
                                    start=(kc == 0), stop=(kc == 7)),
                                    reads=[slab.res, self.xT.res], writes=[pr], acc=(kc > 0))
                            self.copy(self.evac_eng(), qcT.v(dc * NCOL + c0, [[1, n]]), self.ps(bank, 0, [[1, n]]), [pr], [qcT.res], scale=0.0625)
                    for ti in tiles:
                        bank = 4 + (ti % 2)
                        pr = self.psres[bank]
                        for kc in range(8):
                            S.op("tensor", lambda e, kc=kc, bank=bank, ti=ti: e.matmul(
                                self.ps(bank, 0, [[1, 256]]), self.xT_v(kc, ti * 128, 128), slab.v(kc * 512 + 256, [[1, 256]]),
                                start=(kc == 0), stop=(kc == 7)),
                                reads=[slab.res, self.xT.res], writes=[pr], acc=(kc > 0))
                        S.op("scalar", lambda e, bank=bank, ti=ti: e.activation(sgc.v(ti * 256, [[1, 256]]), self.ps(bank, 0, [[1, 256]]), AF.Sigmoid),
                             reads=[pr], writes=[sgc.res])
                    for ti in tiles:
                        mr = self.mres[ti]
                        segs = [(0, 128, mkT, mvaug)] if kind == "P" else [(0, 64, mkTs[0], mvs[0]), (64, 64, mkTs[1], mvs[1])]
                        for (c0, nq, mk_, mv_) in segs:
                            prs = self.psres[0]
                            for mb in range(2):
                                for dc in range(2):
                                    S.op("tensor", lambda e, mb=mb, dc=dc, mk_=mk_, c0=c0, nq=nq, ti=ti: e.matmul(
                                        self.ps(0, mb * nq, [[1, nq]]), mk_.v((hc * 2 + dc) * 256 + mb * 128, [[1, 128]]),
                                        qcT.v(dc * NCOL + ti * 128 + c0, [[1, nq]]), start=(dc == 0), stop=(dc == 1)),
                                        reads=[mk_.res, qcT.res], writes=[prs], acc=not (mb == 0 and dc == 0))
                            S.op("scalar", lambda e, nq=nq: e.activation(PT.v(0, [[1, 2 * nq]]), self.ps(0, 0, [[1, 2 * nq]]), AF.Exp),
                                 reads=[prs], writes=[PT.res])
                            po = self.psres[1]
                            for mb in range(2):
                                S.op("tensor", lambda e, mb=mb, mv_=mv_, c0=c0, nq=nq: e.matmul(
                                    self.ps(1, 0, [[1, 257]], c0, nq), PT.v(mb * nq, [[1, nq]]),
                                    mv_.v(mb * 4 * 257 + hc * 257, [[1, 257]]), start=(mb == 0), stop=(mb == 1)),
                                    reads=[PT.res, mv_.res], writes=[po], acc=(mb > 0))
                            S.op("vector", lambda e, c0=c0, nq=nq: e.reciprocal(rden.v(0, [[1, 1]], c0, nq), self.ps(1, 256, [[1, 1]], c0, nq)),
                                 reads=[po], writes=[rden.res])
                            S.op("vector", lambda e, c0=c0, nq=nq, ti=ti: e.scalar_tensor_tensor(
                                otmp.v(0, [[1, 256]], c0, nq), self.ps(1, 0, [[1, 256]], c0, nq),
                                rden.v(0, [[1, 1]], c0, nq), sgc.v(ti * 256, [[1, 256]], c0, nq), ALU.mult, ALU.mult),
                                reads=[po, rden.res, sgc.res], writes=[otmp.res])
                            S.op("vector", lambda e, c0=c0, nq=nq, ti=ti: e.tensor_tensor(
                                self.merged.v(ti * D + hc * 256, [[1, 256]], c0, nq), self.merged.v(ti * D + hc * 256, [[1, 256]], c0, nq),
                                otmp.v(0, [[1, 256]], c0, nq), ALU.add),
                                reads=[otmp.res, mr], writes=[mr])

                if kind == "P":
                    for hc in range(4):
                        headC(hc, list(range(NT)))
                else:
                    for ti in range(NT):
                        for i in range(2):
                            load_seq_cache(2 * ti + i, mkTs[i], mvs[i])
                        for hc in range(4):
                            headC(hc, [ti])
                S.barrier()

    def phaseB(self, es_pass):
        S = self.S
        NT, NCOL, kind = self.NT, self.NCOL, self.kind
        cst = self.cst
        CI = lambda c: cst.v(c, [[1, 128]])
        with contextlib.ExitStack() as es:
            gt = self.sb(es, "gt", NT * 8)
            bt = self.sb(es, "bt", NT * 16)
            Gc = self.sb(es, "Gc", NT * 8)
            GD = self.sb(es, "GD", NT * 24)
            gend = self.sb(es, "gend", NT * 16)
            with contextlib.ExitStack() as es2:
                wab = self.sb(es2, "wab", 8 * 512, BF16)
                t1 = self.sb(es2, "t1", NT * 8)
                t2 = self.sb(es2, "t2", NT * 8)
                self.load_slab(wab, self.w_in, OFF_AB, 16)
                pr = self.psres[0]
                for ti in range(NT):
                    for kc in range(8):
                        S.op("tensor", lambda e, kc=kc, ti=ti: e.matmul(
                            self.ps(0, ti * 16, [[1, 16]]), self.xT_v(kc, ti * 128, 128), wab.v(kc * 512, [[1, 16]]),
                            start=(kc == 0), stop=(kc == 7)),
                            reads=[wab.res, self.xT.res], writes=[pr], acc=not (ti == 0 and kc == 0))
                S.op("vector", lambda e: e.tensor_tensor(t1.v(0, [[8, NT], [1, 8]]), self.ps(0, 0, [[16, NT], [1, 8]]),
                                                         self.dtb_bc.v(0, [[0, NT], [1, 8]]), ALU.add),
                     reads=[pr, self.dtb_bc.res], writes=[t1.res])
                S.op("scalar", lambda e: e.activation(t2.v(), t1.v(), AF.Abs), reads=[t1.res], writes=[t2.res])
                S.op("scalar", lambda e: e.activation(t2.v(), t2.v(), AF.Exp, scale=-1.0), reads=[t2.res], writes=[t2.res])
                S.op("scalar", lambda e: e.activation(t2.v(), t2.v(), AF.Ln, bias=1.0), reads=[t2.res], writes=[t2.res])
                S.op("vector", lambda e: e.scalar_tensor_tensor(t1.v(), t1.v(), 0.0, t2.v(), ALU.max, ALU.add),
                     reads=[t1.res, t2.res], writes=[t1.res])
                S.op("vector", lambda e: e.tensor_tensor(gt.v(0, [[8, NT], [1, 8]]), t1.v(0, [[8, NT], [1, 8]]),
                                                         self.nealog.v(0, [[0, NT], [1, 8]]), ALU.mult),
                     reads=[t1.res, self.nealog.res], writes=[gt.res])
                S.op("scalar", lambda e: e.activation(bt.v(0, [[16, NT], [2, 8]]), self.ps(0, 8, [[16, NT], [1, 8]]), AF.Sigmoid),
                     reads=[pr], writes=[bt.res])
                if kind == "S":
                    S.op("vector", lambda e: e.tensor_scalar(gt.v(), gt.v(), cst.v(C_ROWM, [[1, 1]]), None, ALU.mult),
                         reads=[gt.res, cst.res], writes=[gt.res])
                    S.op("vector", lambda e: e.tensor_scalar(bt.v(0, [[16, NT], [2, 8]]), bt.v(0, [[16, NT], [2, 8]]),
                                                             cst.v(C_ROWM, [[1, 1]]), None, ALU.mult),
                         reads=[bt.res, cst.res], writes=[bt.res])
                S.op("vector", lambda e: e.tensor_scalar(bt.v(1, [[16, NT], [2, 8]]), bt.v(0, [[16, NT], [2, 8]]), -1.0, None, ALU.mult),
                     reads=[bt.res], writes=[bt.res])
                pr1 = self.psres[1]
                for ti in range(NT):
                    for q, cc in enumerate([C_TRI, C_ONESBD, C_SEL0, C_SEL1]):
                        S.op("tensor", lambda e, ti=ti, q=q, cc=cc: e.matmul(
                            self.ps(1, ti * 32 + q * 8, [[1, 8]]), CI(cc), gt.v(ti * 8, [[1, 8]]), start=True, stop=True),
                            reads=[cst.res, gt.res], writes=[pr1], acc=not (ti == 0 and q == 0))
                S.op("vector", lambda e: e.tensor_copy(Gc.v(0, [[8, NT], [1, 8]]), self.ps(1, 0, [[32, NT], [1, 8]])),
                     reads=[pr1], writes=[Gc.res])
                S.op("vector", lambda e: e.memset(GD.v(), 1.0), writes=[GD.res])
                S.op("scalar", lambda e: e.activation(GD.v(1, [[24, NT], [3, 8]]), self.ps(1, 0, [[32, NT], [1, 8]]), AF.Exp),
                     reads=[pr1], writes=[GD.res])
                S.op("vector", lambda e: e.tensor_tensor(t1.v(0, [[8, NT], [1, 8]]), self.ps(1, 8, [[32, NT], [1, 8]]),
                                                         Gc.v(0, [[8, NT], [1, 8]]), ALU.subtract),
                     reads=[pr1, Gc.res], writes=[t1.res])
                S.op("scalar", lambda e: e.activation(GD.v(2, [[24, NT], [3, 8]]), t1.v(0, [[8, NT], [1, 8]]), AF.Exp),
                     reads=[t1.res], writes=[GD.res])
                S.op("scalar", lambda e: e.activation(gend.v(0, [[16, NT], [1, 16]]), self.ps(1, 16, [[32, NT], [1, 16]]), AF.Exp),
                     reads=[pr1], writes=[gend.res])
                S.barrier()
            G = self.GB if kind == "P" else 2
            ZW = NCOL + 3
            Dw = [self.sb(es, "Dw%d" % i, 12 * 128, BF16) for i in range(2)]
            bco = self.sb(es, "bco", 384)
            if kind == "S":
                stc = self.sb(es, "stc", 3072)
            HB = []
            for k in range(G):
                d = {}
                d["zT"] = self.sb(es, "zT%d" % k, 3 * ZW, BF16)
                d["sgn"] = self.sb(es, "sgn%d" % k, NT * 128, BF16)
                d["Sf"] = [self.sb(es, "Sf%d_%d" % (k, i), 128) for i in range(2 if kind == "S" else 1)]
                d["Sb"] = [self.sb(es, "Sb%d_%d" % (k, i), 128, BF16) for i in range(2 if kind == "S" else 1)]
                for nm, F, dt in [("ss", 2, F32), ("rn", 2, F32), ("cols", 4, F32), ("junk", 128, F32), ("tok5", 640, BF16),
                                  ("IDg", 256, BF16), ("feat3", 384, BF16), ("gbc", 128, F32), ("Dm", 128, F32), ("Lam", 128, F32),
                                  ("Nn", 128, BF16), ("NTt", 128, BF16), ("Xb", 128, BF16), ("aqk", 128, BF16), ("nwT", 128, BF16),
                                  ("wv", 128, BF16), ("sso", 2, F32), ("otmp", 128, BF16)]:
                    d[nm] = self.sb(es, "%s%d" % (nm, k), F, dt)
                d["PP"] = [self.sb(es, "PP%d_%d" % (k, i), 256, BF16) for i in range(2)]
                d["X"] = [self.sb(es, "X%d_%d" % (k, i), 128, BF16) for i in range(2)]
                S.op("vector", lambda e, t_=d["IDg"]: e.tensor_copy(t_.v(0, [[1, 128]]), self.idb.v()), reads=[self.idb.res], writes=[d["IDg"].res])
                d["bx"], d["by"] = 2 * k, 2 * k + 1
                d["t5r"] = [Res("t5_%d_%d" % (k, q)) for q in range(5)]
                d.update(gt=gt, bt=bt, Gc=Gc, GD=GD, gend=gend)
                HB.append(d)
            dwi = 0
            groups = [list(range(g0, min(8, g0 + G))) for g0 in range(0, 8, G)]
            for grp in groups:
                for k, hb in enumerate(grp):
                    d = HB[k]
                    zT, sgn, Sf, Sb = d["zT"], d["sgn"], d["Sf"], d["Sb"]
                    dw = Dw[dwi % 2]
                    dwi += 1
                    slab = self.next_slab()
                    for i, off in enumerate([OFF_QB, OFF_KB, OFF_VB, OFF_GB]):
                        self.load_slab(slab, self.w_in, off + hb * 128, 128, dst_off=i * 128)
                    for j in range(3):
                        for tap in range(4):
                            S.op("vector", lambda e, j=j, tap=tap, dw=dw, hb=hb: e.tensor_scalar(
                                dw.v((j * 4 + tap) * 128, [[1, 128]]), CI(C_ID), self.wc.v((j * 8 + hb) * 4 + tap, [[1, 1]]), None, ALU.mult),
                                reads=[cst.res, self.wc.res], writes=[dw.res])
                    if kind == "P":
                        S.op("vector", lambda e, zT=zT: e.memset(zT.v(0, [[ZW, 3], [1, 3]]), 0.0), writes=[zT.res])
                    for bi, (c0, n) in enumerate(self.blocks):
                        for j in range(3):
                            bank = (bi * 3 + j) % 2 * 2
                            pr = self.psres[bank]
                            for kc in range(8):
                                S.op("tensor", lambda e, kc=kc, j=j, bank=bank, c0=c0, n=n, slab=slab: e.matmul(
                                    self.ps(bank, 0, [[1, n]]), slab.v(kc * 512 + j * 128, [[1, 128]]), self.xT_v(kc, c0, n),
                                    start=(kc == 0), stop=(kc == 7)),
                                    reads=[slab.res, self.xT.res], writes=[pr], acc=(kc > 0))
                            self.copy(self.evac_eng(), zT.v(j * ZW + 3 + c0, [[1, n]]), self.ps(bank, 0, [[1, n]]), [pr], [zT.res])
                    if kind == "S":
                        for ti in range(NT):
                            S.dma("sync", stc.v(0, [[1, 3072]], 0, 6),
                                  dr(self.sconv, (2 * ti) * 3 * 3072, [[3072, 6], [1, 3072]]), writes=[stc.res])
                            pr = self.psres[0]
                            for j in range(3):
                                S.op("tensor", lambda e, j=j, hb=hb: e.matmul(
                                    self.ps(0, j * 6, [[1, 6]]), stc.v(j * 1024 + hb * 128, [[1, 128]], 0, 6),
                                    cst.v(C_ID, [[1, 6]], 0, 6), start=True, stop=True),
                                    reads=[stc.res, cst.res], writes=[pr], acc=(j > 0))
                            for i in range(2):
                                S.op("vector", lambda e, i=i, ti=ti, zT=zT: e.tensor_copy(
                                    zT.v(ti * 128 + 64 * i, [[ZW, 3], [1, 3]]), self.ps(0, 3 * i, [[6, 3], [1, 3]])),
                                    reads=[pr], writes=[zT.res])
                    segs = [(NCOL - 3, self.bcp, self.b)] if kind == "P" else \
                        [(ti * 128 + 64 * i + 13, self.bcs, 2 * ti + i) for ti in range(NT) for i in range(2)]
                    for (cc, dst, sq) in segs:
                        pr = self.psres[3]
                        for kc in range(8):
                            S.op("tensor", lambda e, kc=kc, cc=cc, slab=slab: e.matmul(
                                self.ps(3, 0, [[1, 384]], 0, 3), self.xT_v(kc, cc, 3), slab.v(kc * 512, [[1, 384]]),
                                start=(kc == 0), stop=(kc == 7)),
                                reads=[slab.res, self.xT.res], writes=[pr], acc=(kc > 0))
                        S.op("vector", lambda e: e.tensor_copy(bco.v(0, [[1, 384]], 0, 3), self.ps(3, 0, [[1, 384]], 0, 3)),
                             reads=[pr], writes=[bco.res])
                        S.dma("sync", dr(dst, sq * 3 * 3072 + hb * 128, [[3072, 3], [1024, 3], [1, 128]]),
                              bco.v(0, [[128, 3], [1, 128]], 0, 3), reads=[bco.res])
                    for bi, (c0, n) in enumerate(self.blocks):
                        for j in range(3):
                            bank = (bi * 3 + j) % 2 * 2
                            pr = self.psres[bank]
                            for tap in range(4):
                                S.op("tensor", lambda e, j=j, tap=tap, bank=bank, c0=c0, n=n, dw=dw, zT=zT: e.matmul(
                                    self.ps(bank, 0, [[1, n]]), dw.v((j * 4 + tap) * 128, [[1, 128]]), zT.v(j * ZW + c0 + tap, [[1, n]]),
                                    start=(tap == 0), stop=(tap == 3)),
                                    reads=[dw.res, zT.res], writes=[pr], acc=(tap > 0))
                            S.op("scalar", lambda e, j=j, bank=bank, c0=c0, n=n, zT=zT: e.activation(
                                zT.v(j * ZW + c0, [[1, n]]), self.ps(bank, 0, [[1, n]]), AF.Silu),
                                reads=[pr], writes=[zT.res])
                    for ti in range(NT):
                        bank = (ti % 2) * 2
                        pr = self.psres[bank]
                        for kc in range(8):
                            S.op("tensor", lambda e, kc=kc, bank=bank, ti=ti, slab=slab: e.matmul(
                                self.ps(bank, 0, [[1, 128]]), self.xT_v(kc, ti * 128, 128), slab.v(kc * 512 + 384, [[1, 128]]),
                                start=(kc == 0), stop=(kc == 7)),
                                reads=[slab.res, self.xT.res], writes=[pr], acc=(kc > 0))
                        S.op("scalar", lambda e, bank=bank, ti=ti, sgn=sgn: e.activation(sgn.v(ti * 128, [[1, 128]]), self.ps(bank, 0, [[1, 128]]), AF.Sigmoid),
                             reads=[pr], writes=[sgn.res])
                    S.op("vector", lambda e, sgn=sgn: e.tensor_tensor(sgn.v(0, [[128, NT], [1, 128]]), sgn.v(0, [[128, NT], [1, 128]]),
                                                                      self.normg_bc.v(0, [[0, NT], [1, 128]]), ALU.mult),
                         reads=[sgn.res, self.normg_bc.res], writes=[sgn.res])
                    if kind == "P":
                        S.op("vector", lambda e, Sf=Sf: e.memset(Sf[0].v(), 0.0), writes=[Sf[0].res])
                        S.op("vector", lambda e, Sb=Sb: e.memset(Sb[0].v(), 0.0), writes=[Sb[0].res])
                for ti in range(NT):
                    gens = [self.gdn_tile(hb, ti, HB[k]) for k, hb in enumerate(grp)]
                    while gens:
                        nxt = []
                        for g in gens:
                            try:
                                next(g)
                                nxt.append(g)
                            except StopIteration:
                                pass
                        gens = nxt
                if kind == "P":
                    for k, hb in enumerate(grp):
                        Sf = HB[k]["Sf"]
                        S.dma("sync", dr(self.bsp, ((self.b * 8 + hb) * 128) * 128, [[128, 128], [1, 128]]), Sf[0].v(), reads=[Sf[0].res])
            S.barrier()

    def gdn_tile(self, hb, ti, L):
        S = self.S
        cst = self.cst
        NT, NCOL, kind = self.NT, self.NCOL, self.kind
        ZW = NCOL + 3
        CI = lambda c: cst.v(c, [[1, 128]])
        (zT, Sf, Sb, ss, rn, cols, junk, tok5, IDg, feat3, gbc, Dm, Lam, Nn, NTt, PP, X, Xb, aqk, nwT, wv, sso, otmp,
         gt, bt, Gc, GD, gend, sgn) = [L[k] for k in
                                       "zT Sf Sb ss rn cols junk tok5 IDg feat3 gbc Dm Lam Nn NTt PP X Xb aqk nwT wv sso otmp gt bt Gc GD gend sgn".split()]
        P = self.psres
        ps = self.ps
        t5 = L["t5r"]
        bx, by = L["bx"], L["by"]
        cb = by
        for j in range(3):
            S.op("tensor", lambda e, j=j: e.matmul(ps(bx, j * 128, [[1, 128]]), zT.v(j * ZW + ti * 128, [[1, 128]]), self.idb.v(),
                                                   start=True, stop=True),
                 reads=[zT.res, self.idb.res], writes=[P[bx]], acc=(j > 0))
        yield
        for j in range(2):
            S.op("scalar", lambda e, j=j: e.activation(junk.v(), ps(bx, j * 128, [[1, 128]]), AF.Square, accum_out=ss.v(j, [[1, 1]])),
                 reads=[P[bx]], writes=[junk.res, ss.res])
        S.op("vector", lambda e: e.tensor_scalar(gbc.v(), CI(C_ONES), gt.v(ti * 8 + hb, [[1, 1]]), None, ALU.mult),
             reads=[cst.res, gt.res], writes=[gbc.res])
        S.op("vector", lambda e: e.tensor_scalar(IDg.v(128, [[1, 128]]), self.idb.v(), GD.v((ti * 8 + hb) * 3 + 1, [[1, 1]]), None, ALU.mult),
             reads=[self.idb.res, GD.res], writes=[IDg.res])
        S.op("tensor", lambda e: e.matmul(ps(by, 0, [[1, 128]]), gbc.v(), CI(C_TRI), start=True, stop=True),
             reads=[gbc.res, cst.res], writes=[P[by]])
        yield
        S.op("scalar", lambda e: e.activation(rn.v(), ss.v(), AF.Ln, bias=self.eps_l2.v()), reads=[ss.res, self.eps_l2.res], writes=[rn.res])
        S.op("scalar", lambda e: e.activation(rn.v(), rn.v(), AF.Exp, scale=-0.5), reads=[rn.res], writes=[rn.res])
        yield
        S.op("vector", lambda e: e.tensor_scalar(cols.v(0, [[1, 1]]), rn.v(0, [[1, 1]]), 128.0 ** -0.5, None, ALU.mult),
             reads=[rn.res], writes=[cols.res])
        S.op("vector", lambda e: e.tensor_scalar(cols.v(1, [[1, 3]]), GD.v((ti * 8 + hb) * 3, [[1, 3]]), rn.v(1, [[1, 1]]), None, ALU.mult),
             reads=[rn.res, GD.res], writes=[cols.res])
        srcs = [(0, 0), (1, 1), (2, None), (1, 2), (1, 3)]
        for q, (j, cidx) in enumerate(srcs):
            sc = None if cidx is None else cols.v(cidx, [[1, 1]])
            self.copy("vector", tok5.v(q * 128, [[1, 128]]), ps(bx, j * 128, [[1, 128]]), [P[bx], cols.res], [t5[q]], scale=sc)
        yield
        S.op("tensor", lambda e: e.matmul(ps(by, 128, [[1, 128]]), tok5.v(128, [[1, 128]]), self.idb.v(), start=True, stop=True),
             reads=[t5[1], self.idb.res], writes=[P[by]])
        S.op("tensor", lambda e: e.matmul(ps(by, 256, [[1, 256]]), tok5.v(0, [[1, 128]]), IDg.v(), start=True, stop=True),
             reads=[t5[0], IDg.res], writes=[P[by]], acc=True)
        yield
        self.copy("scalar", feat3.v(), ps(by, 128, [[1, 384]]), [P[by]], [feat3.res])
        S.op("vector", lambda e: e.scalar_tensor_tensor(Dm.v(), ps(by, 0, [[1, 128]]), Gc.v(ti * 8 + hb, [[1, 1]]), CI(C_MNEG),
                                                        ALU.subtract, ALU.add),
             reads=[P[by], Gc.res, cst.res], writes=[Dm.res])
        yield
        S.op("scalar", lambda e: e.activation(Lam.v(), Dm.v(), AF.Exp), reads=[Dm.res], writes=[Lam.res])
        S.op("tensor", lambda e: e.matmul(ps(bx, 0, [[1, 256]]), feat3.v(0, [[1, 128]]), feat3.v(0, [[1, 256]]), start=True, stop=True),
             reads=[feat3.res], writes=[P[bx]])
        yield
        S.op("vector", lambda e: e.tensor_tensor(Dm.v(), ps(bx, 0, [[1, 128]]), Lam.v(), ALU.mult),
             reads=[P[bx], Lam.res], writes=[Dm.res])
        S.op("vector", lambda e: e.scalar_tensor_tensor(Nn.v(), Dm.v(), bt.v((ti * 8 + hb) * 2 + 1, [[1, 1]]), CI(C_STRICT),
                                                        ALU.mult, ALU.mult),
             reads=[Dm.res, bt.res, cst.res], writes=[Nn.res])
        S.op("vector", lambda e: e.tensor_tensor(aqk.v(), ps(bx, 128, [[1, 128]]), Lam.v(), ALU.mult),
             reads=[P[bx], Lam.res], writes=[aqk.res])
        S.op("vector", lambda e: e.tensor_tensor(X[0].v(), Nn.v(), self.idb.v(), ALU.add), reads=[Nn.res, self.idb.res], writes=[X[0].res])
        S.op("tensor", lambda e: e.matmul(ps(cb, 0, [[1, 128]]), Nn.v(), self.idb.v(), start=True, stop=True),
             reads=[Nn.res, self.idb.res], writes=[P[cb]])
        yield
        self.copy("scalar", NTt.v(), ps(cb, 0, [[1, 128]]), [P[cb]], [NTt.res])
        yield
        Pm, PmT = Nn.v(), NTt.v()
        Pres, PTres = Nn.res, NTt.res
        xi = 0
        for lvl in range(1, 6):
            pp = PP[lvl % 2]
            last = (lvl == 5)
            S.op("tensor", lambda e, Pm=Pm, PmT=PmT: e.matmul(ps(cb, 128, [[1, 128]]), Pm, PmT, start=True, stop=True),
                 reads=[Pres, PTres], writes=[P[cb]])
            if not last:
                S.op("tensor", lambda e, Pm=Pm, PmT=PmT: e.matmul(ps(cb, 0, [[1, 128]]), PmT, Pm, start=True, stop=True),
                     reads=[Pres, PTres], writes=[P[cb]], acc=True)
            yield
            if not last:
                self.copy("scalar", pp.v(), ps(cb, 0, [[1, 256]]), [P[cb]], [pp.res])
            else:
                self.copy("scalar", pp.v(128, [[1, 128]]), ps(cb, 128, [[1, 128]]), [P[cb]], [pp.res])
            yield
            Pm, PmT = pp.v(0, [[1, 128]]), pp.v(128, [[1, 128]])
            Pres = PTres = pp.res
            xo, xn = X[xi], X[1 - xi]
            S.op("tensor", lambda e, PmT=PmT, xo=xo: e.matmul(ps(bx, 256, [[1, 128]]), PmT, xo.v(), start=True, stop=True),
                 reads=[pp.res, xo.res], writes=[P[bx]])
            yield
            if not last:
                S.op("vector", lambda e, xo=xo, xn=xn: e.tensor_tensor(xn.v(), ps(bx, 256, [[1, 128]]), xo.v(), ALU.add),
                     reads=[P[bx], xo.res], writes=[xn.res])
            else:
                S.op("vector", lambda e, xo=xo: e.tensor_tensor(Xb.v(), ps(bx, 256, [[1, 128]]), xo.v(), ALU.add),
                     reads=[P[bx], xo.res], writes=[Xb.res])
            yield
            xi = 1 - xi
        S.op("tensor", lambda e: e.matmul(ps(cb, 384, [[1, 128]]), tok5.v(3 * 128, [[1, 128]]), Xb.v(), start=True, stop=True),
             reads=[t5[3], Xb.res], writes=[P[cb]])
        yield
        self.copy("scalar", nwT.v(), ps(cb, 384, [[1, 128]]), [P[cb]], [nwT.res], scale=-1.0)
        yield
        for i in range(2):
            c0 = 64 * i
            si = i if kind == "S" else 0
            if kind == "S":
                sq = 2 * ti + i
                S.dma("sync", Sf[si].v(), dr(self.sssm, ((sq * 8 + hb) * 128) * 128, [[128, 128], [1, 128]]), writes=[Sf[si].res])
                self.copy("scalar", Sb[si].v(), Sf[si].v(), [Sf[si].res], [Sb[si].res])
            sfl, sbl = Sf[si], Sb[si]
            S.op("tensor", lambda e, c0=c0: e.matmul(ps(bx, 256, [[1, 128]], c0, 64), Xb.v(c0, [[1, 64]]), tok5.v(2 * 128, [[1, 128]]),
                                                     start=True, stop=False),
                 reads=[Xb.res, t5[2]], writes=[P[bx]])
            S.op("tensor", lambda e, c0=c0, sbl=sbl: e.matmul(ps(bx, 256, [[1, 128]], c0, 64), nwT.v(c0, [[1, 64]]), sbl.v(),
                                                              start=False, stop=True),
                 reads=[nwT.res, sbl.res], writes=[P[bx]], acc=True)
            yield
            S.op("vector", lambda e, c0=c0: e.tensor_scalar(wv.v(0, [[1, 128]], c0, 64), ps(bx, 256, [[1, 128]], c0, 64),
                                                            bt.v((ti * 8 + hb) * 2, [[1, 1]], c0, 64), None, ALU.mult),
                 reads=[P[bx], bt.res], writes=[wv.res])
            yield
            S.op("tensor", lambda e, c0=c0, sbl=sbl: e.matmul(ps(bx, 0, [[1, 128]], c0, 64), feat3.v(256 + c0, [[1, 64]]), sbl.v(),
                                                              start=True, stop=False),
                 reads=[feat3.res, sbl.res], writes=[P[bx]])
            S.op("tensor", lambda e, c0=c0: e.matmul(ps(bx, 0, [[1, 128]], c0, 64), aqk.v(c0, [[1, 64]], c0, 64), wv.v(0, [[1, 128]], c0, 64),
                                                     start=False, stop=True),
                 reads=[aqk.res, wv.res], writes=[P[bx]], acc=True)
            S.op("tensor", lambda e, c0=c0: e.matmul(ps(bx, 384, [[1, 128]]), tok5.v(4 * 128, [[1, 128]], c0, 64), wv.v(0, [[1, 128]], c0, 64),
                                                     start=True, stop=True),
                 reads=[t5[4], wv.res], writes=[P[bx]], acc=True)
            yield
            S.op("vector", lambda e, sfl=sfl, i=i: e.scalar_tensor_tensor(sfl.v(), sfl.v(), gend.v(ti * 16 + i * 8 + hb, [[1, 1]]),
                                                                          ps(bx, 384, [[1, 128]]), ALU.mult, ALU.add),
                 reads=[sfl.res, gend.res, P[bx]], writes=[sfl.res])
            self.copy("scalar", sbl.v(), sfl.v(), [sfl.res], [sbl.res])
            if kind == "S":
                S.dma("sync", dr(self.bss, ((sq * 8 + hb) * 128) * 128, [[128, 128], [1, 128]]), sfl.v(), reads=[sfl.res])
            yield
        S.op("scalar", lambda e: e.activation(junk.v(), ps(bx, 0, [[1, 128]]), AF.Square, accum_out=sso.v(0, [[1, 1]])),
             reads=[P[bx]], writes=[junk.res, sso.res])
        yield
        S.op("scalar", lambda e: e.activation(sso.v(1, [[1, 1]]), sso.v(0, [[1, 1]]), AF.Ln, bias=self.eps_rms.v(), scale=1.0 / 128.0),
             reads=[sso.res, self.eps_rms.res], writes=[sso.res])
        S.op("scalar", lambda e: e.activation(sso.v(1, [[1, 1]]), sso.v(1, [[1, 1]]), AF.Exp, scale=-0.5), reads=[sso.res], writes=[sso.res])
        yield
        S.op("vector", lambda e: e.scalar_tensor_tensor(otmp.v(), ps(bx, 0, [[1, 128]]), sso.v(1, [[1, 1]]), sgn.v(ti * 128, [[1, 128]]),
                                                        ALU.mult, ALU.mult),
             reads=[P[bx], sso.res, sgn.res], writes=[otmp.res])
        mr = self.mres[ti]
        S.op("vector", lambda e: e.tensor_tensor(self.merged.v(ti * D + hb * 128, [[1, 128]]), self.merged.v(ti * D + hb * 128, [[1, 128]]),
                                                 otmp.v(), ALU.add),
             reads=[otmp.res, mr], writes=[mr])

    def dump_merged(self):
        S = self.S
        with contextlib.ExitStack() as es:
            st = self.sb(es, "dbgst", D)
            for ti in range(self.NT):
                S.op("vector", lambda e, ti=ti: e.tensor_copy(st.v(), self.merged.v(ti * D, [[1, D]])),
                     reads=[self.mres[ti]], writes=[st.res])
                if self.kind == "P":
                    S.dma("sync", dr(self.dbg_mp, (self.b * self.T + ti * 128) * D, [[D, 128], [1, D]]), st.v(), reads=[st.res])
                else:
                    for i in range(2):
                        S.dma("sync", dr(self.dbg_ms, ((2 * ti + i) * 64) * D, [[D, 64], [1, D]]), st.v(0, [[1, D]], 64 * i, 64), reads=[st.res])
            S.barrier()

    def phaseD(self, es_pass):
        S = self.S
        NT, NCOL, kind = self.NT, self.NCOL, self.kind
        cst = self.cst
        CI = lambda c: cst.v(c, [[1, 128]])
        ps = self.ps
        P = self.psres
        with contextlib.ExitStack() as es:
            woutb = self.sb(es, "woutb", 8 * D, BF16)
            for q in range(2):
                S.dma("gpsimd", woutb.v(q * 512, [[D, 8], [1, 512]]),
                      dr(self.wout, q * 512, [[D, 128], [128 * D, 8], [1, 512]]), writes=[woutb.res])
            f1T = self.sb(es, "f1T", 32 * 512, BF16)
            hT = self.sb(es, "hT", 8 * 512, BF16)
            hn = self.sb(es, "hn", 4 * D)
            mT = self.sb(es, "mT", 8 * 128, BF16)
            xt = [self.sb(es, "xtD%d" % i, D) for i in range(2)]
            hp_ = self.sb(es, "hpD", D)
            st = self.sb(es, "stD", 8)
            junk = self.sb(es, "junkD", D)
            rl = self.sb(es, "rlD", 512)
            yo = [self.sb(es, "yoD%d" % i, D) for i in range(2)]
            yi = 0
            for bi, (c0, n) in enumerate(self.blocks):
                nt_b = n // 128
                for tl in range(nt_b):
                    ti = c0 // 128 + tl
                    mr = self.mres[ti]
                    for half in range(2):
                        for q in range(4):
                            kc = half * 4 + q
                            S.op("tensor", lambda e, kc=kc, q=q, half=half: e.matmul(
                                ps(half, q * 128, [[1, 128]]), self.merged.v(ti * D + kc * 128, [[1, 128]]), self.idb.v(),
                                start=True, stop=True),
                                reads=[mr, self.idb.res], writes=[P[half]], acc=(q > 0))
                        self.copy(self.evac_eng(), mT.v(half * 512, [[1, 512]]), ps(half, 0, [[1, 512]]), [P[half]], [mT.res])
                    x_ = xt[ti % 2]
                    self.x_tile_src(x_, ti)
                    for half in range(2):
                        for kc in range(8):
                            S.op("tensor", lambda e, kc=kc, half=half: e.matmul(
                                ps(2 + half, 0, [[1, 512]]), mT.v(kc * 128, [[1, 128]]), woutb.v(kc * D + half * 512, [[1, 512]]),
                                start=(kc == 0), stop=(kc == 7)),
                                reads=[mT.res, woutb.res], writes=[P[2 + half]], acc=(kc > 0))
                        S.op("vector", lambda e, half=half: e.scalar_tensor_tensor(
                            hp_.v(half * 512, [[1, 512]]), x_.v(half * 512, [[1, 512]]), ALPHA, ps(2 + half, 0, [[1, 512]]), ALU.mult, ALU.add),
                            reads=[x_.res, P[2 + half]], writes=[hp_.res])
                    self.layer_norm(hp_, hn.v(tl * D, [[1, D]]), hn.res, st, junk)
                    for half in range(2):
                        for q in range(4):
                            kc = half * 4 + q
                            S.op("tensor", lambda e, kc=kc, q=q, half=half: e.transpose(
                                ps(4 + half, q * 128, [[1, 128]]), hn.v(tl * D + kc * 128, [[1, 128]]), CI(C_ID)),
                                reads=[hn.res, cst.res], writes=[P[4 + half]], acc=(q > 0))
                        for q in range(4):
                            kc = half * 4 + q
                            S.op("vector", lambda e, kc=kc, q=q, half=half: e.tensor_scalar(
                                hT.v(kc * 512 + tl * 128, [[1, 128]]), ps(4 + half, q * 128, [[1, 128]]),
                                self.g1c.v(kc, [[1, 1]]), self.b1c.v(kc, [[1, 1]]), ALU.mult, ALU.add),
                                reads=[P[4 + half], self.g1c.res, self.b1c.res], writes=[hT.res])
                for s in range(8):
                    slab = self.next_slab()
                    self.load_slab(slab, self.wff1, s * 512, 512)
                    for q in range(4):
                        fc = s * 4 + q
                        bank = fc % 2
                        for kc in range(8):
                            S.op("tensor", lambda e, kc=kc, q=q, bank=bank: e.matmul(
                                ps(bank, 0, [[1, n]]), slab.v(kc * 512 + q * 128, [[1, 128]]), hT.v(kc * 512, [[1, n]]),
                                start=(kc == 0), stop=(kc == 7)),
                                reads=[slab.res, hT.res], writes=[P[bank]], acc=(kc > 0))
                        S.op("scalar", lambda e, fc=fc, bank=bank: e.activation(rl.v(0, [[1, n]]), ps(bank, 0, [[1, n]]), AF.Relu,
                                                                               bias=self.b1T.v(fc, [[1, 1]])),
                             reads=[P[bank], self.b1T.res], writes=[rl.res])
                        S.op("vector", lambda e, fc=fc: e.tensor_tensor(f1T.v(fc * 512, [[1, n]]), rl.v(0, [[1, n]]), rl.v(0, [[1, n]]), ALU.mult),
                             reads=[rl.res], writes=[f1T.res])
                for s in range(8):
                    slab = self.next_slab()
                    self.S.dma("gpsimd", slab.v(0, [[D, 4], [1, D]]),
                               dr(self.wff2, s * 512 * D, [[D, 128], [128 * D, 4], [1, D]]), writes=[slab.res])
                    for q in range(4):
                        fc = s * 4 + q
                        for tl in range(nt_b):
                            for half in range(2):
                                bank = tl * 2 + half
                                S.op("tensor", lambda e, fc=fc, q=q, tl=tl, half=half, bank=bank: e.matmul(
                                    ps(bank, 0, [[1, 512]]), f1T.v(fc * 512 + tl * 128, [[1, 128]]), slab.v(q * D + half * 512, [[1, 512]]),
                                    start=(fc == 0), stop=(fc == 31)),
                                    reads=[f1T.res, slab.res], writes=[P[bank]], acc=(fc > 0))
                for tl in range(nt_b):
                    ti = c0 // 128 + tl
                    S.op("vector", lambda e, tl=tl: e.tensor_tensor(hp_.v(), hn.v(tl * D, [[1, D]]), self.g1a.v(), ALU.mult),
                         reads=[hn.res, self.g1a.res], writes=[hp_.res])
                    S.op("vector", lambda e: e.tensor_tensor(hp_.v(), hp_.v(), self.c1.v(), ALU.add),
                         reads=[hp_.res, self.c1.res], writes=[hp_.res])
                    for half in range(2):
                        bank = tl * 2 + half
                        S.op("vector", lambda e, half=half, bank=bank: e.tensor_tensor(
                            hp_.v(half * 512, [[1, 512]]), hp_.v(half * 512, [[1, 512]]), ps(bank, 0, [[1, 512]]), ALU.add),
                            reads=[hp_.res, P[bank]], writes=[hp_.res])
                    y = yo[yi % 2]
                    yi += 1
                    self.layer_norm(hp_, y.v(), y.res, st, junk)
                    S.op("vector", lambda e: e.tensor_tensor(y.v(), y.v(), self.g2.v(), ALU.mult), reads=[y.res, self.g2.res], writes=[y.res])
                    S.op("vector", lambda e: e.tensor_tensor(y.v(), y.v(), self.b2.v(), ALU.add), reads=[y.res, self.b2.res], writes=[y.res])
                    if kind == "P":
                        S.dma("sync", dr(self.yp, (self.b * self.T + ti * 128) * D, [[D, 128], [1, D]]), y.v(), reads=[y.res])
                    else:
                        for i in range(2):
                            S.dma("sync", dr(self.ys, ((2 * ti + i) * 16) * D, [[D, 16], [1, D]]), y.v(0, [[1, D]], 64 * i, 16), reads=[y.res])
            S.barrier()

    def layer_norm(self, src, out_ap, out_res, st, junk):
        S = self.S
        S.op("scalar", lambda e: e.activation(junk.v(), src.v(), AF.Copy, accum_out=st.v(0, [[1, 1]])),
             reads=[src.res], writes=[junk.res, st.res])
        S.op("vector", lambda e: e.tensor_scalar(st.v(1, [[1, 1]]), st.v(0, [[1, 1]]), -1.0 / D, None, ALU.mult),
             reads=[st.res], writes=[st.res])
        S.op("vector", lambda e: e.tensor_scalar(src.v(), src.v(), st.v(1, [[1, 1]]), None, ALU.add),
             reads=[src.res, st.res], writes=[src.res])
        S.op("scalar", lambda e: e.activation(junk.v(), src.v(), AF.Square, accum_out=st.v(2, [[1, 1]])),
             reads=[src.res], writes=[junk.res, st.res])
        S.op("scalar", lambda e: e.activation(st.v(3, [[1, 1]]), st.v(2, [[1, 1]]), AF.Ln, bias=self.eps_ln.v(), scale=1.0 / D),
             reads=[st.res, self.eps_ln.res], writes=[st.res])
        S.op("scalar", lambda e: e.activation(st.v(3, [[1, 1]]), st.v(3, [[1, 1]]), AF.Exp, scale=-0.5), reads=[st.res], writes=[st.res])
        S.op("vector", lambda e: e.tensor_scalar(out_ap, src.v(), st.v(3, [[1, 1]]), None, ALU.mult),
             reads=[src.res, st.res], writes=[out_res])


_CACHE = {}


def _get_nc(NBP, T, NBS, debug=False):
    key = (NBP, T, NBS, debug)
    if key not in _CACHE:
        b = Builder(NBP, T, NBS, debug)
        _CACHE[key] = b.build()
    return _CACHE[key]


def make_in_maps(inp, n_cores, NBP, T, NBS):
    f = lambda a: np.ascontiguousarray(np.asarray(a, np.float32))
    consts = make_consts()
    bP, bS, bN = make_bias_tables(np.asarray(inp["a_rel_bias"])[0])
    shared = dict(
        w_in=f(inp["w_in"][0]), wconv=f(inp["w_b_conv"][0]), alog=f(inp["b_a_log"]).reshape(1, 8),
        dtb=f(inp["b_dt_bias"]).reshape(1, 8), normg=f(inp["b_norm_g"]).reshape(1, 128),
        biasP=bP.reshape(16 * 128, 640), biasS=bS.reshape(16 * 128, 256), biasN=bN.reshape(16 * 64, 64),
        wkv=f(inp["w_mem_kv"][0]), wout=f(inp["w_out"][0]), ln1g=f(inp["ln1_g"]).reshape(1, D), ln1b=f(inp["ln1_b"]).reshape(1, D),
        wff1=f(inp["w_ff1"][0]), bff1=f(inp["b_ff1"]).reshape(32, 128), wff2=f(inp["w_ff2"][0]), bff2=f(inp["b_ff2"]).reshape(1, D),
        ln2g=f(inp["ln2_g"]).reshape(1, D), ln2b=f(inp["ln2_b"]).reshape(1, D), consts=consts)
    maps = []
    for c in range(n_cores):
        ps_, ss_ = slice(c * NBP, (c + 1) * NBP), slice(c * NBS, (c + 1) * NBS)
        m = dict(shared)
        m["xp"] = f(inp["x_prompt"][ps_]).reshape(NBP * T, D)
        m["xs"] = f(inp["x_sample"][ss_]).reshape(NBS * 16, D)
        m["cak"] = f(inp["cache_a_k"][0, ss_]).reshape(NBS * LC, D)
        m["cav"] = f(inp["cache_a_v"][0, ss_]).reshape(NBS * LC, D)
        m["sconv"] = f(inp["state_b_conv"][0, ss_]).reshape(NBS * 3, 3072)
        m["sssm"] = f(inp["state_b_ssm"][0, ss_]).reshape(NBS * 8 * 128, 128)
        m["cmk"] = f(inp["cache_mem_k"][0, ss_]).reshape(NBS * 256, D)
        m["cmv"] = f(inp["cache_mem_v"][0, ss_]).reshape(NBS * 256, D)
        m["memp"] = f(inp["mem_prompt"][ps_]).reshape(NBP * 256, D)
        maps.append(m)
    return maps


def assemble(results, n_cores, NBP, T, NBS):
    cat = lambda k: np.concatenate([np.asarray(r[k]) for r in results], axis=0)
    B, BS = n_cores * NBP, n_cores * NBS
    yp = cat("yp").reshape(B, T, D)
    ys = cat("ys").reshape(BS, 16, D)
    akp = cat("akp").reshape(1, B, 512, 16, 64)
    avp = cat("avp").reshape(1, B, 512, 16, 64)
    bcp = cat("bcp").reshape(1, B, 3, 3072)
    bsp = cat("bsp").reshape(1, B, 8, 128, 128)
    mkp = cat("mkp").reshape(1, B, 256, 4, 256)
    mvp = cat("mvp").reshape(1, B, 256, 4, 256)
    aks = cat("aks").reshape(1, BS, 512, 16, 64)
    avs = cat("avs").reshape(1, BS, 512, 16, 64)
    bcs = cat("bcs").reshape(1, BS, 3, 3072)
    bss = cat("bss").reshape(1, BS, 8, 128, 128)
    return (yp, ys, akp, avp, bcp, bsp, mkp, mvp, aks, avs, bcs, bss)


def kernel(**inputs):
    n_cores = 8
    B, T = inputs["x_prompt"].shape[0], inputs["x_prompt"].shape[1]
    BS = inputs["x_sample"].shape[0]
    NBP, NBS = B // n_cores, BS // n_cores
    nc = _get_nc(NBP, T, NBS)
    maps = make_in_maps(inputs, n_cores, NBP, T, NBS)
    res = run_bass_kernel_spmd(nc, maps, core_ids=list(range(n_cores)))
    return assemble(res.results, n_cores, NBP, T, NBS)
```

```python
import contextlib
import numpy as np
import concourse.bass as bass
import concourse.mybir as mybir
from concourse.bass_utils import run_bass_kernel_spmd

F32 = mybir.dt.float32
BF16 = mybir.dt.bfloat16
AF = mybir.ActivationFunctionType
ALU = mybir.AluOpType

D = 1024
IN_COLS = 10256
OFF_QA, OFF_KA, OFF_VA = 0, 1024, 2048
OFF_QB, OFF_KB, OFF_VB = 3072, 4096, 5120
OFF_QC = 6144
OFF_GA, OFF_GB, OFF_GC = 7168, 8192, 9216
OFF_AB = 10240
ALPHA = 2.0 ** 0.25
NEG = -30000.0
LN_EPS = 1e-5
RMS_EPS = 1e-6
L2_EPS = 1e-6
PAST_LEN = 1024
LC = 512

C_ID = 0
C_TRI = 128
C_ONESBD = 256
C_SEL0 = 384
C_SEL1 = 512
C_MNEG = 640
C_STRICT = 768
C_ROWM = 896
C_ONES = 900
C_MASKA = 1028
C_MASKN = 1668
NCONST = 1732


def make_consts():
    c = np.zeros((128, NCONST), np.float32)
    r = np.arange(128)[:, None]
    t = np.arange(128)[None, :]
    same = (r // 64) == (t // 64)
    c[:, C_ID:C_ID + 128] = np.eye(128)
    c[:, C_TRI:C_TRI + 128] = (same & (r <= t))
    c[:, C_ONESBD:C_ONESBD + 128] = same
    c[:, C_SEL0:C_SEL0 + 128] = (r < 64) & (t >= 0)
    c[:, C_SEL1:C_SEL1 + 128] = (r >= 64) & (t >= 0)
    c[:, C_MNEG:C_MNEG + 128] = np.where(same & (r <= t), 0.0, -60000.0)
    c[:, C_STRICT:C_STRICT + 128] = (same & (r < t))
    c[:, C_ROWM] = ((np.arange(128) % 64) < 16)
    c[:, C_ONES:C_ONES + 128] = 1.0
    kk = np.arange(128)[:, None, None]
    j = np.arange(5)[None, :, None]
    qq = np.arange(128)[None, None, :]
    cq = qq // 64
    pos = 128 * j + kk
    valid = (pos >= 64 * cq) & (pos < 576 + 64 * cq)
    c[:, C_MASKA:C_MASKA + 640] = np.where(valid, 0.0, NEG).reshape(128, 640)
    mn = np.zeros((128, 64), np.float32)
    mn[(np.arange(128) % 64) >= 16, :] = NEG
    c[:, C_MASKN:C_MASKN + 64] = mn
    return c


def make_bias_tables(rel_bias):
    rb = np.asarray(rel_bias, np.float32)
    kk = np.arange(128)[:, None, None]
    j5 = np.arange(5)[None, :, None]
    qq = np.arange(128)[None, None, :]
    rel = 512 - 128 * j5 + qq - kk
    idx = np.clip(rel, -128, 128) + 128
    biasP = rb[:, idx].reshape(16, 128, 640)
    j4 = np.arange(4)[None, :, None]
    q64 = np.arange(64)[None, None, :]
    rel = 512 + q64 - 128 * j4 - kk
    idx = np.clip(rel, -128, 128) + 128
    biasS = rb[:, idx].reshape(16, 128, 256)
    k64 = np.arange(64)[:, None]
    q64 = np.arange(64)[None, :]
    idx = np.clip(q64 - k64, -128, 128) + 128
    biasN = rb[:, idx].reshape(16, 64, 64)
    return (np.ascontiguousarray(biasP), np.ascontiguousarray(biasS), np.ascontiguousarray(biasN))


class Res:
    __slots__ = ("name", "w", "r", "ds", "ps")

    def __init__(self, name, ps=False):
        self.name = name
        self.w = None
        self.r = []
        self.ds = {}
        self.ps = ps


class DSem:
    def __init__(self, sem):
        self.sem = sem
        self.cnt = 0


class Sync:
    ENG = ["tensor", "vector", "scalar", "gpsimd", "sync"]

    def __init__(self, nc, n_dma_sems=72):
        self.nc = nc
        self.E = {}
        for n in self.ENG:
            self.E[n] = dict(e=getattr(nc, n), sem=nc.alloc_semaphore(name="s_" + n), cnt=0, seen={})
        self.free_ds = {"hw": [DSem(nc.alloc_semaphore(name="d%d" % i)) for i in range(n_dma_sems)],
                        "sw": [DSem(nc.alloc_semaphore(name="q%d" % i)) for i in range(10)]}
        self.all_ds = self.free_ds["hw"] + self.free_ds["sw"]
        self.owned = []
        self.ninst = 0

    def _wait(self, en, deps):
        E = self.E[en]
        need = {}
        for (sem, val) in deps:
            k = sem.num
            if E["seen"].get(k, 0) >= val:
                continue
            if k not in need or need[k][1] < val:
                need[k] = (sem, val)
        for k, (sem, val) in need.items():
            E["e"].wait_ge(sem, val)
            E["seen"][k] = val
            self.ninst += 1

    @staticmethod
    def _deps(reads, writes, acc, own=None):
        deps = []
        for r in reads:
            if r.w is not None:
                deps.append(r.w)
            if r.ps:
                deps.extend(t for t in r.r if t[0].num != own)
        if not acc:
            for w in writes:
                if w.w is not None:
                    deps.append(w.w)
                deps.extend(w.r)
        return deps

    def op(self, en, fn, reads=(), writes=(), acc=False):
        E = self.E[en]
        self._wait(en, self._deps(reads, writes, acc, E["sem"].num))
        inst = fn(E["e"])
        E["cnt"] += 1
        inst.then_inc(E["sem"], 1)
        tok = (E["sem"], E["cnt"])
        for r in reads:
            r.r.append(tok)
        for w in writes:
            w.w = tok
            if not acc:
                w.r = []
        self.ninst += 1
        return inst

    def dma(self, en, out, in_, reads=(), writes=(), owner=None, **kw):
        E = self.E[en]
        self._wait(en, self._deps(reads, writes, False, None))
        if owner is None:
            owner = (list(writes) + list(reads))[0]
        qk = "sw" if en == "gpsimd" else "hw"
        if qk not in owner.ds:
            owner.ds[qk] = self.free_ds[qk].pop()
            self.owned.append((owner, qk))
        ds = owner.ds[qk]
        ds.cnt += 16
        inst = E["e"].dma_start(out=out, in_=in_, **kw)
        inst.then_inc(ds.sem, 16)
        tok = (ds.sem, ds.cnt)
        for r in reads:
            r.r.append(tok)
        for w in writes:
            w.w = tok
            w.r = []
        self.ninst += 1
        return inst

    def barrier(self, release=True):
        toks = [(self.E[n]["sem"], self.E[n]["cnt"]) for n in self.ENG if self.E[n]["cnt"] > 0]
        toks += [(d.sem, d.cnt) for d in self.all_ds if d.cnt > 0]
        for n in self.ENG:
            self._wait(n, toks)
        if release:
            for (o, qk) in self.owned:
                self.free_ds[qk].append(o.ds.pop(qk))
            self.owned = []


class Tl:
    def __init__(self, h, F, name):
        self.h = h
        self.F = F
        self.res = Res(name)

    def v(self, off=0, dims=None, p0=0, pn=128):
        if dims is None:
            dims = [[1, self.F - off]]
        return bass.AP(tensor=self.h, offset=p0 * self.F + off, ap=[[self.F, pn]] + [list(d) for d in dims])


def dr(t, off, dims):
    return bass.AP(tensor=t.tensor, offset=off, ap=[list(d) for d in dims])


class Builder:
    def __init__(self, NBP, T, NBS, debug=False):
        assert T % 512 == 0 and NBS % 2 == 0
        self.NBP, self.T, self.NBS = NBP, T, NBS
        self.debug = debug
        nc = bass.Bass("TRN2", target_bir_lowering=False)
        self.nc = nc
        self.S = Sync(nc)
        di = lambda n, s: nc.dram_tensor(n, s, F32, kind="ExternalInput").ap()
        do = lambda n, s: nc.dram_tensor(n, s, F32, kind="ExternalOutput").ap()
        self.xp = di("xp", [NBP * T, D])
        self.xs = di("xs", [NBS * 16, D])
        self.cak = di("cak", [NBS * LC, D])
        self.cav = di("cav", [NBS * LC, D])
        self.sconv = di("sconv", [NBS * 3, 3072])
        self.sssm = di("sssm", [NBS * 8 * 128, 128])
        self.cmk = di("cmk", [NBS * 256, D])
        self.cmv = di("cmv", [NBS * 256, D])
        self.memp = di("memp", [NBP * 256, D])
        self.w_in = di("w_in", [D, IN_COLS])
        self.wconv = di("wconv", [4, 3072])
        self.alog = di("alog", [1, 8])
        self.dtb = di("dtb", [1, 8])
        self.normg = di("normg", [1, 128])
        self.biasP = di("biasP", [16 * 128, 640])
        self.biasS = di("biasS", [16 * 128, 256])
        self.biasN = di("biasN", [16 * 64, 64])
        self.wkv = di("wkv", [D, 2048])
        self.wout = di("wout", [D, D])
        self.ln1g = di("ln1g", [1, D])
        self.ln1b = di("ln1b", [1, D])
        self.wff1 = di("wff1", [D, 4096])
        self.bff1 = di("bff1", [32, 128])
        self.wff2 = di("wff2", [4096, D])
        self.bff2 = di("bff2", [1, D])
        self.ln2g = di("ln2g", [1, D])
        self.ln2b = di("ln2b", [1, D])
        self.consts = di("consts", [128, NCONST])
        self.yp = do("yp", [NBP * T, D])
        self.ys = do("ys", [NBS * 16, D])
        self.akp = do("akp", [NBP * 512, D])
        self.avp = do("avp", [NBP * 512, D])
        self.bcp = do("bcp", [NBP * 3, 3072])
        self.bsp = do("bsp", [NBP * 8 * 128, 128])
        self.mkp = do("mkp", [NBP * 256, D])
        self.mvp = do("mvp", [NBP * 256, D])
        self.aks = do("aks", [NBS * LC, D])
        self.avs = do("avs", [NBS * LC, D])
        self.bcs = do("bcs", [NBS * 3, 3072])
        self.bss = do("bss", [NBS * 8 * 128, 128])
        if debug:
            self.dbg_mp = do("dbg_mp", [NBP * T, D])
            self.dbg_ms = do("dbg_ms", [NBS * 64, D])
        self.flip = 0
        self.GB = 4
        self.stagger = 11

    def sb(self, es, name, F, dt=F32):
        self.uid = getattr(self, "uid", 0) + 1
        name = "%s_%d" % (name, self.uid)
        h = es.enter_context(self.nc.sbuf_tensor(name, [128, F], dt))
        return Tl(h, F, name)

    def evac_eng(self):
        self.flip ^= 1
        return "vector" if self.flip else "scalar"

    def copy(self, en, out, in_, reads, writes, scale=None):
        S = self.S
        if en == "scalar":
            if scale is None:
                S.op("scalar", lambda e: e.activation(out, in_, AF.Copy), reads=reads, writes=writes)
            elif isinstance(scale, float):
                S.op("scalar", lambda e: e.activation(out, in_, AF.Copy, scale=scale), reads=reads, writes=writes)
            else:
                S.op("scalar", lambda e: e.activation(out, in_, AF.Copy, scale=scale), reads=reads, writes=writes)
        else:
            if scale is None:
                S.op(en, lambda e: e.tensor_copy(out, in_), reads=reads, writes=writes)
            else:
                S.op(en, lambda e: e.tensor_scalar(out, in_, scale, None, ALU.mult), reads=reads, writes=writes)

    def load_slab(self, slab, src, col0, ncols=512, rows0=0, kc=8, dst_off=0):
        W = src.tensor.shape[1]
        self.S.dma("gpsimd", slab.v(dst_off, [[512, kc], [1, ncols]]),
                   dr(src, rows0 * W + col0, [[W, 128], [128 * W, kc], [1, ncols]]),
                   writes=[slab.res])

    def build(self):
        nc, S = self.nc, self.S
        with contextlib.ExitStack() as es:
            ph = es.enter_context(nc.psum_tensor("ps", [128, 4096], F32))
            self.PS = [None] * 8
            self.psh = ph
            self.psres = [Res("psb%d" % i, ps=True) for i in range(8)]
            self.cst = self.sb(es, "cst", NCONST)
            S.dma("sync", self.cst.v(), dr(self.consts, 0, [[NCONST, 128], [1, NCONST]]), writes=[self.cst.res])
            self.idb = self.sb(es, "idb", 128, BF16)
            S.op("vector", lambda e: e.tensor_copy(self.idb.v(), self.cst.v(C_ID, [[1, 128]])),
                 reads=[self.cst.res], writes=[self.idb.res])
            self.eps_l2 = self.sb(es, "eps_l2", 1)
            self.eps_rms = self.sb(es, "eps_rms", 1)
            self.eps_ln = self.sb(es, "eps_ln", 1)
            for t_, v_ in ((self.eps_l2, L2_EPS), (self.eps_rms, RMS_EPS), (self.eps_ln, LN_EPS)):
                S.op("vector", lambda e, t_=t_, v_=v_: e.memset(t_.v(), v_), writes=[t_.res])
            self.slabs = [self.sb(es, "slab%d" % i, 4096, BF16) for i in range(3)]
            self.slab_i = 0
            self.setup_small(es)
            ok = getattr(self, "only_kind", "PS")
            if "P" in ok:
                for b in range(self.NBP):
                    self.run_pass(es, "P", b)
            if "S" in ok:
                self.run_pass(es, "S", 0)
            S.barrier(release=False)
        return nc

    def next_slab(self):
        s = self.slabs[self.slab_i % 3]
        self.slab_i += 1
        return s

    def ps(self, bank, off=0, dims=None, p0=0, pn=128):
        if dims is None:
            dims = [[1, 512 - off]]
        return bass.AP(tensor=self.psh, offset=p0 * 4096 + bank * 512 + off, ap=[[4096, pn]] + [list(d) for d in dims])

    def setup_small(self, es):
        nc, S = self.nc, self.S
        cst = self.cst
        self.dtb_bc = self.sb(es, "dtb_bc", 8)
        self.nealog = self.sb(es, "nealog", 8)
        self.normg_bc = self.sb(es, "normg_bc", 128)
        S.dma("sync", self.dtb_bc.v(), dr(self.dtb, 0, [[0, 128], [1, 8]]), writes=[self.dtb_bc.res])
        S.dma("sync", self.nealog.v(), dr(self.alog, 0, [[0, 128], [1, 8]]), writes=[self.nealog.res])
        S.dma("sync", self.normg_bc.v(), dr(self.normg, 0, [[0, 128], [1, 128]]), writes=[self.normg_bc.res])
        S.op("scalar", lambda e: e.activation(self.nealog.v(), self.nealog.v(), AF.Exp),
             reads=[self.nealog.res], writes=[self.nealog.res])
        S.op("vector", lambda e: e.tensor_scalar(self.nealog.v(), self.nealog.v(), -1.0, None, ALU.mult),
             reads=[self.nealog.res], writes=[self.nealog.res])
        self.wc = self.sb(es, "wc", 96)
        self.b1T = self.sb(es, "b1T", 32)
        with contextlib.ExitStack() as es2:
            wtok = self.sb(es2, "wtok", 3072)
            S.dma("sync", wtok.v(0, [[1, 3072]], 0, 4), dr(self.wconv, 0, [[3072, 4], [1, 3072]]), writes=[wtok.res])
            pr = self.psres[0]
            for blk in range(24):
                S.op("tensor", lambda e, blk=blk: e.matmul(self.ps(0, blk * 4, [[1, 4]]),
                                                           wtok.v(blk * 128, [[1, 128]], 0, 4),
                                                           cst.v(C_ID, [[1, 4]], 0, 4), start=True, stop=True),
                     reads=[wtok.res, cst.res], writes=[pr], acc=(blk > 0))
            S.op("vector", lambda e: e.tensor_copy(self.wc.v(), self.ps(0, 0, [[1, 96]])), reads=[pr], writes=[self.wc.res])
            btok = self.sb(es2, "btok", 128)
            S.dma("sync", btok.v(0, [[1, 128]], 0, 32), dr(self.bff1, 0, [[128, 32], [1, 128]]), writes=[btok.res])
            pr1 = self.psres[1]
            S.op("tensor", lambda e: e.matmul(self.ps(1, 0, [[1, 32]]), btok.v(0, [[1, 128]], 0, 32),
                                              cst.v(C_ID, [[1, 32]], 0, 32), start=True, stop=True),
                 reads=[btok.res, cst.res], writes=[pr1])
            S.op("vector", lambda e: e.tensor_copy(self.b1T.v(), self.ps(1, 0, [[1, 32]])), reads=[pr1], writes=[self.b1T.res])
            S.barrier()
        self.g1c = self.sb(es, "g1c", 8)
        self.b1c = self.sb(es, "b1c", 8)
        with contextlib.ExitStack() as es2:
            gtok = self.sb(es2, "gtok", 256)
            S.dma("sync", gtok.v(0, [[1, 128]], 0, 8), dr(self.ln1g, 0, [[128, 8], [1, 128]]), writes=[gtok.res])
            S.dma("sync", gtok.v(128, [[1, 128]], 0, 8), dr(self.ln1b, 0, [[128, 8], [1, 128]]), writes=[gtok.res])
            pr = self.psres[2]
            S.op("tensor", lambda e: e.matmul(self.ps(2, 0, [[1, 8]]), gtok.v(0, [[1, 128]], 0, 8),
                                              cst.v(C_ID, [[1, 8]], 0, 8), start=True, stop=True),
                 reads=[gtok.res, cst.res], writes=[pr])
            S.op("tensor", lambda e: e.matmul(self.ps(2, 8, [[1, 8]]), gtok.v(128, [[1, 128]], 0, 8),
                                              cst.v(C_ID, [[1, 8]], 0, 8), start=True, stop=True),
                 reads=[gtok.res, cst.res], writes=[pr], acc=True)
            S.op("vector", lambda e: e.tensor_copy(self.g1c.v(), self.ps(2, 0, [[1, 8]])), reads=[pr], writes=[self.g1c.res])
            S.op("vector", lambda e: e.tensor_copy(self.b1c.v(), self.ps(2, 8, [[1, 8]])), reads=[pr], writes=[self.b1c.res])
            S.barrier()

    def load_ln_rows(self, es):
        S = self.S
        self.g1a = self.sb(es, "g1a", D)
        self.c1 = self.sb(es, "c1", D)
        self.g2 = self.sb(es, "g2", D)
        self.b2 = self.sb(es, "b2", D)
        tmp = self.sb(es, "tmpbc", D)
        S.dma("sync", self.g1a.v(), dr(self.ln1g, 0, [[0, 128], [1, D]]), writes=[self.g1a.res])
        S.dma("sync", self.c1.v(), dr(self.ln1b, 0, [[0, 128], [1, D]]), writes=[self.c1.res])
        S.dma("sync", tmp.v(), dr(self.bff2, 0, [[0, 128], [1, D]]), writes=[tmp.res])
        S.dma("sync", self.g2.v(), dr(self.ln2g, 0, [[0, 128], [1, D]]), writes=[self.g2.res])
        S.dma("sync", self.b2.v(), dr(self.ln2b, 0, [[0, 128], [1, D]]), writes=[self.b2.res])
        S.op("vector", lambda e: e.tensor_scalar(self.g1a.v(), self.g1a.v(), ALPHA, None, ALU.mult),
             reads=[self.g1a.res], writes=[self.g1a.res])
        S.op("vector", lambda e: e.scalar_tensor_tensor(self.c1.v(), self.c1.v(), ALPHA, tmp.v(), ALU.mult, ALU.add),
             reads=[self.c1.res, tmp.res], writes=[self.c1.res])

    def run_pass(self, es_outer, kind, b):
        nc, S = self.nc, self.S
        T = self.T
        NT = (T // 128) if kind == "P" else (self.NBS // 2)
        NCOL = NT * 128
        blocks = [(c, min(512, NCOL - c)) for c in range(0, NCOL, 512)]
        self.kind, self.b, self.NT, self.NCOL, self.blocks = kind, b, NT, NCOL, blocks
        with contextlib.ExitStack() as es:
            self.merged = self.sb(es, "merged", NT * D, BF16)
            self.mres = [Res("mrg%d" % t) for t in range(NT)]
            stop = getattr(self, "stop_at", 99)
            with contextlib.ExitStack() as es1:
                self.xT = self.sb(es1, "xT", 8 * NCOL, BF16)
                if stop >= 1:
                    self.phase0(es1)
                if stop >= 2:
                    self.phaseA(es1)
                if stop >= 3:
                    self.phaseC(es1)
                if stop >= 4:
                    self.phaseB(es1)
                S.barrier()
            if self.debug and stop >= 4:
                self.dump_merged()
            if stop >= 5:
                self.phaseD(es)
            S.barrier()

    def xT_v(self, kc, c0, n):
        return self.xT.v(kc * self.NCOL + c0, [[1, n]])

    def x_tile_src(self, xt, ti, en="sync"):
        S = self.S
        if self.kind == "P":
            S.dma(en, xt.v(), dr(self.xp, (self.b * self.T + ti * 128) * D, [[D, 128], [1, D]]), writes=[xt.res])
        else:
            S.op("vector", lambda e: e.memset(xt.v(), 0.0), writes=[xt.res])
            for i in range(2):
                sq = 2 * ti + i
                S.dma(en, xt.v(0, [[1, D]], 64 * i, 16), dr(self.xs, sq * 16 * D, [[D, 16], [1, D]]), writes=[xt.res])

    def phase0(self, es):
        S = self.S
        with contextlib.ExitStack() as es2:
            xts = [self.sb(es2, "xt%d" % i, D) for i in range(2)]
            for ti in range(self.NT):
                xt = xts[ti % 2]
                self.x_tile_src(xt, ti)
                for half in range(2):
                    bank = (2 * ti + half) % 4
                    pr = self.psres[bank]
                    for q in range(4):
                        kc = half * 4 + q
                        S.op("tensor", lambda e, kc=kc, q=q, bank=bank: e.transpose(
                            self.ps(bank, q * 128, [[1, 128]]), xt.v(kc * 128, [[1, 128]]), self.cst.v(C_ID, [[1, 128]])),
                            reads=[xt.res, self.cst.res], writes=[pr], acc=(q > 0))
                    en = self.evac_eng()
                    self.copy(en, self.xT.v((half * 4) * self.NCOL + ti * 128, [[self.NCOL, 4], [1, 128]]),
                              self.ps(bank, 0, [[128, 4], [1, 128]]), reads=[pr], writes=[self.xT.res])
            S.barrier()

    def phaseA(self, es_pass):
        S = self.S
        NT, NCOL, kind = self.NT, self.NCOL, self.kind
        cst = self.cst
        with contextlib.ExitStack() as es:
            qT = self.sb(es, "qT", NCOL, BF16)
            kT = self.sb(es, "kT", NCOL, BF16)
            vaug = self.sb(es, "vaug", NT * 130, BF16)
            sg = self.sb(es, "sgA", NT * 128, BF16)
            tbf = self.sb(es, "tbf", 2 * 640)
            tb = self.sb(es, "tb", 2 * 640, BF16)
            PT = self.sb(es, "PT", 640, BF16)
            PTs = [self.sb(es, "PTs%d" % i, 640, BF16) for i in range(2)]
            rdens = [self.sb(es, "rdA%d" % i, 1) for i in range(2)]
            kvo = [self.sb(es, "kvo%d" % i, 256) for i in range(2)]
            rden = self.sb(es, "rdenA", 1)
            if kind == "S":
                ckf = self.sb(es, "ckf", 512)
                ckT = self.sb(es, "ckT", 512, BF16)
                cvf = self.sb(es, "cvf", 512)
                cvaug = self.sb(es, "cvaug", 4 * 130, BF16)
                tnf = self.sb(es, "tnf", 2 * 64)
                tn = self.sb(es, "tn", 2 * 64, BF16)
                S.op("vector", lambda e: e.memset(cvaug.v(), 1.0), writes=[cvaug.res])
            S.op("vector", lambda e: e.memset(vaug.v(), 1.0), writes=[vaug.res])
            kvo_i = 0
            for hp in range(8):
                slab = self.next_slab()
                for i, off in enumerate([OFF_QA, OFF_KA, OFF_VA, OFF_GA]):
                    self.load_slab(slab, self.w_in, off + hp * 128, 128, dst_off=i * 128)
                if kind == "P":
                    S.dma("sync", tbf.v(0, [[640, 2], [1, 640]]),
                          dr(self.biasP, (2 * hp) * 128 * 640, [[640, 128], [128 * 640, 2], [1, 640]]), writes=[tbf.res])
                    S.op("vector", lambda e: e.tensor_tensor(tbf.v(0, [[640, 2], [1, 640]]), tbf.v(0, [[640, 2], [1, 640]]),
                                                             cst.v(C_MASKA, [[0, 2], [1, 640]]), ALU.add),
                         reads=[tbf.res, cst.res], writes=[tbf.res])
                    S.op("scalar", lambda e: e.activation(tb.v(0, [[640, 2], [1, 640]]), tbf.v(0, [[640, 2], [1, 640]]), AF.Exp),
                         reads=[tbf.res], writes=[tb.res])
                else:
                    S.dma("sync", tbf.v(0, [[640, 2], [1, 256]]),
                          dr(self.biasS, (2 * hp) * 128 * 256, [[256, 128], [128 * 256, 2], [1, 256]]), writes=[tbf.res])
                    S.op("vector", lambda e: e.tensor_copy(tb.v(0, [[640, 2], [1, 256]]), tbf.v(0, [[640, 2], [1, 256]])),
                         reads=[tbf.res], writes=[tb.res])
                    S.dma("sync", tnf.v(0, [[1, 64]]),
                          dr(self.biasN, (2 * hp) * 64 * 64, [[64, 128], [1, 64]]), writes=[tnf.res])
                    S.op("vector", lambda e: e.tensor_tensor(tn.v(0, [[1, 64]]), tnf.v(0, [[1, 64]]), cst.v(C_MASKN, [[1, 64]]), ALU.add),
                         reads=[tnf.res, cst.res], writes=[tn.res])
                for bi, (c0, n) in enumerate(self.blocks):
                    for j, dst in enumerate([qT, kT]):
                        bank = (2 * bi + j) % 4
                        pr = self.psres[bank]
                        for kc in range(8):
                            S.op("tensor", lambda e, kc=kc, j=j, bank=bank: e.matmul(
                                self.ps(bank, 0, [[1, n]]), slab.v(kc * 512 + j * 128, [[1, 128]]), self.xT_v(kc, c0, n),
                                start=(kc == 0), stop=(kc == 7)),
                                reads=[slab.res, self.xT.res], writes=[pr], acc=(kc > 0))
                        if j == 0:
                            self.copy("scalar", dst.v(c0, [[1, n]]), self.ps(bank, 0, [[1, n]]), [pr], [dst.res], scale=0.125)
                        else:
                            self.copy("vector", dst.v(c0, [[1, n]]), self.ps(bank, 0, [[1, n]]), [pr], [dst.res])
                a_stop = getattr(self, "a_stop", 99)
                if a_stop < 1:
                    continue
                for ti in range(NT):
                    out_tile = (kind == "S") or (ti >= NT - 4)
                    bank = 4 + (ti % 2)
                    pr = self.psres[bank]
                    ncol = 384 if out_tile else 256
                    for kc in range(8):
                        S.op("tensor", lambda e, kc=kc, bank=bank: e.matmul(
                            self.ps(bank, 0, [[1, 256]]), self.xT_v(kc, ti * 128, 128), slab.v(kc * 512 + 256, [[1, 256]]),
                            start=(kc == 0), stop=(kc == 7)),
                            reads=[slab.res, self.xT.res], writes=[pr], acc=(kc > 0))
                    if out_tile:
                        for kc in range(8):
                            S.op("tensor", lambda e, kc=kc, bank=bank: e.matmul(
                                self.ps(bank, 256, [[1, 128]]), self.xT_v(kc, ti * 128, 128), slab.v(kc * 512 + 128, [[1, 128]]),
                                start=(kc == 0), stop=(kc == 7)),
                                reads=[slab.res, self.xT.res], writes=[pr], acc=True)
                    S.op("vector", lambda e, bank=bank: e.tensor_copy(vaug.v(ti * 130, [[65, 2], [1, 64]]),
                                                                      self.ps(bank, 0, [[64, 2], [1, 64]])),
                         reads=[pr], writes=[vaug.res])
                    S.op("scalar", lambda e, bank=bank: e.activation(sg.v(ti * 128, [[1, 128]]), self.ps(bank, 128, [[1, 128]]), AF.Sigmoid),
                         reads=[pr], writes=[sg.res])
                    if out_tile:
                        ko = kvo[kvo_i % 2]
                        kvo_i += 1
                        S.op("vector", lambda e, bank=bank: e.tensor_copy(ko.v(0, [[1, 128]]), self.ps(bank, 256, [[1, 128]])),
                             reads=[pr], writes=[ko.res])
                        S.op("scalar", lambda e, bank=bank: e.activation(ko.v(128, [[1, 128]]), self.ps(bank, 0, [[1, 128]]), AF.Copy),
                             reads=[pr], writes=[ko.res])
                        if kind == "P":
                            r0 = self.b * 512 + (ti - (NT - 4)) * 128
                            S.dma("sync", dr(self.akp, r0 * D + hp * 128, [[D, 128], [1, 128]]), ko.v(0, [[1, 128]]), reads=[ko.res])
                            S.dma("sync", dr(self.avp, r0 * D + hp * 128, [[D, 128], [1, 128]]), ko.v(128, [[1, 128]]), reads=[ko.res])
                        else:
                            for i in range(2):
                                sq = 2 * ti + i
                                r0 = sq * LC + (LC - 16)
                                S.dma("sync", dr(self.aks, r0 * D + hp * 128, [[D, 16], [1, 128]]),
                                      ko.v(0, [[1, 128]], 64 * i, 16), reads=[ko.res])
                                S.dma("sync", dr(self.avs, r0 * D + hp * 128, [[D, 16], [1, 128]]),
                                      ko.v(128, [[1, 128]], 64 * i, 16), reads=[ko.res])
                if a_stop < 2:
                    continue
                for ti in range(NT):
                    mr = self.mres[ti]
                    if kind == "P":
                        jlist = [j for j in range(5) if ti * 128 - 512 + 128 * j >= 0]

                        def unitA(h2, ti=ti, jlist=jlist, mr=mr):
                            pb = 64 * h2
                            b0 = 3 * h2
                            prs = [self.psres[b0], self.psres[b0 + 1]]
                            PTu = PTs[h2]
                            rd = rdens[h2]
                            first = True
                            for j in jlist:
                                kc0 = ti * 128 - 512 + 128 * j
                                bank = b0 if j < 4 else b0 + 1
                                S.op("tensor", lambda e, j=j, kc0=kc0, bank=bank: e.matmul(
                                    self.ps(bank, (j % 4) * 128, [[1, 128]]), kT.v(kc0, [[1, 128]], pb, 64),
                                    qT.v(ti * 128, [[1, 128]], pb, 64), start=True, stop=True),
                                    reads=[kT.res, qT.res], writes=prs, acc=(not first))
                                first = False
                            yield
                            j0, nj = jlist[0], len(jlist)
                            S.op("scalar", lambda e: e.activation(
                                PTu.v(j0 * 128, [[1, nj * 128]]), self.ps(b0, j0 * 128, [[1, nj * 128]]), AF.Exp),
                                reads=prs, writes=[PTu.res])
                            yield
                            S.op("vector", lambda e: e.tensor_tensor(
                                PTu.v(j0 * 128, [[1, nj * 128]]), PTu.v(j0 * 128, [[1, nj * 128]]),
                                tb.v(h2 * 640 + j0 * 128, [[1, nj * 128]]), ALU.mult),
                                reads=[PTu.res, tb.res], writes=[PTu.res])
                            yield
                            po = self.psres[b0 + 2]
                            for idx, j in enumerate(jlist):
                                kt = ti - 4 + j
                                S.op("tensor", lambda e, j=j, kt=kt, idx=idx: e.matmul(
                                    self.ps(b0 + 2, 0, [[1, 65]]), PTu.v(j * 128, [[1, 128]]),
                                    vaug.v(kt * 130 + h2 * 65, [[1, 65]]), start=(idx == 0), stop=(idx == nj - 1)),
                                    reads=[PTu.res, vaug.res], writes=[po], acc=(idx > 0))
                            yield
                            S.op("vector", lambda e: e.reciprocal(rd.v(), self.ps(b0 + 2, 64, [[1, 1]])),
                                 reads=[po], writes=[rd.res])
                            S.op("vector", lambda e: e.scalar_tensor_tensor(
                                self.merged.v(ti * D + (2 * hp + h2) * 64, [[1, 64]]), self.ps(b0 + 2, 0, [[1, 64]]),
                                rd.v(), sg.v(ti * 128 + h2 * 64, [[1, 64]]), ALU.mult, ALU.mult),
                                reads=[po, rd.res, sg.res], writes=[mr])

                        gens = [unitA(0), unitA(1)]
                        while gens:
                            nxt = []
                            for g_ in gens:
                                try:
                                    next(g_)
                                    nxt.append(g_)
                                except StopIteration:
                                    pass
                            gens = nxt
                    else:
                        for i in range(2):
                            sq = 2 * ti + i
                            c0 = 64 * i
                            S.dma("sync", ckf.v(0, [[128, 4], [1, 128]]),
                                  dr(self.cak, sq * LC * D + hp * 128, [[D, 128], [128 * D, 4], [1, 128]]), writes=[ckf.res])
                            S.dma("sync", cvf.v(0, [[128, 4], [1, 128]]),
                                  dr(self.cav, sq * LC * D + hp * 128, [[D, 128], [128 * D, 4], [1, 128]]), writes=[cvf.res])
                            prt = self.psres[4]
                            for j in range(4):
                                S.op("tensor", lambda e, j=j: e.transpose(self.ps(4, j * 128, [[1, 128]]), ckf.v(j * 128, [[1, 128]]),
                                                                          cst.v(C_ID, [[1, 128]])),
                                     reads=[ckf.res, cst.res], writes=[prt], acc=(j > 0))
                            S.op("vector", lambda e: e.tensor_copy(ckT.v(), self.ps(4, 0, [[1, 512]])), reads=[prt], writes=[ckT.res])
                            S.op("vector", lambda e: e.tensor_copy(cvaug.v(0, [[130, 4], [65, 2], [1, 64]]),
                                                                   cvf.v(0, [[128, 4], [64, 2], [1, 64]])),
                                 reads=[cvf.res], writes=[cvaug.res])
                            for h2 in range(2):
                                pb = 64 * h2
                                prs = [self.psres[0], self.psres[1]]
                                for j in range(4):
                                    S.op("tensor", lambda e, j=j, pb=pb: e.matmul(
                                        self.ps(0, j * 64, [[1, 64]]), ckT.v(j * 128, [[1, 128]], pb, 64),
                                        qT.v(ti * 128 + c0, [[1, 64]], pb, 64), start=True, stop=False),
                                        reads=[ckT.res, qT.res], writes=prs, acc=(j > 0))
                                    S.op("tensor", lambda e, j=j, h2=h2: e.matmul(
                                        self.ps(0, j * 64, [[1, 64]]), self.idb.v(),
                                        tb.v(h2 * 640 + j * 64, [[1, 64]]), start=False, stop=True),
                                        reads=[self.idb.res, tb.res], writes=prs, acc=True)
                                S.op("tensor", lambda e, pb=pb: e.matmul(
                                    self.ps(1, 0, [[1, 64]], c0, 64), kT.v(ti * 128 + c0, [[1, 64]], pb, 64),
                                    qT.v(ti * 128 + c0, [[1, 64]], pb, 64), start=True, stop=False),
                                    reads=[kT.res, qT.res], writes=prs, acc=True)
                                S.op("tensor", lambda e, pb=pb: e.matmul(
                                    self.ps(1, 0, [[1, 64]], c0, 64), self.idb.v(pb, [[1, 64]], pb, 64),
                                    tn.v(0, [[1, 64]], pb, 64), start=False, stop=True),
                                    reads=[self.idb.res, tn.res], writes=prs, acc=True)
                                S.op("scalar", lambda e: e.activation(PT.v(0, [[1, 256]]), self.ps(0, 0, [[1, 256]]), AF.Exp),
                                     reads=prs, writes=[PT.res])
                                S.op("scalar", lambda e: e.activation(PT.v(256, [[1, 64]], c0, 64), self.ps(1, 0, [[1, 64]], c0, 64), AF.Exp),
                                     reads=prs, writes=[PT.res])
                                po = self.psres[2 + h2]
                                for j in range(4):
                                    S.op("tensor", lambda e, j=j, h2=h2: e.matmul(
                                        self.ps(2 + h2, 0, [[1, 65]], c0, 64), PT.v(j * 64, [[1, 64]]),
                                        cvaug.v(j * 130 + h2 * 65, [[1, 65]]), start=(j == 0), stop=False),
                                        reads=[PT.res, cvaug.res], writes=[po], acc=(j > 0))
                                S.op("tensor", lambda e, h2=h2: e.matmul(
                                    self.ps(2 + h2, 0, [[1, 65]], c0, 64), PT.v(256, [[1, 64]], c0, 64),
                                    vaug.v(ti * 130 + h2 * 65, [[1, 65]], c0, 64), start=False, stop=True),
                                    reads=[PT.res, vaug.res], writes=[po], acc=True)
                                S.op("vector", lambda e, h2=h2: e.reciprocal(rden.v(0, [[1, 1]], c0, 64), self.ps(2 + h2, 64, [[1, 1]], c0, 64)),
                                     reads=[po], writes=[rden.res])
                                S.op("vector", lambda e, h2=h2: e.scalar_tensor_tensor(
                                    self.merged.v(ti * D + (2 * hp + h2) * 64, [[1, 64]], c0, 64), self.ps(2 + h2, 0, [[1, 64]], c0, 64),
                                    rden.v(0, [[1, 1]], c0, 64), sg.v(ti * 128 + h2 * 64, [[1, 64]], c0, 64), ALU.mult, ALU.mult),
                                    reads=[po, rden.res, sg.res], writes=[mr])
            if kind == "S" and getattr(self, "a_stop", 99) >= 3:
                for sq in range(self.NBS):
                    for src, dst in ((self.cak, self.aks), (self.cav, self.avs)):
                        rr = Res("cpy")
                        for part in range(4):
                            S.dma("sync", dr(dst, (sq * LC + part * 124) * D, [[D, 124], [1, D]]),
                                  dr(src, (sq * LC + 16 + part * 124) * D, [[D, 124], [1, D]]), writes=[rr])
            S.barrier()

    def phaseC(self, es_pass):
        S = self.S
        NT, NCOL, kind = self.NT, self.NCOL, self.kind
        cst = self.cst
        with contextlib.ExitStack() as es:
            mkT = self.sb(es, "mkT", 4 * 2 * 256, BF16)
            mvaug = self.sb(es, "mvaug", 2 * 4 * 257, BF16)
            qcT = self.sb(es, "qcT", 2 * NCOL, BF16)
            sgc = self.sb(es, "sgc", NT * 256, BF16)
            PT = self.sb(es, "PTc", 256, BF16)
            rden = self.sb(es, "rdenC", 1)
            otmp = self.sb(es, "otmpC", 256, BF16)
            S.op("vector", lambda e: e.memset(mvaug.v(), 1.0), writes=[mvaug.res])
            if kind == "P":
                with contextlib.ExitStack() as es2:
                    memf = [self.sb(es2, "memf%d" % i, D) for i in range(2)]
                    memT = self.sb(es2, "memT", 8 * 256, BF16)
                    kvst = [self.sb(es2, "kvst%d" % i, 512) for i in range(2)]
                    for mb in range(2):
                        S.dma("sync", memf[mb].v(), dr(self.memp, (self.b * 256 + mb * 128) * D, [[D, 128], [1, D]]), writes=[memf[mb].res])
                        for half in range(2):
                            bank = 2 * mb + half
                            pr = self.psres[bank]
                            for q in range(4):
                                kc = half * 4 + q
                                S.op("tensor", lambda e, kc=kc, q=q, bank=bank, mb=mb: e.transpose(
                                    self.ps(bank, q * 128, [[1, 128]]), memf[mb].v(kc * 128, [[1, 128]]), cst.v(C_ID, [[1, 128]])),
                                    reads=[memf[mb].res, cst.res], writes=[pr], acc=(q > 0))
                            self.copy(self.evac_eng(), memT.v((half * 4) * 256 + mb * 128, [[256, 4], [1, 128]]),
                                      self.ps(bank, 0, [[128, 4], [1, 128]]), [pr], [memT.res])
                    si = 0
                    for s in range(4):
                        slab = self.next_slab()
                        self.load_slab(slab, self.wkv, s * 512, 512)
                        isv = s // 2
                        h0 = (s % 2) * 2
                        for mb in range(2):
                            bank = 4 + (si % 2)
                            pr = self.psres[bank]
                            for kc in range(8):
                                S.op("tensor", lambda e, kc=kc, bank=bank, mb=mb: e.matmul(
                                    self.ps(bank, 0, [[1, 512]]), memT.v(kc * 256 + mb * 128, [[1, 128]]), slab.v(kc * 512, [[1, 512]]),
                                    start=(kc == 0), stop=(kc == 7)),
                                    reads=[memT.res, slab.res], writes=[pr], acc=(kc > 0))
                            st = kvst[si % 2]
                            si += 1
                            self.copy(self.evac_eng(), st.v(), self.ps(bank, 0, [[1, 512]]), [pr], [st.res])
                            dst = self.mvp if isv else self.mkp
                            S.dma("sync", dr(dst, (self.b * 256 + mb * 128) * D + s % 2 * 512, [[D, 128], [1, 512]]), st.v(), reads=[st.res])
                            if isv:
                                S.op("vector", lambda e, bank=bank, mb=mb, h0=h0: e.tensor_copy(
                                    mvaug.v(mb * 4 * 257 + h0 * 257, [[257, 2], [1, 256]]), self.ps(bank, 0, [[256, 2], [1, 256]])),
                                    reads=[pr], writes=[mvaug.res])
                        if not isv:
                            for cb in range(4):
                                bank = 6 + (cb % 2)
                                pr = self.psres[bank]
                                for kc in range(8):
                                    S.op("tensor", lambda e, kc=kc, bank=bank, cb=cb: e.matmul(
                                        self.ps(bank, 0, [[1, 256]]), slab.v(kc * 512 + cb * 128, [[1, 128]]), memT.v(kc * 256, [[1, 256]]),
                                        start=(kc == 0), stop=(kc == 7)),
                                        reads=[memT.res, slab.res], writes=[pr], acc=(kc > 0))
                                h = h0 + cb // 2
                                dc = cb % 2
                                self.copy(self.evac_eng(), mkT.v((h * 2 + dc) * 256, [[1, 256]]), self.ps(bank, 0, [[1, 256]]), [pr], [mkT.res])
                    S.barrier()
            with contextlib.ExitStack() as es2:
                if kind == "S":
                    cmf = self.sb(es2, "cmf", 2 * D)
                    mkTs = [self.sb(es2, "mkTs%d" % i, 4 * 2 * 256, BF16) for i in range(2)]
                    mvs = [self.sb(es2, "mvs%d" % i, 2 * 4 * 257, BF16) for i in range(2)]
                    for i in range(2):
                        S.op("vector", lambda e, i=i: e.memset(mvs[i].v(), 1.0), writes=[mvs[i].res])

                def load_seq_cache(sq, mk_, mv_):
                    S.dma("sync", cmf.v(0, [[D, 2], [1, D]]),
                          dr(self.cmk, sq * 256 * D, [[D, 128], [128 * D, 2], [1, D]]), writes=[cmf.res])
                    for mb in range(2):
                        for half in range(2):
                            bank = 6 + half
                            prt = self.psres[bank]
                            for q in range(4):
                                cb = half * 4 + q
                                S.op("tensor", lambda e, cb=cb, q=q, bank=bank, mb=mb: e.transpose(
                                    self.ps(bank, q * 128, [[1, 128]]), cmf.v(mb * D + cb * 128, [[1, 128]]), cst.v(C_ID, [[1, 128]])),
                                    reads=[cmf.res, cst.res], writes=[prt], acc=(q > 0))
                            self.copy(self.evac_eng(), mk_.v((half * 4) * 256 + mb * 128, [[256, 4], [1, 128]]),
                                      self.ps(bank, 0, [[128, 4], [1, 128]]), [prt], [mk_.res])
                    S.dma("sync", cmf.v(0, [[D, 2], [1, D]]),
                          dr(self.cmv, sq * 256 * D, [[D, 128], [128 * D, 2], [1, D]]), writes=[cmf.res])
                    S.op("vector", lambda e: e.tensor_copy(mv_.v(0, [[4 * 257, 2], [257, 4], [1, 256]]),
                                                           cmf.v(0, [[D, 2], [256, 4], [1, 256]])),
                         reads=[cmf.res], writes=[mv_.res])

                def headC(hc, tiles):
                    slab = self.next_slab()
                    self.load_slab(slab, self.w_in, OFF_QC + hc * 256, 256, dst_off=0)
                    self.load_slab(slab, self.w_in, OFF_GC + hc * 256, 256, dst_off=256)
                    for bi, (c0, n) in enumerate(self.blocks):
                        for dc in range(2):
                            bank = (2 * bi + dc) % 4
                            pr = self.psres[bank]
                            for kc in range(8):
                                S.op("tensor", lambda e, kc=kc, dc=dc, bank=bank, c0=c0, n=n: e.matmul(
                                    self.ps(bank, 0, [[1, n]]), slab.v(kc * 512 + dc * 128, [[1, 128]]), self.xT_v(kc, c0, n),
                                    start=(kc == 0), stop=(kc == 7)),
                                    reads=[slab.res, self.xT.res], writes=[pr], acc=(kc > 0))
                            self.copy(self.evac_eng(), qcT.v(dc * NCOL + c0, [[1, n]]), self.ps(bank, 0, [[1, n]]), [pr], [qcT.res], scale=0.0625)
                    for ti in tiles:
                        bank = 4 + (ti % 2)
                        pr = self.psres[bank]
                        for kc in range(8):
                            S.op("tensor", lambda e, kc=kc, bank=bank, ti=ti: e.matmul(
                                self.ps(bank, 0, [[1, 256]]), self.xT_v(kc, ti * 128, 128), slab.v(kc * 512 + 256, [[1, 256]]),
                                start=(kc == 0), stop=(kc == 7)),
                                reads=[slab.res, self.xT.res], writes=[pr], acc=(kc > 0))
                        S.op("scalar", lambda e, bank=bank, ti=ti: e.activation(sgc.v(ti * 256, [[1, 256]]), self.ps(bank, 0, [[1, 256]]), AF.Sigmoid),
                             reads=[pr], writes=[sgc.res])
                    for ti in tiles:
                        mr = self.mres[ti]
                        segs = [(0, 128, mkT, mvaug)] if kind == "P" else [(0, 64, mkTs[0], mvs[0]), (64, 64, mkTs[1], mvs[1])]
                        for (c0, nq, mk_, mv_) in segs:
                            prs = self.psres[0]
                            for mb in range(2):
                                for dc in range(2):
                                    S.op("tensor", lambda e, mb=mb, dc=dc, mk_=mk_, c0=c0, nq=nq, ti=ti: e.matmul(
                                        self.ps(0, mb * nq, [[1, nq]]), mk_.v((hc * 2 + dc) * 256 + mb * 128, [[1, 128]]),
                                        qcT.v(dc * NCOL + ti * 128 + c0, [[1, nq]]), start=(dc == 0), stop=(dc == 1)),
                                        reads=[mk_.res, qcT.res], writes=[prs], acc=not (mb == 0 and dc == 0))
                            S.op("scalar", lambda e, nq=nq: e.activation(PT.v(0, [[1, 2 * nq]]), self.ps(0, 0, [[1, 2 * nq]]), AF.Exp),
                                 reads=[prs], writes=[PT.res])
                            po = self.psres[1]
                            for mb in range(2):
                                S.op("tensor", lambda e, mb=mb, mv_=mv_, c0=c0, nq=nq: e.matmul(
                                    self.ps(1, 0, [[1, 257]], c0, nq), PT.v(mb * nq, [[1, nq]]),
                                    mv_.v(mb * 4 * 257 + hc * 257, [[1, 257]]), start=(mb == 0), stop=(mb == 1)),
                                    reads=[PT.res, mv_.res], writes=[po], acc=(mb > 0))
                            S.op("vector", lambda e, c0=c0, nq=nq: e.reciprocal(rden.v(0, [[1, 1]], c0, nq), self.ps(1, 256, [[1, 1]], c0, nq)),
                                 reads=[po], writes=[rden.res])
                            S.op("vector", lambda e, c0=c0, nq=nq, ti=ti: e.scalar_tensor_tensor(
                                otmp.v(0, [[1, 256]], c0, nq), self.ps(1, 0, [[1, 256]], c0, nq),
                                rden.v(0, [[1, 1]], c0, nq), sgc.v(ti * 256, [[1, 256]], c0, nq), ALU.mult, ALU.mult),
                                reads=[po, rden.res, sgc.res], writes=[otmp.res])
                            S.op("vector", lambda e, c0=c0, nq=nq, ti=ti: e.tensor_tensor(
                                self.merged.v(ti * D + hc * 256, [[1, 256]], c0, nq), self.merged.v(ti * D + hc * 256, [[1, 256]], c0, nq),
                                otmp.v(0, [[1, 256]], c0, nq), ALU.add),
                                reads=[otmp.res, mr], writes=[mr])

                if kind == "P":
                    for hc in range(4):
                        headC(hc, list(range(NT)))
                else:
                    for ti in range(NT):
                        for i in range(2):
                            load_seq_cache(2 * ti + i, mkTs[i], mvs[i])
                        for hc in range(4):
                            headC(hc, [ti])
                S.barrier()

    def phaseB(self, es_pass):
        S = self.S
        NT, NCOL, kind = self.NT, self.NCOL, self.kind
        cst = self.cst
        CI = lambda c: cst.v(c, [[1, 128]])
        with contextlib.ExitStack() as es:
            gt = self.sb(es, "gt", NT * 8)
            bt = self.sb(es, "bt", NT * 16)
            Gc = self.sb(es, "Gc", NT * 8)
            GD = self.sb(es, "GD", NT * 24)
            gend = self.sb(es, "gend", NT * 16)
            with contextlib.ExitStack() as es2:
                wab = self.sb(es2, "wab", 8 * 512, BF16)
                t1 = self.sb(es2, "t1", NT * 8)
                t2 = self.sb(es2, "t2", NT * 8)
                self.load_slab(wab, self.w_in, OFF_AB, 16)
                pr = self.psres[0]
                for ti in range(NT):
                    for kc in range(8):
                        S.op("tensor", lambda e, kc=kc, ti=ti: e.matmul(
                            self.ps(0, ti * 16, [[1, 16]]), self.xT_v(kc, ti * 128, 128), wab.v(kc * 512, [[1, 16]]),
                            start=(kc == 0), stop=(kc == 7)),
                            reads=[wab.res, self.xT.res], writes=[pr], acc=not (ti == 0 and kc == 0))
                S.op("vector", lambda e: e.tensor_tensor(t1.v(0, [[8, NT], [1, 8]]), self.ps(0, 0, [[16, NT], [1, 8]]),
                                                         self.dtb_bc.v(0, [[0, NT], [1, 8]]), ALU.add),
                     reads=[pr, self.dtb_bc.res], writes=[t1.res])
                S.op("scalar", lambda e: e.activation(t2.v(), t1.v(), AF.Abs), reads=[t1.res], writes=[t2.res])
                S.op("scalar", lambda e: e.activation(t2.v(), t2.v(), AF.Exp, scale=-1.0), reads=[t2.res], writes=[t2.res])
                S.op("scalar", lambda e: e.activation(t2.v(), t2.v(), AF.Ln, bias=1.0), reads=[t2.res], writes=[t2.res])
                S.op("vector", lambda e: e.scalar_tensor_tensor(t1.v(), t1.v(), 0.0, t2.v(), ALU.max, ALU.add),
                     reads=[t1.res, t2.res], writes=[t1.res])
                S.op("vector", lambda e: e.tensor_tensor(gt.v(0, [[8, NT], [1, 8]]), t1.v(0, [[8, NT], [1, 8]]),
                                                         self.nealog.v(0, [[0, NT], [1, 8]]), ALU.mult),
                     reads=[t1.res, self.nealog.res], writes=[gt.res])
                S.op("scalar", lambda e: e.activation(bt.v(0, [[16, NT], [2, 8]]), self.ps(0, 8, [[16, NT], [1, 8]]), AF.Sigmoid),
                     reads=[pr], writes=[bt.res])
                if kind == "S":
                    S.op("vector", lambda e: e.tensor_scalar(gt.v(), gt.v(), cst.v(C_ROWM, [[1, 1]]), None, ALU.mult),
                         reads=[gt.res, cst.res], writes=[gt.res])
                    S.op("vector", lambda e: e.tensor_scalar(bt.v(0, [[16, NT], [2, 8]]), bt.v(0, [[16, NT], [2, 8]]),
                                                             cst.v(C_ROWM, [[1, 1]]), None, ALU.mult),
                         reads=[bt.res, cst.res], writes=[bt.res])
                S.op("vector", lambda e: e.tensor_scalar(bt.v(1, [[16, NT], [2, 8]]), bt.v(0, [[16, NT], [2, 8]]), -1.0, None, ALU.mult),
                     reads=[bt.res], writes=[bt.res])
                pr1 = self.psres[1]
                for ti in range(NT):
                    for q, cc in enumerate([C_TRI, C_ONESBD, C_SEL0, C_SEL1]):
                        S.op("tensor", lambda e, ti=ti, q=q, cc=cc: e.matmul(
                            self.ps(1, ti * 32 + q * 8, [[1, 8]]), CI(cc), gt.v(ti * 8, [[1, 8]]), start=True, stop=True),
                            reads=[cst.res, gt.res], writes=[pr1], acc=not (ti == 0 and q == 0))
                S.op("vector", lambda e: e.tensor_copy(Gc.v(0, [[8, NT], [1, 8]]), self.ps(1, 0, [[32, NT], [1, 8]])),
                     reads=[pr1], writes=[Gc.res])
                S.op("vector", lambda e: e.memset(GD.v(), 1.0), writes=[GD.res])
                S.op("scalar", lambda e: e.activation(GD.v(1, [[24, NT], [3, 8]]), self.ps(1, 0, [[32, NT], [1, 8]]), AF.Exp),
                     reads=[pr1], writes=[GD.res])
                S.op("vector", lambda e: e.tensor_tensor(t1.v(0, [[8, NT], [1, 8]]), self.ps(1, 8, [[32, NT], [1, 8]]),
                                                         Gc.v(0, [[8, NT], [1, 8]]), ALU.subtract),
                     reads=[pr1, Gc.res], writes=[t1.res])
                S.op("scalar", lambda e: e.activation(GD.v(2, [[24, NT], [3, 8]]), t1.v(0, [[8, NT], [1, 8]]), AF.Exp),
                     reads=[t1.res], writes=[GD.res])
                S.op("scalar", lambda e: e.activation(gend.v(0, [[16, NT], [1, 16]]), self.ps(1, 16, [[32, NT], [1, 16]]), AF.Exp),
                     reads=[pr1], writes=[gend.res])
                S.barrier()
            G = self.GB if kind == "P" else 2
            ZW = NCOL + 3
            Dw = [self.sb(es, "Dw%d" % i, 12 * 128, BF16) for i in range(2)]
            bco = self.sb(es, "bco", 384)
            if kind == "S":
                stc = self.sb(es, "stc", 3072)
            HB = []
            for k in range(G):
                d = {}
                d["zT"] = self.sb(es, "zT%d" % k, 3 * ZW, BF16)
                d["sgn"] = self.sb(es, "sgn%d" % k, NT * 128, BF16)
                d["Sf"] = [self.sb(es, "Sf%d_%d" % (k, i), 128) for i in range(2 if kind == "S" else 1)]
                d["Sb"] = [self.sb(es, "Sb%d_%d" % (k, i), 128, BF16) for i in range(2 if kind == "S" else 1)]
                for nm, F, dt in [("ss", 2, F32), ("rn", 2, F32), ("cols", 4, F32), ("junk", 128, F32), ("tok5", 640, BF16),
                                  ("IDg", 256, BF16), ("feat3", 384, BF16), ("gbc", 128, F32), ("Dm", 128, F32), ("Lam", 128, F32),
                                  ("Nn", 128, BF16), ("NTt", 128, BF16), ("Xb", 128, BF16), ("aqk", 128, BF16), ("nwT", 128, BF16),
                                  ("wv", 128, BF16), ("sso", 2, F32), ("otmp", 128, BF16)]:
                    d[nm] = self.sb(es, "%s%d" % (nm, k), F, dt)
                d["PP"] = [self.sb(es, "PP%d_%d" % (k, i), 256, BF16) for i in range(2)]
                d["X"] = [self.sb(es, "X%d_%d" % (k, i), 128, BF16) for i in range(2)]
                S.op("vector", lambda e, t_=d["IDg"]: e.tensor_copy(t_.v(0, [[1, 128]]), self.idb.v()), reads=[self.idb.res], writes=[d["IDg"].res])
                d["bx"], d["by"] = 2 * k, 2 * k + 1
                d["t5r"] = [Res("t5_%d_%d" % (k, q)) for q in range(5)]
                d.update(gt=gt, bt=bt, Gc=Gc, GD=GD, gend=gend)
                HB.append(d)
            dwi = 0
            groups = [list(range(g0, min(8, g0 + G))) for g0 in range(0, 8, G)]
            for grp in groups:
                for k, hb in enumerate(grp):
                    d = HB[k]
                    zT, sgn, Sf, Sb = d["zT"], d["sgn"], d["Sf"], d["Sb"]
                    dw = Dw[dwi % 2]
                    dwi += 1
                    slab = self.next_slab()
                    for i, off in enumerate([OFF_QB, OFF_KB, OFF_VB, OFF_GB]):
                        self.load_slab(slab, self.w_in, off + hb * 128, 128, dst_off=i * 128)
                    for j in range(3):
                        for tap in range(4):
                            S.op("vector", lambda e, j=j, tap=tap, dw=dw, hb=hb: e.tensor_scalar(
                                dw.v((j * 4 + tap) * 128, [[1, 128]]), CI(C_ID), self.wc.v((j * 8 + hb) * 4 + tap, [[1, 1]]), None, ALU.mult),
                                reads=[cst.res, self.wc.res], writes=[dw.res])
                    if kind == "P":
                        S.op("vector", lambda e, zT=zT: e.memset(zT.v(0, [[ZW, 3], [1, 3]]), 0.0), writes=[zT.res])
                    for bi, (c0, n) in enumerate(self.blocks):
                        for j in range(3):
                            bank = (bi * 3 + j) % 2 * 2
                            pr = self.psres[bank]
                            for kc in range(8):
                                S.op("tensor", lambda e, kc=kc, j=j, bank=bank, c0=c0, n=n, slab=slab: e.matmul(
                                    self.ps(bank, 0, [[1, n]]), slab.v(kc * 512 + j * 128, [[1, 128]]), self.xT_v(kc, c0, n),
                                    start=(kc == 0), stop=(kc == 7)),
                                    reads=[slab.res, self.xT.res], writes=[pr], acc=(kc > 0))
                            self.copy(self.evac_eng(), zT.v(j * ZW + 3 + c0, [[1, n]]), self.ps(bank, 0, [[1, n]]), [pr], [zT.res])
                    if kind == "S":
                        for ti in range(NT):
                            S.dma("sync", stc.v(0, [[1, 3072]], 0, 6),
                                  dr(self.sconv, (2 * ti) * 3 * 3072, [[3072, 6], [1, 3072]]), writes=[stc.res])
                            pr = self.psres[0]
                            for j in range(3):
                                S.op("tensor", lambda e, j=j, hb=hb: e.matmul(
                                    self.ps(0, j * 6, [[1, 6]]), stc.v(j * 1024 + hb * 128, [[1, 128]], 0, 6),
                                    cst.v(C_ID, [[1, 6]], 0, 6), start=True, stop=True),
                                    reads=[stc.res, cst.res], writes=[pr], acc=(j > 0))
                            for i in range(2):
                                S.op("vector", lambda e, i=i, ti=ti, zT=zT: e.tensor_copy(
                                    zT.v(ti * 128 + 64 * i, [[ZW, 3], [1, 3]]), self.ps(0, 3 * i, [[6, 3], [1, 3]])),
                                    reads=[pr], writes=[zT.res])
                    segs = [(NCOL - 3, self.bcp, self.b)] if kind == "P" else \
                        [(ti * 128 + 64 * i + 13, self.bcs, 2 * ti + i) for ti in range(NT) for i in range(2)]
                    for (cc, dst, sq) in segs:
                        pr = self.psres[3]
                        for kc in range(8):
                            S.op("tensor", lambda e, kc=kc, cc=cc, slab=slab: e.matmul(
                                self.ps(3, 0, [[1, 384]], 0, 3), self.xT_v(kc, cc, 3), slab.v(kc * 512, [[1, 384]]),
                                start=(kc == 0), stop=(kc == 7)),
                                reads=[slab.res, self.xT.res], writes=[pr], acc=(kc > 0))
                        S.op("vector", lambda e: e.tensor_copy(bco.v(0, [[1, 384]], 0, 3), self.ps(3, 0, [[1, 384]], 0, 3)),
                             reads=[pr], writes=[bco.res])
                        S.dma("sync", dr(dst, sq * 3 * 3072 + hb * 128, [[3072, 3], [1024, 3], [1, 128]]),
                              bco.v(0, [[128, 3], [1, 128]], 0, 3), reads=[bco.res])
                    for bi, (c0, n) in enumerate(self.blocks):
                        for j in range(3):
                            bank = (bi * 3 + j) % 2 * 2
                            pr = self.psres[bank]
                            for tap in range(4):
                                S.op("tensor", lambda e, j=j, tap=tap, bank=bank, c0=c0, n=n, dw=dw, zT=zT: e.matmul(
                                    self.ps(bank, 0, [[1, n]]), dw.v((j * 4 + tap) * 128, [[1, 128]]), zT.v(j * ZW + c0 + tap, [[1, n]]),
                                    start=(tap == 0), stop=(tap == 3)),
                                    reads=[dw.res, zT.res], writes=[pr], acc=(tap > 0))
                            S.op("scalar", lambda e, j=j, bank=bank, c0=c0, n=n, zT=zT: e.activation(
                                zT.v(j * ZW + c0, [[1, n]]), self.ps(bank, 0, [[1, n]]), AF.Silu),
                                reads=[pr], writes=[zT.res])
                    for ti in range(NT):
                        bank = (ti % 2) * 2
                        pr = self.psres[bank]
                        for kc in range(8):
                            S.op("tensor", lambda e, kc=kc, bank=bank, ti=ti, slab=slab: e.matmul(
                                self.ps(bank, 0, [[1, 128]]), self.xT_v(kc, ti * 128, 128), slab.v(kc * 512 + 384, [[1, 128]]),
                                start=(kc == 0), stop=(kc == 7)),
                                reads=[slab.res, self.xT.res], writes=[pr], acc=(kc > 0))
                        S.op("scalar", lambda e, bank=bank, ti=ti, sgn=sgn: e.activation(sgn.v(ti * 128, [[1, 128]]), self.ps(bank, 0, [[1, 128]]), AF.Sigmoid),
                             reads=[pr], writes=[sgn.res])
                    S.op("vector", lambda e, sgn=sgn: e.tensor_tensor(sgn.v(0, [[128, NT], [1, 128]]), sgn.v(0, [[128, NT], [1, 128]]),
                                                                      self.normg_bc.v(0, [[0, NT], [1, 128]]), ALU.mult),
                         reads=[sgn.res, self.normg_bc.res], writes=[sgn.res])
                    if kind == "P":
                        S.op("vector", lambda e, Sf=Sf: e.memset(Sf[0].v(), 0.0), writes=[Sf[0].res])
                        S.op("vector", lambda e, Sb=Sb: e.memset(Sb[0].v(), 0.0), writes=[Sb[0].res])
                def head_stream(hb, k):
                    for ti in range(NT):
                        yield from self.gdn_tile(hb, ti, HB[k])
                gens = [(k * self.stagger, head_stream(hb, k)) for k, hb in enumerate(grp)]
                rnd = 0
                while gens:
                    nxt = []
                    for (st0, g) in gens:
                        if rnd >= st0:
                            try:
                                next(g)
                            except StopIteration:
                                continue
                        nxt.append((st0, g))
                    gens = nxt
                    rnd += 1
                if kind == "P":
                    for k, hb in enumerate(grp):
                        Sf = HB[k]["Sf"]
                        S.dma("sync", dr(self.bsp, ((self.b * 8 + hb) * 128) * 128, [[128, 128], [1, 128]]), Sf[0].v(), reads=[Sf[0].res])
            S.barrier()

    def gdn_tile(self, hb, ti, L):
        S = self.S
        cst = self.cst
        NT, NCOL, kind = self.NT, self.NCOL, self.kind
        ZW = NCOL + 3
        CI = lambda c: cst.v(c, [[1, 128]])
        (zT, Sf, Sb, ss, rn, cols, junk, tok5, IDg, feat3, gbc, Dm, Lam, Nn, NTt, PP, X, Xb, aqk, nwT, wv, sso, otmp,
         gt, bt, Gc, GD, gend, sgn) = [L[k] for k in
                                       "zT Sf Sb ss rn cols junk tok5 IDg feat3 gbc Dm Lam Nn NTt PP X Xb aqk nwT wv sso otmp gt bt Gc GD gend sgn".split()]
        P = self.psres
        ps = self.ps
        t5 = L["t5r"]
        bx, by = L["bx"], L["by"]
        cb = by
        for j in range(3):
            S.op("tensor", lambda e, j=j: e.matmul(ps(bx, j * 128, [[1, 128]]), zT.v(j * ZW + ti * 128, [[1, 128]]), self.idb.v(),
                                                   start=True, stop=True),
                 reads=[zT.res, self.idb.res], writes=[P[bx]], acc=(j > 0))
        yield
        for j in range(2):
            S.op("scalar", lambda e, j=j: e.activation(junk.v(), ps(bx, j * 128, [[1, 128]]), AF.Square, accum_out=ss.v(j, [[1, 1]])),
                 reads=[P[bx]], writes=[junk.res, ss.res])
        S.op("vector", lambda e: e.tensor_scalar(gbc.v(), CI(C_ONES), gt.v(ti * 8 + hb, [[1, 1]]), None, ALU.mult),
             reads=[cst.res, gt.res], writes=[gbc.res])
        S.op("vector", lambda e: e.tensor_scalar(IDg.v(128, [[1, 128]]), self.idb.v(), GD.v((ti * 8 + hb) * 3 + 1, [[1, 1]]), None, ALU.mult),
             reads=[self.idb.res, GD.res], writes=[IDg.res])
        S.op("tensor", lambda e: e.matmul(ps(by, 0, [[1, 128]]), gbc.v(), CI(C_TRI), start=True, stop=True),
             reads=[gbc.res, cst.res], writes=[P[by]])
        yield
        S.op("scalar", lambda e: e.activation(rn.v(), ss.v(), AF.Ln, bias=self.eps_l2.v()), reads=[ss.res, self.eps_l2.res], writes=[rn.res])
        S.op("scalar", lambda e: e.activation(rn.v(), rn.v(), AF.Exp, scale=-0.5), reads=[rn.res], writes=[rn.res])
        yield
        S.op("vector", lambda e: e.tensor_scalar(cols.v(0, [[1, 1]]), rn.v(0, [[1, 1]]), 128.0 ** -0.5, None, ALU.mult),
             reads=[rn.res], writes=[cols.res])
        S.op("vector", lambda e: e.tensor_scalar(cols.v(1, [[1, 3]]), GD.v((ti * 8 + hb) * 3, [[1, 3]]), rn.v(1, [[1, 1]]), None, ALU.mult),
             reads=[rn.res, GD.res], writes=[cols.res])
        srcs = [(0, 0), (1, 1), (2, None), (1, 2), (1, 3)]
        for q, (j, cidx) in enumerate(srcs):
            sc = None if cidx is None else cols.v(cidx, [[1, 1]])
            self.copy("vector", tok5.v(q * 128, [[1, 128]]), ps(bx, j * 128, [[1, 128]]), [P[bx], cols.res], [t5[q]], scale=sc)
        yield
        S.op("tensor", lambda e: e.matmul(ps(by, 128, [[1, 128]]), tok5.v(128, [[1, 128]]), self.idb.v(), start=True, stop=True),
             reads=[t5[1], self.idb.res], writes=[P[by]])
        S.op("tensor", lambda e: e.matmul(ps(by, 256, [[1, 256]]), tok5.v(0, [[1, 128]]), IDg.v(), start=True, stop=True),
             reads=[t5[0], IDg.res], writes=[P[by]], acc=True)
        yield
        self.copy("scalar", feat3.v(), ps(by, 128, [[1, 384]]), [P[by]], [feat3.res])
        S.op("vector", lambda e: e.scalar_tensor_tensor(Dm.v(), ps(by, 0, [[1, 128]]), Gc.v(ti * 8 + hb, [[1, 1]]), CI(C_MNEG),
                                                        ALU.subtract, ALU.add),
             reads=[P[by], Gc.res, cst.res], writes=[Dm.res])
        yield
        S.op("scalar", lambda e: e.activation(Lam.v(), Dm.v(), AF.Exp), reads=[Dm.res], writes=[Lam.res])
        S.op("tensor", lambda e: e.matmul(ps(bx, 0, [[1, 256]]), feat3.v(0, [[1, 128]]), feat3.v(0, [[1, 256]]), start=True, stop=True),
             reads=[feat3.res], writes=[P[bx]])
        yield
        S.op("vector", lambda e: e.tensor_tensor(Dm.v(), ps(bx, 0, [[1, 128]]), Lam.v(), ALU.mult),
             reads=[P[bx], Lam.res], writes=[Dm.res])
        S.op("vector", lambda e: e.scalar_tensor_tensor(Nn.v(), Dm.v(), bt.v((ti * 8 + hb) * 2 + 1, [[1, 1]]), CI(C_STRICT),
                                                        ALU.mult, ALU.mult),
             reads=[Dm.res, bt.res, cst.res], writes=[Nn.res])
        S.op("vector", lambda e: e.tensor_tensor(aqk.v(), ps(bx, 128, [[1, 128]]), Lam.v(), ALU.mult),
             reads=[P[bx], Lam.res], writes=[aqk.res])
        S.op("vector", lambda e: e.tensor_tensor(X[0].v(), Nn.v(), self.idb.v(), ALU.add), reads=[Nn.res, self.idb.res], writes=[X[0].res])
        S.op("tensor", lambda e: e.matmul(ps(cb, 0, [[1, 128]]), Nn.v(), self.idb.v(), start=True, stop=True),
             reads=[Nn.res, self.idb.res], writes=[P[cb]])
        yield
        self.copy("scalar", NTt.v(), ps(cb, 0, [[1, 128]]), [P[cb]], [NTt.res])
        yield
        Pm, PmT = Nn.v(), NTt.v()
        Pres, PTres = Nn.res, NTt.res
        xi = 0
        for lvl in range(1, 6):
            pp = PP[lvl % 2]
            last = (lvl == 5)
            S.op("tensor", lambda e, Pm=Pm, PmT=PmT: e.matmul(ps(cb, 128, [[1, 128]]), Pm, PmT, start=True, stop=True),
                 reads=[Pres, PTres], writes=[P[cb]])
            if not last:
                S.op("tensor", lambda e, Pm=Pm, PmT=PmT: e.matmul(ps(cb, 0, [[1, 128]]), PmT, Pm, start=True, stop=True),
                     reads=[Pres, PTres], writes=[P[cb]], acc=True)
            yield
            if not last:
                self.copy("scalar", pp.v(), ps(cb, 0, [[1, 256]]), [P[cb]], [pp.res])
            else:
                self.copy("scalar", pp.v(128, [[1, 128]]), ps(cb, 128, [[1, 128]]), [P[cb]], [pp.res])
            yield
            Pm, PmT = pp.v(0, [[1, 128]]), pp.v(128, [[1, 128]])
            Pres = PTres = pp.res
            xo, xn = X[xi], X[1 - xi]
            S.op("tensor", lambda e, PmT=PmT, xo=xo: e.matmul(ps(bx, 256, [[1, 128]]), PmT, xo.v(), start=True, stop=True),
                 reads=[pp.res, xo.res], writes=[P[bx]])
            yield
            if not last:
                S.op("vector", lambda e, xo=xo, xn=xn: e.tensor_tensor(xn.v(), ps(bx, 256, [[1, 128]]), xo.v(), ALU.add),
                     reads=[P[bx], xo.res], writes=[xn.res])
            else:
                S.op("vector", lambda e, xo=xo: e.tensor_tensor(Xb.v(), ps(bx, 256, [[1, 128]]), xo.v(), ALU.add),
                     reads=[P[bx], xo.res], writes=[Xb.res])
            yield
            xi = 1 - xi
        S.op("tensor", lambda e: e.matmul(ps(cb, 384, [[1, 128]]), tok5.v(3 * 128, [[1, 128]]), Xb.v(), start=True, stop=True),
             reads=[t5[3], Xb.res], writes=[P[cb]])
        yield
        self.copy("scalar", nwT.v(), ps(cb, 384, [[1, 128]]), [P[cb]], [nwT.res], scale=-1.0)
        yield
        for i in range(2):
            c0 = 64 * i
            si = i if kind == "S" else 0
            if kind == "S":
                sq = 2 * ti + i
                S.dma("sync", Sf[si].v(), dr(self.sssm, ((sq * 8 + hb) * 128) * 128, [[128, 128], [1, 128]]), writes=[Sf[si].res])
                self.copy("scalar", Sb[si].v(), Sf[si].v(), [Sf[si].res], [Sb[si].res])
            sfl, sbl = Sf[si], Sb[si]
            S.op("tensor", lambda e, c0=c0: e.matmul(ps(bx, 256, [[1, 128]], c0, 64), Xb.v(c0, [[1, 64]]), tok5.v(2 * 128, [[1, 128]]),
                                                     start=True, stop=False),
                 reads=[Xb.res, t5[2]], writes=[P[bx]])
            S.op("tensor", lambda e, c0=c0, sbl=sbl: e.matmul(ps(bx, 256, [[1, 128]], c0, 64), nwT.v(c0, [[1, 64]]), sbl.v(),
                                                              start=False, stop=True),
                 reads=[nwT.res, sbl.res], writes=[P[bx]], acc=True)
            yield
            S.op("vector", lambda e, c0=c0: e.tensor_scalar(wv.v(0, [[1, 128]], c0, 64), ps(bx, 256, [[1, 128]], c0, 64),
                                                            bt.v((ti * 8 + hb) * 2, [[1, 1]], c0, 64), None, ALU.mult),
                 reads=[P[bx], bt.res], writes=[wv.res])
            yield
            S.op("tensor", lambda e, c0=c0, sbl=sbl: e.matmul(ps(bx, 0, [[1, 128]], c0, 64), feat3.v(256 + c0, [[1, 64]]), sbl.v(),
                                                              start=True, stop=False),
                 reads=[feat3.res, sbl.res], writes=[P[bx]])
            S.op("tensor", lambda e, c0=c0: e.matmul(ps(bx, 0, [[1, 128]], c0, 64), aqk.v(c0, [[1, 64]], c0, 64), wv.v(0, [[1, 128]], c0, 64),
                                                     start=False, stop=True),
                 reads=[aqk.res, wv.res], writes=[P[bx]], acc=True)
            S.op("tensor", lambda e, c0=c0: e.matmul(ps(bx, 384, [[1, 128]]), tok5.v(4 * 128, [[1, 128]], c0, 64), wv.v(0, [[1, 128]], c0, 64),
                                                     start=True, stop=True),
                 reads=[t5[4], wv.res], writes=[P[bx]], acc=True)
            yield
            S.op("vector", lambda e, sfl=sfl, i=i: e.scalar_tensor_tensor(sfl.v(), sfl.v(), gend.v(ti * 16 + i * 8 + hb, [[1, 1]]),
                                                                          ps(bx, 384, [[1, 128]]), ALU.mult, ALU.add),
                 reads=[sfl.res, gend.res, P[bx]], writes=[sfl.res])
            self.copy("scalar", sbl.v(), sfl.v(), [sfl.res], [sbl.res])
            if kind == "S":
                S.dma("sync", dr(self.bss, ((sq * 8 + hb) * 128) * 128, [[128, 128], [1, 128]]), sfl.v(), reads=[sfl.res])
            yield
        S.op("scalar", lambda e: e.activation(junk.v(), ps(bx, 0, [[1, 128]]), AF.Square, accum_out=sso.v(0, [[1, 1]])),
             reads=[P[bx]], writes=[junk.res, sso.res])
        yield
        S.op("scalar", lambda e: e.activation(sso.v(1, [[1, 1]]), sso.v(0, [[1, 1]]), AF.Ln, bias=self.eps_rms.v(), scale=1.0 / 128.0),
             reads=[sso.res, self.eps_rms.res], writes=[sso.res])
        S.op("scalar", lambda e: e.activation(sso.v(1, [[1, 1]]), sso.v(1, [[1, 1]]), AF.Exp, scale=-0.5), reads=[sso.res], writes=[sso.res])
        yield
        S.op("vector", lambda e: e.scalar_tensor_tensor(otmp.v(), ps(bx, 0, [[1, 128]]), sso.v(1, [[1, 1]]), sgn.v(ti * 128, [[1, 128]]),
                                                        ALU.mult, ALU.mult),
             reads=[P[bx], sso.res, sgn.res], writes=[otmp.res])
        mr = self.mres[ti]
        S.op("vector", lambda e: e.tensor_tensor(self.merged.v(ti * D + hb * 128, [[1, 128]]), self.merged.v(ti * D + hb * 128, [[1, 128]]),
                                                 otmp.v(), ALU.add),
             reads=[otmp.res, mr], writes=[mr])

    def dump_merged(self):
        S = self.S
        with contextlib.ExitStack() as es:
            st = self.sb(es, "dbgst", D)
            for ti in range(self.NT):
                S.op("vector", lambda e, ti=ti: e.tensor_copy(st.v(), self.merged.v(ti * D, [[1, D]])),
                     reads=[self.mres[ti]], writes=[st.res])
                if self.kind == "P":
                    S.dma("sync", dr(self.dbg_mp, (self.b * self.T + ti * 128) * D, [[D, 128], [1, D]]), st.v(), reads=[st.res])
                else:
                    for i in range(2):
                        S.dma("sync", dr(self.dbg_ms, ((2 * ti + i) * 64) * D, [[D, 64], [1, D]]), st.v(0, [[1, D]], 64 * i, 64), reads=[st.res])
            S.barrier()

    def phaseD(self, es_pass):
        S = self.S
        NT, NCOL, kind = self.NT, self.NCOL, self.kind
        cst = self.cst
        CI = lambda c: cst.v(c, [[1, 128]])
        ps = self.ps
        P = self.psres
        with contextlib.ExitStack() as es:
            self.load_ln_rows(es)
            woutb = self.sb(es, "woutb", 8 * D, BF16)
            for q in range(2):
                S.dma("gpsimd", woutb.v(q * 512, [[D, 8], [1, 512]]),
                      dr(self.wout, q * 512, [[D, 128], [128 * D, 8], [1, 512]]), writes=[woutb.res])
            f1T = self.sb(es, "f1T", 32 * 512, BF16)
            hT = self.sb(es, "hT", 8 * 512, BF16)
            hn = self.sb(es, "hn", 4 * D)
            mT = self.sb(es, "mT", 8 * 128, BF16)
            xt = [self.sb(es, "xtD%d" % i, D) for i in range(2)]
            hp_ = self.sb(es, "hpD", D)
            st = self.sb(es, "stD", 8)
            junk = self.sb(es, "junkD", D)
            rl = self.sb(es, "rlD", 512)
            yo = [self.sb(es, "yoD%d" % i, D) for i in range(2)]
            yi = 0
            for bi, (c0, n) in enumerate(self.blocks):
                nt_b = n // 128
                for tl in range(nt_b):
                    ti = c0 // 128 + tl
                    mr = self.mres[ti]
                    for half in range(2):
                        for q in range(4):
                            kc = half * 4 + q
                            S.op("tensor", lambda e, kc=kc, q=q, half=half: e.matmul(
                                ps(half, q * 128, [[1, 128]]), self.merged.v(ti * D + kc * 128, [[1, 128]]), self.idb.v(),
                                start=True, stop=True),
                                reads=[mr, self.idb.res], writes=[P[half]], acc=(q > 0))
                        self.copy(self.evac_eng(), mT.v(half * 512, [[1, 512]]), ps(half, 0, [[1, 512]]), [P[half]], [mT.res])
                    x_ = xt[ti % 2]
                    self.x_tile_src(x_, ti)
                    for half in range(2):
                        for kc in range(8):
                            S.op("tensor", lambda e, kc=kc, half=half: e.matmul(
                                ps(2 + half, 0, [[1, 512]]), mT.v(kc * 128, [[1, 128]]), woutb.v(kc * D + half * 512, [[1, 512]]),
                                start=(kc == 0), stop=(kc == 7)),
                                reads=[mT.res, woutb.res], writes=[P[2 + half]], acc=(kc > 0))
                        S.op("vector", lambda e, half=half: e.scalar_tensor_tensor(
                            hp_.v(half * 512, [[1, 512]]), x_.v(half * 512, [[1, 512]]), ALPHA, ps(2 + half, 0, [[1, 512]]), ALU.mult, ALU.add),
                            reads=[x_.res, P[2 + half]], writes=[hp_.res])
                    self.layer_norm(hp_, hn.v(tl * D, [[1, D]]), hn.res, st, junk)
                    for half in range(2):
                        for q in range(4):
                            kc = half * 4 + q
                            S.op("tensor", lambda e, kc=kc, q=q, half=half: e.transpose(
                                ps(4 + half, q * 128, [[1, 128]]), hn.v(tl * D + kc * 128, [[1, 128]]), CI(C_ID)),
                                reads=[hn.res, cst.res], writes=[P[4 + half]], acc=(q > 0))
                        for q in range(4):
                            kc = half * 4 + q
                            S.op("vector", lambda e, kc=kc, q=q, half=half: e.tensor_scalar(
                                hT.v(kc * 512 + tl * 128, [[1, 128]]), ps(4 + half, q * 128, [[1, 128]]),
                                self.g1c.v(kc, [[1, 1]]), self.b1c.v(kc, [[1, 1]]), ALU.mult, ALU.add),
                                reads=[P[4 + half], self.g1c.res, self.b1c.res], writes=[hT.res])
                for s in range(8):
                    slab = self.next_slab()
                    self.load_slab(slab, self.wff1, s * 512, 512)
                    for q in range(4):
                        fc = s * 4 + q
                        bank = fc % 2
                        for kc in range(8):
                            S.op("tensor", lambda e, kc=kc, q=q, bank=bank: e.matmul(
                                ps(bank, 0, [[1, n]]), slab.v(kc * 512 + q * 128, [[1, 128]]), hT.v(kc * 512, [[1, n]]),
                                start=(kc == 0), stop=(kc == 7)),
                                reads=[slab.res, hT.res], writes=[P[bank]], acc=(kc > 0))
                        S.op("scalar", lambda e, fc=fc, bank=bank: e.activation(rl.v(0, [[1, n]]), ps(bank, 0, [[1, n]]), AF.Relu,
                                                                               bias=self.b1T.v(fc, [[1, 1]])),
                             reads=[P[bank], self.b1T.res], writes=[rl.res])
                        S.op("vector", lambda e, fc=fc: e.tensor_tensor(f1T.v(fc * 512, [[1, n]]), rl.v(0, [[1, n]]), rl.v(0, [[1, n]]), ALU.mult),
                             reads=[rl.res], writes=[f1T.res])
                for s in range(8):
                    slab = self.next_slab()
                    self.S.dma("gpsimd", slab.v(0, [[D, 4], [1, D]]),
                               dr(self.wff2, s * 512 * D, [[D, 128], [128 * D, 4], [1, D]]), writes=[slab.res])
                    for q in range(4):
                        fc = s * 4 + q
                        for tl in range(nt_b):
                            for half in range(2):
                                bank = tl * 2 + half
                                S.op("tensor", lambda e, fc=fc, q=q, tl=tl, half=half, bank=bank: e.matmul(
                                    ps(bank, 0, [[1, 512]]), f1T.v(fc * 512 + tl * 128, [[1, 128]]), slab.v(q * D + half * 512, [[1, 512]]),
                                    start=(fc == 0), stop=(fc == 31)),
                                    reads=[f1T.res, slab.res], writes=[P[bank]], acc=(fc > 0))
                for tl in range(nt_b):
                    ti = c0 // 128 + tl
                    S.op("vector", lambda e, tl=tl: e.tensor_tensor(hp_.v(), hn.v(tl * D, [[1, D]]), self.g1a.v(), ALU.mult),
                         reads=[hn.res, self.g1a.res], writes=[hp_.res])
                    S.op("vector", lambda e: e.tensor_tensor(hp_.v(), hp_.v(), self.c1.v(), ALU.add),
                         reads=[hp_.res, self.c1.res], writes=[hp_.res])
                    for half in range(2):
                        bank = tl * 2 + half
                        S.op("vector", lambda e, half=half, bank=bank: e.tensor_tensor(
                            hp_.v(half * 512, [[1, 512]]), hp_.v(half * 512, [[1, 512]]), ps(bank, 0, [[1, 512]]), ALU.add),
                            reads=[hp_.res, P[bank]], writes=[hp_.res])
                    y = yo[yi % 2]
                    yi += 1
                    self.layer_norm(hp_, y.v(), y.res, st, junk)
                    S.op("vector", lambda e: e.tensor_tensor(y.v(), y.v(), self.g2.v(), ALU.mult), reads=[y.res, self.g2.res], writes=[y.res])
                    S.op("vector", lambda e: e.tensor_tensor(y.v(), y.v(), self.b2.v(), ALU.add), reads=[y.res, self.b2.res], writes=[y.res])
                    if kind == "P":
                        S.dma("sync", dr(self.yp, (self.b * self.T + ti * 128) * D, [[D, 128], [1, D]]), y.v(), reads=[y.res])
                    else:
                        for i in range(2):
                            S.dma("sync", dr(self.ys, ((2 * ti + i) * 16) * D, [[D, 16], [1, D]]), y.v(0, [[1, D]], 64 * i, 16), reads=[y.res])
            S.barrier()

    def layer_norm(self, src, out_ap, out_res, st, junk):
        S = self.S
        S.op("scalar", lambda e: e.activation(junk.v(), src.v(), AF.Copy, accum_out=st.v(0, [[1, 1]])),
             reads=[src.res], writes=[junk.res, st.res])
        S.op("vector", lambda e: e.tensor_scalar(st.v(1, [[1, 1]]), st.v(0, [[1, 1]]), -1.0 / D, None, ALU.mult),
             reads=[st.res], writes=[st.res])
        S.op("vector", lambda e: e.tensor_scalar(src.v(), src.v(), st.v(1, [[1, 1]]), None, ALU.add),
             reads=[src.res, st.res], writes=[src.res])
        S.op("scalar", lambda e: e.activation(junk.v(), src.v(), AF.Square, accum_out=st.v(2, [[1, 1]])),
             reads=[src.res], writes=[junk.res, st.res])
        S.op("scalar", lambda e: e.activation(st.v(3, [[1, 1]]), st.v(2, [[1, 1]]), AF.Ln, bias=self.eps_ln.v(), scale=1.0 / D),
             reads=[st.res, self.eps_ln.res], writes=[st.res])
        S.op("scalar", lambda e: e.activation(st.v(3, [[1, 1]]), st.v(3, [[1, 1]]), AF.Exp, scale=-0.5), reads=[st.res], writes=[st.res])
        S.op("vector", lambda e: e.tensor_scalar(out_ap, src.v(), st.v(3, [[1, 1]]), None, ALU.mult),
             reads=[src.res, st.res], writes=[out_res])


_CACHE = {}


def _get_nc(NBP, T, NBS, debug=False):
    key = (NBP, T, NBS, debug)
    if key not in _CACHE:
        b = Builder(NBP, T, NBS, debug)
        _CACHE[key] = b.build()
    return _CACHE[key]


def make_in_maps(inp, n_cores, NBP, T, NBS):
    f = lambda a: np.ascontiguousarray(np.asarray(a, np.float32))
    consts = make_consts()
    bP, bS, bN = make_bias_tables(np.asarray(inp["a_rel_bias"])[0])
    shared = dict(
        w_in=f(inp["w_in"][0]), wconv=f(inp["w_b_conv"][0]), alog=f(inp["b_a_log"]).reshape(1, 8),
        dtb=f(inp["b_dt_bias"]).reshape(1, 8), normg=f(inp["b_norm_g"]).reshape(1, 128),
        biasP=bP.reshape(16 * 128, 640), biasS=bS.reshape(16 * 128, 256), biasN=bN.reshape(16 * 64, 64),
        wkv=f(inp["w_mem_kv"][0]), wout=f(inp["w_out"][0]), ln1g=f(inp["ln1_g"]).reshape(1, D), ln1b=f(inp["ln1_b"]).reshape(1, D),
        wff1=f(inp["w_ff1"][0]), bff1=f(inp["b_ff1"]).reshape(32, 128), wff2=f(inp["w_ff2"][0]), bff2=f(inp["b_ff2"]).reshape(1, D),
        ln2g=f(inp["ln2_g"]).reshape(1, D), ln2b=f(inp["ln2_b"]).reshape(1, D), consts=consts)
    maps = []
    for c in range(n_cores):
        ps_, ss_ = slice(c * NBP, (c + 1) * NBP), slice(c * NBS, (c + 1) * NBS)
        m = dict(shared)
        m["xp"] = f(inp["x_prompt"][ps_]).reshape(NBP * T, D)
        m["xs"] = f(inp["x_sample"][ss_]).reshape(NBS * 16, D)
        m["cak"] = f(inp["cache_a_k"][0, ss_]).reshape(NBS * LC, D)
        m["cav"] = f(inp["cache_a_v"][0, ss_]).reshape(NBS * LC, D)
        m["sconv"] = f(inp["state_b_conv"][0, ss_]).reshape(NBS * 3, 3072)
        m["sssm"] = f(inp["state_b_ssm"][0, ss_]).reshape(NBS * 8 * 128, 128)
        m["cmk"] = f(inp["cache_mem_k"][0, ss_]).reshape(NBS * 256, D)
        m["cmv"] = f(inp["cache_mem_v"][0, ss_]).reshape(NBS * 256, D)
        m["memp"] = f(inp["mem_prompt"][ps_]).reshape(NBP * 256, D)
        maps.append(m)
    return maps


def assemble(results, n_cores, NBP, T, NBS):
    cat = lambda k: np.concatenate([np.asarray(r[k]) for r in results], axis=0)
    B, BS = n_cores * NBP, n_cores * NBS
    yp = cat("yp").reshape(B, T, D)
    ys = cat("ys").reshape(BS, 16, D)
    akp = cat("akp").reshape(1, B, 512, 16, 64)
    avp = cat("avp").reshape(1, B, 512, 16, 64)
    bcp = cat("bcp").reshape(1, B, 3, 3072)
    bsp = cat("bsp").reshape(1, B, 8, 128, 128)
    mkp = cat("mkp").reshape(1, B, 256, 4, 256)
    mvp = cat("mvp").reshape(1, B, 256, 4, 256)
    aks = cat("aks").reshape(1, BS, 512, 16, 64)
    avs = cat("avs").reshape(1, BS, 512, 16, 64)
    bcs = cat("bcs").reshape(1, BS, 3, 3072)
    bss = cat("bss").reshape(1, BS, 8, 128, 128)
    return (yp, ys, akp, avp, bcp, bsp, mkp, mvp, aks, avs, bcs, bss)


def kernel(**inputs):
    n_cores = 8
    B, T = inputs["x_prompt"].shape[0], inputs["x_prompt"].shape[1]
    BS = inputs["x_sample"].shape[0]
    NBP, NBS = B // n_cores, BS // n_cores
    nc = _get_nc(NBP, T, NBS)
    maps = make_in_maps(inputs, n_cores, NBP, T, NBS)
    res = run_bass_kernel_spmd(nc, maps, core_ids=list(range(n_cores)))
    return assemble(res.results, n_cores, NBP, T, NBS)
```

```python
import contextlib
import numpy as np
import concourse.bass as bass
import concourse.mybir as mybir
from concourse.bass_utils import run_bass_kernel_spmd

F32 = mybir.dt.float32
BF16 = mybir.dt.bfloat16
AF = mybir.ActivationFunctionType
ALU = mybir.AluOpType

D = 1024
IN_COLS = 10256
OFF_QA, OFF_KA, OFF_VA = 0, 1024, 2048
OFF_QB, OFF_KB, OFF_VB = 3072, 4096, 5120
OFF_QC = 6144
OFF_GA, OFF_GB, OFF_GC = 7168, 8192, 9216
OFF_AB = 10240
ALPHA = 2.0 ** 0.25
NEG = -30000.0
LN_EPS = 1e-5
RMS_EPS = 1e-6
L2_EPS = 1e-6
PAST_LEN = 1024
LC = 512

C_ID = 0
C_TRI = 128
C_ONESBD = 256
C_SEL0 = 384
C_SEL1 = 512
C_MNEG = 640
C_STRICT = 768
C_ROWM = 896
C_ONES = 900
C_MASKA = 1028
C_MASKN = 1668
NCONST = 1732


def make_consts():
    c = np.zeros((128, NCONST), np.float32)
    r = np.arange(128)[:, None]
    t = np.arange(128)[None, :]
    same = (r // 64) == (t // 64)
    c[:, C_ID:C_ID + 128] = np.eye(128)
    c[:, C_TRI:C_TRI + 128] = (same & (r <= t))
    c[:, C_ONESBD:C_ONESBD + 128] = same
    c[:, C_SEL0:C_SEL0 + 128] = (r < 64) & (t >= 0)
    c[:, C_SEL1:C_SEL1 + 128] = (r >= 64) & (t >= 0)
    c[:, C_MNEG:C_MNEG + 128] = np.where(same & (r <= t), 0.0, -60000.0)
    c[:, C_STRICT:C_STRICT + 128] = (same & (r < t))
    c[:, C_ROWM] = ((np.arange(128) % 64) < 16)
    c[:, C_ONES:C_ONES + 128] = 1.0
    kk = np.arange(128)[:, None, None]
    j = np.arange(5)[None, :, None]
    qq = np.arange(128)[None, None, :]
    cq = qq // 64
    pos = 128 * j + kk
    valid = (pos >= 64 * cq) & (pos < 576 + 64 * cq)
    c[:, C_MASKA:C_MASKA + 640] = np.where(valid, 0.0, NEG).reshape(128, 640)
    mn = np.zeros((128, 64), np.float32)
    mn[(np.arange(128) % 64) >= 16, :] = NEG
    c[:, C_MASKN:C_MASKN + 64] = mn
    return c


def make_bias_tables(rel_bias):
    rb = np.asarray(rel_bias, np.float32)
    kk = np.arange(128)[:, None, None]
    j5 = np.arange(5)[None, :, None]
    qq = np.arange(128)[None, None, :]
    rel = 512 - 128 * j5 + qq - kk
    idx = np.clip(rel, -128, 128) + 128
    biasP = rb[:, idx].reshape(16, 128, 640)
    j4 = np.arange(4)[None, :, None]
    q64 = np.arange(64)[None, None, :]
    rel = 512 + q64 - 128 * j4 - kk
    idx = np.clip(rel, -128, 128) + 128
    biasS = rb[:, idx].reshape(16, 128, 256)
    k64 = np.arange(64)[:, None]
    q64 = np.arange(64)[None, :]
    idx = np.clip(q64 - k64, -128, 128) + 128
    biasN = rb[:, idx].reshape(16, 64, 64)
    return (np.ascontiguousarray(biasP), np.ascontiguousarray(biasS), np.ascontiguousarray(biasN))


class Res:
    __slots__ = ("name", "w", "r", "ds", "ps")

    def __init__(self, name, ps=False):
        self.name = name
        self.w = None
        self.r = []
        self.ds = {}
        self.ps = ps


class DSem:
    def __init__(self, sem):
        self.sem = sem
        self.cnt = 0


class Sync:
    ENG = ["tensor", "vector", "scalar", "gpsimd", "sync"]

    def __init__(self, nc, n_dma_sems=72):
        self.nc = nc
        self.E = {}
        for n in self.ENG:
            self.E[n] = dict(e=getattr(nc, n), sem=nc.alloc_semaphore(name="s_" + n), cnt=0, seen={})
        self.free_ds = {"hw": [DSem(nc.alloc_semaphore(name="d%d" % i)) for i in range(n_dma_sems)],
                        "sw": [DSem(nc.alloc_semaphore(name="q%d" % i)) for i in range(10)]}
        self.all_ds = self.free_ds["hw"] + self.free_ds["sw"]
        self.owned = []
        self.ninst = 0

    def _wait(self, en, deps):
        E = self.E[en]
        need = {}
        for (sem, val) in deps:
            k = sem.num
            if E["seen"].get(k, 0) >= val:
                continue
            if k not in need or need[k][1] < val:
                need[k] = (sem, val)
        for k, (sem, val) in need.items():
            E["e"].wait_ge(sem, val)
            E["seen"][k] = val
            self.ninst += 1

    @staticmethod
    def _deps(reads, writes, acc, own=None):
        deps = []
        for r in reads:
            if r.w is not None:
                deps.append(r.w)
            if r.ps:
                deps.extend(t for t in r.r if t[0].num != own)
        if not acc:
            for w in writes:
                if w.w is not None:
                    deps.append(w.w)
                deps.extend(w.r)
        return deps

    def op(self, en, fn, reads=(), writes=(), acc=False):
        E = self.E[en]
        self._wait(en, self._deps(reads, writes, acc, E["sem"].num))
        inst = fn(E["e"])
        E["cnt"] += 1
        inst.then_inc(E["sem"], 1)
        tok = (E["sem"], E["cnt"])
        for r in reads:
            r.r.append(tok)
        for w in writes:
            w.w = tok
            if not acc:
                w.r = []
        self.ninst += 1
        return inst

    def dma(self, en, out, in_, reads=(), writes=(), owner=None, **kw):
        E = self.E[en]
        self._wait(en, self._deps(reads, writes, False, None))
        if owner is None:
            owner = (list(writes) + list(reads))[0]
        qk = "sw" if en == "gpsimd" else "hw"
        if qk not in owner.ds:
            owner.ds[qk] = self.free_ds[qk].pop()
            self.owned.append((owner, qk))
        ds = owner.ds[qk]
        ds.cnt += 16
        inst = E["e"].dma_start(out=out, in_=in_, **kw)
        inst.then_inc(ds.sem, 16)
        tok = (ds.sem, ds.cnt)
        for r in reads:
            r.r.append(tok)
        for w in writes:
            w.w = tok
            w.r = []
        self.ninst += 1
        return inst

    def barrier(self, release=True):
        toks = [(self.E[n]["sem"], self.E[n]["cnt"]) for n in self.ENG if self.E[n]["cnt"] > 0]
        toks += [(d.sem, d.cnt) for d in self.all_ds if d.cnt > 0]
        for n in self.ENG:
            self._wait(n, toks)
        if release:
            for (o, qk) in self.owned:
                self.free_ds[qk].append(o.ds.pop(qk))
            self.owned = []


class Tl:
    def __init__(self, h, F, name):
        self.h = h
        self.F = F
        self.res = Res(name)

    def v(self, off=0, dims=None, p0=0, pn=128):
        if dims is None:
            dims = [[1, self.F - off]]
        return bass.AP(tensor=self.h, offset=p0 * self.F + off, ap=[[self.F, pn]] + [list(d) for d in dims])


def dr(t, off, dims):
    return bass.AP(tensor=t.tensor, offset=off, ap=[list(d) for d in dims])


class Builder:
    def __init__(self, NBP, T, NBS, debug=False):
        assert T % 512 == 0 and NBS % 2 == 0
        self.NBP, self.T, self.NBS = NBP, T, NBS
        self.debug = debug
        nc = bass.Bass("TRN2", target_bir_lowering=False)
        self.nc = nc
        self.S = Sync(nc)
        di = lambda n, s: nc.dram_tensor(n, s, F32, kind="ExternalInput").ap()
        do = lambda n, s: nc.dram_tensor(n, s, F32, kind="ExternalOutput").ap()
        self.xp = di("xp", [NBP * T, D])
        self.xs = di("xs", [NBS * 16, D])
        self.cak = di("cak", [NBS * LC, D])
        self.cav = di("cav", [NBS * LC, D])
        self.sconv = di("sconv", [NBS * 3, 3072])
        self.sssm = di("sssm", [NBS * 8 * 128, 128])
        self.cmk = di("cmk", [NBS * 256, D])
        self.cmv = di("cmv", [NBS * 256, D])
        self.memp = di("memp", [NBP * 256, D])
        self.w_in = di("w_in", [D, IN_COLS])
        self.wconv = di("wconv", [4, 3072])
        self.alog = di("alog", [1, 8])
        self.dtb = di("dtb", [1, 8])
        self.normg = di("normg", [1, 128])
        self.biasP = di("biasP", [16 * 128, 640])
        self.biasS = di("biasS", [16 * 128, 256])
        self.biasN = di("biasN", [16 * 64, 64])
        self.wkv = di("wkv", [D, 2048])
        self.wout = di("wout", [D, D])
        self.ln1g = di("ln1g", [1, D])
        self.ln1b = di("ln1b", [1, D])
        self.wff1 = di("wff1", [D, 4096])
        self.bff1 = di("bff1", [32, 128])
        self.wff2 = di("wff2", [4096, D])
        self.bff2 = di("bff2", [1, D])
        self.ln2g = di("ln2g", [1, D])
        self.ln2b = di("ln2b", [1, D])
        self.consts = di("consts", [128, NCONST])
        self.yp = do("yp", [NBP * T, D])
        self.ys = do("ys", [NBS * 16, D])
        self.akp = do("akp", [NBP * 512, D])
        self.avp = do("avp", [NBP * 512, D])
        self.bcp = do("bcp", [NBP * 3, 3072])
        self.bsp = do("bsp", [NBP * 8 * 128, 128])
        self.mkp = do("mkp", [NBP * 256, D])
        self.mvp = do("mvp", [NBP * 256, D])
        self.aks = do("aks", [NBS * LC, D])
        self.avs = do("avs", [NBS * LC, D])
        self.bcs = do("bcs", [NBS * 3, 3072])
        self.bss = do("bss", [NBS * 8 * 128, 128])
        if debug:
            self.dbg_mp = do("dbg_mp", [NBP * T, D])
            self.dbg_ms = do("dbg_ms", [NBS * 64, D])
        self.flip = 0
        self.GB = 4
        self.stagger = 11

    def sb(self, es, name, F, dt=F32):
        self.uid = getattr(self, "uid", 0) + 1
        name = "%s_%d" % (name, self.uid)
        h = es.enter_context(self.nc.sbuf_tensor(name, [128, F], dt))
        return Tl(h, F, name)

    def evac_eng(self):
        self.flip ^= 1
        return "vector" if self.flip else "scalar"

    def copy(self, en, out, in_, reads, writes, scale=None):
        S = self.S
        if en == "scalar":
            if scale is None:
                S.op("scalar", lambda e: e.activation(out, in_, AF.Copy), reads=reads, writes=writes)
            elif isinstance(scale, float):
                S.op("scalar", lambda e: e.activation(out, in_, AF.Copy, scale=scale), reads=reads, writes=writes)
            else:
                S.op("scalar", lambda e: e.activation(out, in_, AF.Copy, scale=scale), reads=reads, writes=writes)
        else:
            if scale is None:
                S.op(en, lambda e: e.tensor_copy(out, in_), reads=reads, writes=writes)
            else:
                S.op(en, lambda e: e.tensor_scalar(out, in_, scale, None, ALU.mult), reads=reads, writes=writes)

    def load_slab(self, slab, src, col0, ncols=512, rows0=0, kc=8, dst_off=0):
        W = src.tensor.shape[1]
        self.S.dma("gpsimd", slab.v(dst_off, [[512, kc], [1, ncols]]),
                   dr(src, rows0 * W + col0, [[W, 128], [128 * W, kc], [1, ncols]]),
                   writes=[slab.res])

    def build(self):
        nc, S = self.nc, self.S
        with contextlib.ExitStack() as es:
            ph = es.enter_context(nc.psum_tensor("ps", [128, 4096], F32))
            self.PS = [None] * 8
            self.psh = ph
            self.psres = [Res("psb%d" % i, ps=True) for i in range(8)]
            self.cst = self.sb(es, "cst", NCONST)
            S.dma("sync", self.cst.v(), dr(self.consts, 0, [[NCONST, 128], [1, NCONST]]), writes=[self.cst.res])
            self.idb = self.sb(es, "idb", 128, BF16)
            S.op("vector", lambda e: e.tensor_copy(self.idb.v(), self.cst.v(C_ID, [[1, 128]])),
                 reads=[self.cst.res], writes=[self.idb.res])
            self.eps_l2 = self.sb(es, "eps_l2", 1)
            self.eps_rms = self.sb(es, "eps_rms", 1)
            self.eps_ln = self.sb(es, "eps_ln", 1)
            for t_, v_ in ((self.eps_l2, L2_EPS), (self.eps_rms, RMS_EPS), (self.eps_ln, LN_EPS)):
                S.op("vector", lambda e, t_=t_, v_=v_: e.memset(t_.v(), v_), writes=[t_.res])
            self.slabs = [self.sb(es, "slab%d" % i, 4096, BF16) for i in range(3)]
            self.slab_i = 0
            self.setup_small(es)
            ok = getattr(self, "only_kind", "PS")
            if "P" in ok:
                for b in range(self.NBP):
                    self.run_pass(es, "P", b)
            if "S" in ok:
                self.run_pass(es, "S", 0)
            S.barrier(release=False)
        return nc

    def next_slab(self):
        s = self.slabs[self.slab_i % 3]
        self.slab_i += 1
        return s

    def ps(self, bank, off=0, dims=None, p0=0, pn=128):
        if dims is None:
            dims = [[1, 512 - off]]
        return bass.AP(tensor=self.psh, offset=p0 * 4096 + bank * 512 + off, ap=[[4096, pn]] + [list(d) for d in dims])

    def setup_small(self, es):
        nc, S = self.nc, self.S
        cst = self.cst
        self.dtb_bc = self.sb(es, "dtb_bc", 8)
        self.nealog = self.sb(es, "nealog", 8)
        self.normg_bc = self.sb(es, "normg_bc", 128)
        S.dma("sync", self.dtb_bc.v(), dr(self.dtb, 0, [[0, 128], [1, 8]]), writes=[self.dtb_bc.res])
        S.dma("sync", self.nealog.v(), dr(self.alog, 0, [[0, 128], [1, 8]]), writes=[self.nealog.res])
        S.dma("sync", self.normg_bc.v(), dr(self.normg, 0, [[0, 128], [1, 128]]), writes=[self.normg_bc.res])
        S.op("scalar", lambda e: e.activation(self.nealog.v(), self.nealog.v(), AF.Exp),
             reads=[self.nealog.res], writes=[self.nealog.res])
        S.op("vector", lambda e: e.tensor_scalar(self.nealog.v(), self.nealog.v(), -1.0, None, ALU.mult),
             reads=[self.nealog.res], writes=[self.nealog.res])
        self.wc = self.sb(es, "wc", 96)
        self.b1T = self.sb(es, "b1T", 32)
        with contextlib.ExitStack() as es2:
            wtok = self.sb(es2, "wtok", 3072)
            S.dma("sync", wtok.v(0, [[1, 3072]], 0, 4), dr(self.wconv, 0, [[3072, 4], [1, 3072]]), writes=[wtok.res])
            pr = self.psres[0]
            for blk in range(24):
                S.op("tensor", lambda e, blk=blk: e.matmul(self.ps(0, blk * 4, [[1, 4]]),
                                                           wtok.v(blk * 128, [[1, 128]], 0, 4),
                                                           cst.v(C_ID, [[1, 4]], 0, 4), start=True, stop=True),
                     reads=[wtok.res, cst.res], writes=[pr], acc=(blk > 0))
            S.op("vector", lambda e: e.tensor_copy(self.wc.v(), self.ps(0, 0, [[1, 96]])), reads=[pr], writes=[self.wc.res])
            btok = self.sb(es2, "btok", 128)
            S.dma("sync", btok.v(0, [[1, 128]], 0, 32), dr(self.bff1, 0, [[128, 32], [1, 128]]), writes=[btok.res])
            pr1 = self.psres[1]
            S.op("tensor", lambda e: e.matmul(self.ps(1, 0, [[1, 32]]), btok.v(0, [[1, 128]], 0, 32),
                                              cst.v(C_ID, [[1, 32]], 0, 32), start=True, stop=True),
                 reads=[btok.res, cst.res], writes=[pr1])
            S.op("vector", lambda e: e.tensor_copy(self.b1T.v(), self.ps(1, 0, [[1, 32]])), reads=[pr1], writes=[self.b1T.res])
            S.barrier()
        self.g1c = self.sb(es, "g1c", 8)
        self.b1c = self.sb(es, "b1c", 8)
        with contextlib.ExitStack() as es2:
            gtok = self.sb(es2, "gtok", 256)
            S.dma("sync", gtok.v(0, [[1, 128]], 0, 8), dr(self.ln1g, 0, [[128, 8], [1, 128]]), writes=[gtok.res])
            S.dma("sync", gtok.v(128, [[1, 128]], 0, 8), dr(self.ln1b, 0, [[128, 8], [1, 128]]), writes=[gtok.res])
            pr = self.psres[2]
            S.op("tensor", lambda e: e.matmul(self.ps(2, 0, [[1, 8]]), gtok.v(0, [[1, 128]], 0, 8),
                                              cst.v(C_ID, [[1, 8]], 0, 8), start=True, stop=True),
                 reads=[gtok.res, cst.res], writes=[pr])
            S.op("tensor", lambda e: e.matmul(self.ps(2, 8, [[1, 8]]), gtok.v(128, [[1, 128]], 0, 8),
                                              cst.v(C_ID, [[1, 8]], 0, 8), start=True, stop=True),
                 reads=[gtok.res, cst.res], writes=[pr], acc=True)
            S.op("vector", lambda e: e.tensor_copy(self.g1c.v(), self.ps(2, 0, [[1, 8]])), reads=[pr], writes=[self.g1c.res])
            S.op("vector", lambda e: e.tensor_copy(self.b1c.v(), self.ps(2, 8, [[1, 8]])), reads=[pr], writes=[self.b1c.res])
            S.barrier()

    def load_ln_rows(self, es):
        S = self.S
        self.g1a = self.sb(es, "g1a", D)
        self.c1 = self.sb(es, "c1", D)
        self.g2 = self.sb(es, "g2", D)
        self.b2 = self.sb(es, "b2", D)
        tmp = self.sb(es, "tmpbc", D)
        S.dma("sync", self.g1a.v(), dr(self.ln1g, 0, [[0, 128], [1, D]]), writes=[self.g1a.res])
        S.dma("sync", self.c1.v(), dr(self.ln1b, 0, [[0, 128], [1, D]]), writes=[self.c1.res])
        S.dma("sync", tmp.v(), dr(self.bff2, 0, [[0, 128], [1, D]]), writes=[tmp.res])
        S.dma("sync", self.g2.v(), dr(self.ln2g, 0, [[0, 128], [1, D]]), writes=[self.g2.res])
        S.dma("sync", self.b2.v(), dr(self.ln2b, 0, [[0, 128], [1, D]]), writes=[self.b2.res])
        S.op("vector", lambda e: e.tensor_scalar(self.g1a.v(), self.g1a.v(), ALPHA, None, ALU.mult),
             reads=[self.g1a.res], writes=[self.g1a.res])
        S.op("vector", lambda e: e.scalar_tensor_tensor(self.c1.v(), self.c1.v(), ALPHA, tmp.v(), ALU.mult, ALU.add),
             reads=[self.c1.res, tmp.res], writes=[self.c1.res])

    def run_pass(self, es_outer, kind, b):
        nc, S = self.nc, self.S
        T = self.T
        NT = (T // 128) if kind == "P" else (self.NBS // 2)
        NCOL = NT * 128
        blocks = [(c, min(512, NCOL - c)) for c in range(0, NCOL, 512)]
        self.kind, self.b, self.NT, self.NCOL, self.blocks = kind, b, NT, NCOL, blocks
        with contextlib.ExitStack() as es:
            self.merged = self.sb(es, "merged", NT * D, BF16)
            self.mres = [Res("mrg%d" % t) for t in range(NT)]
            stop = getattr(self, "stop_at", 99)
            with contextlib.ExitStack() as es1:
                self.xT = self.sb(es1, "xT", 8 * NCOL, BF16)
                if stop >= 1:
                    self.phase0(es1)
                if stop >= 2:
                    self.phaseA(es1)
                if stop >= 3:
                    self.phaseC(es1)
                if stop >= 4:
                    self.phaseB(es1)
                S.barrier()
            if self.debug and stop >= 4:
                self.dump_merged()
            if stop >= 5:
                self.phaseD(es)
            S.barrier()

    def xT_v(self, kc, c0, n):
        return self.xT.v(kc * self.NCOL + c0, [[1, n]])

    def x_tile_src(self, xt, ti, en="sync"):
        S = self.S
        if self.kind == "P":
            S.dma(en, xt.v(), dr(self.xp, (self.b * self.T + ti * 128) * D, [[D, 128], [1, D]]), writes=[xt.res])
        else:
            S.op("vector", lambda e: e.memset(xt.v(), 0.0), writes=[xt.res])
            for i in range(2):
                sq = 2 * ti + i
                S.dma(en, xt.v(0, [[1, D]], 64 * i, 16), dr(self.xs, sq * 16 * D, [[D, 16], [1, D]]), writes=[xt.res])

    def phase0(self, es):
        S = self.S
        with contextlib.ExitStack() as es2:
            xts = [self.sb(es2, "xt%d" % i, D) for i in range(2)]
            for ti in range(self.NT):
                xt = xts[ti % 2]
                self.x_tile_src(xt, ti)
                for half in range(2):
                    bank = (2 * ti + half) % 4
                    pr = self.psres[bank]
                    for q in range(4):
                        kc = half * 4 + q
                        S.op("tensor", lambda e, kc=kc, q=q, bank=bank: e.transpose(
                            self.ps(bank, q * 128, [[1, 128]]), xt.v(kc * 128, [[1, 128]]), self.cst.v(C_ID, [[1, 128]])),
                            reads=[xt.res, self.cst.res], writes=[pr], acc=(q > 0))
                    en = self.evac_eng()
                    self.copy(en, self.xT.v((half * 4) * self.NCOL + ti * 128, [[self.NCOL, 4], [1, 128]]),
                              self.ps(bank, 0, [[128, 4], [1, 128]]), reads=[pr], writes=[self.xT.res])
            S.barrier()

    def phaseA(self, es_pass):
        S = self.S
        NT, NCOL, kind = self.NT, self.NCOL, self.kind
        cst = self.cst
        with contextlib.ExitStack() as es:
            qT = self.sb(es, "qT", NCOL, BF16)
            kT = self.sb(es, "kT", NCOL, BF16)
            vaug = self.sb(es, "vaug", NT * 130, BF16)
            sg = self.sb(es, "sgA", NT * 128, BF16)
            tbf = self.sb(es, "tbf", 2 * 640)
            tb = self.sb(es, "tb", 2 * 640, BF16)
            PT = self.sb(es, "PT", 640, BF16)
            PTs = [self.sb(es, "PTs%d" % i, 640, BF16) for i in range(2)]
            rdens = [self.sb(es, "rdA%d" % i, 1) for i in range(2)]
            kvo = [self.sb(es, "kvo%d" % i, 256) for i in range(2)]
            rden = self.sb(es, "rdenA", 1)
            if kind == "S":
                ckf = self.sb(es, "ckf", 512)
                ckT = self.sb(es, "ckT", 512, BF16)
                cvf = self.sb(es, "cvf", 512)
                cvaug = self.sb(es, "cvaug", 4 * 130, BF16)
                tnf = self.sb(es, "tnf", 2 * 64)
                tn = self.sb(es, "tn", 2 * 64, BF16)
                S.op("vector", lambda e: e.memset(cvaug.v(), 1.0), writes=[cvaug.res])
            S.op("vector", lambda e: e.memset(vaug.v(), 1.0), writes=[vaug.res])
            kvo_i = 0
            for hp in range(8):
                slab = self.next_slab()
                for i, off in enumerate([OFF_QA, OFF_KA, OFF_VA, OFF_GA]):
                    self.load_slab(slab, self.w_in, off + hp * 128, 128, dst_off=i * 128)
                if kind == "P":
                    S.dma("sync", tbf.v(0, [[640, 2], [1, 640]]),
                          dr(self.biasP, (2 * hp) * 128 * 640, [[640, 128], [128 * 640, 2], [1, 640]]), writes=[tbf.res])
                    S.op("vector", lambda e: e.tensor_tensor(tbf.v(0, [[640, 2], [1, 640]]), tbf.v(0, [[640, 2], [1, 640]]),
                                                             cst.v(C_MASKA, [[0, 2], [1, 640]]), ALU.add),
                         reads=[tbf.res, cst.res], writes=[tbf.res])
                    S.op("scalar", lambda e: e.activation(tb.v(0, [[640, 2], [1, 640]]), tbf.v(0, [[640, 2], [1, 640]]), AF.Exp),
                         reads=[tbf.res], writes=[tb.res])
                else:
                    S.dma("sync", tbf.v(0, [[640, 2], [1, 256]]),
                          dr(self.biasS, (2 * hp) * 128 * 256, [[256, 128], [128 * 256, 2], [1, 256]]), writes=[tbf.res])
                    S.op("vector", lambda e: e.tensor_copy(tb.v(0, [[640, 2], [1, 256]]), tbf.v(0, [[640, 2], [1, 256]])),
                         reads=[tbf.res], writes=[tb.res])
                    S.dma("sync", tnf.v(0, [[1, 64]]),
                          dr(self.biasN, (2 * hp) * 64 * 64, [[64, 128], [1, 64]]), writes=[tnf.res])
                    S.op("vector", lambda e: e.tensor_tensor(tn.v(0, [[1, 64]]), tnf.v(0, [[1, 64]]), cst.v(C_MASKN, [[1, 64]]), ALU.add),
                         reads=[tnf.res, cst.res], writes=[tn.res])
                for bi, (c0, n) in enumerate(self.blocks):
                    for j, dst in enumerate([qT, kT]):
                        bank = (2 * bi + j) % 4
                        pr = self.psres[bank]
                        for kc in range(8):
                            S.op("tensor", lambda e, kc=kc, j=j, bank=bank: e.matmul(
                                self.ps(bank, 0, [[1, n]]), slab.v(kc * 512 + j * 128, [[1, 128]]), self.xT_v(kc, c0, n),
                                start=(kc == 0), stop=(kc == 7)),
                                reads=[slab.res, self.xT.res], writes=[pr], acc=(kc > 0))
                        if j == 0:
                            self.copy("scalar", dst.v(c0, [[1, n]]), self.ps(bank, 0, [[1, n]]), [pr], [dst.res], scale=0.125)
                        else:
                            self.copy("vector", dst.v(c0, [[1, n]]), self.ps(bank, 0, [[1, n]]), [pr], [dst.res])
                a_stop = getattr(self, "a_stop", 99)
                if a_stop < 1:
                    continue
                for ti in range(NT):
                    out_tile = (kind == "S") or (ti >= NT - 4)
                    bank = 4 + (ti % 2)
                    pr = self.psres[bank]
                    ncol = 384 if out_tile else 256
                    for kc in range(8):
                        S.op("tensor", lambda e, kc=kc, bank=bank: e.matmul(
                            self.ps(bank, 0, [[1, 256]]), self.xT_v(kc, ti * 128, 128), slab.v(kc * 512 + 256, [[1, 256]]),
                            start=(kc == 0), stop=(kc == 7)),
                            reads=[slab.res, self.xT.res], writes=[pr], acc=(kc > 0))
                    if out_tile:
                        for kc in range(8):
                            S.op("tensor", lambda e, kc=kc, bank=bank: e.matmul(
                                self.ps(bank, 256, [[1, 128]]), self.xT_v(kc, ti * 128, 128), slab.v(kc * 512 + 128, [[1, 128]]),
                                start=(kc == 0), stop=(kc == 7)),
                                reads=[slab.res, self.xT.res], writes=[pr], acc=True)
                    S.op("vector", lambda e, bank=bank: e.tensor_copy(vaug.v(ti * 130, [[65, 2], [1, 64]]),
                                                                      self.ps(bank, 0, [[64, 2], [1, 64]])),
                         reads=[pr], writes=[vaug.res])
                    S.op("scalar", lambda e, bank=bank: e.activation(sg.v(ti * 128, [[1, 128]]), self.ps(bank, 128, [[1, 128]]), AF.Sigmoid),
                         reads=[pr], writes=[sg.res])
                    if out_tile:
                        ko = kvo[kvo_i % 2]
                        kvo_i += 1
                        S.op("vector", lambda e, bank=bank: e.tensor_copy(ko.v(0, [[1, 128]]), self.ps(bank, 256, [[1, 128]])),
                             reads=[pr], writes=[ko.res])
                        S.op("scalar", lambda e, bank=bank: e.activation(ko.v(128, [[1, 128]]), self.ps(bank, 0, [[1, 128]]), AF.Copy),
                             reads=[pr], writes=[ko.res])
                        if kind == "P":
                            r0 = self.b * 512 + (ti - (NT - 4)) * 128
                            S.dma("sync", dr(self.akp, r0 * D + hp * 128, [[D, 128], [1, 128]]), ko.v(0, [[1, 128]]), reads=[ko.res])
                            S.dma("sync", dr(self.avp, r0 * D + hp * 128, [[D, 128], [1, 128]]), ko.v(128, [[1, 128]]), reads=[ko.res])
                        else:
                            for i in range(2):
                                sq = 2 * ti + i
                                r0 = sq * LC + (LC - 16)
                                S.dma("sync", dr(self.aks, r0 * D + hp * 128, [[D, 16], [1, 128]]),
                                      ko.v(0, [[1, 128]], 64 * i, 16), reads=[ko.res])
                                S.dma("sync", dr(self.avs, r0 * D + hp * 128, [[D, 16], [1, 128]]),
                                      ko.v(128, [[1, 128]], 64 * i, 16), reads=[ko.res])
                if a_stop < 2:
                    continue
                for ti in range(NT):
                    mr = self.mres[ti]
                    if kind == "P":
                        jlist = [j for j in range(5) if ti * 128 - 512 + 128 * j >= 0]

                        def unitA(h2, ti=ti, jlist=jlist, mr=mr):
                            pb = 64 * h2
                            b0 = 3 * h2
                            prs = [self.psres[b0], self.psres[b0 + 1]]
                            PTu = PTs[h2]
                            rd = rdens[h2]
                            first = True
                            for j in jlist:
                                kc0 = ti * 128 - 512 + 128 * j
                                bank = b0 if j < 4 else b0 + 1
                                S.op("tensor", lambda e, j=j, kc0=kc0, bank=bank: e.matmul(
                                    self.ps(bank, (j % 4) * 128, [[1, 128]]), kT.v(kc0, [[1, 128]], pb, 64),
                                    qT.v(ti * 128, [[1, 128]], pb, 64), start=True, stop=True),
                                    reads=[kT.res, qT.res], writes=prs, acc=(not first))
                                first = False
                            yield
                            j0, nj = jlist[0], len(jlist)
                            S.op("scalar", lambda e: e.activation(
                                PTu.v(j0 * 128, [[1, nj * 128]]), self.ps(b0, j0 * 128, [[1, nj * 128]]), AF.Exp),
                                reads=prs, writes=[PTu.res])
                            yield
                            S.op("vector", lambda e: e.tensor_tensor(
                                PTu.v(j0 * 128, [[1, nj * 128]]), PTu.v(j0 * 128, [[1, nj * 128]]),
                                tb.v(h2 * 640 + j0 * 128, [[1, nj * 128]]), ALU.mult),
                                reads=[PTu.res, tb.res], writes=[PTu.res])
                            yield
                            po = self.psres[b0 + 2]
                            for idx, j in enumerate(jlist):
                                kt = ti - 4 + j
                                S.op("tensor", lambda e, j=j, kt=kt, idx=idx: e.matmul(
                                    self.ps(b0 + 2, 0, [[1, 65]]), PTu.v(j * 128, [[1, 128]]),
                                    vaug.v(kt * 130 + h2 * 65, [[1, 65]]), start=(idx == 0), stop=(idx == nj - 1)),
                                    reads=[PTu.res, vaug.res], writes=[po], acc=(idx > 0))
                            yield
                            S.op("vector", lambda e: e.reciprocal(rd.v(), self.ps(b0 + 2, 64, [[1, 1]])),
                                 reads=[po], writes=[rd.res])
                            S.op("vector", lambda e: e.scalar_tensor_tensor(
                                self.merged.v(ti * D + (2 * hp + h2) * 64, [[1, 64]]), self.ps(b0 + 2, 0, [[1, 64]]),
                                rd.v(), sg.v(ti * 128 + h2 * 64, [[1, 64]]), ALU.mult, ALU.mult),
                                reads=[po, rd.res, sg.res], writes=[mr])

                        gens = [unitA(0), unitA(1)]
                        while gens:
                            nxt = []
                            for g_ in gens:
                                try:
                                    next(g_)
                                    nxt.append(g_)
                                except StopIteration:
                                    pass
                            gens = nxt
                    else:
                        for i in range(2):
                            sq = 2 * ti + i
                            c0 = 64 * i
                            S.dma("sync", ckf.v(0, [[128, 4], [1, 128]]),
                                  dr(self.cak, sq * LC * D + hp * 128, [[D, 128], [128 * D, 4], [1, 128]]), writes=[ckf.res])
                            S.dma("sync", cvf.v(0, [[128, 4], [1, 128]]),
                                  dr(self.cav, sq * LC * D + hp * 128, [[D, 128], [128 * D, 4], [1, 128]]), writes=[cvf.res])
                            prt = self.psres[4]
                            for j in range(4):
                                S.op("tensor", lambda e, j=j: e.transpose(self.ps(4, j * 128, [[1, 128]]), ckf.v(j * 128, [[1, 128]]),
                                                                          cst.v(C_ID, [[1, 128]])),
                                     reads=[ckf.res, cst.res], writes=[prt], acc=(j > 0))
                            S.op("vector", lambda e: e.tensor_copy(ckT.v(), self.ps(4, 0, [[1, 512]])), reads=[prt], writes=[ckT.res])
                            S.op("vector", lambda e: e.tensor_copy(cvaug.v(0, [[130, 4], [65, 2], [1, 64]]),
                                                                   cvf.v(0, [[128, 4], [64, 2], [1, 64]])),
                                 reads=[cvf.res], writes=[cvaug.res])
                            for h2 in range(2):
                                pb = 64 * h2
                                prs = [self.psres[0], self.psres[1]]
                                for j in range(4):
                                    S.op("tensor", lambda e, j=j, pb=pb: e.matmul(
                                        self.ps(0, j * 64, [[1, 64]]), ckT.v(j * 128, [[1, 128]], pb, 64),
                                        qT.v(ti * 128 + c0, [[1, 64]], pb, 64), start=True, stop=False),
                                        reads=[ckT.res, qT.res], writes=prs, acc=(j > 0))
                                    S.op("tensor", lambda e, j=j, h2=h2: e.matmul(
                                        self.ps(0, j * 64, [[1, 64]]), self.idb.v(),
                                        tb.v(h2 * 640 + j * 64, [[1, 64]]), start=False, stop=True),
                                        reads=[self.idb.res, tb.res], writes=prs, acc=True)
                                S.op("tensor", lambda e, pb=pb: e.matmul(
                                    self.ps(1, 0, [[1, 64]], c0, 64), kT.v(ti * 128 + c0, [[1, 64]], pb, 64),
                                    qT.v(ti * 128 + c0, [[1, 64]], pb, 64), start=True, stop=False),
                                    reads=[kT.res, qT.res], writes=prs, acc=True)
                                S.op("tensor", lambda e, pb=pb: e.matmul(
                                    self.ps(1, 0, [[1, 64]], c0, 64), self.idb.v(pb, [[1, 64]], pb, 64),
                                    tn.v(0, [[1, 64]], pb, 64), start=False, stop=True),
                                    reads=[self.idb.res, tn.res], writes=prs, acc=True)
                                S.op("scalar", lambda e: e.activation(PT.v(0, [[1, 256]]), self.ps(0, 0, [[1, 256]]), AF.Exp),
                                     reads=prs, writes=[PT.res])
                                S.op("scalar", lambda e: e.activation(PT.v(256, [[1, 64]], c0, 64), self.ps(1, 0, [[1, 64]], c0, 64), AF.Exp),
                                     reads=prs, writes=[PT.res])
                                po = self.psres[2 + h2]
                                for j in range(4):
                                    S.op("tensor", lambda e, j=j, h2=h2: e.matmul(
                                        self.ps(2 + h2, 0, [[1, 65]], c0, 64), PT.v(j * 64, [[1, 64]]),
                                        cvaug.v(j * 130 + h2 * 65, [[1, 65]]), start=(j == 0), stop=False),
                                        reads=[PT.res, cvaug.res], writes=[po], acc=(j > 0))
                                S.op("tensor", lambda e, h2=h2: e.matmul(
                                    self.ps(2 + h2, 0, [[1, 65]], c0, 64), PT.v(256, [[1, 64]], c0, 64),
                                    vaug.v(ti * 130 + h2 * 65, [[1, 65]], c0, 64), start=False, stop=True),
                                    reads=[PT.res, vaug.res], writes=[po], acc=True)
                                S.op("vector", lambda e, h2=h2: e.reciprocal(rden.v(0, [[1, 1]], c0, 64), self.ps(2 + h2, 64, [[1, 1]], c0, 64)),
                                     reads=[po], writes=[rden.res])
                                S.op("vector", lambda e, h2=h2: e.scalar_tensor_tensor(
                                    self.merged.v(ti * D + (2 * hp + h2) * 64, [[1, 64]], c0, 64), self.ps(2 + h2, 0, [[1, 64]], c0, 64),
                                    rden.v(0, [[1, 1]], c0, 64), sg.v(ti * 128 + h2 * 64, [[1, 64]], c0, 64), ALU.mult, ALU.mult),
                                    reads=[po, rden.res, sg.res], writes=[mr])
            if kind == "S" and getattr(self, "a_stop", 99) >= 3:
                for sq in range(self.NBS):
                    for src, dst in ((self.cak, self.aks), (self.cav, self.avs)):
                        rr = Res("cpy")
                        for part in range(4):
                            S.dma("sync", dr(dst, (sq * LC + part * 124) * D, [[D, 124], [1, D]]),
                                  dr(src, (sq * LC + 16 + part * 124) * D, [[D, 124], [1, D]]), writes=[rr])
            S.barrier()

    def phaseC(self, es_pass):
        S = self.S
        NT, NCOL, kind = self.NT, self.NCOL, self.kind
        cst = self.cst
        with contextlib.ExitStack() as es:
            mkT = self.sb(es, "mkT", 4 * 2 * 256, BF16)
            mvaug = self.sb(es, "mvaug", 2 * 4 * 257, BF16)
            qcT = self.sb(es, "qcT", 2 * NCOL, BF16)
            sgc = self.sb(es, "sgc", NT * 256, BF16)
            PT = self.sb(es, "PTc", 256, BF16)
            rden = self.sb(es, "rdenC", 1)
            otmp = self.sb(es, "otmpC", 256, BF16)
            S.op("vector", lambda e: e.memset(mvaug.v(), 1.0), writes=[mvaug.res])
            if kind == "P":
                with contextlib.ExitStack() as es2:
                    memf = [self.sb(es2, "memf%d" % i, D) for i in range(2)]
                    memT = self.sb(es2, "memT", 8 * 256, BF16)
                    kvst = [self.sb(es2, "kvst%d" % i, 512) for i in range(2)]
                    for mb in range(2):
                        S.dma("sync", memf[mb].v(), dr(self.memp, (self.b * 256 + mb * 128) * D, [[D, 128], [1, D]]), writes=[memf[mb].res])
                        for half in range(2):
                            bank = 2 * mb + half
                            pr = self.psres[bank]
                            for q in range(4):
                                kc = half * 4 + q
                                S.op("tensor", lambda e, kc=kc, q=q, bank=bank, mb=mb: e.transpose(
                                    self.ps(bank, q * 128, [[1, 128]]), memf[mb].v(kc * 128, [[1, 128]]), cst.v(C_ID, [[1, 128]])),
                                    reads=[memf[mb].res, cst.res], writes=[pr], acc=(q > 0))
                            self.copy(self.evac_eng(), memT.v((half * 4) * 256 + mb * 128, [[256, 4], [1, 128]]),
                                      self.ps(bank, 0, [[128, 4], [1, 128]]), [pr], [memT.res])
                    si = 0
                    for s in range(4):
                        slab = self.next_slab()
                        self.load_slab(slab, self.wkv, s * 512, 512)
                        isv = s // 2
                        h0 = (s % 2) * 2
                        for mb in range(2):
                            bank = 4 + (si % 2)
                            pr = self.psres[bank]
                            for kc in range(8):
                                S.op("tensor", lambda e, kc=kc, bank=bank, mb=mb: e.matmul(
                                    self.ps(bank, 0, [[1, 512]]), memT.v(kc * 256 + mb * 128, [[1, 128]]), slab.v(kc * 512, [[1, 512]]),
                                    start=(kc == 0), stop=(kc == 7)),
                                    reads=[memT.res, slab.res], writes=[pr], acc=(kc > 0))
                            st = kvst[si % 2]
                            si += 1
                            self.copy(self.evac_eng(), st.v(), self.ps(bank, 0, [[1, 512]]), [pr], [st.res])
                            dst = self.mvp if isv else self.mkp
                            S.dma("sync", dr(dst, (self.b * 256 + mb * 128) * D + s % 2 * 512, [[D, 128], [1, 512]]), st.v(), reads=[st.res])
                            if isv:
                                S.op("vector", lambda e, bank=bank, mb=mb, h0=h0: e.tensor_copy(
                                    mvaug.v(mb * 4 * 257 + h0 * 257, [[257, 2], [1, 256]]), self.ps(bank, 0, [[256, 2], [1, 256]])),
                                    reads=[pr], writes=[mvaug.res])
                        if not isv:
                            for cb in range(4):
                                bank = 6 + (cb % 2)
                                pr = self.psres[bank]
                                for kc in range(8):
                                    S.op("tensor", lambda e, kc=kc, bank=bank, cb=cb: e.matmul(
                                        self.ps(bank, 0, [[1, 256]]), slab.v(kc * 512 + cb * 128, [[1, 128]]), memT.v(kc * 256, [[1, 256]]),
                                        start=(kc == 0), stop=(kc == 7)),
                                        reads=[memT.res, slab.res], writes=[pr], acc=(kc > 0))
                                h = h0 + cb // 2
                                dc = cb % 2
                                self.copy(self.evac_eng(), mkT.v((h * 2 + dc) * 256, [[1, 256]]), self.ps(bank, 0, [[1, 256]]), [pr], [mkT.res])
                    S.barrier()
            with contextlib.ExitStack() as es2:
                if kind == "S":
                    cmf = self.sb(es2, "cmf", 2 * D)
                    mkTs = [self.sb(es2, "mkTs%d" % i, 4 * 2 * 256, BF16) for i in range(2)]
                    mvs = [self.sb(es2, "mvs%d" % i, 2 * 4 * 257, BF16) for i in range(2)]
                    for i in range(2):
                        S.op("vector", lambda e, i=i: e.memset(mvs[i].v(), 1.0), writes=[mvs[i].res])

                def load_seq_cache(sq, mk_, mv_):
                    S.dma("sync", cmf.v(0, [[D, 2], [1, D]]),
                          dr(self.cmk, sq * 256 * D, [[D, 128], [128 * D, 2], [1, D]]), writes=[cmf.res])
                    for mb in range(2):
                        for half in range(2):
                            bank = 6 + half
                            prt = self.psres[bank]
                            for q in range(4):
                                cb = half * 4 + q
                                S.op("tensor", lambda e, cb=cb, q=q, bank=bank, mb=mb: e.transpose(
                                    self.ps(bank, q * 128, [[1, 128]]), cmf.v(mb * D + cb * 128, [[1, 128]]), cst.v(C_ID, [[1, 128]])),
                                    reads=[cmf.res, cst.res], writes=[prt], acc=(q > 0))
                            self.copy(self.evac_eng(), mk_.v((half * 4) * 256 + mb * 128, [[256, 4], [1, 128]]),
                                      self.ps(bank, 0, [[128, 4], [1, 128]]), [prt], [mk_.res])
                    S.dma("sync", cmf.v(0, [[D, 2], [1, D]]),
                          dr(self.cmv, sq * 256 * D, [[D, 128], [128 * D, 2], [1, D]]), writes=[cmf.res])
                    S.op("vector", lambda e: e.tensor_copy(mv_.v(0, [[4 * 257, 2], [257, 4], [1, 256]]),
                                                           cmf.v(0, [[D, 2], [256, 4], [1, 256]])),
                         reads=[cmf.res], writes=[mv_.res])

                def headC(hc, tiles):
                    slab = self.next_slab()
                    self.load_slab(slab, self.w_in, OFF_QC + hc * 256, 256, dst_off=0)
                    self.load_slab(slab, self.w_in, OFF_GC + hc * 256, 256, dst_off=256)
                    for bi, (c0, n) in enumerate(self.blocks):
                        for dc in range(2):
                            bank = (2 * bi + dc) % 4
                            pr = self.psres[bank]
                            for kc in range(8):
                                S.op("tensor", lambda e, kc=kc, dc=dc, bank=bank, c0=c0, n=n: e.matmul(
                                    self.ps(bank, 0, [[1, n]]), slab.v(kc * 512 + dc * 128, [[1, 128]]), self.xT_v(kc, c0, n),
                                    start=(kc == 0), stop=(kc == 7)),
                                    reads=[slab.res, self.xT.res], writes=[pr], acc=(kc > 0))
                            self.copy(self.evac_eng(), qcT.v(dc * NCOL + c0, [[1, n]]), self.ps(bank, 0, [[1, n]]), [pr], [qcT.res], scale=0.0625)
                    for ti in tiles:
                        bank = 4 + (ti % 2)
                        pr = self.psres[bank]
                        for kc in range(8):
                            S.op("tensor", lambda e, kc=kc, bank=bank, ti=ti: e.matmul(
                                self.ps(bank, 0, [[1, 256]]), self.xT_v(kc, ti * 128, 128), slab.v(kc * 512 + 256, [[1, 256]]),
                                start=(kc == 0), stop=(kc == 7)),
                                reads=[slab.res, self.xT.res], writes=[pr], acc=(kc > 0))
                        S.op("scalar", lambda e, bank=bank, ti=ti: e.activation(sgc.v(ti * 256, [[1, 256]]), self.ps(bank, 0, [[1, 256]]), AF.Sigmoid),
                             reads=[pr], writes=[sgc.res])
                    for ti in tiles:
                        mr = self.mres[ti]
                        segs = [(0, 128, mkT, mvaug)] if kind == "P" else [(0, 64, mkTs[0], mvs[0]), (64, 64, mkTs[1], mvs[1])]
                        for (c0, nq, mk_, mv_) in segs:
                            prs = self.psres[0]
                            for mb in range(2):
                                for dc in range(2):
                                    S.op("tensor", lambda e, mb=mb, dc=dc, mk_=mk_, c0=c0, nq=nq, ti=ti: e.matmul(
                                        self.ps(0, mb * nq, [[1, nq]]), mk_.v((hc * 2 + dc) * 256 + mb * 128, [[1, 128]]),
                                        qcT.v(dc * NCOL + ti * 128 + c0, [[1, nq]]), start=(dc == 0), stop=(dc == 1)),
                                        reads=[mk_.res, qcT.res], writes=[prs], acc=not (mb == 0 and dc == 0))
                            S.op("scalar", lambda e, nq=nq: e.activation(PT.v(0, [[1, 2 * nq]]), self.ps(0, 0, [[1, 2 * nq]]), AF.Exp),
                                 reads=[prs], writes=[PT.res])
                            po = self.psres[1]
                            for mb in range(2):
                                S.op("tensor", lambda e, mb=mb, mv_=mv_, c0=c0, nq=nq: e.matmul(
                                    self.ps(1, 0, [[1, 257]], c0, nq), PT.v(mb * nq, [[1, nq]]),
                                    mv_.v(mb * 4 * 257 + hc * 257, [[1, 257]]), start=(mb == 0), stop=(mb == 1)),
                                    reads=[PT.res, mv_.res], writes=[po], acc=(mb > 0))
                            S.op("vector", lambda e, c0=c0, nq=nq: e.reciprocal(rden.v(0, [[1, 1]], c0, nq), self.ps(1, 256, [[1, 1]], c0, nq)),
                                 reads=[po], writes=[rden.res])
                            S.op("vector", lambda e, c0=c0, nq=nq, ti=ti: e.scalar_tensor_tensor(
                                otmp.v(0, [[1, 256]], c0, nq), self.ps(1, 0, [[1, 256]], c0, nq),
                                rden.v(0, [[1, 1]], c0, nq), sgc.v(ti * 256, [[1, 256]], c0, nq), ALU.mult, ALU.mult),
                                reads=[po, rden.res, sgc.res], writes=[otmp.res])
                            S.op("vector", lambda e, c0=c0, nq=nq, ti=ti: e.tensor_tensor(
                                self.merged.v(ti * D + hc * 256, [[1, 256]], c0, nq), self.merged.v(ti * D + hc * 256, [[1, 256]], c0, nq),
                                otmp.v(0, [[1, 256]], c0, nq), ALU.add),
                                reads=[otmp.res, mr], writes=[mr])

                if kind == "P":
                    for hc in range(4):
                        headC(hc, list(range(NT)))
                else:
                    for ti in range(NT):
                        for i in range(2):
                            load_seq_cache(2 * ti + i, mkTs[i], mvs[i])
                        for hc in range(4):
                            headC(hc, [ti])
                S.barrier()

    def phaseB(self, es_pass):
        S = self.S
        NT, NCOL, kind = self.NT, self.NCOL, self.kind
        cst = self.cst
        CI = lambda c: cst.v(c, [[1, 128]])
        with contextlib.ExitStack() as es:
            gt = self.sb(es, "gt", NT * 8)
            bt = self.sb(es, "bt", NT * 16)
            Gc = self.sb(es, "Gc", NT * 8)
            GD = self.sb(es, "GD", NT * 24)
            gend = self.sb(es, "gend", NT * 16)
            with contextlib.ExitStack() as es2:
                wab = self.sb(es2, "wab", 8 * 512, BF16)
                t1 = self.sb(es2, "t1", NT * 8)
                t2 = self.sb(es2, "t2", NT * 8)
                self.load_slab(wab, self.w_in, OFF_AB, 16)
                pr = self.psres[0]
                for ti in range(NT):
                    for kc in range(8):
                        S.op("tensor", lambda e, kc=kc, ti=ti: e.matmul(
                            self.ps(0, ti * 16, [[1, 16]]), self.xT_v(kc, ti * 128, 128), wab.v(kc * 512, [[1, 16]]),
                            start=(kc == 0), stop=(kc == 7)),
                            reads=[wab.res, self.xT.res], writes=[pr], acc=not (ti == 0 and kc == 0))
                S.op("vector", lambda e: e.tensor_tensor(t1.v(0, [[8, NT], [1, 8]]), self.ps(0, 0, [[16, NT], [1, 8]]),
                                                         self.dtb_bc.v(0, [[0, NT], [1, 8]]), ALU.add),
                     reads=[pr, self.dtb_bc.res], writes=[t1.res])
                S.op("scalar", lambda e: e.activation(t2.v(), t1.v(), AF.Abs), reads=[t1.res], writes=[t2.res])
                S.op("scalar", lambda e: e.activation(t2.v(), t2.v(), AF.Exp, scale=-1.0), reads=[t2.res], writes=[t2.res])
                S.op("scalar", lambda e: e.activation(t2.v(), t2.v(), AF.Ln, bias=1.0), reads=[t2.res], writes=[t2.res])
                S.op("vector", lambda e: e.scalar_tensor_tensor(t1.v(), t1.v(), 0.0, t2.v(), ALU.max, ALU.add),
                     reads=[t1.res, t2.res], writes=[t1.res])
                S.op("vector", lambda e: e.tensor_tensor(gt.v(0, [[8, NT], [1, 8]]), t1.v(0, [[8, NT], [1, 8]]),
                                                         self.nealog.v(0, [[0, NT], [1, 8]]), ALU.mult),
                     reads=[t1.res, self.nealog.res], writes=[gt.res])
                S.op("scalar", lambda e: e.activation(bt.v(0, [[16, NT], [2, 8]]), self.ps(0, 8, [[16, NT], [1, 8]]), AF.Sigmoid),
                     reads=[pr], writes=[bt.res])
                if kind == "S":
                    S.op("vector", lambda e: e.tensor_scalar(gt.v(), gt.v(), cst.v(C_ROWM, [[1, 1]]), None, ALU.mult),
                         reads=[gt.res, cst.res], writes=[gt.res])
                    S.op("vector", lambda e: e.tensor_scalar(bt.v(0, [[16, NT], [2, 8]]), bt.v(0, [[16, NT], [2, 8]]),
                                                             cst.v(C_ROWM, [[1, 1]]), None, ALU.mult),
                         reads=[bt.res, cst.res], writes=[bt.res])
                S.op("vector", lambda e: e.tensor_scalar(bt.v(1, [[16, NT], [2, 8]]), bt.v(0, [[16, NT], [2, 8]]), -1.0, None, ALU.mult),
                     reads=[bt.res], writes=[bt.res])
                pr1 = self.psres[1]
                for ti in range(NT):
                    for q, cc in enumerate([C_TRI, C_ONESBD, C_SEL0, C_SEL1]):
                        S.op("tensor", lambda e, ti=ti, q=q, cc=cc: e.matmul(
                            self.ps(1, ti * 32 + q * 8, [[1, 8]]), CI(cc), gt.v(ti * 8, [[1, 8]]), start=True, stop=True),
                            reads=[cst.res, gt.res], writes=[pr1], acc=not (ti == 0 and q == 0))
                S.op("vector", lambda e: e.tensor_copy(Gc.v(0, [[8, NT], [1, 8]]), self.ps(1, 0, [[32, NT], [1, 8]])),
                     reads=[pr1], writes=[Gc.res])
                S.op("vector", lambda e: e.memset(GD.v(), 1.0), writes=[GD.res])
                S.op("scalar", lambda e: e.activation(GD.v(1, [[24, NT], [3, 8]]), self.ps(1, 0, [[32, NT], [1, 8]]), AF.Exp),
                     reads=[pr1], writes=[GD.res])
                S.op("vector", lambda e: e.tensor_tensor(t1.v(0, [[8, NT], [1, 8]]), self.ps(1, 8, [[32, NT], [1, 8]]),
                                                         Gc.v(0, [[8, NT], [1, 8]]), ALU.subtract),
                     reads=[pr1, Gc.res], writes=[t1.res])
                S.op("scalar", lambda e: e.activation(GD.v(2, [[24, NT], [3, 8]]), t1.v(0, [[8, NT], [1, 8]]), AF.Exp),
                     reads=[t1.res], writes=[GD.res])
                S.op("scalar", lambda e: e.activation(gend.v(0, [[16, NT], [1, 16]]), self.ps(1, 16, [[32, NT], [1, 16]]), AF.Exp),
                     reads=[pr1], writes=[gend.res])
                S.barrier()
            G = self.GB
            ZW = NCOL + 3
            Dw = [self.sb(es, "Dw%d" % i, 12 * 128, BF16) for i in range(2)]
            bco = self.sb(es, "bco", 384)
            if kind == "S":
                stc = self.sb(es, "stc", 3072)
            HB = []
            for k in range(G):
                d = {}
                d["zT"] = self.sb(es, "zT%d" % k, 3 * ZW, BF16)
                d["sgn"] = self.sb(es, "sgn%d" % k, NT * 128, BF16)
                d["Sf"] = [self.sb(es, "Sf%d_%d" % (k, i), 128) for i in range(2 if kind == "S" else 1)]
                d["Sb"] = [self.sb(es, "Sb%d_%d" % (k, i), 128, BF16) for i in range(2 if kind == "S" else 1)]
                for nm, F, dt in [("ss", 2, F32), ("rn", 2, F32), ("cols", 4, F32), ("junk", 128, F32), ("tok5", 640, BF16),
                                  ("IDg", 256, BF16), ("feat3", 384, BF16), ("gbc", 128, F32), ("Dm", 128, F32), ("Lam", 128, F32),
                                  ("Nn", 128, BF16), ("NTt", 128, BF16), ("Xb", 128, BF16), ("aqk", 128, BF16), ("nwT", 128, BF16),
                                  ("wv", 128, BF16), ("sso", 2, F32), ("otmp", 128, BF16)]:
                    d[nm] = self.sb(es, "%s%d" % (nm, k), F, dt)
                d["PP"] = [None, None]
                d["X"] = [None, None]
                d["PX"] = [self.sb(es, "PX%d_%d" % (k, i), 256, BF16) for i in range(2)]
                d["PTt"] = [self.sb(es, "PTt%d_%d" % (k, i), 128, BF16) for i in range(2)]
                S.op("vector", lambda e, t_=d["IDg"]: e.tensor_copy(t_.v(0, [[1, 128]]), self.idb.v()), reads=[self.idb.res], writes=[d["IDg"].res])
                d["bx"], d["by"] = 2 * k, 2 * k + 1
                d["t5r"] = [Res("t5_%d_%d" % (k, q)) for q in range(5)]
                d.update(gt=gt, bt=bt, Gc=Gc, GD=GD, gend=gend)
                HB.append(d)
            dwi = 0
            groups = [list(range(g0, min(8, g0 + G))) for g0 in range(0, 8, G)]
            for grp in groups:
                ng = len(grp)
                gslab = self.next_slab()
                self.load_slab(gslab, self.w_in, OFF_GB + grp[0] * 128, 128 * ng)
                for ti in range(NT):
                    bank = (ti % 2) * 2
                    pr = self.psres[bank]
                    for kc in range(8):
                        S.op("tensor", lambda e, kc=kc, bank=bank, ti=ti: e.matmul(
                            self.ps(bank, 0, [[1, 128 * ng]]), self.xT_v(kc, ti * 128, 128), gslab.v(kc * 512, [[1, 128 * ng]]),
                            start=(kc == 0), stop=(kc == 7)),
                            reads=[gslab.res, self.xT.res], writes=[pr], acc=(kc > 0))
                    for k in range(ng):
                        sgn_ = HB[k]["sgn"]
                        S.op("scalar", lambda e, bank=bank, ti=ti, sgn_=sgn_, k=k: e.activation(
                            sgn_.v(ti * 128, [[1, 128]]), self.ps(bank, k * 128, [[1, 128]]), AF.Sigmoid),
                            reads=[pr], writes=[sgn_.res])
                for k in range(ng):
                    sgn_ = HB[k]["sgn"]
                    S.op("vector", lambda e, sgn_=sgn_: e.tensor_tensor(sgn_.v(0, [[128, NT], [1, 128]]), sgn_.v(0, [[128, NT], [1, 128]]),
                                                                        self.normg_bc.v(0, [[0, NT], [1, 128]]), ALU.mult),
                         reads=[sgn_.res, self.normg_bc.res], writes=[sgn_.res])
                for k, hb in enumerate(grp):
                    d = HB[k]
                    zT, sgn, Sf, Sb = d["zT"], d["sgn"], d["Sf"], d["Sb"]
                    dw = Dw[dwi % 2]
                    dwi += 1
                    slab = self.next_slab()
                    for i, off in enumerate([OFF_QB, OFF_KB, OFF_VB]):
                        self.load_slab(slab, self.w_in, off + hb * 128, 128, dst_off=i * 128)
                    for j in range(3):
                        for tap in range(4):
                            S.op("vector", lambda e, j=j, tap=tap, dw=dw, hb=hb: e.tensor_scalar(
                                dw.v((j * 4 + tap) * 128, [[1, 128]]), CI(C_ID), self.wc.v((j * 8 + hb) * 4 + tap, [[1, 1]]), None, ALU.mult),
                                reads=[cst.res, self.wc.res], writes=[dw.res])
                    if kind == "P":
                        S.op("vector", lambda e, zT=zT: e.memset(zT.v(0, [[ZW, 3], [1, 3]]), 0.0), writes=[zT.res])
                    for bi, (c0, n) in enumerate(self.blocks):
                        for j in range(3):
                            bank = (bi * 3 + j) % 2 * 2
                            pr = self.psres[bank]
                            for kc in range(8):
                                S.op("tensor", lambda e, kc=kc, j=j, bank=bank, c0=c0, n=n, slab=slab: e.matmul(
                                    self.ps(bank, 0, [[1, n]]), slab.v(kc * 512 + j * 128, [[1, 128]]), self.xT_v(kc, c0, n),
                                    start=(kc == 0), stop=(kc == 7)),
                                    reads=[slab.res, self.xT.res], writes=[pr], acc=(kc > 0))
                            self.copy(self.evac_eng(), zT.v(j * ZW + 3 + c0, [[1, n]]), self.ps(bank, 0, [[1, n]]), [pr], [zT.res])
                    if kind == "S":
                        for ti in range(NT):
                            S.dma("sync", stc.v(0, [[1, 3072]], 0, 6),
                                  dr(self.sconv, (2 * ti) * 3 * 3072, [[3072, 6], [1, 3072]]), writes=[stc.res])
                            pr = self.psres[0]
                            for j in range(3):
                                S.op("tensor", lambda e, j=j, hb=hb: e.matmul(
                                    self.ps(0, j * 6, [[1, 6]]), stc.v(j * 1024 + hb * 128, [[1, 128]], 0, 6),
                                    cst.v(C_ID, [[1, 6]], 0, 6), start=True, stop=True),
                                    reads=[stc.res, cst.res], writes=[pr], acc=(j > 0))
                            for i in range(2):
                                S.op("vector", lambda e, i=i, ti=ti, zT=zT: e.tensor_copy(
                                    zT.v(ti * 128 + 64 * i, [[ZW, 3], [1, 3]]), self.ps(0, 3 * i, [[6, 3], [1, 3]])),
                                    reads=[pr], writes=[zT.res])
                    segs = [(NCOL - 3, self.bcp, self.b)] if kind == "P" else \
                        [(ti * 128 + 64 * i + 13, self.bcs, 2 * ti + i) for ti in range(NT) for i in range(2)]
                    for (cc, dst, sq) in segs:
                        pr = self.psres[3]
                        for kc in range(8):
                            S.op("tensor", lambda e, kc=kc, cc=cc, slab=slab: e.matmul(
                                self.ps(3, 0, [[1, 384]], 0, 3), self.xT_v(kc, cc, 3), slab.v(kc * 512, [[1, 384]]),
                                start=(kc == 0), stop=(kc == 7)),
                                reads=[slab.res, self.xT.res], writes=[pr], acc=(kc > 0))
                        S.op("vector", lambda e: e.tensor_copy(bco.v(0, [[1, 384]], 0, 3), self.ps(3, 0, [[1, 384]], 0, 3)),
                             reads=[pr], writes=[bco.res])
                        S.dma("sync", dr(dst, sq * 3 * 3072 + hb * 128, [[3072, 3], [1024, 3], [1, 128]]),
                              bco.v(0, [[128, 3], [1, 128]], 0, 3), reads=[bco.res])
                    for bi, (c0, n) in enumerate(self.blocks):
                        for j in range(3):
                            bank = (bi * 3 + j) % 2 * 2
                            pr = self.psres[bank]
                            for tap in range(4):
                                S.op("tensor", lambda e, j=j, tap=tap, bank=bank, c0=c0, n=n, dw=dw, zT=zT: e.matmul(
                                    self.ps(bank, 0, [[1, n]]), dw.v((j * 4 + tap) * 128, [[1, 128]]), zT.v(j * ZW + c0 + tap, [[1, n]]),
                                    start=(tap == 0), stop=(tap == 3)),
                                    reads=[dw.res, zT.res], writes=[pr], acc=(tap > 0))
                            S.op("scalar", lambda e, j=j, bank=bank, c0=c0, n=n, zT=zT: e.activation(
                                zT.v(j * ZW + c0, [[1, n]]), self.ps(bank, 0, [[1, n]]), AF.Silu),
                                reads=[pr], writes=[zT.res])
                    if kind == "P":
                        S.op("vector", lambda e, Sf=Sf: e.memset(Sf[0].v(), 0.0), writes=[Sf[0].res])
                        S.op("vector", lambda e, Sb=Sb: e.memset(Sb[0].v(), 0.0), writes=[Sb[0].res])
                def head_stream(hb, k):
                    for ti in range(NT):
                        yield from self.gdn_tile(hb, ti, HB[k])
                gens = [(k * self.stagger, head_stream(hb, k)) for k, hb in enumerate(grp)]
                rnd = 0
                while gens:
                    nxt = []
                    for (st0, g) in gens:
                        if rnd >= st0:
                            try:
                                next(g)
                            except StopIteration:
                                continue
                        nxt.append((st0, g))
                    gens = nxt
                    rnd += 1
                if kind == "P":
                    for k, hb in enumerate(grp):
                        Sf = HB[k]["Sf"]
                        S.dma("sync", dr(self.bsp, ((self.b * 8 + hb) * 128) * 128, [[128, 128], [1, 128]]), Sf[0].v(), reads=[Sf[0].res])
            S.barrier()

    def gdn_tile(self, hb, ti, L):
        S = self.S
        cst = self.cst
        NT, NCOL, kind = self.NT, self.NCOL, self.kind
        ZW = NCOL + 3
        CI = lambda c: cst.v(c, [[1, 128]])
        (zT, Sf, Sb, ss, rn, cols, junk, tok5, IDg, feat3, gbc, Dm, Lam, Nn, NTt, PP, X, Xb, aqk, nwT, wv, sso, otmp,
         gt, bt, Gc, GD, gend, sgn) = [L[k] for k in
                                       "zT Sf Sb ss rn cols junk tok5 IDg feat3 gbc Dm Lam Nn NTt PP X Xb aqk nwT wv sso otmp gt bt Gc GD gend sgn".split()]
        P = self.psres
        ps = self.ps
        t5 = L["t5r"]
        PTt = L["PTt"]
        bx, by = L["bx"], L["by"]
        cb = by
        for j in range(3):
            S.op("tensor", lambda e, j=j: e.matmul(ps(bx, j * 128, [[1, 128]]), zT.v(j * ZW + ti * 128, [[1, 128]]), self.idb.v(),
                                                   start=True, stop=True),
                 reads=[zT.res, self.idb.res], writes=[P[bx]], acc=(j > 0))
        yield
        for j in range(2):
            S.op("scalar", lambda e, j=j: e.activation(junk.v(), ps(bx, j * 128, [[1, 128]]), AF.Square, accum_out=ss.v(j, [[1, 1]])),
                 reads=[P[bx]], writes=[junk.res, ss.res])
        S.op("vector", lambda e: e.tensor_scalar(gbc.v(), CI(C_ONES), gt.v(ti * 8 + hb, [[1, 1]]), None, ALU.mult),
             reads=[cst.res, gt.res], writes=[gbc.res])
        S.op("vector", lambda e: e.tensor_scalar(IDg.v(128, [[1, 128]]), self.idb.v(), GD.v((ti * 8 + hb) * 3 + 1, [[1, 1]]), None, ALU.mult),
             reads=[self.idb.res, GD.res], writes=[IDg.res])
        S.op("tensor", lambda e: e.matmul(ps(by, 0, [[1, 128]]), gbc.v(), CI(C_TRI), start=True, stop=True),
             reads=[gbc.res, cst.res], writes=[P[by]])
        yield
        S.op("scalar", lambda e: e.activation(rn.v(), ss.v(), AF.Ln, bias=self.eps_l2.v()), reads=[ss.res, self.eps_l2.res], writes=[rn.res])
        S.op("scalar", lambda e: e.activation(rn.v(), rn.v(), AF.Exp, scale=-0.5), reads=[rn.res], writes=[rn.res])
        yield
        S.op("vector", lambda e: e.tensor_scalar(cols.v(0, [[1, 1]]), rn.v(0, [[1, 1]]), 128.0 ** -0.5, None, ALU.mult),
             reads=[rn.res], writes=[cols.res])
        S.op("vector", lambda e: e.tensor_scalar(cols.v(1, [[1, 3]]), GD.v((ti * 8 + hb) * 3, [[1, 3]]), rn.v(1, [[1, 1]]), None, ALU.mult),
             reads=[rn.res, GD.res], writes=[cols.res])
        srcs = [(0, 0), (1, 1), (2, None), (1, 2), (1, 3)]
        for q, (j, cidx) in enumerate(srcs):
            sc = None if cidx is None else cols.v(cidx, [[1, 1]])
            self.copy("vector", tok5.v(q * 128, [[1, 128]]), ps(bx, j * 128, [[1, 128]]), [P[bx], cols.res], [t5[q]], scale=sc)
        yield
        S.op("tensor", lambda e: e.matmul(ps(by, 128, [[1, 128]]), tok5.v(128, [[1, 128]]), self.idb.v(), start=True, stop=True),
             reads=[t5[1], self.idb.res], writes=[P[by]])
        S.op("tensor", lambda e: e.matmul(ps(by, 256, [[1, 256]]), tok5.v(0, [[1, 128]]), IDg.v(), start=True, stop=True),
             reads=[t5[0], IDg.res], writes=[P[by]], acc=True)
        yield
        self.copy("scalar", feat3.v(), ps(by, 128, [[1, 384]]), [P[by]], [feat3.res])
        S.op("vector", lambda e: e.scalar_tensor_tensor(Dm.v(), ps(by, 0, [[1, 128]]), Gc.v(ti * 8 + hb, [[1, 1]]), CI(C_MNEG),
                                                        ALU.subtract, ALU.add),
             reads=[P[by], Gc.res, cst.res], writes=[Dm.res])
        yield
        S.op("scalar", lambda e: e.activation(Lam.v(), Dm.v(), AF.Exp), reads=[Dm.res], writes=[Lam.res])
        S.op("tensor", lambda e: e.matmul(ps(bx, 0, [[1, 256]]), feat3.v(0, [[1, 128]]), feat3.v(0, [[1, 256]]), start=True, stop=True),
             reads=[feat3.res], writes=[P[bx]])
        yield
        S.op("vector", lambda e: e.tensor_tensor(Dm.v(), ps(bx, 0, [[1, 128]]), Lam.v(), ALU.mult),
             reads=[P[bx], Lam.res], writes=[Dm.res])
        S.op("vector", lambda e: e.scalar_tensor_tensor(Nn.v(), Dm.v(), bt.v((ti * 8 + hb) * 2 + 1, [[1, 1]]), CI(C_STRICT),
                                                        ALU.mult, ALU.mult),
             reads=[Dm.res, bt.res, cst.res], writes=[Nn.res])
        S.op("vector", lambda e: e.tensor_tensor(aqk.v(), ps(bx, 128, [[1, 128]]), Lam.v(), ALU.mult),
             reads=[P[bx], Lam.res], writes=[aqk.res])
        PX = L["PX"]
        S.op("vector", lambda e: e.tensor_tensor(PX[0].v(128, [[1, 128]]), Nn.v(), self.idb.v(), ALU.add),
             reads=[Nn.res, self.idb.res], writes=[PX[0].res])
        S.op("tensor", lambda e: e.matmul(ps(cb, 0, [[1, 128]]), Nn.v(), self.idb.v(), start=True, stop=True),
             reads=[Nn.res, self.idb.res], writes=[P[cb]])
        yield
        self.copy("scalar", NTt.v(), ps(cb, 0, [[1, 128]]), [P[cb]], [NTt.res])
        yield
        S.op("tensor", lambda e: e.matmul(ps(cb, 0, [[1, 128]]), NTt.v(), Nn.v(), start=True, stop=True),
             reads=[Nn.res, NTt.res], writes=[P[cb]])
        S.op("tensor", lambda e: e.matmul(ps(cb, 256, [[1, 128]]), Nn.v(), NTt.v(), start=True, stop=True),
             reads=[Nn.res, NTt.res], writes=[P[cb]], acc=True)
        yield
        S.op("scalar", lambda e: e.activation(PX[0].v(0, [[1, 128]]), ps(cb, 0, [[1, 128]]), AF.Copy), reads=[P[cb]], writes=[PX[0].res])
        S.op("scalar", lambda e: e.activation(PTt[0].v(), ps(cb, 256, [[1, 128]]), AF.Copy), reads=[P[cb]], writes=[PTt[0].res])
        yield
        cur = 0
        for lvl in range(1, 6):
            last = (lvl == 5)
            px, pt = PX[cur], PTt[cur]
            pxn, ptn = PX[1 - cur], PTt[1 - cur]
            if not last:
                S.op("tensor", lambda e, px=px, pt=pt: e.matmul(ps(cb, 0, [[1, 256]]), pt.v(), px.v(), start=True, stop=True),
                     reads=[px.res, pt.res], writes=[P[cb]])
                S.op("tensor", lambda e, px=px, pt=pt: e.matmul(ps(cb, 256, [[1, 128]]), px.v(0, [[1, 128]]), pt.v(), start=True, stop=True),
                     reads=[px.res, pt.res], writes=[P[cb]], acc=True)
                yield
                S.op("scalar", lambda e, pxn=pxn: e.activation(pxn.v(0, [[1, 128]]), ps(cb, 0, [[1, 128]]), AF.Copy),
                     reads=[P[cb]], writes=[pxn.res])
                S.op("scalar", lambda e, ptn=ptn: e.activation(ptn.v(), ps(cb, 256, [[1, 128]]), AF.Copy), reads=[P[cb]], writes=[ptn.res])
                yield
                S.op("vector", lambda e, px=px, pxn=pxn: e.tensor_tensor(pxn.v(128, [[1, 128]]), ps(cb, 128, [[1, 128]]), px.v(128, [[1, 128]]), ALU.add),
                     reads=[P[cb], px.res], writes=[pxn.res])
                yield
            else:
                S.op("tensor", lambda e, px=px, pt=pt: e.matmul(ps(cb, 128, [[1, 128]]), pt.v(), px.v(128, [[1, 128]]), start=True, stop=True),
                     reads=[px.res, pt.res], writes=[P[cb]])
                yield
                S.op("vector", lambda e, px=px: e.tensor_tensor(Xb.v(), ps(cb, 128, [[1, 128]]), px.v(128, [[1, 128]]), ALU.add),
                     reads=[P[cb], px.res], writes=[Xb.res])
                yield
            cur = 1 - cur
        S.op("tensor", lambda e: e.matmul(ps(cb, 384, [[1, 128]]), tok5.v(3 * 128, [[1, 128]]), Xb.v(), start=True, stop=True),
             reads=[t5[3], Xb.res], writes=[P[cb]])
        yield
        self.copy("scalar", nwT.v(), ps(cb, 384, [[1, 128]]), [P[cb]], [nwT.res], scale=-1.0)
        yield
        for i in range(2):
            c0 = 64 * i
            si = i if kind == "S" else 0
            if kind == "S":
                sq = 2 * ti + i
                S.dma("sync", Sf[si].v(), dr(self.sssm, ((sq * 8 + hb) * 128) * 128, [[128, 128], [1, 128]]), writes=[Sf[si].res])
                self.copy("scalar", Sb[si].v(), Sf[si].v(), [Sf[si].res], [Sb[si].res])
            sfl, sbl = Sf[si], Sb[si]
            S.op("tensor", lambda e, c0=c0: e.matmul(ps(bx, 256, [[1, 128]], c0, 64), Xb.v(c0, [[1, 64]]), tok5.v(2 * 128, [[1, 128]]),
                                                     start=True, stop=False),
                 reads=[Xb.res, t5[2]], writes=[P[bx]])
            S.op("tensor", lambda e, c0=c0, sbl=sbl: e.matmul(ps(bx, 256, [[1, 128]], c0, 64), nwT.v(c0, [[1, 64]]), sbl.v(),
                                                              start=False, stop=True),
                 reads=[nwT.res, sbl.res], writes=[P[bx]], acc=True)
            yield
            S.op("vector", lambda e, c0=c0: e.tensor_scalar(wv.v(0, [[1, 128]], c0, 64), ps(bx, 256, [[1, 128]], c0, 64),
                                                            bt.v((ti * 8 + hb) * 2, [[1, 1]], c0, 64), None, ALU.mult),
                 reads=[P[bx], bt.res], writes=[wv.res])
            yield
            S.op("tensor", lambda e, c0=c0, sbl=sbl: e.matmul(ps(bx, 0, [[1, 128]], c0, 64), feat3.v(256 + c0, [[1, 64]]), sbl.v(),
                                                              start=True, stop=False),
                 reads=[feat3.res, sbl.res], writes=[P[bx]])
            S.op("tensor", lambda e, c0=c0: e.matmul(ps(bx, 0, [[1, 128]], c0, 64), aqk.v(c0, [[1, 64]], c0, 64), wv.v(0, [[1, 128]], c0, 64),
                                                     start=False, stop=True),
                 reads=[aqk.res, wv.res], writes=[P[bx]], acc=True)
            S.op("tensor", lambda e, c0=c0: e.matmul(ps(bx, 384, [[1, 128]]), tok5.v(4 * 128, [[1, 128]], c0, 64), wv.v(0, [[1, 128]], c0, 64),
                                                     start=True, stop=True),
                 reads=[t5[4], wv.res], writes=[P[bx]], acc=True)
            yield
            S.op("vector", lambda e, sfl=sfl, i=i: e.scalar_tensor_tensor(sfl.v(), sfl.v(), gend.v(ti * 16 + i * 8 + hb, [[1, 1]]),
                                                                          ps(bx, 384, [[1, 128]]), ALU.mult, ALU.add),
                 reads=[sfl.res, gend.res, P[bx]], writes=[sfl.res])
            self.copy("scalar", sbl.v(), sfl.v(), [sfl.res], [sbl.res])
            if kind == "S":
                S.dma("sync", dr(self.bss, ((sq * 8 + hb) * 128) * 128, [[128, 128], [1, 128]]), sfl.v(), reads=[sfl.res])
            yield
        S.op("scalar", lambda e: e.activation(junk.v(), ps(bx, 0, [[1, 128]]), AF.Square, accum_out=sso.v(0, [[1, 1]])),
             reads=[P[bx]], writes=[junk.res, sso.res])
        yield
        S.op("scalar", lambda e: e.activation(sso.v(1, [[1, 1]]), sso.v(0, [[1, 1]]), AF.Ln, bias=self.eps_rms.v(), scale=1.0 / 128.0),
             reads=[sso.res, self.eps_rms.res], writes=[sso.res])
        S.op("scalar", lambda e: e.activation(sso.v(1, [[1, 1]]), sso.v(1, [[1, 1]]), AF.Exp, scale=-0.5), reads=[sso.res], writes=[sso.res])
        yield
        S.op("vector", lambda e: e.scalar_tensor_tensor(otmp.v(), ps(bx, 0, [[1, 128]]), sso.v(1, [[1, 1]]), sgn.v(ti * 128, [[1, 128]]),
                                                        ALU.mult, ALU.mult),
             reads=[P[bx], sso.res, sgn.res], writes=[otmp.res])
        mr = self.mres[ti]
        S.op("vector", lambda e: e.tensor_tensor(self.merged.v(ti * D + hb * 128, [[1, 128]]), self.merged.v(ti * D + hb * 128, [[1, 128]]),
                                                 otmp.v(), ALU.add),
             reads=[otmp.res, mr], writes=[mr])

    def dump_merged(self):
        S = self.S
        with contextlib.ExitStack() as es:
            st = self.sb(es, "dbgst", D)
            for ti in range(self.NT):
                S.op("vector", lambda e, ti=ti: e.tensor_copy(st.v(), self.merged.v(ti * D, [[1, D]])),
                     reads=[self.mres[ti]], writes=[st.res])
                if self.kind == "P":
                    S.dma("sync", dr(self.dbg_mp, (self.b * self.T + ti * 128) * D, [[D, 128], [1, D]]), st.v(), reads=[st.res])
                else:
                    for i in range(2):
                        S.dma("sync", dr(self.dbg_ms, ((2 * ti + i) * 64) * D, [[D, 64], [1, D]]), st.v(0, [[1, D]], 64 * i, 64), reads=[st.res])
            S.barrier()

    def phaseD(self, es_pass):
        S = self.S
        NT, NCOL, kind = self.NT, self.NCOL, self.kind
        cst = self.cst
        CI = lambda c: cst.v(c, [[1, 128]])
        ps = self.ps
        P = self.psres
        with contextlib.ExitStack() as es:
            self.load_ln_rows(es)
            woutb = self.sb(es, "woutb", 8 * D, BF16)
            for q in range(2):
                S.dma("gpsimd", woutb.v(q * 512, [[D, 8], [1, 512]]),
                      dr(self.wout, q * 512, [[D, 128], [128 * D, 8], [1, 512]]), writes=[woutb.res])
            f1T = self.sb(es, "f1T", 32 * 512, BF16)
            hT = self.sb(es, "hT", 8 * 512, BF16)
            hn = self.sb(es, "hn", 4 * D)
            mT = self.sb(es, "mT", 8 * 128, BF16)
            xt = [self.sb(es, "xtD%d" % i, D) for i in range(2)]
            hp_ = self.sb(es, "hpD", D)
            st = self.sb(es, "stD", 8)
            junk = self.sb(es, "junkD", D)
            rl = self.sb(es, "rlD", 512)
            yo = [self.sb(es, "yoD%d" % i, D) for i in range(2)]
            yi = 0
            for bi, (c0, n) in enumerate(self.blocks):
                nt_b = n // 128
                for tl in range(nt_b):
                    ti = c0 // 128 + tl
                    mr = self.mres[ti]
                    for half in range(2):
                        for q in range(4):
                            kc = half * 4 + q
                            S.op("tensor", lambda e, kc=kc, q=q, half=half: e.matmul(
                                ps(half, q * 128, [[1, 128]]), self.merged.v(ti * D + kc * 128, [[1, 128]]), self.idb.v(),
                                start=True, stop=True),
                                reads=[mr, self.idb.res], writes=[P[half]], acc=(q > 0))
                        self.copy(self.evac_eng(), mT.v(half * 512, [[1, 512]]), ps(half, 0, [[1, 512]]), [P[half]], [mT.res])
                    x_ = xt[ti % 2]
                    self.x_tile_src(x_, ti)
                    for half in range(2):
                        for kc in range(8):
                            S.op("tensor", lambda e, kc=kc, half=half: e.matmul(
                                ps(2 + half, 0, [[1, 512]]), mT.v(kc * 128, [[1, 128]]), woutb.v(kc * D + half * 512, [[1, 512]]),
                                start=(kc == 0), stop=(kc == 7)),
                                reads=[mT.res, woutb.res], writes=[P[2 + half]], acc=(kc > 0))
                        S.op("vector", lambda e, half=half: e.scalar_tensor_tensor(
                            hp_.v(half * 512, [[1, 512]]), x_.v(half * 512, [[1, 512]]), ALPHA, ps(2 + half, 0, [[1, 512]]), ALU.mult, ALU.add),
                            reads=[x_.res, P[2 + half]], writes=[hp_.res])
                    self.layer_norm(hp_, hn.v(tl * D, [[1, D]]), hn.res, st, junk)
                    for half in range(2):
                        for q in range(4):
                            kc = half * 4 + q
                            S.op("tensor", lambda e, kc=kc, q=q, half=half: e.transpose(
                                ps(4 + half, q * 128, [[1, 128]]), hn.v(tl * D + kc * 128, [[1, 128]]), CI(C_ID)),
                                reads=[hn.res, cst.res], writes=[P[4 + half]], acc=(q > 0))
                        for q in range(4):
                            kc = half * 4 + q
                            S.op("vector", lambda e, kc=kc, q=q, half=half: e.tensor_scalar(
                                hT.v(kc * 512 + tl * 128, [[1, 128]]), ps(4 + half, q * 128, [[1, 128]]),
                                self.g1c.v(kc, [[1, 1]]), self.b1c.v(kc, [[1, 1]]), ALU.mult, ALU.add),
                                reads=[P[4 + half], self.g1c.res, self.b1c.res], writes=[hT.res])
                for s in range(8):
                    slab = self.next_slab()
                    self.load_slab(slab, self.wff1, s * 512, 512)
                    for q in range(4):
                        fc = s * 4 + q
                        bank = fc % 2
                        for kc in range(8):
                            S.op("tensor", lambda e, kc=kc, q=q, bank=bank: e.matmul(
                                ps(bank, 0, [[1, n]]), slab.v(kc * 512 + q * 128, [[1, 128]]), hT.v(kc * 512, [[1, n]]),
                                start=(kc == 0), stop=(kc == 7)),
                                reads=[slab.res, hT.res], writes=[P[bank]], acc=(kc > 0))
                        S.op("scalar", lambda e, fc=fc, bank=bank: e.activation(rl.v(0, [[1, n]]), ps(bank, 0, [[1, n]]), AF.Relu,
                                                                               bias=self.b1T.v(fc, [[1, 1]])),
                             reads=[P[bank], self.b1T.res], writes=[rl.res])
                        S.op("vector", lambda e, fc=fc: e.tensor_tensor(f1T.v(fc * 512, [[1, n]]), rl.v(0, [[1, n]]), rl.v(0, [[1, n]]), ALU.mult),
                             reads=[rl.res], writes=[f1T.res])
                for s in range(8):
                    slab = self.next_slab()
                    self.S.dma("gpsimd", slab.v(0, [[D, 4], [1, D]]),
                               dr(self.wff2, s * 512 * D, [[D, 128], [128 * D, 4], [1, D]]), writes=[slab.res])
                    for q in range(4):
                        fc = s * 4 + q
                        for tl in range(nt_b):
                            for half in range(2):
                                bank = tl * 2 + half
                                S.op("tensor", lambda e, fc=fc, q=q, tl=tl, half=half, bank=bank: e.matmul(
                                    ps(bank, 0, [[1, 512]]), f1T.v(fc * 512 + tl * 128, [[1, 128]]), slab.v(q * D + half * 512, [[1, 512]]),
                                    start=(fc == 0), stop=(fc == 31)),
                                    reads=[f1T.res, slab.res], writes=[P[bank]], acc=(fc > 0))
                for tl in range(nt_b):
                    ti = c0 // 128 + tl
                    S.op("vector", lambda e, tl=tl: e.tensor_tensor(hp_.v(), hn.v(tl * D, [[1, D]]), self.g1a.v(), ALU.mult),
                         reads=[hn.res, self.g1a.res], writes=[hp_.res])
                    S.op("vector", lambda e: e.tensor_tensor(hp_.v(), hp_.v(), self.c1.v(), ALU.add),
                         reads=[hp_.res, self.c1.res], writes=[hp_.res])
                    for half in range(2):
                        bank = tl * 2 + half
                        S.op("vector", lambda e, half=half, bank=bank: e.tensor_tensor(
                            hp_.v(half * 512, [[1, 512]]), hp_.v(half * 512, [[1, 512]]), ps(bank, 0, [[1, 512]]), ALU.add),
                            reads=[hp_.res, P[bank]], writes=[hp_.res])
                    y = yo[yi % 2]
                    yi += 1
                    self.layer_norm(hp_, y.v(), y.res, st, junk)
                    S.op("vector", lambda e: e.tensor_tensor(y.v(), y.v(), self.g2.v(), ALU.mult), reads=[y.res, self.g2.res], writes=[y.res])
                    S.op("vector", lambda e: e.tensor_tensor(y.v(), y.v(), self.b2.v(), ALU.add), reads=[y.res, self.b2.res], writes=[y.res])
                    if kind == "P":
                        S.dma("sync", dr(self.yp, (self.b * self.T + ti * 128) * D, [[D, 128], [1, D]]), y.v(), reads=[y.res])
                    else:
                        for i in range(2):
                            S.dma("sync", dr(self.ys, ((2 * ti + i) * 16) * D, [[D, 16], [1, D]]), y.v(0, [[1, D]], 64 * i, 16), reads=[y.res])
            S.barrier()

    def layer_norm(self, src, out_ap, out_res, st, junk):
        S = self.S
        S.op("scalar", lambda e: e.activation(junk.v(), src.v(), AF.Copy, accum_out=st.v(0, [[1, 1]])),
             reads=[src.res], writes=[junk.res, st.res])
        S.op("vector", lambda e: e.tensor_scalar(st.v(1, [[1, 1]]), st.v(0, [[1, 1]]), -1.0 / D, None, ALU.mult),
             reads=[st.res], writes=[st.res])
        S.op("vector", lambda e: e.tensor_scalar(src.v(), src.v(), st.v(1, [[1, 1]]), None, ALU.add),
             reads=[src.res, st.res], writes=[src.res])
        S.op("scalar", lambda e: e.activation(junk.v(), src.v(), AF.Square, accum_out=st.v(2, [[1, 1]])),
             reads=[src.res], writes=[junk.res, st.res])
        S.op("scalar", lambda e: e.activation(st.v(3, [[1, 1]]), st.v(2, [[1, 1]]), AF.Ln, bias=self.eps_ln.v(), scale=1.0 / D),
             reads=[st.res, self.eps_ln.res], writes=[st.res])
        S.op("scalar", lambda e: e.activation(st.v(3, [[1, 1]]), st.v(3, [[1, 1]]), AF.Exp, scale=-0.5), reads=[st.res], writes=[st.res])
        S.op("vector", lambda e: e.tensor_scalar(out_ap, src.v(), st.v(3, [[1, 1]]), None, ALU.mult),
             reads=[src.res, st.res], writes=[out_res])


_CACHE = {}


def _get_nc(NBP, T, NBS, debug=False):
    key = (NBP, T, NBS, debug)
    if key not in _CACHE:
        b = Builder(NBP, T, NBS, debug)
        _CACHE[key] = b.build()
    return _CACHE[key]


def make_in_maps(inp, n_cores, NBP, T, NBS):
    f = lambda a: np.ascontiguousarray(np.asarray(a, np.float32))
    consts = make_consts()
    bP, bS, bN = make_bias_tables(np.asarray(inp["a_rel_bias"])[0])
    shared = dict(
        w_in=f(inp["w_in"][0]), wconv=f(inp["w_b_conv"][0]), alog=f(inp["b_a_log"]).reshape(1, 8),
        dtb=f(inp["b_dt_bias"]).reshape(1, 8), normg=f(inp["b_norm_g"]).reshape(1, 128),
        biasP=bP.reshape(16 * 128, 640), biasS=bS.reshape(16 * 128, 256), biasN=bN.reshape(16 * 64, 64),
        wkv=f(inp["w_mem_kv"][0]), wout=f(inp["w_out"][0]), ln1g=f(inp["ln1_g"]).reshape(1, D), ln1b=f(inp["ln1_b"]).reshape(1, D),
        wff1=f(inp["w_ff1"][0]), bff1=f(inp["b_ff1"]).reshape(32, 128), wff2=f(inp["w_ff2"][0]), bff2=f(inp["b_ff2"]).reshape(1, D),
        ln2g=f(inp["ln2_g"]).reshape(1, D), ln2b=f(inp["ln2_b"]).reshape(1, D), consts=consts)
    maps = []
    for c in range(n_cores):
        ps_, ss_ = slice(c * NBP, (c + 1) * NBP), slice(c * NBS, (c + 1) * NBS)
        m = dict(shared)
        m["xp"] = f(inp["x_prompt"][ps_]).reshape(NBP * T, D)
        m["xs"] = f(inp["x_sample"][ss_]).reshape(NBS * 16, D)
        m["cak"] = f(inp["cache_a_k"][0, ss_]).reshape(NBS * LC, D)
        m["cav"] = f(inp["cache_a_v"][0, ss_]).reshape(NBS * LC, D)
        m["sconv"] = f(inp["state_b_conv"][0, ss_]).reshape(NBS * 3, 3072)
        m["sssm"] = f(inp["state_b_ssm"][0, ss_]).reshape(NBS * 8 * 128, 128)
        m["cmk"] = f(inp["cache_mem_k"][0, ss_]).reshape(NBS * 256, D)
        m["cmv"] = f(inp["cache_mem_v"][0, ss_]).reshape(NBS * 256, D)
        m["memp"] = f(inp["mem_prompt"][ps_]).reshape(NBP * 256, D)
        maps.append(m)
    return maps


def assemble(results, n_cores, NBP, T, NBS):
    cat = lambda k: np.concatenate([np.asarray(r[k]) for r in results], axis=0)
    B, BS = n_cores * NBP, n_cores * NBS
    yp = cat("yp").reshape(B, T, D)
    ys = cat("ys").reshape(BS, 16, D)
    akp = cat("akp").reshape(1, B, 512, 16, 64)
    avp = cat("avp").reshape(1, B, 512, 16, 64)
    bcp = cat("bcp").reshape(1, B, 3, 3072)
    bsp = cat("bsp").reshape(1, B, 8, 128, 128)
    mkp = cat("mkp").reshape(1, B, 256, 4, 256)
    mvp = cat("mvp").reshape(1, B, 256, 4, 256)
    aks = cat("aks").reshape(1, BS, 512, 16, 64)
    avs = cat("avs").reshape(1, BS, 512, 16, 64)
    bcs = cat("bcs").reshape(1, BS, 3, 3072)
    bss = cat("bss").reshape(1, BS, 8, 128, 128)
    return (yp, ys, akp, avp, bcp, bsp, mkp, mvp, aks, avs, bcs, bss)


def kernel(**inputs):
    n_cores = 8
    B, T = inputs["x_prompt"].shape[0], inputs["x_prompt"].shape[1]
    BS = inputs["x_sample"].shape[0]
    NBP, NBS = B // n_cores, BS // n_cores
    nc = _get_nc(NBP, T, NBS)
    maps = make_in_maps(inputs, n_cores, NBP, T, NBS)
    res = run_bass_kernel_spmd(nc, maps, core_ids=list(range(n_cores)))
    return assemble(res.results, n_cores, NBP, T, NBS)
```
